# Optimizing a Trainium2 kernel written in Bass

```python
import jax, jax.numpy as jnp
from jax import lax
import numpy as np

D_MODEL = 2048
BATCH = 4
SEQ = 8192
DEPTH = 1

MIX_WIDTH = D_MODEL
NSA_HEAD_DIM = 64
NSA_HEADS = (MIX_WIDTH // 2) // NSA_HEAD_DIM
NSA_KV_HEADS = max(1, NSA_HEADS // 4)
NSA_GROUP = NSA_HEADS // NSA_KV_HEADS
NSA_WIDTH = NSA_HEADS * NSA_HEAD_DIM
NSA_KV_WIDTH = NSA_KV_HEADS * NSA_HEAD_DIM
CMP_LEN = 32
CMP_STRIDE = 16
CMP_HIDDEN = 128
SLC_BLOCK = 64
SLC_TOPK = 16
SLC_LOCAL = 2
WINDOW = 512
NSA_QBLOCK = 64
SLC_FORCE_SCORE = 1e4
NEG_INF = -1e30
RET_HEADS = 8
RET_WIDTH = MIX_WIDTH - NSA_WIDTH
RET_V_DIM = RET_WIDTH // RET_HEADS
RET_QK_DIM = RET_V_DIM
RET_CHUNK = 128
IN_SIZES = (NSA_WIDTH,) + (NSA_KV_WIDTH,) * 6 + (3 * NSA_HEADS, RET_HEADS * RET_QK_DIM, RET_HEADS * RET_QK_DIM, RET_WIDTH, RET_WIDTH)
IN_WIDTH = NSA_WIDTH + 6 * NSA_KV_WIDTH + 3 * NSA_HEADS + 2 * RET_HEADS * RET_QK_DIM + 2 * RET_WIDTH
PEER_HEADS = 8
PEER_N_KEYS = 128
PEER_EXPERTS = PEER_N_KEYS * PEER_N_KEYS
PEER_TOPK = 16
PEER_QUERY_DIM = 256
PEER_CHUNK = 128
EPS = 1e-6

kernel_name = "hymba_nsa_retention_peer_adaln"

F32 = jnp.float32


def rmsnorm(x, g):
    xf = x.astype(F32)
    y = xf * lax.rsqrt(jnp.mean(xf * xf, axis=-1, keepdims=True) + EPS)
    return (y * g.astype(F32)).astype(x.dtype)


def alibi_slopes(n):
    return jnp.exp2(-8.0 * (jnp.arange(n, dtype=F32) + 1.0) / n)


def split_cols(a, sizes):
    idx = []
    acc = 0
    for s in sizes[:-1]:
        acc += s
        idx.append(acc)
    return jnp.split(a, idx, axis=-1)


def nsa_compress(k, pe, w1, w2, idx):
    B, n_cmp, L = k.shape[0], idx.shape[0], idx.shape[1]
    blk = k[:, idx] + pe[:, None, :]
    blk = jnp.swapaxes(blk, 2, 3).reshape(B, n_cmp, NSA_KV_HEADS, L * NSA_HEAD_DIM)
    out = jax.nn.gelu(blk @ w1) @ w2
    return jnp.swapaxes(out, 1, 2)


def nsa_attention(q, kc, vc, ks, vs, kw, vw, gates, pe_k, pe_v, ck_w1, ck_w2, cv_w1, cv_w2):
    B, S = q.shape[0], q.shape[1]
    G, R, dk, QB = NSA_KV_HEADS, NSA_GROUP, NSA_HEAD_DIM, NSA_QBLOCK
    scale = dk ** -0.5
    dt = q.dtype
    n_cmp = (S - CMP_LEN) // CMP_STRIDE + 1
    cmp_idx = np.arange(n_cmp)[:, None] * CMP_STRIDE + np.arange(CMP_LEN)[None, :]
    cmp_end = jnp.asarray(cmp_idx[:, -1], jnp.int32)
    Kc = nsa_compress(kc, pe_k, ck_w1, ck_w2, cmp_idx)
    Vc = nsa_compress(vc, pe_v, cv_w1, cv_w2, cmp_idx)
    n_slc = S // SLC_BLOCK
    top_n = min(SLC_TOPK, n_slc)
    slc_start = np.arange(n_slc) * SLC_BLOCK
    overlap = jnp.asarray(((cmp_idx[:, :1] < slc_start[None, :] + SLC_BLOCK)
                           & (cmp_idx[:, -1:] >= slc_start[None, :])).astype(np.float32))
    Ks = ks.reshape(B, n_slc, SLC_BLOCK, G, dk).transpose(0, 3, 1, 2, 4)
    Vs = vs.reshape(B, n_slc, SLC_BLOCK, G, dk).transpose(0, 3, 1, 2, 4)
    Kw = jnp.pad(kw, ((0, 0), (WINDOW, 0), (0, 0), (0, 0)))
    Vw = jnp.pad(vw, ((0, 0), (WINDOW, 0), (0, 0), (0, 0)))
    slopes = alibi_slopes(NSA_HEADS).reshape(G, R)[:, :, None, None]
    nqb = S // QB
    qb = q.reshape(B, nqb, QB, G, R, dk).transpose(1, 0, 3, 4, 2, 5)
    gb = gates.reshape(B, nqb, QB, G, R, 3).transpose(1, 0, 3, 4, 2, 5)
    b_ix = jnp.arange(B)[:, None, None, None]
    g_ix = jnp.arange(G)[None, :, None, None]
    blk_ids = jnp.arange(n_slc)
    in_blk = jnp.arange(SLC_BLOCK)
    win_off = jnp.arange(WINDOW + QB)

    def query_block(args):
        i, qi, gi = args
        t = i * QB + jnp.arange(QB)
        d_c = t[:, None] - cmp_end[None, :]
        valid_c = d_c >= 0
        s_c = jnp.einsum('bgrqd,bgnd->bgrqn', qi, Kc).astype(F32) * scale - slopes * d_c.astype(F32)
        p_c = jnp.where(valid_c, jax.nn.softmax(jnp.where(valid_c, s_c, NEG_INF), axis=-1), 0.0)
        o_c = jnp.einsum('bgrqn,bgnd->bgrqd', p_c.astype(dt), Vc)
        imp = jnp.einsum('bgrqn,nm->bgqm', p_c, overlap)
        back = (t // SLC_BLOCK)[:, None] - blk_ids[None, :]
        valid_s = back >= 0
        forced = valid_s & ((blk_ids[None, :] == 0) | (back < SLC_LOCAL))
        imp = jnp.where(forced, SLC_FORCE_SCORE, jnp.where(valid_s, imp, -1.0))
        _, sel = lax.top_k(imp, top_n)
        K_sel = Ks[b_ix, g_ix, sel].reshape(B, G, QB, top_n * SLC_BLOCK, dk)
        V_sel = Vs[b_ix, g_ix, sel].reshape(B, G, QB, top_n * SLC_BLOCK, dk)
        pos_s = (sel[..., None] * SLC_BLOCK + in_blk).reshape(B, G, QB, top_n * SLC_BLOCK)
        d_s = (t[None, None, :, None] - pos_s)[:, :, None]
        s_s = jnp.einsum('bgrqd,bgqkd->bgrqk', qi, K_sel).astype(F32) * scale - slopes * d_s.astype(F32)
        p_s = jax.nn.softmax(jnp.where(d_s >= 0, s_s, NEG_INF), axis=-1)
        o_s = jnp.einsum('bgrqk,bgqkd->bgrqd', p_s.astype(dt), V_sel)
        K_w = lax.dynamic_slice_in_dim(Kw, i * QB, WINDOW + QB, axis=1)
        V_w = lax.dynamic_slice_in_dim(Vw, i * QB, WINDOW + QB, axis=1)
        pos_w = i * QB - WINDOW + win_off
        d_w = t[:, None] - pos_w[None, :]
        valid_w = (d_w >= 0) & (d_w < WINDOW) & (pos_w[None, :] >= 0)
        s_w = jnp.einsum('bgrqd,bkgd->bgrqk', qi, K_w).astype(F32) * scale - slopes * d_w.astype(F32)
        p_w = jax.nn.softmax(jnp.where(valid_w, s_w, NEG_INF), axis=-1)
        o_w = jnp.einsum('bgrqk,bkgd->bgrqd', p_w.astype(dt), V_w)
        gw = jax.nn.sigmoid(gi.astype(F32))
        o = gw[..., 0:1] * o_c + gw[..., 1:2] * o_s + gw[..., 2:3] * o_w
        return o.astype(dt)

    out = lax.map(query_block, (jnp.arange(nqb), qb, gb))
    return out.transpose(1, 0, 4, 2, 3, 5).reshape(B, S, NSA_HEADS, dk)


def retention(q, k, v, gate, g_out):
    B, S = q.shape[0], q.shape[1]
    H, dk, dv, C = RET_HEADS, RET_QK_DIM, RET_V_DIM, RET_CHUNK
    nC = S // C
    dt = q.dtype
    lg = jnp.log1p(-jnp.exp2(-5.0 - jnp.arange(H, dtype=F32)))
    pos = jnp.arange(C, dtype=F32)
    diff = pos[:, None] - pos[None, :]
    intra_decay = jnp.where(diff >= 0, jnp.exp(lg[:, None, None] * jnp.maximum(diff, 0.0)), 0.0)
    k_decay = jnp.exp(lg[:, None] * (C - 1.0 - pos))
    q_decay = jnp.exp(lg[:, None] * (pos + 1.0))
    chunk_decay = jnp.exp(lg * C)

    def to_chunks(a):
        return a.reshape(B, nC, C, H, a.shape[-1]).transpose(0, 3, 1, 2, 4).astype(F32)

    qc = to_chunks(q)
    kc = to_chunks(k) * (dk ** -0.5)
    vc = to_chunks(v)
    att = jnp.einsum('bhncd,bhnsd->bhncs', qc, kc) * intra_decay[:, None]
    o_intra = jnp.einsum('bhncs,bhnse->bhnce', att, vc)
    kv = jnp.einsum('bhncd,bhnce->nbhde', kc * k_decay[:, None, :, None], vc)

    def step(state, kv_n):
        return state * chunk_decay[None, :, None, None] + kv_n, state

    _, r_prev = lax.scan(step, jnp.zeros((B, H, dk, dv), F32), kv)
    o_cross = jnp.einsum('bhncd,nbhde->bhnce', qc * q_decay[:, None, :, None], r_prev)
    o = (o_intra + o_cross).transpose(0, 2, 3, 1, 4).reshape(B, S, H, dv)
    mu = jnp.mean(o, axis=-1, keepdims=True)
    var = jnp.mean(jnp.square(o - mu), axis=-1, keepdims=True)
    y = (o - mu) * lax.rsqrt(var + EPS) * g_out.astype(F32)
    return (jax.nn.silu(gate.astype(F32)) * y.reshape(B, S, H * dv)).astype(dt)


def peer_ffn(h, w_q, sub_keys, w_u, w_v):
    B, S, D = h.shape
    T = B * S
    K = PEER_TOPK
    dt = h.dtype
    hf = h.reshape(T, D)
    q = (hf @ w_q).reshape(T, PEER_HEADS, 2, PEER_QUERY_DIM // 2)
    s_half = jnp.einsum('thpd,hpkd->thpk', q, sub_keys).astype(F32)
    v_half, i_half = lax.top_k(s_half, K)
    cand_s = (v_half[:, :, 0, :, None] + v_half[:, :, 1, None, :]).reshape(T, PEER_HEADS, K * K)
    cand_i = (i_half[:, :, 0, :, None] * PEER_N_KEYS + i_half[:, :, 1, None, :]).reshape(T, PEER_HEADS, K * K)
    top_s, pick = lax.top_k(cand_s, K)
    experts = jnp.take_along_axis(cand_i, pick, axis=-1)
    gates = jax.nn.softmax(top_s, axis=-1)
    n_chunks = T // PEER_CHUNK
    E = PEER_HEADS * K

    def chunk(args):
        hc, ec, gc = args
        a = jnp.einsum('cd,ced->ce', hc, w_u[ec]).astype(F32)
        coef = (gc * jax.nn.gelu(a)).astype(dt)
        return jnp.einsum('ce,ced->cd', coef, w_v[ec])

    out = lax.map(chunk, (hf.reshape(n_chunks, PEER_CHUNK, D),
                          experts.reshape(n_chunks, PEER_CHUNK, E),
                          gates.reshape(n_chunks, PEER_CHUNK, E)))
    return out.reshape(B, S, D).astype(dt)


def setup_inputs(seed: int = 0) -> dict:
    key = jax.random.key(seed)
    ks = jax.random.split(key, 24)
    L = DEPTH

    def nrm(k, shape, s):
        return jax.random.normal(k, shape, F32) * s

    return {
        "x": nrm(ks[0], (BATCH, SEQ, D_MODEL), 1.0),
        "c": nrm(ks[1], (BATCH, D_MODEL), 1.0),
        "w_ada": nrm(ks[2], (L, D_MODEL, 6 * D_MODEL), 0.5 * D_MODEL ** -0.5),
        "b_ada": nrm(ks[3], (L, 6 * D_MODEL), 0.01),
        "g_norm_mix": 1.0 + nrm(ks[4], (L, D_MODEL), 0.02),
        "g_norm_ffn": 1.0 + nrm(ks[5], (L, D_MODEL), 0.02),
        "g_norm_final": 1.0 + nrm(ks[6], (D_MODEL,), 0.02),
        "w_in": nrm(ks[7], (L, D_MODEL, IN_WIDTH), D_MODEL ** -0.5),
        "cmp_pe_k": nrm(ks[8], (L, CMP_LEN, NSA_HEAD_DIM), 0.1),
        "cmp_pe_v": nrm(ks[9], (L, CMP_LEN, NSA_HEAD_DIM), 0.1),
        "cmp_k_w1": nrm(ks[10], (L, CMP_LEN * NSA_HEAD_DIM, CMP_HIDDEN), (CMP_LEN * NSA_HEAD_DIM) ** -0.5),
        "cmp_k_w2": nrm(ks[11], (L, CMP_HIDDEN, NSA_HEAD_DIM), CMP_HIDDEN ** -0.5),
        "cmp_v_w1": nrm(ks[12], (L, CMP_LEN * NSA_HEAD_DIM, CMP_HIDDEN), (CMP_LEN * NSA_HEAD_DIM) ** -0.5),
        "cmp_v_w2": nrm(ks[13], (L, CMP_HIDDEN, NSA_HEAD_DIM), CMP_HIDDEN ** -0.5),
        "g_nsa_out": 1.0 + nrm(ks[14], (L, NSA_HEADS, NSA_HEAD_DIM), 0.02),
        "g_ret_out": 1.0 + nrm(ks[15], (L, RET_HEADS, RET_V_DIM), 0.02),
        "w_out": nrm(ks[16], (L, MIX_WIDTH, D_MODEL), MIX_WIDTH ** -0.5),
        "peer_w_q": nrm(ks[17], (L, D_MODEL, PEER_HEADS * PEER_QUERY_DIM), D_MODEL ** -0.5),
        "peer_sub_keys": nrm(ks[18], (L, PEER_HEADS, 2, PEER_N_KEYS, PEER_QUERY_DIM // 2), (PEER_QUERY_DIM // 2) ** -0.5),
        "peer_u": nrm(ks[19], (L, PEER_EXPERTS, D_MODEL), D_MODEL ** -0.5),
        "peer_v": nrm(ks[20], (L, PEER_EXPERTS, D_MODEL), 0.25),
    }


def reference(x, c, w_ada, b_ada, g_norm_mix, g_norm_ffn, g_norm_final, w_in, cmp_pe_k, cmp_pe_v,
              cmp_k_w1, cmp_k_w2, cmp_v_w1, cmp_v_w2, g_nsa_out, g_ret_out, w_out,
              peer_w_q, peer_sub_keys, peer_u, peer_v):
    B, S, D = x.shape
    c_act = jax.nn.silu(c)
    for l in range(DEPTH):
        mod = (c_act @ w_ada[l] + b_ada[l]).reshape(B, 6, 1, D)
        shift1, scale1, gate1, shift2, scale2, gate2 = jnp.moveaxis(mod, 1, 0)
        h = rmsnorm(x, g_norm_mix[l]) * (1.0 + scale1) + shift1
        q_a, k_c, v_c, k_s, v_s, k_w, v_w, g_a, q_r, k_r, v_r, g_r = split_cols(h @ w_in[l], IN_SIZES)
        o_nsa = nsa_attention(q_a.reshape(B, S, NSA_HEADS, NSA_HEAD_DIM),
                              k_c.reshape(B, S, NSA_KV_HEADS, NSA_HEAD_DIM), v_c.reshape(B, S, NSA_KV_HEADS, NSA_HEAD_DIM),
                              k_s.reshape(B, S, NSA_KV_HEADS, NSA_HEAD_DIM), v_s.reshape(B, S, NSA_KV_HEADS, NSA_HEAD_DIM),
                              k_w.reshape(B, S, NSA_KV_HEADS, NSA_HEAD_DIM), v_w.reshape(B, S, NSA_KV_HEADS, NSA_HEAD_DIM),
                              g_a, cmp_pe_k[l], cmp_pe_v[l], cmp_k_w1[l], cmp_k_w2[l], cmp_v_w1[l], cmp_v_w2[l])
        o_nsa = rmsnorm(o_nsa, g_nsa_out[l]).reshape(B, S, NSA_WIDTH)
        o_ret = retention(q_r.reshape(B, S, RET_HEADS, RET_QK_DIM), k_r.reshape(B, S, RET_HEADS, RET_QK_DIM),
                          v_r.reshape(B, S, RET_HEADS, RET_V_DIM), g_r, g_ret_out[l])
        x = x + gate1 * (jnp.concatenate([o_nsa, o_ret], axis=-1) @ w_out[l])
        h2 = rmsnorm(x, g_norm_ffn[l]) * (1.0 + scale2) + shift2
        x = x + gate2 * peer_ffn(h2, peer_w_q[l], peer_sub_keys[l], peer_u[l], peer_v[l])
    return rmsnorm(x, g_norm_final)
```

```python
import contextlib
import numpy as np
import ml_dtypes
import concourse.bass as bass
import concourse.mybir as mybir
from concourse.bass_utils import run_bass_kernel_spmd

F32 = mybir.dt.float32
BF16 = mybir.dt.bfloat16
ALU = mybir.AluOpType
AF = mybir.ActivationFunctionType
AX = mybir.AxisListType
EPOCH = 30000
D = 2048
IN_W = 6704
C_QA, C_KC, C_VC, C_KS, C_VS, C_KW, C_VW, C_GA, C_QR, C_KR, C_VR, C_GR = (
    0, 1024, 1280, 1536, 1792, 2048, 2304, 2560, 2608, 3632, 4656, 5680)
NEG = -30000.0
NEXP = 16384


class Buf:
    __slots__ = ("name", "w", "r")

    def __init__(self, name=""):
        self.name = name
        self.w = None
        self.r = {}


class Sync:
    def __init__(self, nc, stack):
        self.nc = nc
        self.stack = stack
        self.engines = {"pe": nc.tensor, "dve": nc.vector, "act": nc.scalar,
                        "pool": nc.gpsimd, "sp": nc.sync}
        self.sems = {}
        self.cnt = {}
        self.epoch = {e: 0 for e in self.engines}
        self.ecount = {e: 0 for e in self.engines}
        self.known = {e: {} for e in self.engines}
        self.nsem = 0
        self.ninst = {e: 0 for e in self.engines}

    def _sem(self, key):
        if key not in self.sems:
            self.sems[key] = self.stack.enter_context(self.nc.semaphore("s%d" % self.nsem))
            self.nsem += 1
            self.cnt[key] = 0
        return self.sems[key]

    def _ekey(self, e):
        return ("E", e, self.epoch[e])

    def _wait(self, e, deps, skip_same_pe=True):
        eng = self.engines[e]
        need = {}
        for d in deps:
            if d is None:
                continue
            k, v = d
            if e == "pe" and k[0] == "E" and k[1] == "pe" and skip_same_pe:
                continue
            if k[0] == "D":
                v = self.cnt[k]
            if self.known[e].get(k, 0) >= v:
                continue
            if need.get(k, 0) < v:
                need[k] = v
        for k, v in need.items():
            eng.wait_ge(self.sems[k], v)
            self.known[e][k] = v
            self.ninst[e] += 1

    def _deps(self, reads, writes):
        deps = []
        for b in reads:
            deps.append(b.w)
        for b in writes:
            deps.append(b.w)
            for k, v in b.r.items():
                deps.append((k, v))
        return deps

    def op(self, e, fn, reads=(), writes=(), inc=True):
        self._wait(e, self._deps(reads, writes))
        ins = fn()
        self.ninst[e] += 1
        if self.ecount[e] >= EPOCH and inc:
            self.epoch[e] += 1
            self.ecount[e] = 0
        k = self._ekey(e)
        sem = self._sem(k)
        if inc:
            ins.then_inc(sem, 1)
            self.cnt[k] += 1
            self.ecount[e] += 1
            v = self.cnt[k]
        else:
            v = self.cnt[k] + 1
        for b in reads:
            if b.r.get(k, 0) < v:
                b.r[k] = v
        for b in writes:
            b.w = (k, v)
            b.r = {}
        return ins

    def dma(self, q, out, in_, reads=(), writes=(), chan=None, **kw):
        self._wait(q, self._deps(reads, writes), skip_same_pe=False)
        k = ("D", chan if chan is not None else (writes[0].name if writes else "dma"))
        sem = self._sem(k)
        ins = self.engines[q].dma_start(out=out, in_=in_, **kw)
        ins.then_inc(sem, 16)
        self.ninst[q] += 1
        self.cnt[k] += 16
        v = self.cnt[k]
        for b in reads:
            if b.r.get(k, 0) < v:
                b.r[k] = v
        for b in writes:
            b.w = (k, v)
            b.r = {}
        return ins

    def finish_all(self):
        deps = [(k, v) for k, v in self.cnt.items() if v > 0]
        self._wait("sp", deps, skip_same_pe=False)

    def barrier(self):
        for e in self.engines:
            deps = [(k, v) for k, v in self.cnt.items() if v > 0]
            self._wait(e, deps, skip_same_pe=False)


class Ring:
    def __init__(self, tiles, name):
        self.tiles = tiles
        self.bufs = [Buf("%s%d" % (name, i)) for i in range(len(tiles))]
        self.i = -1

    def next(self):
        self.i = (self.i + 1) % len(self.tiles)
        return self.tiles[self.i], self.bufs[self.i]


def bf(x):
    return np.asarray(x).astype(ml_dtypes.bfloat16)


def split3(v):
    v = np.asarray(v, np.float64)
    a = bf(v)
    r = v - a.astype(np.float64)
    b = bf(r)
    r2 = r - b.astype(np.float64)
    c = bf(r2)
    return a, b, c


def host_consts(SEQ, j):
    NT = SEQ // 128
    NO = NT // 2
    NSLC = SEQ // 64
    NCMP = SEQ // 16
    NCC = max(1, NCMP // 128)
    n_cmp = (SEQ - 32) // 16 + 1
    cs = {}
    cs["ident"] = np.eye(128, dtype=np.float32)
    pos = np.arange(SEQ)
    cs["kaug"] = bf(np.stack([pos // 128, pos // 128, pos % 128, pos % 128,
                              np.ones(SEQ), np.ones(SEQ), np.ones(SEQ)]).astype(np.float32))
    cpos = np.arange(NCC * 128) * 16 + 31
    cs["caug"] = bf(np.stack([cpos // 128, cpos // 128, cpos % 128, cpos % 128,
                              np.ones_like(cpos), np.ones_like(cpos), np.ones_like(cpos)]).astype(np.float32))
    slopes = np.exp2(-8.0 * (np.arange(16, dtype=np.float64) + 1.0) / 16).astype(np.float32).astype(np.float64)
    qaug = np.zeros((NO, 4, 7, 4, 128), dtype=ml_dtypes.bfloat16)
    for m in range(NO):
        t = 128.0 * (2 * m + j) + np.arange(128)
        for g in range(4):
            for r in range(4):
                s = slopes[4 * g + r]
                shi = bf(s)
                slo = bf(s - float(shi))
                seff = float(shi) + float(slo)
                a, b, c = split3(seff * t)
                qaug[m, g, 0, r, :] = bf(128.0 * float(shi))
                qaug[m, g, 1, r, :] = bf(128.0 * float(slo))
                qaug[m, g, 2, r, :] = shi
                qaug[m, g, 3, r, :] = slo
                qaug[m, g, 4, r, :] = bf(-a.astype(np.float32))
                qaug[m, g, 5, r, :] = bf(-b.astype(np.float32))
                qaug[m, g, 6, r, :] = bf(-c.astype(np.float32))
    cs["qaug"] = qaug.reshape(NO, 4, 7, 512)
    n = np.arange(NCC * 128)
    cm1 = np.zeros((NO, 128, NCC, 128), np.float32)
    vm = np.zeros((NO, 128, NSLC), np.float32)
    am = np.zeros((NO, 128, NSLC), np.float32)
    mb = np.arange(NSLC)
    for m in range(NO):
        t = 128 * (2 * m + j) + np.arange(128)
        ok = (n[:, None] < n_cmp) & ((16 * n[:, None] + 31) <= t[None, :])
        cm1[m] = np.where(ok, 0.0, NEG).reshape(NCC, 128, 128).transpose(1, 0, 2)
        back = (t // 64)[:, None] - mb[None, :]
        valid = back >= 0
        forced = valid & ((mb[None, :] == 0) | (back < 2))
        vm[m] = (valid & ~forced).astype(np.float32)
        am[m] = np.where(forced, 1e4 + mb[None, :], np.where(valid, 0.0, -1.0))
    cs["cm1"] = bf(cm1)
    cs["vmask"] = vm
    cs["amask"] = am
    ov = ((n[:, None] < n_cmp) & (n[:, None] >= 4 * mb[None, :] - 1) & (n[:, None] <= 4 * mb[None, :] + 3))
    cs["overlap"] = bf(ov.astype(np.float32).reshape(NCC, 128, NSLC).transpose(1, 0, 2))
    E = np.zeros((NSLC, NT, 128), np.float32)
    for c in range(NT):
        E[2 * c, c, :64] = 1.0
        E[2 * c + 1, c, 64:] = 1.0
    cs["eall"] = bf(E)
    k = np.arange(128)[:, None]
    q = np.arange(128)[None, :]
    tri = np.where(k <= q, 0.0, NEG).astype(np.float32)
    full = np.full((128, 128), NEG, np.float32)
    zero = np.zeros((128, 128), np.float32)
    cs["cmab"] = bf(np.stack([tri, full] if j == 0 else [zero, tri], axis=1))
    wm = np.zeros((128, 6, 128), np.float32)
    for u in range(6):
        d = 128 * (j + 4 - u) + q - k
        wm[:, u, :] = np.where((d >= 0) & (d < 512), 0.0, NEG)
    cs["wmask"] = bf(wm)
    lg = np.log1p(-np.exp2(-5.0 - np.arange(8, dtype=np.float64)))
    p = np.arange(128, dtype=np.float64)
    dec = np.zeros((128, 8, 128), np.float32)
    for h in range(8):
        diff = p[None, :] - p[:, None]
        dec[:, h, :] = np.where(diff >= 0, np.exp(lg[h] * np.maximum(diff, 0.0)), 0.0) * (128.0 ** -0.5)
    cs["decT"] = dec
    cs["kdec"] = np.stack([np.exp(lg[h] * (127.0 - p)) * (128.0 ** -0.5) for h in range(8)], axis=1).astype(np.float32)
    qd = np.stack([np.exp(lg[h] * (p + 1.0)) for h in range(8)], axis=0).astype(np.float32)
    cs["qdecB"] = np.broadcast_to(qd[None], (128, 8, 128)).copy()
    cs["cdec"] = np.exp(lg * 128.0)
    cs["jsel"] = np.full((128, 1), float(j), np.float32)
    return cs


def build(SEQ, dbg=None):
    NT = SEQ // 128
    NO = NT // 2
    NS = SEQ // 512
    NOS = NO // 4
    NSLC = SEQ // 64
    NCMP = SEQ // 16
    NCC = max(1, NCMP // 128)
    SO = SEQ // 2
    cdec = host_consts(256, 0)["cdec"] if False else np.exp(np.log1p(-np.exp2(-5.0 - np.arange(8, dtype=np.float64))) * 128.0)
    nc = bass.Bass("TRN2", target_bir_lowering=False)

    def din(name, shape, dt=F32):
        return nc.dram_tensor(name, list(shape), dt, kind="ExternalInput").ap()

    def dscr(name, shape, dt):
        return nc.dram_tensor(name, list(shape), dt).ap()

    x_all = din("x_all", [SEQ, D]); x_own = din("x_own", [SO, D]); c_in = din("c_in", [128, 16])
    w_ada = din("w_ada", [D, 6 * D]); b_ada = din("b_ada", [1, 6 * D])
    g_mix = din("g_mix", [1, D]); g_ffn = din("g_ffn", [1, D]); g_fin = din("g_fin", [1, D])
    w_in = din("w_in", [D, IN_W])
    pe_k = din("pe_k", [32, 64]); pe_v = din("pe_v", [32, 64])
    ck_w1 = din("ck_w1", [2048, 128]); ck_w2 = din("ck_w2", [128, 64])
    cv_w1 = din("cv_w1", [2048, 128]); cv_w2 = din("cv_w2", [128, 64])
    g_nsa = din("g_nsa", [1, 1024]); g_ret = din("g_ret", [1, 1024])
    w_out = din("w_out", [D, D]); w_q = din("w_q", [D, D]); subk = din("subk", [16, 128, 128])
    U = din("U", [NEXP, D]); V = din("V", [NEXP, D])
    h_ident = din("ident", [128, 128]); h_kaug = din("kaug", [7, SEQ], BF16); h_caug = din("caug", [7, NCC * 128], BF16)
    h_qaug = din("qaug", [NO, 4, 7, 512], BF16); h_cm1 = din("cm1", [NO, 128, NCC, 128], BF16)
    h_vmask = din("vmask", [NO, 128, NSLC]); h_amask = din("amask", [NO, 128, NSLC])
    h_overlap = din("overlap", [128, NCC, NSLC], BF16); h_eall = din("eall", [NSLC, NT, 128], BF16)
    h_cmab = din("cmab", [128, 2, 128], BF16); h_wmask = din("wmask", [128, 6, 128], BF16)
    h_decT = din("decT", [128, 8, 128]); h_kdec = din("kdec", [128, 8]); h_qdecB = din("qdecB", [128, 8, 128])
    h_jsel = din("jsel", [128, 1])
    y = nc.dram_tensor("y", [SO, D], F32, kind="ExternalOutput").ap()
    dbg_out = {}
    if dbg:
        for nm, shp in dbg.items():
            dbg_out[nm] = nc.dram_tensor("dbg_" + nm, list(shp), F32, kind="ExternalOutput").ap()

    w_in_bf = dscr("w_in_bf", [D, IN_W], BF16); w_out_bf = dscr("w_out_bf", [D, D], BF16); w_q_bf = dscr("w_q_bf", [D, D], BF16)
    UT_bf = dscr("UT_bf", [D, NEXP], BF16); V_bf = dscr("V_bf", [NEXP, D], BF16)
    kS_d = dscr("kS_d", [4, 64, SEQ], BF16); kW_d = dscr("kW_d", [4, 64, SEQ], BF16)
    vS_d = dscr("vS_d", [SEQ, 256], BF16); vW_d = dscr("vW_d", [SEQ, 256], BF16)
    qA_d = dscr("qA_d", [16, 64, SO], BF16); gA_d = dscr("gA_d", [SO, 48], F32)
    qR_d = dscr("qR_d", [8, 128, SO], BF16); kR_d = dscr("kR_d", [8, 128, SO], BF16)
    vR_d = dscr("vR_d", [SO, 1024], BF16); gR_d = dscr("gR_d", [SO, 1024], F32)
    R_d = dscr("R_d", [NO, 128, 8, 128], BF16)
    omix_d = dscr("omix_d", [SO, D], BF16)
    x1_d = dscr("x1_d", [SO, D], F32); h2T_d = dscr("h2T_d", [16, 128, SO], BF16)
    sc_d = dscr("sc_d", [SO, 16, 128], F32)

    with contextlib.ExitStack() as st:
        S = Sync(nc, st)

        def sbp(name, shape, dt):
            return st.enter_context(nc.sbuf_tensor("t_" + name, list(shape), dt))

        ident_f = sbp("ident_f", [128, 128], F32); b_ident = Buf("ident")
        ident_b = sbp("ident_b", [128, 128], BF16)
        gate1B = sbp("gate1B", [128, D], F32); gate2B = sbp("gate2B", [128, D], F32); b_gate = Buf("gate")
        modT = sbp("modT", [128, 4, 16], F32); b_modT = Buf("modT")
        KcT = sbp("KcT", [71, 4, NCC * 128], BF16); b_KcT = Buf("KcT")
        VcA = sbp("VcA", [128, NCC, 4, 65 + NSLC], BF16); b_VcA = Buf("VcA")
        skT = sbp("skT", [128, 16, 128], BF16); b_skT = Buf("skT")
        jsel = sbp("jsel", [128, 1], F32); b_jsel = Buf("jsel")
        thr_all = sbp("thr_all", [128, NO, 8], F32)
        lnz_all = sbp("lnz_all", [128, NO, 8], F32)
        b_thr = [Buf("thr%d" % i) for i in range(NO)]

        S.dma("sp", ident_f[:], h_ident[:, :], writes=[b_ident])
        S.op("dve", lambda: nc.vector.tensor_copy(out=ident_b[:], in_=ident_f[:]), reads=[b_ident], writes=[b_ident])
        S.dma("sp", jsel[:], h_jsel[:, :], writes=[b_jsel])

        def scope():
            return contextlib.ExitStack()

        def sbl(ls, name, shape, dt):
            return ls.enter_context(nc.sbuf_tensor("t_" + name, list(shape), dt))

        def psl(ls, name, shape, dt):
            return ls.enter_context(nc.psum_tensor("p_" + name, list(shape), dt))

        def ring(ls, name, n, shape, dt):
            return Ring([sbl(ls, "%s%d" % (name, i), shape, dt) for i in range(n)], name)

        def pring(ls, name, n, shape, dt):
            return Ring([psl(ls, "%s%d" % (name, i), shape, dt) for i in range(n)], name)

        def mmg(out_ap, pairs, rbufs, wbuf):
            n = len(pairs)
            for i, (l, r) in enumerate(pairs):
                S.op("pe", lambda: nc.tensor.matmul(out=out_ap, lhsT=l, rhs=r, start=(i == 0), stop=(i == n - 1)),
                     reads=rbufs, writes=[wbuf], inc=(i == n - 1))

        cast_rr = [0]

        def cast(out_ap, in_ap, rb, wb, psum=False):
            e = ("dve", "act")[cast_rr[0] % 2] if psum else ("dve", "pool", "act")[cast_rr[0] % 3]
            cast_rr[0] += 1
            if e == "act":
                S.op("act", lambda: nc.scalar.copy(out=out_ap, in_=in_ap), reads=rb, writes=wb)
            elif e == "dve":
                S.op("dve", lambda: nc.vector.tensor_copy(out=out_ap, in_=in_ap), reads=rb, writes=wb)
            else:
                S.op("pool", lambda: nc.gpsimd.tensor_copy(out=out_ap, in_=in_ap), reads=rb, writes=wb)

        def rstd_from_ss(rstd_ap, ss_ap, n, rb, wb):
            S.op("dve", lambda: nc.vector.tensor_scalar(out=rstd_ap, in0=ss_ap, scalar1=1.0 / n, scalar2=1e-6,
                                                        op0=ALU.mult, op1=ALU.add), reads=rb, writes=wb)
            S.op("act", lambda: nc.scalar.activation(out=rstd_ap, in_=rstd_ap, func=AF.Sqrt), reads=wb, writes=wb)
            S.op("dve", lambda: nc.vector.reciprocal(out=rstd_ap, in_=rstd_ap), reads=wb, writes=wb)

        with scope() as ls:
            cin_t = sbl(ls, "cin_t", [128, 16], F32); b_cin = Buf("cin")
            cb = sbl(ls, "cb", [128, 16, 128], F32); b_cb = Buf("cb")
            modB = sbl(ls, "modB", [128, 6, D], F32); b_modB = [Buf("modB%d" % v) for v in range(6)]
            wr = ring(ls, "wada", 3, [128, D], F32)
            btile = sbl(ls, "btile", [128, D], F32); b_bt = Buf("bt")
            gB = sbl(ls, "gB", [128, 2, D], F32); b_gB = Buf("gB")
            vecB = sbl(ls, "vecB", [128, 4, D], F32); b_vecB = Buf("vecB")
            psA = pring(ls, "psA", 8, [128, 512], F32)
            S.dma("sp", cin_t[:], c_in[:, :], writes=[b_cin])
            S.op("act", lambda: nc.scalar.activation(out=cin_t[:], in_=cin_t[:], func=AF.Silu), reads=[b_cin], writes=[b_cin])
            S.op("dve", lambda: nc.vector.tensor_copy(out=cb[:], in_=cin_t[:].unsqueeze(2).to_broadcast([128, 16, 128])),
                 reads=[b_cin], writes=[b_cb])
            for v in range(6):
                banks = [psA.next() for _ in range(4)]
                for k in range(16):
                    wt, bw = wr.next()
                    S.dma("sp" if k % 2 == 0 else "pool", wt[:], w_ada[k * 128:(k + 1) * 128, v * D:(v + 1) * D], writes=[bw])
                    for n4 in range(4):
                        pt, bp = banks[n4]
                        S.op("pe", lambda: nc.tensor.matmul(out=pt[:], lhsT=cb[:, k, :], rhs=wt[:, n4 * 512:(n4 + 1) * 512],
                                                            start=(k == 0), stop=(k == 15)),
                             reads=[b_cb, bw], writes=[bp], inc=(k == 15 or n4 == 3))
                S.dma("sp", btile[:], b_ada[0:1, v * D:(v + 1) * D].partition_broadcast(128), writes=[b_bt])
                for n4 in range(4):
                    pt, bp = banks[n4]
                    S.op("dve", lambda: nc.vector.tensor_tensor(out=modB[:, v, n4 * 512:(n4 + 1) * 512], in0=pt[:],
                                                                in1=btile[:, n4 * 512:(n4 + 1) * 512], op=ALU.add),
                         reads=[bp, b_bt], writes=[b_modB[v]])
            S.dma("sp", gB[:, 0, :], g_mix[0:1, :].partition_broadcast(128), writes=[b_gB])
            S.dma("sp", gB[:, 1, :], g_ffn[0:1, :].partition_broadcast(128), writes=[b_gB])
            S.op("dve", lambda: nc.vector.scalar_tensor_tensor(out=vecB[:, 0, :], in0=modB[:, 1, :], scalar=1.0, in1=gB[:, 0, :],
                                                               op0=ALU.add, op1=ALU.mult), reads=[b_modB[1], b_gB], writes=[b_vecB])
            S.op("dve", lambda: nc.vector.tensor_copy(out=vecB[:, 1, :], in_=modB[:, 0, :]), reads=[b_modB[0]], writes=[b_vecB])
            S.op("dve", lambda: nc.vector.scalar_tensor_tensor(out=vecB[:, 2, :], in0=modB[:, 4, :], scalar=1.0, in1=gB[:, 1, :],
                                                               op0=ALU.add, op1=ALU.mult), reads=[b_modB[4], b_gB], writes=[b_vecB])
            S.op("dve", lambda: nc.vector.tensor_copy(out=vecB[:, 3, :], in_=modB[:, 3, :]), reads=[b_modB[3]], writes=[b_vecB])
            S.op("pool", lambda: nc.gpsimd.tensor_copy(out=gate1B[:], in_=modB[:, 2, :]), reads=[b_modB[2]], writes=[b_gate])
            S.op("pool", lambda: nc.gpsimd.tensor_copy(out=gate2B[:], in_=modB[:, 5, :]), reads=[b_modB[5]], writes=[b_gate])
            for vi in range(4):
                for kc in range(16):
                    pt, bp = psA.next()
                    S.op("pe", lambda: nc.tensor.transpose(out=pt[:, 0:128], in_=vecB[:, vi, kc * 128:(kc + 1) * 128], identity=ident_f[:]),
                         reads=[b_vecB, b_ident], writes=[bp])
                    S.op("dve", lambda: nc.vector.tensor_copy(out=modT[:, vi, kc:kc + 1], in_=pt[:, 0:1]), reads=[bp], writes=[b_modT])
            S.barrier()

        b_wscr = Buf("wscr")
        with scope() as ls:
            fr = ring(ls, "cf", 2, [128, IN_W], F32)
            br = ring(ls, "cbf", 2, [128, IN_W], BF16)
            psT = pring(ls, "psTa", 2, [128, 1024], BF16)
            ust = ring(ls, "ust", 2, [128, 16, 512], BF16)
            for k in range(16):
                ft, bfb = fr.next(); bt_, bbb = br.next()
                S.dma("sp", ft[:], w_in[k * 128:(k + 1) * 128, :], writes=[bfb])
                cast(bt_[:], ft[:], [bfb], [bbb])
                S.dma("pool", w_in_bf[k * 128:(k + 1) * 128, :], bt_[:], reads=[bbb], writes=[b_wscr], chan="wscr")
            for (src, dst) in ((w_out, w_out_bf), (w_q, w_q_bf)):
                for k in range(0, 16, 2):
                    ft, bfb = fr.next(); bt_, bbb = br.next()
                    S.dma("sp", ft[:, 0:2 * D].rearrange("p (a n) -> p a n", a=2),
                          src[k * 128:(k + 2) * 128, :].rearrange("(a p) n -> p a n", p=128), writes=[bfb])
                    cast(bt_[:, 0:2 * D], ft[:, 0:2 * D], [bfb], [bbb])
                    S.dma("pool", dst[k * 128:(k + 2) * 128, :].rearrange("(a p) n -> p a n", p=128),
                          bt_[:, 0:2 * D].rearrange("p (a n) -> p a n", a=2), reads=[bbb], writes=[b_wscr], chan="wscr")
            for e0 in range(0, 128, 2):
                ft, bfb = fr.next(); bt_, bbb = br.next()
                S.dma("sp", ft[:, 0:2 * D].rearrange("p (a n) -> p a n", a=2),
                      V[e0 * 128:(e0 + 2) * 128, :].rearrange("(a p) n -> p a n", p=128), writes=[bfb])
                cast(bt_[:, 0:2 * D], ft[:, 0:2 * D], [bfb], [bbb])
                S.dma("pool", V_bf[e0 * 128:(e0 + 2) * 128, :].rearrange("(a p) n -> p a n", p=128),
                      bt_[:, 0:2 * D].rearrange("p (a n) -> p a n", a=2), reads=[bbb], writes=[b_wscr], chan="wscr")
            for e4 in range(32):
                us, bus = ust.next()
                for a in range(0, 4, 2):
                    ft, bfb = fr.next(); bt_, bbb = br.next()
                    e0 = e4 * 4 + a
                    S.dma("sp", ft[:, 0:2 * D].rearrange("p (a n) -> p a n", a=2),
                          U[e0 * 128:(e0 + 2) * 128, :].rearrange("(a p) n -> p a n", p=128), writes=[bfb])
                    cast(bt_[:, 0:2 * D], ft[:, 0:2 * D], [bfb], [bbb])
                    for a2 in range(2):
                        for half in range(2):
                            pt, bp = psT.next()
                            for k8 in range(8):
                                kc = half * 8 + k8
                                S.op("pe", lambda: nc.tensor.transpose(out=pt[:, k8 * 128:(k8 + 1) * 128],
                                                                       in_=bt_[:, a2 * D + kc * 128:a2 * D + (kc + 1) * 128], identity=ident_b[:]),
                                     reads=[bbb, b_ident], writes=[bp], inc=(k8 == 7))
                            ee = (a + a2) * 128
                            cast(us[:, half * 8:(half + 1) * 8, ee:ee + 128], pt[:].rearrange("p (k e) -> p k e", k=8), [bp], [bus], psum=True)
                S.dma("pool", UT_bf.rearrange("(k p) e -> p k e", p=128)[:, :, e4 * 512:(e4 + 1) * 512], us[:],
                      reads=[bus], writes=[b_wscr], chan="wscr")
            ft, bfb = fr.next(); bt_, bbb = br.next()
            S.dma("sp", ft[:, 0:2048].rearrange("p (a d) -> p a d", a=16), subk.rearrange("a k d -> k a d"), writes=[bfb])
            cast(bt_[:, 0:2048], ft[:, 0:2048], [bfb], [bbb])
            for half in range(2):
                pt, bp = psT.next()
                for k8 in range(8):
                    a = half * 8 + k8
                    S.op("pe", lambda: nc.tensor.transpose(out=pt[:, k8 * 128:(k8 + 1) * 128], in_=bt_[:, a * 128:(a + 1) * 128], identity=ident_b[:]),
                         reads=[bbb, b_ident], writes=[bp], inc=(k8 == 7))
                cast(skT[:, half * 8:(half + 1) * 8, :], pt[:].rearrange("p (k e) -> p k e", k=8), [bp], [b_skT], psum=True)
            S.barrier()

        if dbg and "modT" in dbg_out:
            S.dma("sp", dbg_out["modT"][:, :], modT[:].rearrange("p a b -> p (a b)"), reads=[b_modT], writes=[Buf("dbgo")])


        b_y = Buf("y")
        b_scr = Buf("scr")
        gelu_c = 1.5957691216057308

        def gelu_tanh(out_ap, x_ap, tmp_ap, rb, wb, tb, shape_bias=None):
            S.op("dve", lambda: nc.vector.tensor_tensor(out=tmp_ap, in0=x_ap, in1=x_ap, op=ALU.mult), reads=rb, writes=[tb])
            S.op("dve", lambda: nc.vector.tensor_scalar(out=tmp_ap, in0=tmp_ap, scalar1=0.044715, scalar2=1.0, op0=ALU.mult, op1=ALU.add),
                 reads=[tb], writes=[tb])
            S.op("dve", lambda: nc.vector.tensor_tensor(out=tmp_ap, in0=tmp_ap, in1=x_ap, op=ALU.mult), reads=rb + [tb], writes=[tb])
            S.op("act", lambda: nc.scalar.activation(out=tmp_ap, in_=tmp_ap, func=AF.Sigmoid, scale=gelu_c), reads=[tb], writes=[tb])
            S.op("dve", lambda: nc.vector.tensor_tensor(out=out_ap, in0=tmp_ap, in1=x_ap, op=ALU.mult), reads=rb + [tb], writes=wb)

        w_in_v = w_in_bf.rearrange("(k p) n -> p k n", p=128)

        with scope() as ls:
            psT = pring(ls, "psT", 2, [128, 1024], BF16)
            psM = pring(ls, "psM", 4, [128, 512], F32)
            psK = pring(ls, "psK", 2, [128, 512], F32)
            W1s = sbl(ls, "W1s", [128, 2, 32, 128], BF16); b_W1s = Buf("W1s")
            W2s = sbl(ls, "W2s", [128, 2, 64], BF16); b_W2 = Buf("W2")
            peTb = sbl(ls, "peTb", [64, 2, 32], BF16); b_pe = Buf("pe")
            peb = sbl(ls, "peb", [128, 2], F32); b_peb = Buf("peb")
            with scope() as ls2:
                W1f = sbl(ls2, "W1f", [128, 32, 128], F32); b_W1f = Buf("W1f")
                W2f = sbl(ls2, "W2f", [128, 2, 64], F32)
                peT = sbl(ls2, "peT", [64, 2, 32], F32)
                for kv, (w1, w2, pe) in enumerate(((ck_w1, ck_w2, pe_k), (cv_w1, cv_w2, pe_v))):
                    for half in range(2):
                        S.dma("sp", W1f[half * 64:(half + 1) * 64, :, :], w1.rearrange("(l d) h -> d l h", d=64), writes=[b_W1f])
                    S.op("dve", lambda: nc.vector.tensor_copy(out=W1s[:, kv, :, :], in_=W1f[:]), reads=[b_W1f], writes=[b_W1s])
                    S.dma("sp", W2f[:, kv, :], w2[:, :], writes=[b_W2])
                    S.dma("sp", peT[:, kv, :], pe.rearrange("l d -> d l"), writes=[b_pe], allow_slow_non_contiguous=True)
                S.op("dve", lambda: nc.vector.tensor_copy(out=W2s[:], in_=W2f[:]), reads=[b_W2], writes=[b_W2])
                S.op("dve", lambda: nc.vector.tensor_copy(out=peTb[:], in_=peT[:]), reads=[b_pe], writes=[b_pe])
                for kv in range(2):
                    pt, bp = psM.next()
                    mmg(pt[:, 0:1], [(W1s[0:64, kv, l, :], peTb[:, kv, l:l + 1]) for l in range(32)], [b_W1s, b_pe], bp)
                    S.op("dve", lambda: nc.vector.tensor_copy(out=peb[:, kv:kv + 1], in_=pt[:, 0:1]), reads=[bp], writes=[b_peb])
                S.barrier()
            xr = ring(ls, "xt", 2, [128, D], F32)
            junk = sbl(ls, "junk", [128, D], BF16); b_junk = Buf("junk")
            ssr = ring(ls, "ss", 2, [128, 2], F32)
            xnr = ring(ls, "xn", 2, [128, D], BF16)
            hTr = ring(ls, "hT", 2, [128, 16, 512], BF16)
            wr = ring(ls, "wblk", 3, [128, 16, 512], BF16)
            stg = ring(ls, "stg", 4, [128, 512], BF16)
            stgf = ring(ls, "stgf", 2, [128, 512], F32)
            kcb = sbl(ls, "kcb", [128, 2, 2, 528], BF16); b_kcb = Buf("kcb")
            hid = sbl(ls, "hid", [128, 32], BF16); hidx = sbl(ls, "hidx", [128, 32], F32); hidt = sbl(ls, "hidt", [128, 32], F32)
            b_hid = Buf("hid"); b_hidx = Buf("hidx"); b_hidt = Buf("hidt")
            VcT = sbl(ls, "VcT", [64, 4, NCC * 128], BF16); b_VcT = Buf("VcT")
            kdec = sbl(ls, "kdec", [128, 8], F32); b_kdec = Buf("kdec")
            krd = sbl(ls, "krd", [128, 4, 1024], BF16); b_krd = Buf("krd")
            vrb = sbl(ls, "vrb", [128, 4, 1024], BF16); b_vrb = Buf("vrb")
            Rst = sbl(ls, "Rst", [128, 2, 8, 128], F32); b_Rst = Buf("Rst")
            Rown = sbl(ls, "Rown", [128, 8, 128], F32); Rownb = sbl(ls, "Rownb", [128, 8, 128], BF16); b_Rown = Buf("Rown")
            S.dma("sp", kdec[:], h_kdec[:, :], writes=[b_kdec])
            S.op("dve", lambda: nc.vector.memset(kcb[:], 0.0), writes=[b_kcb])
            S.op("dve", lambda: nc.vector.memset(Rst[:], 0.0), writes=[b_Rst])
            S.op("pool", lambda: nc.gpsimd.memset(KcT[:], 0.0), writes=[b_KcT])
            S.op("pool", lambda: nc.gpsimd.memset(VcT[:], 0.0), writes=[b_VcT])
            S.dma("sp", KcT[64:71, 0, :], h_caug[:, :], writes=[b_KcT])
            for g in range(1, 4):
                S.dma("sp", KcT[64:71, g, :], h_caug[:, :], writes=[b_KcT])

            def norm_T(xsrc, vi, hT, bh, col0):
                xt, bx = xr.next()
                S.dma("sp", xt[:], xsrc, writes=[bx])
                ss, bs = ssr.next()
                S.op("act", lambda: nc.scalar.activation(out=junk[:], in_=xt[:], func=AF.Square, accum_out=ss[:, 0:1]),
                     reads=[bx], writes=[b_junk, bs])
                rstd_from_ss(ss[:, 1:2], ss[:, 0:1], D, [bs], [bs])
                xn, bn = xnr.next()
                S.op("act", lambda: nc.scalar.activation(out=xn[:], in_=xt[:], func=AF.Copy, scale=ss[:, 1:2]), reads=[bx, bs], writes=[bn])
                for half in range(2):
                    pt, bp = psT.next()
                    for k8 in range(8):
                        kc = half * 8 + k8
                        S.op("pe", lambda: nc.tensor.transpose(out=pt[:, k8 * 128:(k8 + 1) * 128], in_=xn[:, kc * 128:(kc + 1) * 128],
                                                               identity=ident_b[:]), reads=[bn, b_ident], writes=[bp], inc=(k8 == 7))
                    dst = hT[:, half * 8:(half + 1) * 8, col0:col0 + 128]
                    S.op("dve", lambda: nc.vector.tensor_tensor(out=dst, in0=pt[:].rearrange("p (k t) -> p k t", k=8),
                                                                in1=modT[:, vi, half * 8:(half + 1) * 8].unsqueeze(2).to_broadcast([128, 8, 128]),
                                                                op=ALU.mult), reads=[bp, b_modT], writes=[bh])
                    S.op("dve", lambda: nc.vector.tensor_tensor(out=dst, in0=dst,
                                                                in1=modT[:, vi + 1, half * 8:(half + 1) * 8].unsqueeze(2).to_broadcast([128, 8, 128]),
                                                                op=ALU.add), reads=[bh, b_modT], writes=[bh])

            def load_w(c0, ncols):
                wt, bw = wr.next()
                S.dma("pool", wt[:, :, 0:ncols], w_in_v[:, :, c0:c0 + ncols], reads=[b_wscr], writes=[bw])
                return wt, bw

            def fm(wt, bw, col, M, hT, bh):
                pt, bp = psM.next()
                mmg(pt[0:M, :], [(wt[:, k, col:col + M], hT[:, k, :]) for k in range(16)], [bw, bh], bp)
                return pt, bp

            def tm(wt, bw, col, N, hT, bh, tile):
                pt, bp = psM.next()
                mmg(pt[:, 0:N], [(hT[:, k, tile * 128:(tile + 1) * 128], wt[:, k, col:col + N]) for k in range(16)], [bh, bw], bp)
                return pt, bp

            ev_rr = [0]

            def evac(out_ap, in_ap, rb, wb, scale=None):
                e = ("dve", "act")[ev_rr[0] % 2]
                ev_rr[0] += 1
                if e == "act":
                    if scale is None:
                        S.op("act", lambda: nc.scalar.copy(out=out_ap, in_=in_ap), reads=rb, writes=wb)
                    else:
                        S.op("act", lambda: nc.scalar.mul(out=out_ap, in_=in_ap, mul=scale), reads=rb, writes=wb)
                else:
                    if scale is None:
                        S.op("dve", lambda: nc.vector.tensor_copy(out=out_ap, in_=in_ap), reads=rb, writes=wb)
                    else:
                        S.op("dve", lambda: nc.vector.tensor_scalar(out=out_ap, in0=in_ap, scalar1=scale, scalar2=None, op0=ALU.mult),
                             reads=rb, writes=wb)

            for s in range(NS):
                hT, bh = hTr.next()
                for tt in range(4):
                    norm_T(x_all[(s * 4 + tt) * 128:(s * 4 + tt + 1) * 128, :], 0, hT, bh, tt * 128)
                wt, bw = load_w(C_KC, 512)
                S.op("dve", lambda: nc.vector.tensor_copy(out=kcb[:, :, :, 0:16], in_=kcb[:, :, :, 512:528]), reads=[b_kcb], writes=[b_kcb])
                for kv in range(2):
                    for gp in range(2):
                        pt, bp = fm(wt, bw, kv * 256 + gp * 128, 128, hT, bh)
                        evac(kcb[:, kv, gp, 16:528], pt[:, :], [bp], [b_kcb])
                i0 = 1 if s == 0 else 0
                ni = 32 - i0
                n0 = 32 * s - 1 + i0
                for kv in range(2):
                    for g in range(4):
                        base = (g % 2) * 64
                        pt, bp = psM.next()
                        mmg(pt[:, 0:ni], [(W1s[base:base + 64, kv, l, :],
                                            kcb[base:base + 64, kv, g // 2, 16 * i0 + l:16 * i0 + l + 16 * (ni - 1) + 1:16]) for l in range(32)],
                            [b_W1s, b_kcb], bp)
                        S.op("act", lambda: nc.scalar.activation(out=hidx[:, 0:ni], in_=pt[:, 0:ni], func=AF.Identity, bias=peb[:, kv:kv + 1]),
                             reads=[bp, b_peb], writes=[b_hidx])
                        gelu_tanh(hid[:, 0:ni], hidx[:, 0:ni], hidt[:, 0:ni], [b_hidx], [b_hid], b_hidt)
                        pt2, bp2 = psM.next()
                        mmg(pt2[0:64, 0:ni], [(W2s[:, kv, :], hid[:, 0:ni])], [b_W2, b_hid], bp2)
                        if kv == 0:
                            evac(KcT[0:64, g, n0:n0 + ni], pt2[0:64, 0:ni], [bp2], [b_KcT])
                        else:
                            evac(VcT[0:64, g, n0:n0 + ni], pt2[0:64, 0:ni], [bp2], [b_VcT])
                for (c0, kd, vd) in ((C_KS, kS_d, vS_d), (C_KW, kW_d, vW_d)):
                    wt, bw = load_w(c0, 512)
                    for g in range(4):
                        pt, bp = fm(wt, bw, g * 64, 64, hT, bh)
                        sg, bsg = stg.next()
                        evac(sg[0:64, :], pt[0:64, :], [bp], [bsg])
                        S.dma("sp", kd[g, :, s * 512:(s + 1) * 512], sg[0:64, :], reads=[bsg], writes=[b_scr], chan="scr")
                    for tt in range(4):
                        pt, bp = tm(wt, bw, 256, 256, hT, bh, tt)
                        sg, bsg = stg.next()
                        evac(sg[:, 0:256], pt[:, 0:256], [bp], [bsg])
                        S.dma("sp", vd[(s * 4 + tt) * 128:(s * 4 + tt + 1) * 128, :], sg[:, 0:256], reads=[bsg], writes=[b_scr], chan="scr")
                wk = [load_w(C_KR, 512), load_w(C_KR + 512, 512)]
                for tt in range(4):
                    for hh in range(2):
                        pt, bp = tm(wk[hh][0], wk[hh][1], 0, 512, hT, bh, tt)
                        S.op("dve", lambda: nc.vector.tensor_tensor(out=krd[:, tt, hh * 512:(hh + 1) * 512].rearrange("p (h d) -> p h d", h=4),
                                                                    in0=pt[:].rearrange("p (h d) -> p h d", h=4),
                                                                    in1=kdec[:, hh * 4:(hh + 1) * 4].unsqueeze(2).to_broadcast([128, 4, 128]),
                                                                    op=ALU.mult), reads=[bp, b_kdec], writes=[b_krd])
                wv = [load_w(C_VR, 512), load_w(C_VR + 512, 512)]
                for tt in range(4):
                    for hh in range(2):
                        pt, bp = tm(wv[hh][0], wv[hh][1], 0, 512, hT, bh, tt)
                        evac(vrb[:, tt, hh * 512:(hh + 1) * 512], pt[:, :], [bp], [b_vrb])
                for tt in range(4):
                    ci = s * 4 + tt
                    par = ci % 2
                    for hh in range(2):
                        pk, bpk = psK.next()
                        for h4 in range(4):
                            h = hh * 4 + h4
                            S.op("pe", lambda: nc.tensor.matmul(out=pk[:, h4 * 128:(h4 + 1) * 128], lhsT=krd[:, tt, h * 128:(h + 1) * 128],
                                                                rhs=vrb[:, tt, h * 128:(h + 1) * 128], start=True, stop=True),
                                 reads=[b_krd, b_vrb], writes=[bpk], inc=(h4 == 3))
                        for h4 in range(4):
                            h = hh * 4 + h4
                            S.op("dve", lambda: nc.vector.scalar_tensor_tensor(out=Rst[:, 1 - par, h, :], in0=Rst[:, par, h, :], scalar=float(cdec[h]),
                                                                               in1=pk[:, h4 * 128:(h4 + 1) * 128], op0=ALU.mult, op1=ALU.add),
                                 reads=[b_Rst, bpk], writes=[b_Rst])
                    if par == 1:
                        pass
                    if par == 0:
                        S.op("pool", lambda: nc.gpsimd.tensor_tensor(out=Rown[:], in0=Rst[:, 1, :, :], in1=Rst[:, 0, :, :], op=ALU.subtract),
                             reads=[b_Rst], writes=[b_Rown])
                        S.op("dve", lambda: nc.vector.scalar_tensor_tensor(out=Rownb[:], in0=Rown[:], scalar=jsel[:, 0:1], in1=Rst[:, 0, :, :],
                                                                            op0=ALU.mult, op1=ALU.add), reads=[b_Rown, b_Rst, b_jsel], writes=[b_Rown])
                        S.dma("sp", R_d[ci // 2], Rownb[:], reads=[b_Rown], writes=[b_scr], chan="scr")

            S.op("dve", lambda: nc.vector.memset(VcA[:, :, :, 64:65], 1.0), writes=[b_VcA])
            for g in range(4):
                S.dma("sp", VcA[:, :, g, 65:65 + NSLC], h_overlap[:, :, :], writes=[b_VcA])
            for cc in range(NCC):
                for g in range(4):
                    pt, bp = psT.next()
                    S.op("pe", lambda: nc.tensor.transpose(out=pt[:, 0:64], in_=VcT[0:64, g, cc * 128:(cc + 1) * 128], identity=ident_b[0:64, 0:64]),
                         reads=[b_VcT, b_ident], writes=[bp])
                    evac(VcA[:, cc, g, 0:64], pt[:, 0:64], [bp], [b_VcA])

            for so in range(NOS):
                hT, bh = hTr.next()
                for tt in range(4):
                    norm_T(x_own[(so * 4 + tt) * 128:(so * 4 + tt + 1) * 128, :], 0, hT, bh, tt * 128)
                t0, t1 = so * 512, (so + 1) * 512
                for blk in range(2):
                    wt, bw = load_w(C_QA + blk * 512, 512)
                    for h8 in range(8):
                        pt, bp = fm(wt, bw, h8 * 64, 64, hT, bh)
                        sg, bsg = stg.next()
                        evac(sg[0:64, :], pt[0:64, :], [bp], [bsg], scale=0.125)
                        S.dma("sp", qA_d[blk * 8 + h8, :, t0:t1], sg[0:64, :], reads=[bsg], writes=[b_scr], chan="scr")
                for (c0, dd) in ((C_QR, qR_d), (C_KR, kR_d)):
                    for blk in range(2):
                        wt, bw = load_w(c0 + blk * 512, 512)
                        for h4 in range(4):
                            pt, bp = fm(wt, bw, h4 * 128, 128, hT, bh)
                            sg, bsg = stg.next()
                            evac(sg[:, :], pt[:, :], [bp], [bsg])
                            S.dma("sp", dd[blk * 4 + h4, :, t0:t1], sg[:, :], reads=[bsg], writes=[b_scr], chan="scr")
                for blk in range(2):
                    wt, bw = load_w(C_VR + blk * 512, 512)
                    for tt in range(4):
                        pt, bp = tm(wt, bw, 0, 512, hT, bh, tt)
                        sg, bsg = stg.next()
                        evac(sg[:, :], pt[:, :], [bp], [bsg])
                        S.dma("sp", vR_d[t0 + tt * 128:t0 + (tt + 1) * 128, blk * 512:(blk + 1) * 512], sg[:, :], reads=[bsg], writes=[b_scr], chan="scr")
                for blk in range(2):
                    wt, bw = load_w(C_GR + blk * 512, 512)
                    for tt in range(4):
                        pt, bp = tm(wt, bw, 0, 512, hT, bh, tt)
                        sf, bsf = stgf.next()
                        evac(sf[:, 0:512], pt[:, :], [bp], [bsf])
                        S.dma("sp", gR_d[t0 + tt * 128:t0 + (tt + 1) * 128, blk * 512:(blk + 1) * 512], sf[:, 0:512], reads=[bsf], writes=[b_scr], chan="scr")
                wt, bw = load_w(C_GA, 48)
                for tt in range(4):
                    pt, bp = tm(wt, bw, 0, 48, hT, bh, tt)
                    sf, bsf = stgf.next()
                    evac(sf[:, 0:48], pt[:, 0:48], [bp], [bsf])
                    S.dma("sp", gA_d[t0 + tt * 128:t0 + (tt + 1) * 128, :], sf[:, 0:48], reads=[bsf], writes=[b_scr], chan="scr")
            S.barrier()


        b_omix = Buf("omix")
        NB1 = 65 + NSLC
        with scope() as ls:
            KsT = sbl(ls, "KsT", [71, 4, SEQ], BF16); b_KsT = Buf("KsT")
            VsA = sbl(ls, "VsA", [128, NT, 4, 65], BF16); b_VsA = Buf("VsA")
            Eall = sbl(ls, "Eall", [NSLC, NT, 128], BF16); b_E = Buf("Eall")
            cmab = sbl(ls, "cmab", [128, 2, 128], BF16); wmask = sbl(ls, "wmask", [128, 6, 128], BF16); b_cm = Buf("cm")
            maskrep = sbl(ls, "maskrep", [128, 8, 4, 128], BF16); b_mr = Buf("maskrep")
            g_nsaB = sbl(ls, "g_nsaB", [128, 1024], F32); b_gn = Buf("gnsa")
            QTr = ring(ls, "QT", 1, [71, 4, 512], BF16)
            KwTr = ring(ls, "KwT", 1, [71, 4, 768], BF16)
            VwAr = ring(ls, "VwA", 1, [128, 6, 4, 65], BF16)
            cm1r = ring(ls, "cm1", 2, [128, NCC, 128], BF16)
            cm1rep = sbl(ls, "cm1rep", [128, NCC, 4, 128], BF16); b_c1r = Buf("cm1rep")
            vmr = ring(ls, "vm", 1, [128, 2, NSLC], F32)
            gar = ring(ls, "ga", 2, [128, 48], F32)
            PTr = ring(ls, "PT", 3, [128, 512], BF16)
            nsT = sbl(ls, "nsT", [NSLC, 4, 128], BF16); b_nsT = Buf("nsT")
            imp = sbl(ls, "imp", [128, NSLC], F32); impr = sbl(ls, "impr", [128, NSLC], F32); b_imp = Buf("imp"); b_impr = Buf("impr")
            m8 = sbl(ls, "m8", [128, 16], F32); b_m8 = Buf("m8")
            selb = sbl(ls, "selb", [128, NSLC], F32); negs = sbl(ls, "negs", [128, NSLC], BF16); b_sel = Buf("sel")
            rden = sbl(ls, "rden", [128, 3, 4], F32); cf = sbl(ls, "cf", [128, 3, 4], F32); b_rden = Buf("rden")
            oacc = sbl(ls, "oacc", [128, 16, 64], F32); otmp = sbl(ls, "otmp", [128, 4, 64], F32); b_oacc = Buf("oacc"); b_otmp = Buf("otmp")
            osq = sbl(ls, "osq", [128, 16, 64], F32); oss = sbl(ls, "oss", [128, 16], F32); b_osq = Buf("osq")
            onb = ring(ls, "onb", 1, [128, 1024], BF16)
            psS = pring(ls, "psS", 3, [128, 512], F32)
            psO1 = psl(ls, "psO1", [128, 4, 256], F32); b_pO1 = Buf("pO1")
            psO2 = psl(ls, "psO2", [128, 512], F32); b_pO2 = Buf("pO2")
            psO3 = psl(ls, "psO3", [128, 512], F32); b_pO3 = Buf("pO3")
            psX = psl(ls, "psX", [128, 1024], BF16); b_pX = Buf("pX")
            assert NB1 <= 256
            for g in range(4):
                S.dma("sp", KsT[0:64, g, :], kS_d[g, :, :], reads=[b_scr], writes=[b_KsT])
                S.dma("sp", KsT[64:71, g, :], h_kaug[:, :], writes=[b_KsT])
            for c4 in range(0, NT, 4):
                for g in range(4):
                    S.dma("pool", VsA[:, c4:c4 + 4, g, 0:64], vS_d[c4 * 128:(c4 + 4) * 128, g * 64:(g + 1) * 64].rearrange("(c p) d -> p c d", p=128),
                          reads=[b_scr], writes=[b_VsA])
            S.op("dve", lambda: nc.vector.memset(VsA[:, :, :, 64:65], 1.0), writes=[b_VsA])
            S.dma("sp", Eall[:], h_eall[:, :, :], writes=[b_E])
            S.dma("sp", cmab[:], h_cmab[:, :, :], writes=[b_cm])
            S.dma("sp", wmask[:], h_wmask[:, :, :], writes=[b_cm])
            S.op("dve", lambda: nc.vector.tensor_copy(out=maskrep[:, 0:2, :, :], in_=cmab[:].unsqueeze(2).to_broadcast([128, 2, 4, 128])),
                 reads=[b_cm], writes=[b_mr])
            S.op("dve", lambda: nc.vector.tensor_copy(out=maskrep[:, 2:8, :, :], in_=wmask[:].unsqueeze(2).to_broadcast([128, 6, 4, 128])),
                 reads=[b_cm], writes=[b_mr])
            S.dma("sp", g_nsaB[:], g_nsa[0:1, :].partition_broadcast(128), writes=[b_gn])
            for (vt, bv) in zip(VwAr.tiles, VwAr.bufs):
                S.op("dve", lambda: nc.vector.memset(vt[:, :, :, 64:65], 1.0), writes=[bv])

            def attn_chunk(pairs, rb, rhsV, bV, psO, b_pO, width, first, last):
                pS, bS = psS.next()
                mmg(pS[:], pairs, rb, bS)
                PT, bP = PTr.next()
                S.op("act", lambda: nc.scalar.activation(out=PT[:], in_=pS[:], func=AF.Exp), reads=[bS], writes=[bP])
                for r in range(4):
                    st_ = first and (r == 0 or (width == NB1 and r == 2))
                    S.op("pe", lambda: nc.tensor.matmul(out=psO(r), lhsT=PT[:, r * 128:(r + 1) * 128], rhs=rhsV, start=st_, stop=last,
                                                        skip_group_check=True),
                         reads=[bP, bV], writes=[b_pO], inc=(r == 3))

            def finish_branch(bi, g, psO, b_pO, first_branch):
                for r in range(4):
                    S.op("dve", lambda: nc.vector.tensor_scalar(out=rden[:, bi, r:r + 1], in0=psO(r)[:, 64:65], scalar1=1e-30, scalar2=None, op0=ALU.max),
                         reads=[b_pO], writes=[b_rden])
                S.op("dve", lambda: nc.vector.reciprocal(out=rden[:, bi, :], in_=rden[:, bi, :]), reads=[b_rden], writes=[b_rden])
                S.op("dve", lambda: nc.vector.tensor_tensor(out=cf[:, bi, :], in0=rden[:, bi, :],
                                                            in1=ga[:, g * 12:(g + 1) * 12].rearrange("p (r b) -> p r b", b=3)[:, :, bi], op=ALU.mult),
                     reads=[b_rden, bga], writes=[b_rden])
                for r in range(4):
                    h = g * 4 + r
                    if first_branch:
                        S.op("dve", lambda: nc.vector.tensor_scalar(out=oacc[:, h, :], in0=psO(r)[:, 0:64], scalar1=cf[:, bi, r:r + 1], scalar2=None, op0=ALU.mult),
                             reads=[b_pO, b_rden], writes=[b_oacc])
                    else:
                        S.op("dve", lambda: nc.vector.scalar_tensor_tensor(out=oacc[:, h, :], in0=psO(r)[:, 0:64], scalar=cf[:, bi, r:r + 1], in1=oacc[:, h, :],
                                                                           op0=ALU.mult, op1=ALU.add), reads=[b_pO, b_rden, b_oacc], writes=[b_oacc])

            for m in range(NO):
                QT, bQ = QTr.next()
                for g in range(4):
                    S.dma("sp", QT[0:64, g, :].rearrange("d (r q) -> d r q", r=4),
                          qA_d[4 * g:4 * g + 4, :, m * 128:(m + 1) * 128].rearrange("r d q -> d r q"), reads=[b_scr], writes=[bQ])
                S.dma("sp", QT[64:71, :, :], h_qaug[m].rearrange("g a q -> a g q"), writes=[bQ])
                c0 = max(0, 2 * m - 4)
                u0 = c0 - (2 * m - 4)
                nch = 2 * m + 2 - c0
                KwT, bKw = KwTr.next()
                VwA, bVw = VwAr.next()
                for g in range(4):
                    S.dma("pool", KwT[0:64, g, u0 * 128:(u0 + nch) * 128], kW_d[g, :, c0 * 128:(c0 + nch) * 128], reads=[b_scr], writes=[bKw])
                    S.dma("pool", KwT[64:71, g, u0 * 128:(u0 + nch) * 128], h_kaug[:, c0 * 128:(c0 + nch) * 128], writes=[bKw])
                for g in range(4):
                    S.dma("pool", VwA[:, u0:u0 + nch, g, 0:64], vW_d[c0 * 128:(c0 + nch) * 128, g * 64:(g + 1) * 64].rearrange("(c p) d -> p c d", p=128),
                          reads=[b_scr], writes=[bVw])
                cm1, bc1 = cm1r.next()
                S.dma("sp", cm1[:], h_cm1[m], writes=[bc1])
                S.op("pool", lambda: nc.gpsimd.tensor_copy(out=cm1rep[:], in_=cm1[:].unsqueeze(2).to_broadcast([128, NCC, 4, 128])),
                     reads=[bc1], writes=[b_c1r])
                vm, bvm = vmr.next()
                S.dma("sp", vm[:, 0, :], h_vmask[m], writes=[bvm])
                S.dma("sp", vm[:, 1, :], h_amask[m], writes=[bvm])
                ga, bga = gar.next()
                S.dma("sp", ga[:], gA_d[m * 128:(m + 1) * 128, :], reads=[b_scr], writes=[bga])
                S.op("act", lambda: nc.scalar.activation(out=ga[:], in_=ga[:], func=AF.Sigmoid), reads=[bga], writes=[bga])
                for g in range(4):
                    for cc in range(NCC):
                        attn_chunk([(KcT[0:71, g, cc * 128:(cc + 1) * 128], QT[0:71, g, :]),
                                    (ident_b[:], cm1rep[:, cc, :, :].rearrange("p r q -> p (r q)"))],
                                   [b_KcT, bQ, b_ident, b_c1r], VcA[:, cc, g, :], b_VcA,
                                   lambda r: psO1[:, r, 0:NB1], b_pO1, NB1, cc == 0, cc == NCC - 1)
                    finish_branch(0, g, lambda r: psO1[:, r, 0:NB1], b_pO1, True)
                    S.op("dve", lambda: nc.vector.tensor_scalar(out=imp[:], in0=psO1[:, 0, 65:NB1], scalar1=rden[:, 0, 0:1], scalar2=None, op0=ALU.mult),
                         reads=[b_pO1, b_rden], writes=[b_imp])
                    for r in range(1, 4):
                        S.op("dve", lambda: nc.vector.scalar_tensor_tensor(out=imp[:], in0=psO1[:, r, 65:NB1], scalar=rden[:, 0, r:r + 1], in1=imp[:],
                                                                           op0=ALU.mult, op1=ALU.add), reads=[b_pO1, b_rden, b_imp], writes=[b_imp])
                    S.op("dve", lambda: nc.vector.tensor_tensor(out=imp[:], in0=imp[:], in1=vm[:, 0, :], op=ALU.mult), reads=[b_imp, bvm], writes=[b_imp])
                    S.op("dve", lambda: nc.vector.tensor_tensor(out=imp[:], in0=imp[:], in1=vm[:, 1, :], op=ALU.add), reads=[b_imp, bvm], writes=[b_imp])
                    S.op("dve", lambda: nc.vector.max(out=m8[:, 0:8], in_=imp[:]), reads=[b_imp], writes=[b_m8])
                    S.op("dve", lambda: nc.vector.match_replace(out=impr[:], in_to_replace=m8[:, 0:8], in_values=imp[:], imm_value=-1e30),
                         reads=[b_imp, b_m8], writes=[b_impr])
                    S.op("dve", lambda: nc.vector.max(out=m8[:, 8:16], in_=impr[:]), reads=[b_impr], writes=[b_m8])
                    S.op("dve", lambda: nc.vector.tensor_scalar(out=selb[:], in0=imp[:], scalar1=m8[:, 15:16], scalar2=1.0, op0=ALU.is_ge, op1=ALU.subtract),
                         reads=[b_imp, b_m8], writes=[b_sel])
                    S.op("dve", lambda: nc.vector.tensor_scalar(out=negs[:], in0=selb[:], scalar1=-NEG, scalar2=None, op0=ALU.mult), reads=[b_sel], writes=[b_sel])
                    S.op("pe", lambda: nc.tensor.transpose(out=psX[0:NSLC, 0:128], in_=negs[:], identity=ident_b[:]), reads=[b_sel, b_ident], writes=[b_pX])
                    S.op("dve", lambda: nc.vector.tensor_copy(out=nsT[:], in_=psX[0:NSLC, 0:128].unsqueeze(1).to_broadcast([NSLC, 4, 128])),
                         reads=[b_pX], writes=[b_nsT])
                    nc2 = 2 * m + 2
                    for c in range(nc2):
                        pairs = [(KsT[0:71, g, c * 128:(c + 1) * 128], QT[0:71, g, :]),
                                 (Eall[0:NSLC, c, :], nsT[:].rearrange("p r q -> p (r q)"))]
                        rb = [b_KsT, bQ, b_E, b_nsT]
                        if c >= 2 * m:
                            pairs.append((ident_b[:], maskrep[:, c - 2 * m, :, :].rearrange("p r q -> p (r q)")))
                            rb = rb + [b_ident, b_mr]
                        attn_chunk(pairs, rb, VsA[:, c, g, :], b_VsA, lambda r: psO2[:, r * 65:(r + 1) * 65], b_pO2, 65, c == 0, c == nc2 - 1)
                    finish_branch(1, g, lambda r: psO2[:, r * 65:(r + 1) * 65], b_pO2, False)
                    for u in range(u0, 6):
                        pairs = [(KwT[0:71, g, u * 128:(u + 1) * 128], QT[0:71, g, :]),
                                 (ident_b[:], maskrep[:, 2 + u, :, :].rearrange("p r q -> p (r q)"))]
                        attn_chunk(pairs, [bKw, bQ, b_ident, b_mr], VwA[:, u, g, :], bVw, lambda r: psO3[:, r * 65:(r + 1) * 65], b_pO3, 65, u == u0, u == 5)
                    finish_branch(2, g, lambda r: psO3[:, r * 65:(r + 1) * 65], b_pO3, False)
                S.op("dve", lambda: nc.vector.tensor_tensor(out=osq[:], in0=oacc[:], in1=oacc[:], op=ALU.mult), reads=[b_oacc], writes=[b_osq])
                S.op("dve", lambda: nc.vector.tensor_reduce(out=oss[:], in_=osq[:], axis=AX.X, op=ALU.add), reads=[b_osq], writes=[b_osq])
                rstd_from_ss(oss[:], oss[:], 64, [b_osq], [b_osq])
                S.op("dve", lambda: nc.vector.tensor_tensor(out=osq[:], in0=oacc[:], in1=oss[:].unsqueeze(2).to_broadcast([128, 16, 64]), op=ALU.mult),
                     reads=[b_oacc, b_osq], writes=[b_osq])
                on, bon = onb.next()
                S.op("dve", lambda: nc.vector.tensor_tensor(out=on[:], in0=osq[:].rearrange("p h d -> p (h d)"), in1=g_nsaB[:], op=ALU.mult),
                     reads=[b_osq, b_gn], writes=[bon])
                S.dma("sp", omix_d[m * 128:(m + 1) * 128, 0:1024], on[:], reads=[bon], writes=[b_omix], chan="omix")
            S.barrier()


        with scope() as ls:
            decT = sbl(ls, "decT", [128, 8, 128], F32); qdecB = sbl(ls, "qdecB", [128, 8, 128], F32); g_retB = sbl(ls, "g_retB", [128, 1024], F32)
            b_rc = Buf("rconst")
            S.dma("sp", decT[:], h_decT[:, :, :], writes=[b_rc]); S.dma("sp", qdecB[:], h_qdecB[:, :, :], writes=[b_rc])
            S.dma("sp", g_retB[:], g_ret[0:1, :].partition_broadcast(128), writes=[b_rc])
            qTr = ring(ls, "rqT", 2, [128, 8, 128], BF16); kTr = ring(ls, "rkT", 2, [128, 8, 128], BF16)
            vr_ = ring(ls, "rv", 2, [128, 1024], BF16); Rr = ring(ls, "rR", 2, [128, 8, 128], BF16); grr = ring(ls, "rg", 2, [128, 1024], F32)
            qd = sbl(ls, "rqd", [128, 8, 128], BF16); b_qd = Buf("qd")
            AT = sbl(ls, "rAT", [128, 8, 128], BF16); b_AT = Buf("AT")
            ro = sbl(ls, "ro", [128, 8, 128], F32); rsq = sbl(ls, "rsq", [128, 8, 128], F32); b_ro = Buf("ro"); b_rsq = Buf("rsq")
            rmu = sbl(ls, "rmu", [128, 2, 8], F32); b_rmu = Buf("rmu")
            orb = ring(ls, "orb", 2, [128, 1024], BF16)
            psA = pring(ls, "rpsA", 2, [128, 512], F32); psO = pring(ls, "rpsO", 2, [128, 512], F32)
            for m in range(NO):
                sl = slice(m * 128, (m + 1) * 128)
                qT, bq = qTr.next(); kT, bk = kTr.next(); v_, bv = vr_.next(); R_, bR = Rr.next(); gr, bg = grr.next()
                S.dma("sp", qT[:], qR_d[:, :, sl].rearrange("h d c -> d h c"), reads=[b_scr], writes=[bq])
                S.dma("sp", kT[:], kR_d[:, :, sl].rearrange("h d c -> d h c"), reads=[b_scr], writes=[bk])
                S.dma("pool", v_[:], vR_d[sl, :], reads=[b_scr], writes=[bv])
                S.dma("pool", R_[:], R_d[m], reads=[b_scr], writes=[bR])
                S.dma("pool", gr[:], gR_d[sl, :], reads=[b_scr], writes=[bg])
                S.op("pool", lambda: nc.gpsimd.tensor_tensor(out=qd[:], in0=qT[:], in1=qdecB[:], op=ALU.mult), reads=[bq, b_rc], writes=[b_qd])
                for hh in range(2):
                    pa, bpa = psA.next()
                    for h4 in range(4):
                        h = hh * 4 + h4
                        S.op("pe", lambda: nc.tensor.matmul(out=pa[:, h4 * 128:(h4 + 1) * 128], lhsT=kT[:, h, :], rhs=qT[:, h, :], start=True, stop=True),
                             reads=[bk, bq], writes=[bpa], inc=(h4 == 3))
                    S.op("dve", lambda: nc.vector.tensor_tensor(out=AT[:, hh * 4:(hh + 1) * 4, :], in0=pa[:].rearrange("p (h c) -> p h c", h=4),
                                                                in1=decT[:, hh * 4:(hh + 1) * 4, :], op=ALU.mult), reads=[bpa, b_rc], writes=[b_AT])
                for hh in range(2):
                    po, bpo = psO.next()
                    for h4 in range(4):
                        h = hh * 4 + h4
                        mmg(po[:, h4 * 128:(h4 + 1) * 128], [(AT[:, h, :], v_[:, h * 128:(h + 1) * 128]), (qd[:, h, :], R_[:, h, :])],
                            [b_AT, bv, b_qd, bR], bpo)
                    S.op("act", lambda: nc.scalar.copy(out=ro[:, hh * 4:(hh + 1) * 4, :], in_=po[:].rearrange("p (h c) -> p h c", h=4)), reads=[bpo], writes=[b_ro])
                S.op("dve", lambda: nc.vector.tensor_reduce(out=rmu[:, 0, :], in_=ro[:], axis=AX.X, op=ALU.add), reads=[b_ro], writes=[b_rmu])
                S.op("dve", lambda: nc.vector.tensor_scalar(out=rmu[:, 0, :], in0=rmu[:, 0, :], scalar1=-1.0 / 128, scalar2=None, op0=ALU.mult), reads=[b_rmu], writes=[b_rmu])
                S.op("dve", lambda: nc.vector.tensor_tensor(out=ro[:], in0=ro[:], in1=rmu[:, 0, :].unsqueeze(2).to_broadcast([128, 8, 128]), op=ALU.add),
                     reads=[b_ro, b_rmu], writes=[b_ro])
                S.op("dve", lambda: nc.vector.tensor_tensor(out=rsq[:], in0=ro[:], in1=ro[:], op=ALU.mult), reads=[b_ro], writes=[b_rsq])
                S.op("dve", lambda: nc.vector.tensor_reduce(out=rmu[:, 1, :], in_=rsq[:], axis=AX.X, op=ALU.add), reads=[b_rsq], writes=[b_rmu])
                rstd_from_ss(rmu[:, 1, :], rmu[:, 1, :], 128, [b_rmu], [b_rmu])
                S.op("dve", lambda: nc.vector.tensor_tensor(out=rsq[:], in0=ro[:], in1=rmu[:, 1, :].unsqueeze(2).to_broadcast([128, 8, 128]), op=ALU.mult),
                     reads=[b_ro, b_rmu], writes=[b_rsq])
                S.op("dve", lambda: nc.vector.tensor_tensor(out=rsq[:].rearrange("p h d -> p (h d)"), in0=rsq[:].rearrange("p h d -> p (h d)"), in1=g_retB[:], op=ALU.mult),
                     reads=[b_rsq, b_rc], writes=[b_rsq])
                S.op("act", lambda: nc.scalar.activation(out=gr[:], in_=gr[:], func=AF.Silu), reads=[bg], writes=[bg])
                ob, bob = orb.next()
                S.op("dve", lambda: nc.vector.tensor_tensor(out=ob[:], in0=rsq[:].rearrange("p h d -> p (h d)"), in1=gr[:], op=ALU.mult),
                     reads=[b_rsq, bg], writes=[bob])
                S.dma("sp", omix_d[sl, 1024:2048], ob[:], reads=[bob], writes=[b_omix], chan="omix")
            S.barrier()

        b_x1 = Buf("x1d")
        with scope() as ls:
            wo = sbl(ls, "wo", [128, 16, D], BF16); b_wo = Buf("wo")
            S.dma("sp", wo[:], w_out_bf.rearrange("(k p) n -> p k n", p=128), reads=[b_wscr], writes=[b_wo])
            omr = ring(ls, "om", 2, [128, D], BF16); oTr = ring(ls, "oT", 2, [128, 16, 128], BF16)
            xor_ = ring(ls, "xo", 2, [128, D], F32); x1r = ring(ls, "x1", 2, [128, D], F32)
            junk = sbl(ls, "junkD", [128, D], BF16); b_junk = Buf("junkD")
            ssr = ring(ls, "ssD", 2, [128, 2], F32); xnr = ring(ls, "xnD", 2, [128, D], BF16)
            h2r = ring(ls, "h2T", 2, [128, 16, 128], BF16)
            psT = pring(ls, "DpsT", 2, [128, 1024], BF16); psM = pring(ls, "DpsM", 4, [128, 512], F32)
            for m in range(NO):
                sl = slice(m * 128, (m + 1) * 128)
                om, bom = omr.next()
                S.dma("sp", om[:], omix_d[sl, :], reads=[b_omix], writes=[bom])
                oT, boT = oTr.next()
                for half in range(2):
                    pt, bp = psT.next()
                    for k8 in range(8):
                        kc = half * 8 + k8
                        S.op("pe", lambda: nc.tensor.transpose(out=pt[:, k8 * 128:(k8 + 1) * 128], in_=om[:, kc * 128:(kc + 1) * 128], identity=ident_b[:]),
                             reads=[bom, b_ident], writes=[bp], inc=(k8 == 7))
                    S.op("act", lambda: nc.scalar.copy(out=oT[:, half * 8:(half + 1) * 8, :], in_=pt[:].rearrange("p (k t) -> p k t", k=8)), reads=[bp], writes=[boT])
                xo, bxo = xor_.next()
                S.dma("pool", xo[:], x_own[sl, :], writes=[bxo])
                x1, bx1 = x1r.next()
                for n4 in range(4):
                    pt, bp = psM.next()
                    ns = slice(n4 * 512, (n4 + 1) * 512)
                    mmg(pt[:], [(oT[:, k, :], wo[:, k, ns]) for k in range(16)], [boT, b_wo], bp)
                    S.op("dve", lambda: nc.vector.tensor_tensor(out=x1[:, ns], in0=pt[:], in1=gate1B[:, ns], op=ALU.mult), reads=[bp, b_gate], writes=[bx1])
                    S.op("pool", lambda: nc.gpsimd.tensor_tensor(out=x1[:, ns], in0=x1[:, ns], in1=xo[:, ns], op=ALU.add), reads=[bx1, bxo], writes=[bx1])
                S.dma("sp", x1_d[sl, :], x1[:], reads=[bx1], writes=[b_x1], chan="x1d")
                ss, bs = ssr.next()
                S.op("act", lambda: nc.scalar.activation(out=junk[:], in_=x1[:], func=AF.Square, accum_out=ss[:, 0:1]), reads=[bx1], writes=[b_junk, bs])
                rstd_from_ss(ss[:, 1:2], ss[:, 0:1], D, [bs], [bs])
                xn, bn = xnr.next()
                S.op("act", lambda: nc.scalar.activation(out=xn[:], in_=x1[:], func=AF.Copy, scale=ss[:, 1:2]), reads=[bx1, bs], writes=[bn])
                h2T, bh2 = h2r.next()
                for half in range(2):
                    pt, bp = psT.next()
                    for k8 in range(8):
                        kc = half * 8 + k8
                        S.op("pe", lambda: nc.tensor.transpose(out=pt[:, k8 * 128:(k8 + 1) * 128], in_=xn[:, kc * 128:(kc + 1) * 128], identity=ident_b[:]),
                             reads=[bn, b_ident], writes=[bp], inc=(k8 == 7))
                    dst = h2T[:, half * 8:(half + 1) * 8, :]
                    S.op("dve", lambda: nc.vector.tensor_tensor(out=dst, in0=pt[:].rearrange("p (k t) -> p k t", k=8),
                                                                in1=modT[:, 2, half * 8:(half + 1) * 8].unsqueeze(2).to_broadcast([128, 8, 128]), op=ALU.mult),
                         reads=[bp, b_modT], writes=[bh2])
                    S.op("dve", lambda: nc.vector.tensor_tensor(out=dst, in0=dst,
                                                                in1=modT[:, 3, half * 8:(half + 1) * 8].unsqueeze(2).to_broadcast([128, 8, 128]), op=ALU.add),
                         reads=[bh2, b_modT], writes=[bh2])
                S.dma("sp", h2T_d[:, :, sl].rearrange("k p t -> p k t"), h2T[:], reads=[bh2], writes=[b_x1], chan="x1d")
            S.barrier()

        with scope() as ls:
            wq = sbl(ls, "wq", [128, 16, D], BF16); b_wo = Buf("wq")
            S.dma("pool", wq[:], w_q_bf.rearrange("(k p) n -> p k n", p=128), reads=[b_wscr], writes=[b_wo])
            h2r = ring(ls, "D2h2T", 2, [128, 16, 128], BF16)
            qTa = ring(ls, "qTa", 3, [128, 128], BF16)
            scr_ = ring(ls, "sc", 2, [128, 16, 128], F32); scx = sbl(ls, "scx", [128, 256], F32); b_scx = Buf("scx")
            v16 = sbl(ls, "v16", [128, 16, 16], F32); b_v16 = Buf("v16")
            cand = sbl(ls, "cand", [128, 8, 256], F32); b_cand = Buf("cand")
            t16 = sbl(ls, "t16", [128, 8, 16], F32); b_t16 = Buf("t16")
            tneg = sbl(ls, "tneg", [128, 8], F32); zs = sbl(ls, "zs", [128, 8], F32); ej = sbl(ls, "ej", [128, 16], F32); b_z = Buf("z")
            psQ = pring(ls, "DpsQ", 4, [128, 512], F32)
            for m in range(NO):
                sl = slice(m * 128, (m + 1) * 128)
                h2T, bh2 = h2r.next()
                S.dma("sp", h2T[:], h2T_d[:, :, sl].rearrange("k p t -> p k t"), reads=[b_x1], writes=[bh2])
                sc, bsc = scr_.next()
                for a in range(16):
                    pq, bpq = psQ.next()
                    mmg(pq[:, 0:128], [(wq[:, k, a * 128:(a + 1) * 128], h2T[:, k, :]) for k in range(16)], [b_wo, bh2], bpq)
                    qa, bqa = qTa.next()
                    S.op("act", lambda: nc.scalar.copy(out=qa[:], in_=pq[:, 0:128]), reads=[bpq], writes=[bqa])
                    pq2, bpq2 = psQ.next()
                    mmg(pq2[:, 0:128], [(qa[:], skT[:, a, :])], [bqa, b_skT], bpq2)
                    S.op("act", lambda: nc.scalar.copy(out=sc[:, a, :], in_=pq2[:, 0:128]), reads=[bpq2], writes=[bsc])
                    S.op("dve", lambda: nc.vector.max(out=v16[:, a, 0:8], in_=sc[:, a, :]), reads=[bsc], writes=[b_v16])
                    S.op("dve", lambda: nc.vector.match_replace(out=scx[:, 0:128], in_to_replace=v16[:, a, 0:8], in_values=sc[:, a, :], imm_value=-1e30),
                         reads=[bsc, b_v16], writes=[b_scx])
                    S.op("dve", lambda: nc.vector.max(out=v16[:, a, 8:16], in_=scx[:, 0:128]), reads=[b_scx], writes=[b_v16])
                S.dma("sp", sc_d[sl, :, :], sc[:], reads=[bsc], writes=[b_x1], chan="x1d")
                v16v = v16[:].rearrange("p (h t) k -> p h t k", t=2)
                S.op("dve", lambda: nc.vector.tensor_tensor(out=cand[:].rearrange("p h (i j) -> p h i j", i=16),
                                                            in0=v16v[:, :, 0, :].unsqueeze(3).to_broadcast([128, 8, 16, 16]),
                                                            in1=v16v[:, :, 1, :].unsqueeze(2).to_broadcast([128, 8, 16, 16]), op=ALU.add),
                     reads=[b_v16], writes=[b_cand])
                for h in range(8):
                    S.op("dve", lambda: nc.vector.max(out=t16[:, h, 0:8], in_=cand[:, h, :]), reads=[b_cand], writes=[b_t16])
                    S.op("dve", lambda: nc.vector.match_replace(out=scx[:], in_to_replace=t16[:, h, 0:8], in_values=cand[:, h, :], imm_value=-1e30),
                         reads=[b_cand, b_t16], writes=[b_scx])
                    S.op("dve", lambda: nc.vector.max(out=t16[:, h, 8:16], in_=scx[:]), reads=[b_scx], writes=[b_t16])
                S.op("dve", lambda: nc.vector.tensor_scalar(out=tneg[:], in0=t16[:, :, 15], scalar1=-1.0, scalar2=None, op0=ALU.mult), reads=[b_t16], writes=[b_z])
                for h in range(8):
                    S.op("act", lambda: nc.scalar.activation(out=ej[:], in_=t16[:, h, :], func=AF.Exp, bias=tneg[:, h:h + 1], accum_out=zs[:, h:h + 1]),
                         reads=[b_t16, b_z], writes=[b_z])
                S.op("act", lambda: nc.scalar.activation(out=zs[:], in_=zs[:], func=AF.Ln), reads=[b_z], writes=[b_z])
                S.op("dve", lambda: nc.vector.tensor_tensor(out=lnz_all[:, m, :], in0=zs[:], in1=t16[:, :, 15], op=ALU.add), reads=[b_z, b_t16], writes=[b_thr[m]])
                S.op("dve", lambda: nc.vector.tensor_scalar(out=thr_all[:, m, :], in0=t16[:, :, 15], scalar1=-1e-5, scalar2=None, op0=ALU.add),
                     reads=[b_t16], writes=[b_thr[m]])
            S.barrier()

        TT = 4
        EBW = 256
        NEB = NEXP // EBW
        with scope() as ls:
            g_finB = sbl(ls, "g_finB", [128, D], F32); b_gf = Buf("gfin")
            S.dma("sp", g_finB[:], g_fin[0:1, :].partition_broadcast(128), writes=[b_gf])
            h2T4 = sbl(ls, "Eh2T", [128, TT, 16, 128], BF16); b_h2 = [Buf("Eh2T%d" % t) for t in range(TT)]
            scr_ = ring(ls, "Esc", 1, [128, 16, 128], F32); x1r = ring(ls, "Ex1", 1, [128, D], F32)
            e14 = sbl(ls, "e1", [128, TT, 8, 128], F32); e24 = sbl(ls, "e2", [128, TT, 8, 128], F32); tau4 = sbl(ls, "tau", [128, TT, 8], F32)
            b_e = [Buf("e%d" % t) for t in range(TT)]
            acc = sbl(ls, "acc", [128, TT, D], F32); b_acc = [Buf("acc%d" % t) for t in range(TT)]
            otmp2 = sbl(ls, "otmp2", [128, D], F32); b_ot2 = Buf("otmp2")
            UTr = ring(ls, "UTb", 2, [128, 16, EBW], BF16); Vr = ring(ls, "Vb", 2, [128, 2, D], BF16)
            Er = ring(ls, "E", 2, [128, 8, EBW], F32)
            Gr = ring(ls, "Gb", 2, [128, 8, EBW], BF16)
            axr = ring(ls, "ax", 2, [128, EBW], F32); atr = ring(ls, "at", 1, [128, EBW], F32); agr = ring(ls, "ag", 2, [128, EBW], F32)
            cTr = ring(ls, "coefT", 2, [128, 2, 128], BF16)
            junk = otmp2; b_junk = b_ot2
            ssE = sbl(ls, "ssE", [128, 2], F32); b_ssE = Buf("ssE")
            psA = pring(ls, "EpsA", 2, [128, 512], F32); psG = pring(ls, "EpsG", 2, [128, 512], F32)
            psOut = psl(ls, "EpsO", [128, D], F32); b_pOut = Buf("EpsO")
            UT_v = UT_bf.rearrange("(k p) e -> p k e", p=128)
            for sup in range(NO // TT):
                for t in range(TT):
                    m = sup * TT + t
                    sl = slice(m * 128, (m + 1) * 128)
                    sc, bsc = scr_.next()
                    S.dma("sp", h2T4[:, t, :, :], h2T_d[:, :, sl].rearrange("k p t -> p k t"), reads=[b_x1], writes=[b_h2[t]])
                    S.dma("sp", sc[:], sc_d[sl, :, :], reads=[b_x1], writes=[bsc])
                    scv = sc[:].rearrange("p (h t) k -> p h t k", t=2)
                    S.op("dve", lambda: nc.vector.tensor_tensor(out=e14[:, t, :, :], in0=scv[:, :, 0, :],
                                                                in1=lnz_all[:, m, :].unsqueeze(2).to_broadcast([128, 8, 128]), op=ALU.subtract),
                         reads=[bsc, b_thr[m]], writes=[b_e[t]])
                    S.op("act", lambda: nc.scalar.activation(out=e14[:, t, :, :], in_=e14[:, t, :, :], func=AF.Exp), reads=[b_e[t]], writes=[b_e[t]])
                    S.op("act", lambda: nc.scalar.activation(out=e24[:, t, :, :], in_=scv[:, :, 1, :], func=AF.Exp), reads=[bsc], writes=[b_e[t]])
                    S.op("dve", lambda: nc.vector.tensor_tensor(out=tau4[:, t, :], in0=thr_all[:, m, :], in1=lnz_all[:, m, :], op=ALU.subtract),
                         reads=[b_thr[m]], writes=[b_e[t]])
                    S.op("act", lambda: nc.scalar.activation(out=tau4[:, t, :], in_=tau4[:, t, :], func=AF.Exp), reads=[b_e[t]], writes=[b_e[t]])
                for eb in range(NEB):
                    UTb, bU = UTr.next(); Vb, bV = Vr.next()
                    S.dma("sp", UTb[:], UT_v[:, :, eb * EBW:(eb + 1) * EBW], reads=[b_wscr], writes=[bU])
                    S.dma("pool", Vb[:], V_bf[eb * EBW:(eb + 1) * EBW, :].rearrange("(c p) n -> p c n", p=128), reads=[b_wscr], writes=[bV])
                    for t in range(TT):
                        pa, bpa = psA.next()
                        for c in range(2):
                            mmg(pa[:, c * 128:(c + 1) * 128], [(UTb[:, k, c * 128:(c + 1) * 128], h2T4[:, t, k, :]) for k in range(16)], [bU, b_h2[t]], bpa)
                        ax, bax = axr.next(); at_, bat = atr.next(); ag, bag = agr.next()
                        S.op("act", lambda: nc.scalar.copy(out=ax[:], in_=pa[:, 0:EBW]), reads=[bpa], writes=[bax])
                        S.op("pool", lambda: nc.gpsimd.tensor_tensor(out=at_[:], in0=ax[:], in1=ax[:], op=ALU.mult), reads=[bax], writes=[bat])
                        S.op("dve", lambda: nc.vector.tensor_scalar(out=at_[:], in0=at_[:], scalar1=0.044715, scalar2=1.0, op0=ALU.mult, op1=ALU.add),
                             reads=[bat], writes=[bat])
                        S.op("pool", lambda: nc.gpsimd.tensor_tensor(out=at_[:], in0=at_[:], in1=ax[:], op=ALU.mult), reads=[bat, bax], writes=[bat])
                        S.op("act", lambda: nc.scalar.activation(out=at_[:], in_=at_[:], func=AF.Sigmoid, scale=gelu_c), reads=[bat], writes=[bat])
                        S.op("pool", lambda: nc.gpsimd.tensor_tensor(out=ag[:], in0=at_[:], in1=ax[:], op=ALU.mult), reads=[bat, bax], writes=[bag])
                        Et, bE = Er.next(); Gb, bG = Gr.next()
                        S.op("pool", lambda: nc.gpsimd.tensor_tensor(out=Et[:].rearrange("p h (a k) -> p h a k", a=2),
                                                                     in0=e14[:, t, :, eb * 2:(eb + 1) * 2].unsqueeze(3).to_broadcast([128, 8, 2, 128]),
                                                                     in1=e24[:, t, :, :].unsqueeze(2).to_broadcast([128, 8, 2, 128]), op=ALU.mult),
                             reads=[b_e[t]], writes=[bE])
                        for h in range(8):
                            S.op("dve", lambda: nc.vector.scalar_tensor_tensor(out=Gb[:, h, :], in0=Et[:, h, :], scalar=tau4[:, t, h:h + 1], in1=Et[:, h, :],
                                                                               op0=ALU.is_ge, op1=ALU.mult), reads=[bE, b_e[t]], writes=[bG])
                        pg, bpg = psG.next()
                        for c in range(2):
                            mmg(pg[:, c * 128:(c + 1) * 128], [(Gb[:, h, c * 128:(c + 1) * 128], ident_b[:]) for h in range(8)], [bG, b_ident], bpg)
                        cT, bcT = cTr.next()
                        S.op("dve", lambda: nc.vector.tensor_tensor(out=cT[:].rearrange("p c t -> p (c t)"), in0=pg[:, 0:EBW], in1=ag[:], op=ALU.mult),
                             reads=[bpg, bag], writes=[bcT])
                        for n4 in range(4):
                            for c in range(2):
                                S.op("pe", lambda: nc.tensor.matmul(out=psOut[:, n4 * 512:(n4 + 1) * 512], lhsT=cT[:, c, :], rhs=Vb[:, c, n4 * 512:(n4 + 1) * 512],
                                                                    start=(c == 0), stop=(c == 1)),
                                     reads=[bcT, bV], writes=[b_pOut], inc=(n4 == 3 and c == 1))
                        if eb == 0:
                            S.op("act", lambda: nc.scalar.copy(out=acc[:, t, :], in_=psOut[:]), reads=[b_pOut], writes=[b_acc[t]])
                        else:
                            S.op("act", lambda: nc.scalar.copy(out=otmp2[:], in_=psOut[:]), reads=[b_pOut], writes=[b_ot2])
                            S.op("pool", lambda: nc.gpsimd.tensor_tensor(out=acc[:, t, :], in0=acc[:, t, :], in1=otmp2[:], op=ALU.add),
                                 reads=[b_ot2, b_acc[t]], writes=[b_acc[t]])
                for t in range(TT):
                    m = sup * TT + t
                    sl = slice(m * 128, (m + 1) * 128)
                    x1, bx1 = x1r.next()
                    S.dma("sp", x1[:], x1_d[sl, :], reads=[b_x1], writes=[bx1])
                    x2 = acc[:, t, :]
                    S.op("dve", lambda: nc.vector.tensor_tensor(out=x2, in0=x2, in1=gate2B[:], op=ALU.mult), reads=[b_acc[t], b_gate], writes=[b_acc[t]])
                    S.op("pool", lambda: nc.gpsimd.tensor_tensor(out=x2, in0=x2, in1=x1[:], op=ALU.add), reads=[b_acc[t], bx1], writes=[b_acc[t]])
                    S.op("act", lambda: nc.scalar.activation(out=junk[:], in_=x2, func=AF.Square, accum_out=ssE[:, 0:1]), reads=[b_acc[t]], writes=[b_junk, b_ssE])
                    rstd_from_ss(ssE[:, 1:2], ssE[:, 0:1], D, [b_ssE], [b_ssE])
                    S.op("dve", lambda: nc.vector.scalar_tensor_tensor(out=x1[:], in0=x2, scalar=ssE[:, 1:2], in1=g_finB[:], op0=ALU.mult, op1=ALU.mult),
                         reads=[b_acc[t], b_ssE, b_gf], writes=[bx1])
                    S.dma("sp", y[sl, :], x1[:], reads=[bx1], writes=[b_y], chan="yout")
            S.barrier()

        if dbg:
            with scope() as ls:
                dt_ = sbl(ls, "dbgt", [128, 4096], F32); b_dt = Buf("dbgt")
                dtb = sbl(ls, "dbgtb", [128, 4096], BF16)
                if "KcT" in dbg_out:
                    S.op("dve", lambda: nc.vector.tensor_copy(out=dt_[0:71, 0:4 * NCC * 128], in_=KcT[:].rearrange("p a b -> p (a b)")), reads=[b_KcT], writes=[b_dt])
                    S.dma("sp", dbg_out["KcT"][:, :], dt_[0:71, 0:4 * NCC * 128], reads=[b_dt], writes=[Buf("o1")])
                if "qA" in dbg_out:
                    S.dma("sp", dtb[0:64, 0:SO], qA_d[5, :, :], writes=[b_dt])
                    S.op("dve", lambda: nc.vector.tensor_copy(out=dt_[0:64, 0:SO], in_=dtb[0:64, 0:SO]), reads=[b_dt], writes=[b_dt])
                    S.dma("sp", dbg_out["qA"][:, :], dt_[0:64, 0:SO], reads=[b_dt], writes=[Buf("o2")])
                if "R" in dbg_out:
                    S.dma("sp", dtb[:, 0:1024], R_d[NO - 1].rearrange("p a b -> p (a b)"), writes=[b_dt])
                    S.op("dve", lambda: nc.vector.tensor_copy(out=dt_[:, 0:1024], in_=dtb[:, 0:1024]), reads=[b_dt], writes=[b_dt])
                    S.dma("sp", dbg_out["R"][:, :], dt_[:, 0:1024], reads=[b_dt], writes=[Buf("o3")])
                S.barrier()

        if dbg and ("omix" in dbg_out or "x1" in dbg_out):
            with scope() as ls:
                tb = sbl(ls, "tapb", [128, D], BF16); tf = sbl(ls, "tapf", [128, D], F32); b_tap = Buf("tap")
                for m in range(NO):
                    sl = slice(m * 128, (m + 1) * 128)
                    if "omix" in dbg_out:
                        S.dma("sp", tb[:], omix_d[sl, :], writes=[b_tap])
                        S.op("dve", lambda: nc.vector.tensor_copy(out=tf[:], in_=tb[:]), reads=[b_tap], writes=[b_tap])
                        S.dma("sp", dbg_out["omix"][sl, :], tf[:], reads=[b_tap], writes=[b_tap])
                    if "x1" in dbg_out:
                        S.dma("sp", tf[:], x1_d[sl, :], writes=[b_tap])
                        S.dma("sp", dbg_out["x1"][sl, :], tf[:], reads=[b_tap], writes=[b_tap])
        S.barrier()
        S.finish_all()
        print("ninst", S.ninst, "nsem", S.nsem)
    return nc


def core_inputs(inp, b, j, SEQ):
    NT = SEQ // 128
    xb = np.ascontiguousarray(inp["x"][b])
    own = np.ascontiguousarray(xb.reshape(NT // 2, 2, 128, D)[:, j].reshape(SEQ // 2, D))
    m = {
        "x_all": xb, "x_own": own,
        "c_in": np.ascontiguousarray(inp["c"][b].reshape(16, 128).T),
        "w_ada": inp["w_ada"][0], "b_ada": inp["b_ada"][0].reshape(1, -1),
        "g_mix": inp["g_norm_mix"][0].reshape(1, -1), "g_ffn": inp["g_norm_ffn"][0].reshape(1, -1),
        "g_fin": inp["g_norm_final"].reshape(1, -1), "w_in": inp["w_in"][0],
        "pe_k": inp["cmp_pe_k"][0], "pe_v": inp["cmp_pe_v"][0],
        "ck_w1": inp["cmp_k_w1"][0], "ck_w2": inp["cmp_k_w2"][0], "cv_w1": inp["cmp_v_w1"][0], "cv_w2": inp["cmp_v_w2"][0],
        "g_nsa": inp["g_nsa_out"][0].reshape(1, -1), "g_ret": inp["g_ret_out"][0].reshape(1, -1),
        "w_out": inp["w_out"][0], "w_q": inp["peer_w_q"][0], "subk": inp["peer_sub_keys"][0].reshape(16, 128, 128),
        "U": inp["peer_u"][0], "V": inp["peer_v"][0],
    }
    cs = host_consts(SEQ, j)
    for k in ("ident", "kaug", "caug", "qaug", "cm1", "vmask", "amask", "overlap", "eall", "cmab", "wmask",
              "decT", "kdec", "qdecB", "jsel"):
        m[k] = np.ascontiguousarray(cs[k])
    return {k: np.ascontiguousarray(np.asarray(v)) for k, v in m.items()}


def kernel(**inputs):
    inp = {k: np.asarray(v) for k, v in inputs.items()}
    B, SEQ, _ = inp["x"].shape
    nc = build(SEQ)
    maps = [core_inputs(inp, b, j, SEQ) for b in range(B) for j in range(2)]
    res = run_bass_kernel_spmd(nc, maps, core_ids=list(range(2 * B)))
    out = np.zeros((B, SEQ, D), np.float32)
    NT = SEQ // 128
    for b in range(B):
        ov = out[b].reshape(NT // 2, 2, 128, D)
        for j in range(2):
            ov[:, j] = np.asarray(res.results[2 * b + j]["y"]).reshape(NT // 2, 128, D)
    return out
```

```python
import contextlib
import numpy as np
import ml_dtypes
import concourse.bass as bass
import concourse.mybir as mybir
from concourse.bass_utils import run_bass_kernel_spmd

F32 = mybir.dt.float32
BF16 = mybir.dt.bfloat16
ALU = mybir.AluOpType
AF = mybir.ActivationFunctionType
AX = mybir.AxisListType
EPOCH = 30000
D = 2048
IN_W = 6704
C_QA, C_KC, C_VC, C_KS, C_VS, C_KW, C_VW, C_GA, C_QR, C_KR, C_VR, C_GR = (
    0, 1024, 1280, 1536, 1792, 2048, 2304, 2560, 2608, 3632, 4656, 5680)
NEG = -30000.0
NEXP = 16384


class Buf:
    __slots__ = ("name", "w", "r")

    def __init__(self, name=""):
        self.name = name
        self.w = None
        self.r = {}


class Sync:
    def __init__(self, nc, stack):
        self.nc = nc
        self.stack = stack
        self.engines = {"pe": nc.tensor, "dve": nc.vector, "act": nc.scalar,
                        "pool": nc.gpsimd, "sp": nc.sync}
        self.sems = {}
        self.cnt = {}
        self.epoch = {e: 0 for e in self.engines}
        self.ecount = {e: 0 for e in self.engines}
        self.known = {e: {} for e in self.engines}
        self.nsem = 0
        self.ninst = {e: 0 for e in self.engines}

    def _sem(self, key):
        if key not in self.sems:
            self.sems[key] = self.stack.enter_context(self.nc.semaphore("s%d" % self.nsem))
            self.nsem += 1
            self.cnt[key] = 0
        return self.sems[key]

    def _ekey(self, e):
        return ("E", e, self.epoch[e])

    def _wait(self, e, deps, skip_same_pe=True):
        eng = self.engines[e]
        need = {}
        for d in deps:
            if d is None:
                continue
            k, v = d
            if e == "pe" and k[0] == "E" and k[1] == "pe" and skip_same_pe:
                continue
            if k[0] == "D":
                v = self.cnt[k]
            if self.known[e].get(k, 0) >= v:
                continue
            if need.get(k, 0) < v:
                need[k] = v
        for k, v in need.items():
            eng.wait_ge(self.sems[k], v)
            self.known[e][k] = v
            self.ninst[e] += 1

    def _deps(self, reads, writes):
        deps = []
        for b in reads:
            deps.append(b.w)
        for b in writes:
            deps.append(b.w)
            for k, v in b.r.items():
                deps.append((k, v))
        return deps

    def op(self, e, fn, reads=(), writes=(), inc=True):
        self._wait(e, self._deps(reads, writes))
        ins = fn()
        self.ninst[e] += 1
        if self.ecount[e] >= EPOCH and inc:
            self.epoch[e] += 1
            self.ecount[e] = 0
        k = self._ekey(e)
        sem = self._sem(k)
        if inc:
            ins.then_inc(sem, 1)
            self.cnt[k] += 1
            self.ecount[e] += 1
            v = self.cnt[k]
        else:
            v = self.cnt[k] + 1
        for b in reads:
            if b.r.get(k, 0) < v:
                b.r[k] = v
        for b in writes:
            b.w = (k, v)
            b.r = {}
        return ins

    def dma(self, q, out, in_, reads=(), writes=(), chan=None, **kw):
        self._wait(q, self._deps(reads, writes), skip_same_pe=False)
        k = ("D", chan if chan is not None else (writes[0].name if writes else "dma"))
        sem = self._sem(k)
        ins = self.engines[q].dma_start(out=out, in_=in_, **kw)
        ins.then_inc(sem, 16)
        self.ninst[q] += 1
        self.cnt[k] += 16
        v = self.cnt[k]
        for b in reads:
            if b.r.get(k, 0) < v:
                b.r[k] = v
        for b in writes:
            b.w = (k, v)
            b.r = {}
        return ins

    def finish_all(self):
        deps = [(k, v) for k, v in self.cnt.items() if v > 0]
        self._wait("sp", deps, skip_same_pe=False)

    def barrier(self):
        for e in self.engines:
            deps = [(k, v) for k, v in self.cnt.items() if v > 0]
            self._wait(e, deps, skip_same_pe=False)


class Ring:
    def __init__(self, tiles, name):
        self.tiles = tiles
        self.bufs = [Buf("%s%d" % (name, i)) for i in range(len(tiles))]
        self.i = -1

    def next(self):
        self.i = (self.i + 1) % len(self.tiles)
        return self.tiles[self.i], self.bufs[self.i]


def bf(x):
    return np.asarray(x).astype(ml_dtypes.bfloat16)


def split3(v):
    v = np.asarray(v, np.float64)
    a = bf(v)
    r = v - a.astype(np.float64)
    b = bf(r)
    r2 = r - b.astype(np.float64)
    c = bf(r2)
    return a, b, c


def host_consts(SEQ, j):
    NT = SEQ // 128
    NO = NT // 2
    NSLC = SEQ // 64
    NCMP = SEQ // 16
    NCC = max(1, NCMP // 128)
    n_cmp = (SEQ - 32) // 16 + 1
    cs = {}
    cs["ident"] = np.eye(128, dtype=np.float32)
    pos = np.arange(SEQ)
    cs["kaug"] = bf(np.stack([pos // 128, pos // 128, pos % 128, pos % 128,
                              np.ones(SEQ), np.ones(SEQ), np.ones(SEQ)]).astype(np.float32))
    cpos = np.arange(NCC * 128) * 16 + 31
    cs["caug"] = bf(np.stack([cpos // 128, cpos // 128, cpos % 128, cpos % 128,
                              np.ones_like(cpos), np.ones_like(cpos), np.ones_like(cpos)]).astype(np.float32))
    slopes = np.exp2(-8.0 * (np.arange(16, dtype=np.float64) + 1.0) / 16).astype(np.float32).astype(np.float64)
    qaug = np.zeros((NO, 4, 7, 4, 128), dtype=ml_dtypes.bfloat16)
    for m in range(NO):
        t = 128.0 * (2 * m + j) + np.arange(128)
        for g in range(4):
            for r in range(4):
                s = slopes[4 * g + r]
                shi = bf(s)
                slo = bf(s - float(shi))
                seff = float(shi) + float(slo)
                a, b, c = split3(seff * t)
                qaug[m, g, 0, r, :] = bf(128.0 * float(shi))
                qaug[m, g, 1, r, :] = bf(128.0 * float(slo))
                qaug[m, g, 2, r, :] = shi
                qaug[m, g, 3, r, :] = slo
                qaug[m, g, 4, r, :] = bf(-a.astype(np.float32))
                qaug[m, g, 5, r, :] = bf(-b.astype(np.float32))
                qaug[m, g, 6, r, :] = bf(-c.astype(np.float32))
    cs["qaug"] = qaug.reshape(NO, 4, 7, 512)
    n = np.arange(NCC * 128)
    cm1 = np.zeros((NO, 128, NCC, 128), np.float32)
    vm = np.zeros((NO, 128, NSLC), np.float32)
    am = np.zeros((NO, 128, NSLC), np.float32)
    mb = np.arange(NSLC)
    for m in range(NO):
        t = 128 * (2 * m + j) + np.arange(128)
        ok = (n[:, None] < n_cmp) & ((16 * n[:, None] + 31) <= t[None, :])
        cm1[m] = np.where(ok, 0.0, NEG).reshape(NCC, 128, 128).transpose(1, 0, 2)
        back = (t // 64)[:, None] - mb[None, :]
        valid = back >= 0
        forced = valid & ((mb[None, :] == 0) | (back < 2))
        vm[m] = (valid & ~forced).astype(np.float32)
        am[m] = np.where(forced, 1e4 + mb[None, :], np.where(valid, 0.0, -1.0))
    cs["cm1"] = bf(cm1)
    cs["vmask"] = vm
    cs["amask"] = am
    ov = ((n[:, None] < n_cmp) & (n[:, None] >= 4 * mb[None, :] - 1) & (n[:, None] <= 4 * mb[None, :] + 3))
    cs["overlap"] = bf(ov.astype(np.float32).reshape(NCC, 128, NSLC).transpose(1, 0, 2))
    E = np.zeros((NSLC, NT, 128), np.float32)
    for c in range(NT):
        E[2 * c, c, :64] = 1.0
        E[2 * c + 1, c, 64:] = 1.0
    cs["eall"] = bf(E)
    k = np.arange(128)[:, None]
    q = np.arange(128)[None, :]
    tri = np.where(k <= q, 0.0, NEG).astype(np.float32)
    full = np.full((128, 128), NEG, np.float32)
    zero = np.zeros((128, 128), np.float32)
    cs["cmab"] = bf(np.stack([tri, full] if j == 0 else [zero, tri], axis=1))
    wm = np.zeros((128, 6, 128), np.float32)
    for u in range(6):
        d = 128 * (j + 4 - u) + q - k
        wm[:, u, :] = np.where((d >= 0) & (d < 512), 0.0, NEG)
    cs["wmask"] = bf(wm)
    lg = np.log1p(-np.exp2(-5.0 - np.arange(8, dtype=np.float64)))
    p = np.arange(128, dtype=np.float64)
    dec = np.zeros((128, 8, 128), np.float32)
    for h in range(8):
        diff = p[None, :] - p[:, None]
        dec[:, h, :] = np.where(diff >= 0, np.exp(lg[h] * np.maximum(diff, 0.0)), 0.0) * (128.0 ** -0.5)
    cs["decT"] = dec
    cs["kdec"] = np.stack([np.exp(lg[h] * (127.0 - p)) * (128.0 ** -0.5) for h in range(8)], axis=1).astype(np.float32)
    qd = np.stack([np.exp(lg[h] * (p + 1.0)) for h in range(8)], axis=0).astype(np.float32)
    cs["qdecB"] = np.broadcast_to(qd[None], (128, 8, 128)).copy()
    cs["cdec"] = np.exp(lg * 128.0)
    cs["jsel"] = np.full((128, 1), float(j), np.float32)
    return cs


def build(SEQ, dbg=None):
    NT = SEQ // 128
    NO = NT // 2
    NS = SEQ // 512
    NOS = NO // 4
    NSLC = SEQ // 64
    NCMP = SEQ // 16
    NCC = max(1, NCMP // 128)
    SO = SEQ // 2
    cdec = host_consts(256, 0)["cdec"] if False else np.exp(np.log1p(-np.exp2(-5.0 - np.arange(8, dtype=np.float64))) * 128.0)
    nc = bass.Bass("TRN2", target_bir_lowering=False)

    def din(name, shape, dt=F32):
        return nc.dram_tensor(name, list(shape), dt, kind="ExternalInput").ap()

    def dscr(name, shape, dt):
        return nc.dram_tensor(name, list(shape), dt).ap()

    x_all = din("x_all", [SEQ, D]); x_own = din("x_own", [SO, D]); c_in = din("c_in", [128, 16])
    w_ada = din("w_ada", [D, 6 * D]); b_ada = din("b_ada", [1, 6 * D])
    g_mix = din("g_mix", [1, D]); g_ffn = din("g_ffn", [1, D]); g_fin = din("g_fin", [1, D])
    w_in = din("w_in", [D, IN_W])
    pe_k = din("pe_k", [32, 64]); pe_v = din("pe_v", [32, 64])
    ck_w1 = din("ck_w1", [2048, 128]); ck_w2 = din("ck_w2", [128, 64])
    cv_w1 = din("cv_w1", [2048, 128]); cv_w2 = din("cv_w2", [128, 64])
    g_nsa = din("g_nsa", [1, 1024]); g_ret = din("g_ret", [1, 1024])
    w_out = din("w_out", [D, D]); w_q = din("w_q", [D, D]); subk = din("subk", [16, 128, 128])
    U = din("U", [NEXP, D]); V = din("V", [NEXP, D])
    h_ident = din("ident", [128, 128]); h_kaug = din("kaug", [7, SEQ], BF16); h_caug = din("caug", [7, NCC * 128], BF16)
    h_qaug = din("qaug", [NO, 4, 7, 512], BF16); h_cm1 = din("cm1", [NO, 128, NCC, 128], BF16)
    h_vmask = din("vmask", [NO, 128, NSLC]); h_amask = din("amask", [NO, 128, NSLC])
    h_overlap = din("overlap", [128, NCC, NSLC], BF16); h_eall = din("eall", [NSLC, NT, 128], BF16)
    h_cmab = din("cmab", [128, 2, 128], BF16); h_wmask = din("wmask", [128, 6, 128], BF16)
    h_decT = din("decT", [128, 8, 128]); h_kdec = din("kdec", [128, 8]); h_qdecB = din("qdecB", [128, 8, 128])
    h_jsel = din("jsel", [128, 1])
    y = nc.dram_tensor("y", [SO, D], F32, kind="ExternalOutput").ap()
    dbg_out = {}
    if dbg:
        for nm, shp in dbg.items():
            dbg_out[nm] = nc.dram_tensor("dbg_" + nm, list(shp), F32, kind="ExternalOutput").ap()

    w_in_bf = dscr("w_in_bf", [D, IN_W], BF16); w_out_bf = dscr("w_out_bf", [D, D], BF16); w_q_bf = dscr("w_q_bf", [D, D], BF16)
    UT_bf = dscr("UT_bf", [D, NEXP], BF16); V_bf = dscr("V_bf", [NEXP, D], BF16)
    kS_d = dscr("kS_d", [4, 64, SEQ], BF16); kW_d = dscr("kW_d", [4, 64, SEQ], BF16)
    vS_d = dscr("vS_d", [SEQ, 256], BF16); vW_d = dscr("vW_d", [SEQ, 256], BF16)
    qA_d = dscr("qA_d", [16, 64, SO], BF16); gA_d = dscr("gA_d", [SO, 48], F32)
    qR_d = dscr("qR_d", [8, 128, SO], BF16); kR_d = dscr("kR_d", [8, 128, SO], BF16)
    vR_d = dscr("vR_d", [SO, 1024], BF16); gR_d = dscr("gR_d", [SO, 1024], F32)
    R_d = dscr("R_d", [NO, 128, 8, 128], BF16)
    omix_d = dscr("omix_d", [SO, D], BF16)
    x1_d = dscr("x1_d", [SO, D], F32); h2T_d = dscr("h2T_d", [16, 128, SO], BF16)
    sc_d = dscr("sc_d", [SO, 16, 128], F32)

    with contextlib.ExitStack() as st:
        S = Sync(nc, st)

        def sbp(name, shape, dt):
            return st.enter_context(nc.sbuf_tensor("t_" + name, list(shape), dt))

        ident_f = sbp("ident_f", [128, 128], F32); b_ident = Buf("ident")
        ident_b = sbp("ident_b", [128, 128], BF16)
        gate1B = sbp("gate1B", [128, D], F32); gate2B = sbp("gate2B", [128, D], F32); b_gate = Buf("gate")
        modT = sbp("modT", [128, 4, 16], F32); b_modT = Buf("modT")
        KcT = sbp("KcT", [71, 4, NCC * 128], BF16); b_KcT = Buf("KcT")
        VcA = sbp("VcA", [128, NCC, 4, 65 + NSLC], BF16); b_VcA = Buf("VcA")
        skT = sbp("skT", [128, 16, 128], BF16); b_skT = Buf("skT")
        jsel = sbp("jsel", [128, 1], F32); b_jsel = Buf("jsel")
        thr_all = sbp("thr_all", [128, NO, 8], F32)
        lnz_all = sbp("lnz_all", [128, NO, 8], F32)
        b_thr = [Buf("thr%d" % i) for i in range(NO)]

        S.dma("sp", ident_f[:], h_ident[:, :], writes=[b_ident])
        S.op("dve", lambda: nc.vector.tensor_copy(out=ident_b[:], in_=ident_f[:]), reads=[b_ident], writes=[b_ident])
        S.dma("sp", jsel[:], h_jsel[:, :], writes=[b_jsel])

        def scope():
            return contextlib.ExitStack()

        def sbl(ls, name, shape, dt):
            return ls.enter_context(nc.sbuf_tensor("t_" + name, list(shape), dt))

        def psl(ls, name, shape, dt):
            return ls.enter_context(nc.psum_tensor("p_" + name, list(shape), dt))

        def ring(ls, name, n, shape, dt):
            return Ring([sbl(ls, "%s%d" % (name, i), shape, dt) for i in range(n)], name)

        def pring(ls, name, n, shape, dt):
            return Ring([psl(ls, "%s%d" % (name, i), shape, dt) for i in range(n)], name)

        def mmg(out_ap, pairs, rbufs, wbuf):
            n = len(pairs)
            for i, (l, r) in enumerate(pairs):
                S.op("pe", lambda: nc.tensor.matmul(out=out_ap, lhsT=l, rhs=r, start=(i == 0), stop=(i == n - 1)),
                     reads=rbufs, writes=[wbuf], inc=(i == n - 1))

        cast_rr = [0]

        def cast(out_ap, in_ap, rb, wb, psum=False):
            e = ("dve", "act")[cast_rr[0] % 2] if psum else ("dve", "pool", "act")[cast_rr[0] % 3]
            cast_rr[0] += 1
            if e == "act":
                S.op("act", lambda: nc.scalar.copy(out=out_ap, in_=in_ap), reads=rb, writes=wb)
            elif e == "dve":
                S.op("dve", lambda: nc.vector.tensor_copy(out=out_ap, in_=in_ap), reads=rb, writes=wb)
            else:
                S.op("pool", lambda: nc.gpsimd.tensor_copy(out=out_ap, in_=in_ap), reads=rb, writes=wb)

        def rstd_from_ss(rstd_ap, ss_ap, n, rb, wb):
            S.op("dve", lambda: nc.vector.tensor_scalar(out=rstd_ap, in0=ss_ap, scalar1=1.0 / n, scalar2=1e-6,
                                                        op0=ALU.mult, op1=ALU.add), reads=rb, writes=wb)
            S.op("act", lambda: nc.scalar.activation(out=rstd_ap, in_=rstd_ap, func=AF.Sqrt), reads=wb, writes=wb)
            S.op("dve", lambda: nc.vector.reciprocal(out=rstd_ap, in_=rstd_ap), reads=wb, writes=wb)

        with scope() as ls:
            cin_t = sbl(ls, "cin_t", [128, 16], F32); b_cin = Buf("cin")
            cb = sbl(ls, "cb", [128, 16, 128], F32); b_cb = Buf("cb")
            modB = sbl(ls, "modB", [128, 6, D], F32); b_modB = [Buf("modB%d" % v) for v in range(6)]
            wr = ring(ls, "wada", 3, [128, D], F32)
            btile = sbl(ls, "btile", [128, D], F32); b_bt = Buf("bt")
            gB = sbl(ls, "gB", [128, 2, D], F32); b_gB = Buf("gB")
            vecB = sbl(ls, "vecB", [128, 4, D], F32); b_vecB = Buf("vecB")
            psA = pring(ls, "psA", 8, [128, 512], F32)
            S.dma("sp", cin_t[:], c_in[:, :], writes=[b_cin])
            S.op("act", lambda: nc.scalar.activation(out=cin_t[:], in_=cin_t[:], func=AF.Silu), reads=[b_cin], writes=[b_cin])
            S.op("dve", lambda: nc.vector.tensor_copy(out=cb[:], in_=cin_t[:].unsqueeze(2).to_broadcast([128, 16, 128])),
                 reads=[b_cin], writes=[b_cb])
            for v in range(6):
                banks = [psA.next() for _ in range(4)]
                for k in range(16):
                    wt, bw = wr.next()
                    S.dma("sp" if k % 2 == 0 else "pool", wt[:], w_ada[k * 128:(k + 1) * 128, v * D:(v + 1) * D], writes=[bw])
                    for n4 in range(4):
                        pt, bp = banks[n4]
                        S.op("pe", lambda: nc.tensor.matmul(out=pt[:], lhsT=cb[:, k, :], rhs=wt[:, n4 * 512:(n4 + 1) * 512],
                                                            start=(k == 0), stop=(k == 15)),
                             reads=[b_cb, bw], writes=[bp], inc=(k == 15 or n4 == 3))
                S.dma("sp", btile[:], b_ada[0:1, v * D:(v + 1) * D].partition_broadcast(128), writes=[b_bt])
                for n4 in range(4):
                    pt, bp = banks[n4]
                    S.op("dve", lambda: nc.vector.tensor_tensor(out=modB[:, v, n4 * 512:(n4 + 1) * 512], in0=pt[:],
                                                                in1=btile[:, n4 * 512:(n4 + 1) * 512], op=ALU.add),
                         reads=[bp, b_bt], writes=[b_modB[v]])
            S.dma("sp", gB[:, 0, :], g_mix[0:1, :].partition_broadcast(128), writes=[b_gB])
            S.dma("sp", gB[:, 1, :], g_ffn[0:1, :].partition_broadcast(128), writes=[b_gB])
            S.op("dve", lambda: nc.vector.scalar_tensor_tensor(out=vecB[:, 0, :], in0=modB[:, 1, :], scalar=1.0, in1=gB[:, 0, :],
                                                               op0=ALU.add, op1=ALU.mult), reads=[b_modB[1], b_gB], writes=[b_vecB])
            S.op("dve", lambda: nc.vector.tensor_copy(out=vecB[:, 1, :], in_=modB[:, 0, :]), reads=[b_modB[0]], writes=[b_vecB])
            S.op("dve", lambda: nc.vector.scalar_tensor_tensor(out=vecB[:, 2, :], in0=modB[:, 4, :], scalar=1.0, in1=gB[:, 1, :],
                                                               op0=ALU.add, op1=ALU.mult), reads=[b_modB[4], b_gB], writes=[b_vecB])
            S.op("dve", lambda: nc.vector.tensor_copy(out=vecB[:, 3, :], in_=modB[:, 3, :]), reads=[b_modB[3]], writes=[b_vecB])
            S.op("pool", lambda: nc.gpsimd.tensor_copy(out=gate1B[:], in_=modB[:, 2, :]), reads=[b_modB[2]], writes=[b_gate])
            S.op("pool", lambda: nc.gpsimd.tensor_copy(out=gate2B[:], in_=modB[:, 5, :]), reads=[b_modB[5]], writes=[b_gate])
            for vi in range(4):
                for kc in range(16):
                    pt, bp = psA.next()
                    S.op("pe", lambda: nc.tensor.transpose(out=pt[:, 0:128], in_=vecB[:, vi, kc * 128:(kc + 1) * 128], identity=ident_f[:]),
                         reads=[b_vecB, b_ident], writes=[bp])
                    S.op("dve", lambda: nc.vector.tensor_copy(out=modT[:, vi, kc:kc + 1], in_=pt[:, 0:1]), reads=[bp], writes=[b_modT])
            S.barrier()

        b_wscr = Buf("wscr")
        with scope() as ls:
            fr = ring(ls, "cf", 2, [128, IN_W], F32)
            br = ring(ls, "cbf", 2, [128, IN_W], BF16)
            psT = pring(ls, "psTa", 2, [128, 1024], BF16)
            ust = ring(ls, "ust", 2, [128, 16, 512], BF16)
            for k in range(16):
                ft, bfb = fr.next(); bt_, bbb = br.next()
                S.dma("sp", ft[:], w_in[k * 128:(k + 1) * 128, :], writes=[bfb])
                cast(bt_[:], ft[:], [bfb], [bbb])
                S.dma("pool", w_in_bf[k * 128:(k + 1) * 128, :], bt_[:], reads=[bbb], writes=[b_wscr], chan="wscr")
            for (src, dst) in ((w_out, w_out_bf), (w_q, w_q_bf)):
                for k in range(0, 16, 2):
                    ft, bfb = fr.next(); bt_, bbb = br.next()
                    S.dma("sp", ft[:, 0:2 * D].rearrange("p (a n) -> p a n", a=2),
                          src[k * 128:(k + 2) * 128, :].rearrange("(a p) n -> p a n", p=128), writes=[bfb])
                    cast(bt_[:, 0:2 * D], ft[:, 0:2 * D], [bfb], [bbb])
                    S.dma("pool", dst[k * 128:(k + 2) * 128, :].rearrange("(a p) n -> p a n", p=128),
                          bt_[:, 0:2 * D].rearrange("p (a n) -> p a n", a=2), reads=[bbb], writes=[b_wscr], chan="wscr")
            for e0 in range(0, 128, 2):
                ft, bfb = fr.next(); bt_, bbb = br.next()
                S.dma("sp", ft[:, 0:2 * D].rearrange("p (a n) -> p a n", a=2),
                      V[e0 * 128:(e0 + 2) * 128, :].rearrange("(a p) n -> p a n", p=128), writes=[bfb])
                cast(bt_[:, 0:2 * D], ft[:, 0:2 * D], [bfb], [bbb])
                S.dma("pool", V_bf[e0 * 128:(e0 + 2) * 128, :].rearrange("(a p) n -> p a n", p=128),
                      bt_[:, 0:2 * D].rearrange("p (a n) -> p a n", a=2), reads=[bbb], writes=[b_wscr], chan="wscr")
            for e4 in range(32):
                us, bus = ust.next()
                for a in range(0, 4, 2):
                    ft, bfb = fr.next(); bt_, bbb = br.next()
                    e0 = e4 * 4 + a
                    S.dma("sp", ft[:, 0:2 * D].rearrange("p (a n) -> p a n", a=2),
                          U[e0 * 128:(e0 + 2) * 128, :].rearrange("(a p) n -> p a n", p=128), writes=[bfb])
                    cast(bt_[:, 0:2 * D], ft[:, 0:2 * D], [bfb], [bbb])
                    for a2 in range(2):
                        for half in range(2):
                            pt, bp = psT.next()
                            for k8 in range(8):
                                kc = half * 8 + k8
                                S.op("pe", lambda: nc.tensor.transpose(out=pt[:, k8 * 128:(k8 + 1) * 128],
                                                                       in_=bt_[:, a2 * D + kc * 128:a2 * D + (kc + 1) * 128], identity=ident_b[:]),
                                     reads=[bbb, b_ident], writes=[bp], inc=(k8 == 7))
                            ee = (a + a2) * 128
                            cast(us[:, half * 8:(half + 1) * 8, ee:ee + 128], pt[:].rearrange("p (k e) -> p k e", k=8), [bp], [bus], psum=True)
                S.dma("pool", UT_bf.rearrange("(k p) e -> p k e", p=128)[:, :, e4 * 512:(e4 + 1) * 512], us[:],
                      reads=[bus], writes=[b_wscr], chan="wscr")
            ft, bfb = fr.next(); bt_, bbb = br.next()
            S.dma("sp", ft[:, 0:2048].rearrange("p (a d) -> p a d", a=16), subk.rearrange("a k d -> k a d"), writes=[bfb])
            cast(bt_[:, 0:2048], ft[:, 0:2048], [bfb], [bbb])
            for half in range(2):
                pt, bp = psT.next()
                for k8 in range(8):
                    a = half * 8 + k8
                    S.op("pe", lambda: nc.tensor.transpose(out=pt[:, k8 * 128:(k8 + 1) * 128], in_=bt_[:, a * 128:(a + 1) * 128], identity=ident_b[:]),
                         reads=[bbb, b_ident], writes=[bp], inc=(k8 == 7))
                cast(skT[:, half * 8:(half + 1) * 8, :], pt[:].rearrange("p (k e) -> p k e", k=8), [bp], [b_skT], psum=True)
            S.barrier()

        if dbg and "modT" in dbg_out:
            S.dma("sp", dbg_out["modT"][:, :], modT[:].rearrange("p a b -> p (a b)"), reads=[b_modT], writes=[Buf("dbgo")])


        b_y = Buf("y")
        b_scr = Buf("scr")
        gelu_c = 1.5957691216057308

        def gelu_tanh(out_ap, x_ap, tmp_ap, rb, wb, tb, shape_bias=None):
            S.op("dve", lambda: nc.vector.tensor_tensor(out=tmp_ap, in0=x_ap, in1=x_ap, op=ALU.mult), reads=rb, writes=[tb])
            S.op("dve", lambda: nc.vector.tensor_scalar(out=tmp_ap, in0=tmp_ap, scalar1=0.044715, scalar2=1.0, op0=ALU.mult, op1=ALU.add),
                 reads=[tb], writes=[tb])
            S.op("dve", lambda: nc.vector.tensor_tensor(out=tmp_ap, in0=tmp_ap, in1=x_ap, op=ALU.mult), reads=rb + [tb], writes=[tb])
            S.op("act", lambda: nc.scalar.activation(out=tmp_ap, in_=tmp_ap, func=AF.Sigmoid, scale=gelu_c), reads=[tb], writes=[tb])
            S.op("dve", lambda: nc.vector.tensor_tensor(out=out_ap, in0=tmp_ap, in1=x_ap, op=ALU.mult), reads=rb + [tb], writes=wb)

        w_in_v = w_in_bf.rearrange("(k p) n -> p k n", p=128)

        with scope() as ls:
            psT = pring(ls, "psT", 2, [128, 1024], BF16)
            psM = pring(ls, "psM", 4, [128, 512], F32)
            psK = pring(ls, "psK", 2, [128, 512], F32)
            W1s = sbl(ls, "W1s", [128, 2, 32, 128], BF16); b_W1s = Buf("W1s")
            W2s = sbl(ls, "W2s", [128, 2, 64], BF16); b_W2 = Buf("W2")
            peTb = sbl(ls, "peTb", [64, 2, 32], BF16); b_pe = Buf("pe")
            peb = sbl(ls, "peb", [128, 2], F32); b_peb = Buf("peb")
            with scope() as ls2:
                W1f = sbl(ls2, "W1f", [128, 32, 128], F32); b_W1f = Buf("W1f")
                W2f = sbl(ls2, "W2f", [128, 2, 64], F32)
                peT = sbl(ls2, "peT", [64, 2, 32], F32)
                for kv, (w1, w2, pe) in enumerate(((ck_w1, ck_w2, pe_k), (cv_w1, cv_w2, pe_v))):
                    for half in range(2):
                        S.dma("sp", W1f[half * 64:(half + 1) * 64, :, :], w1.rearrange("(l d) h -> d l h", d=64), writes=[b_W1f])
                    S.op("dve", lambda: nc.vector.tensor_copy(out=W1s[:, kv, :, :], in_=W1f[:]), reads=[b_W1f], writes=[b_W1s])
                    S.dma("sp", W2f[:, kv, :], w2[:, :], writes=[b_W2])
                    S.dma("sp", peT[:, kv, :], pe.rearrange("l d -> d l"), writes=[b_pe], allow_slow_non_contiguous=True)
                S.op("dve", lambda: nc.vector.tensor_copy(out=W2s[:], in_=W2f[:]), reads=[b_W2], writes=[b_W2])
                S.op("dve", lambda: nc.vector.tensor_copy(out=peTb[:], in_=peT[:]), reads=[b_pe], writes=[b_pe])
                for kv in range(2):
                    pt, bp = psM.next()
                    mmg(pt[:, 0:1], [(W1s[0:64, kv, l, :], peTb[:, kv, l:l + 1]) for l in range(32)], [b_W1s, b_pe], bp)
                    S.op("dve", lambda: nc.vector.tensor_copy(out=peb[:, kv:kv + 1], in_=pt[:, 0:1]), reads=[bp], writes=[b_peb])
                S.barrier()
            xr = ring(ls, "xt", 2, [128, D], F32)
            junk = sbl(ls, "junk", [128, D], BF16); b_junk = Buf("junk")
            ssr = ring(ls, "ss", 2, [128, 2], F32)
            xnr = ring(ls, "xn", 2, [128, D], BF16)
            hTr = ring(ls, "hT", 2, [128, 16, 512], BF16)
            wr = ring(ls, "wblk", 3, [128, 16, 512], BF16)
            stg = ring(ls, "stg", 4, [128, 512], BF16)
            stgf = ring(ls, "stgf", 2, [128, 512], F32)
            kcb = sbl(ls, "kcb", [128, 2, 2, 528], BF16); b_kcb = Buf("kcb")
            hid = sbl(ls, "hid", [128, 32], BF16); hidx = sbl(ls, "hidx", [128, 32], F32); hidt = sbl(ls, "hidt", [128, 32], F32)
            b_hid = Buf("hid"); b_hidx = Buf("hidx"); b_hidt = Buf("hidt")
            VcT = sbl(ls, "VcT", [64, 4, NCC * 128], BF16); b_VcT = Buf("VcT")
            kdec = sbl(ls, "kdec", [128, 8], F32); b_kdec = Buf("kdec")
            krd = sbl(ls, "krd", [128, 4, 1024], BF16); b_krd = Buf("krd")
            vrb = sbl(ls, "vrb", [128, 4, 1024], BF16); b_vrb = Buf("vrb")
            Rst = sbl(ls, "Rst", [128, 2, 8, 128], F32); b_Rst = Buf("Rst")
            Rown = sbl(ls, "Rown", [128, 8, 128], F32); Rownb = sbl(ls, "Rownb", [128, 8, 128], BF16); b_Rown = Buf("Rown")
            S.dma("sp", kdec[:], h_kdec[:, :], writes=[b_kdec])
            S.op("dve", lambda: nc.vector.memset(kcb[:], 0.0), writes=[b_kcb])
            S.op("dve", lambda: nc.vector.memset(Rst[:], 0.0), writes=[b_Rst])
            S.op("pool", lambda: nc.gpsimd.memset(KcT[:], 0.0), writes=[b_KcT])
            S.op("pool", lambda: nc.gpsimd.memset(VcT[:], 0.0), writes=[b_VcT])
            S.dma("sp", KcT[64:71, 0, :], h_caug[:, :], writes=[b_KcT])
            for g in range(1, 4):
                S.dma("sp", KcT[64:71, g, :], h_caug[:, :], writes=[b_KcT])

            def norm_T(xsrc, vi, hT, bh, col0):
                xt, bx = xr.next()
                S.dma("sp", xt[:], xsrc, writes=[bx])
                ss, bs = ssr.next()
                S.op("act", lambda: nc.scalar.activation(out=junk[:], in_=xt[:], func=AF.Square, accum_out=ss[:, 0:1]),
                     reads=[bx], writes=[b_junk, bs])
                rstd_from_ss(ss[:, 1:2], ss[:, 0:1], D, [bs], [bs])
                xn, bn = xnr.next()
                S.op("act", lambda: nc.scalar.activation(out=xn[:], in_=xt[:], func=AF.Copy, scale=ss[:, 1:2]), reads=[bx, bs], writes=[bn])
                for half in range(2):
                    pt, bp = psT.next()
                    for k8 in range(8):
                        kc = half * 8 + k8
                        S.op("pe", lambda: nc.tensor.transpose(out=pt[:, k8 * 128:(k8 + 1) * 128], in_=xn[:, kc * 128:(kc + 1) * 128],
                                                               identity=ident_b[:]), reads=[bn, b_ident], writes=[bp], inc=(k8 == 7))
                    dst = hT[:, half * 8:(half + 1) * 8, col0:col0 + 128]
                    S.op("dve", lambda: nc.vector.tensor_tensor(out=dst, in0=pt[:].rearrange("p (k t) -> p k t", k=8),
                                                                in1=modT[:, vi, half * 8:(half + 1) * 8].unsqueeze(2).to_broadcast([128, 8, 128]),
                                                                op=ALU.mult), reads=[bp, b_modT], writes=[bh])
                    S.op("dve", lambda: nc.vector.tensor_tensor(out=dst, in0=dst,
                                                                in1=modT[:, vi + 1, half * 8:(half + 1) * 8].unsqueeze(2).to_broadcast([128, 8, 128]),
                                                                op=ALU.add), reads=[bh, b_modT], writes=[bh])

            def load_w(c0, ncols):
                wt, bw = wr.next()
                S.dma("pool", wt[:, :, 0:ncols], w_in_v[:, :, c0:c0 + ncols], reads=[b_wscr], writes=[bw])
                return wt, bw

            def fm(wt, bw, col, M, hT, bh):
                pt, bp = psM.next()
                mmg(pt[0:M, :], [(wt[:, k, col:col + M], hT[:, k, :]) for k in range(16)], [bw, bh], bp)
                return pt, bp

            def tm(wt, bw, col, N, hT, bh, tile):
                pt, bp = psM.next()
                mmg(pt[:, 0:N], [(hT[:, k, tile * 128:(tile + 1) * 128], wt[:, k, col:col + N]) for k in range(16)], [bh, bw], bp)
                return pt, bp

            ev_rr = [0]

            def evac(out_ap, in_ap, rb, wb, scale=None):
                e = ("dve", "act")[ev_rr[0] % 2]
                ev_rr[0] += 1
                if e == "act":
                    if scale is None:
                        S.op("act", lambda: nc.scalar.copy(out=out_ap, in_=in_ap), reads=rb, writes=wb)
                    else:
                        S.op("act", lambda: nc.scalar.mul(out=out_ap, in_=in_ap, mul=scale), reads=rb, writes=wb)
                else:
                    if scale is None:
                        S.op("dve", lambda: nc.vector.tensor_copy(out=out_ap, in_=in_ap), reads=rb, writes=wb)
                    else:
                        S.op("dve", lambda: nc.vector.tensor_scalar(out=out_ap, in0=in_ap, scalar1=scale, scalar2=None, op0=ALU.mult),
                             reads=rb, writes=wb)

            for s in range(NS):
                hT, bh = hTr.next()
                for tt in range(4):
                    norm_T(x_all[(s * 4 + tt) * 128:(s * 4 + tt + 1) * 128, :], 0, hT, bh, tt * 128)
                wt, bw = load_w(C_KC, 512)
                S.op("dve", lambda: nc.vector.tensor_copy(out=kcb[:, :, :, 0:16], in_=kcb[:, :, :, 512:528]), reads=[b_kcb], writes=[b_kcb])
                for kv in range(2):
                    for gp in range(2):
                        pt, bp = fm(wt, bw, kv * 256 + gp * 128, 128, hT, bh)
                        evac(kcb[:, kv, gp, 16:528], pt[:, :], [bp], [b_kcb])
                i0 = 1 if s == 0 else 0
                ni = 32 - i0
                n0 = 32 * s - 1 + i0
                for kv in range(2):
                    for g in range(4):
                        base = (g % 2) * 64
                        pt, bp = psM.next()
                        mmg(pt[:, 0:ni], [(W1s[base:base + 64, kv, l, :],
                                            kcb[base:base + 64, kv, g // 2, 16 * i0 + l:16 * i0 + l + 16 * (ni - 1) + 1:16]) for l in range(32)],
                            [b_W1s, b_kcb], bp)
                        S.op("act", lambda: nc.scalar.activation(out=hidx[:, 0:ni], in_=pt[:, 0:ni], func=AF.Identity, bias=peb[:, kv:kv + 1]),
                             reads=[bp, b_peb], writes=[b_hidx])
                        gelu_tanh(hid[:, 0:ni], hidx[:, 0:ni], hidt[:, 0:ni], [b_hidx], [b_hid], b_hidt)
                        pt2, bp2 = psM.next()
                        mmg(pt2[0:64, 0:ni], [(W2s[:, kv, :], hid[:, 0:ni])], [b_W2, b_hid], bp2)
                        if kv == 0:
                            evac(KcT[0:64, g, n0:n0 + ni], pt2[0:64, 0:ni], [bp2], [b_KcT])
                        else:
                            evac(VcT[0:64, g, n0:n0 + ni], pt2[0:64, 0:ni], [bp2], [b_VcT])
                for (c0, kd, vd) in ((C_KS, kS_d, vS_d), (C_KW, kW_d, vW_d)):
                    wt, bw = load_w(c0, 512)
                    for g in range(4):
                        pt, bp = fm(wt, bw, g * 64, 64, hT, bh)
                        sg, bsg = stg.next()
                        evac(sg[0:64, :], pt[0:64, :], [bp], [bsg])
                        S.dma("sp", kd[g, :, s * 512:(s + 1) * 512], sg[0:64, :], reads=[bsg], writes=[b_scr], chan="scr")
                    for tt in range(4):
                        pt, bp = tm(wt, bw, 256, 256, hT, bh, tt)
                        sg, bsg = stg.next()
                        evac(sg[:, 0:256], pt[:, 0:256], [bp], [bsg])
                        S.dma("sp", vd[(s * 4 + tt) * 128:(s * 4 + tt + 1) * 128, :], sg[:, 0:256], reads=[bsg], writes=[b_scr], chan="scr")
                wk = [load_w(C_KR, 512), load_w(C_KR + 512, 512)]
                for tt in range(4):
                    for hh in range(2):
                        pt, bp = tm(wk[hh][0], wk[hh][1], 0, 512, hT, bh, tt)
                        S.op("dve", lambda: nc.vector.tensor_tensor(out=krd[:, tt, hh * 512:(hh + 1) * 512].rearrange("p (h d) -> p h d", h=4),
                                                                    in0=pt[:].rearrange("p (h d) -> p h d", h=4),
                                                                    in1=kdec[:, hh * 4:(hh + 1) * 4].unsqueeze(2).to_broadcast([128, 4, 128]),
                                                                    op=ALU.mult), reads=[bp, b_kdec], writes=[b_krd])
                wv = [load_w(C_VR, 512), load_w(C_VR + 512, 512)]
                for tt in range(4):
                    for hh in range(2):
                        pt, bp = tm(wv[hh][0], wv[hh][1], 0, 512, hT, bh, tt)
                        evac(vrb[:, tt, hh * 512:(hh + 1) * 512], pt[:, :], [bp], [b_vrb])
                for tt in range(4):
                    ci = s * 4 + tt
                    par = ci % 2
                    for hh in range(2):
                        pk, bpk = psK.next()
                        for h4 in range(4):
                            h = hh * 4 + h4
                            S.op("pe", lambda: nc.tensor.matmul(out=pk[:, h4 * 128:(h4 + 1) * 128], lhsT=krd[:, tt, h * 128:(h + 1) * 128],
                                                                rhs=vrb[:, tt, h * 128:(h + 1) * 128], start=True, stop=True),
                                 reads=[b_krd, b_vrb], writes=[bpk], inc=(h4 == 3))
                        for h4 in range(4):
                            h = hh * 4 + h4
                            S.op("dve", lambda: nc.vector.scalar_tensor_tensor(out=Rst[:, 1 - par, h, :], in0=Rst[:, par, h, :], scalar=float(cdec[h]),
                                                                               in1=pk[:, h4 * 128:(h4 + 1) * 128], op0=ALU.mult, op1=ALU.add),
                                 reads=[b_Rst, bpk], writes=[b_Rst])
                    if par == 1:
                        pass
                    if par == 0:
                        S.op("pool", lambda: nc.gpsimd.tensor_tensor(out=Rown[:], in0=Rst[:, 1, :, :], in1=Rst[:, 0, :, :], op=ALU.subtract),
                             reads=[b_Rst], writes=[b_Rown])
                        S.op("dve", lambda: nc.vector.scalar_tensor_tensor(out=Rownb[:], in0=Rown[:], scalar=jsel[:, 0:1], in1=Rst[:, 0, :, :],
                                                                            op0=ALU.mult, op1=ALU.add), reads=[b_Rown, b_Rst, b_jsel], writes=[b_Rown])
                        S.dma("sp", R_d[ci // 2], Rownb[:], reads=[b_Rown], writes=[b_scr], chan="scr")

            S.op("dve", lambda: nc.vector.memset(VcA[:, :, :, 64:65], 1.0), writes=[b_VcA])
            for g in range(4):
                S.dma("sp", VcA[:, :, g, 65:65 + NSLC], h_overlap[:, :, :], writes=[b_VcA])
            for cc in range(NCC):
                for g in range(4):
                    pt, bp = psT.next()
                    S.op("pe", lambda: nc.tensor.transpose(out=pt[:, 0:64], in_=VcT[0:64, g, cc * 128:(cc + 1) * 128], identity=ident_b[0:64, 0:64]),
                         reads=[b_VcT, b_ident], writes=[bp])
                    evac(VcA[:, cc, g, 0:64], pt[:, 0:64], [bp], [b_VcA])

            for so in range(NOS):
                hT, bh = hTr.next()
                for tt in range(4):
                    norm_T(x_own[(so * 4 + tt) * 128:(so * 4 + tt + 1) * 128, :], 0, hT, bh, tt * 128)
                t0, t1 = so * 512, (so + 1) * 512
                for blk in range(2):
                    wt, bw = load_w(C_QA + blk * 512, 512)
                    for h8 in range(8):
                        pt, bp = fm(wt, bw, h8 * 64, 64, hT, bh)
                        sg, bsg = stg.next()
                        evac(sg[0:64, :], pt[0:64, :], [bp], [bsg], scale=0.125)
                        S.dma("sp", qA_d[blk * 8 + h8, :, t0:t1], sg[0:64, :], reads=[bsg], writes=[b_scr], chan="scr")
                for (c0, dd) in ((C_QR, qR_d), (C_KR, kR_d)):
                    for blk in range(2):
                        wt, bw = load_w(c0 + blk * 512, 512)
                        for h4 in range(4):
                            pt, bp = fm(wt, bw, h4 * 128, 128, hT, bh)
                            sg, bsg = stg.next()
                            evac(sg[:, :], pt[:, :], [bp], [bsg])
                            S.dma("sp", dd[blk * 4 + h4, :, t0:t1], sg[:, :], reads=[bsg], writes=[b_scr], chan="scr")
                for blk in range(2):
                    wt, bw = load_w(C_VR + blk * 512, 512)
                    for tt in range(4):
                        pt, bp = tm(wt, bw, 0, 512, hT, bh, tt)
                        sg, bsg = stg.next()
                        evac(sg[:, :], pt[:, :], [bp], [bsg])
                        S.dma("sp", vR_d[t0 + tt * 128:t0 + (tt + 1) * 128, blk * 512:(blk + 1) * 512], sg[:, :], reads=[bsg], writes=[b_scr], chan="scr")
                for blk in range(2):
                    wt, bw = load_w(C_GR + blk * 512, 512)
                    for tt in range(4):
                        pt, bp = tm(wt, bw, 0, 512, hT, bh, tt)
                        sf, bsf = stgf.next()
                        evac(sf[:, 0:512], pt[:, :], [bp], [bsf])
                        S.dma("sp", gR_d[t0 + tt * 128:t0 + (tt + 1) * 128, blk * 512:(blk + 1) * 512], sf[:, 0:512], reads=[bsf], writes=[b_scr], chan="scr")
                wt, bw = load_w(C_GA, 48)
                for tt in range(4):
                    pt, bp = tm(wt, bw, 0, 48, hT, bh, tt)
                    sf, bsf = stgf.next()
                    evac(sf[:, 0:48], pt[:, 0:48], [bp], [bsf])
                    S.dma("sp", gA_d[t0 + tt * 128:t0 + (tt + 1) * 128, :], sf[:, 0:48], reads=[bsf], writes=[b_scr], chan="scr")
            S.barrier()


        b_omix = Buf("omix")
        NB1 = 65 + NSLC
        with scope() as ls:
            KsT = sbl(ls, "KsT", [71, 4, SEQ], BF16); b_KsT = Buf("KsT")
            VsA = sbl(ls, "VsA", [128, NT, 4, 65], BF16); b_VsA = Buf("VsA")
            Eall = sbl(ls, "Eall", [NSLC, NT, 128], BF16); b_E = Buf("Eall")
            cmab = sbl(ls, "cmab", [128, 2, 128], BF16); wmask = sbl(ls, "wmask", [128, 6, 128], BF16); b_cm = Buf("cm")
            maskrep = sbl(ls, "maskrep", [128, 8, 4, 128], BF16); b_mr = Buf("maskrep")
            g_nsaB = sbl(ls, "g_nsaB", [128, 1024], F32); b_gn = Buf("gnsa")
            QTr = ring(ls, "QT", 1, [71, 4, 512], BF16)
            KwTr = ring(ls, "KwT", 1, [71, 4, 768], BF16)
            VwAr = ring(ls, "VwA", 1, [128, 6, 4, 65], BF16)
            cm1r = ring(ls, "cm1", 2, [128, NCC, 128], BF16)
            cm1rep = sbl(ls, "cm1rep", [128, NCC, 4, 128], BF16); b_c1r = Buf("cm1rep")
            vmr = ring(ls, "vm", 1, [128, 2, NSLC], F32)
            gar = ring(ls, "ga", 2, [128, 48], F32)
            PTr = ring(ls, "PT", 3, [128, 512], BF16)
            nsT = sbl(ls, "nsT", [NSLC, 4, 128], BF16); b_nsT = Buf("nsT")
            imp = sbl(ls, "imp", [128, NSLC], F32); impr = sbl(ls, "impr", [128, NSLC], F32); b_imp = Buf("imp"); b_impr = Buf("impr")
            m8 = sbl(ls, "m8", [128, 16], F32); b_m8 = Buf("m8")
            selb = sbl(ls, "selb", [128, NSLC], F32); negs = sbl(ls, "negs", [128, NSLC], BF16); b_sel = Buf("sel")
            rden = sbl(ls, "rden", [128, 3, 4], F32); cf = sbl(ls, "cf", [128, 3, 4], F32); b_rden = Buf("rden")
            oacc = sbl(ls, "oacc", [128, 16, 64], F32); otmp = sbl(ls, "otmp", [128, 4, 64], F32); b_oacc = Buf("oacc"); b_otmp = Buf("otmp")
            osq = sbl(ls, "osq", [128, 16, 64], F32); oss = sbl(ls, "oss", [128, 16], F32); b_osq = Buf("osq")
            onb = ring(ls, "onb", 1, [128, 1024], BF16)
            psS = pring(ls, "psS", 3, [128, 512], F32)
            psO1 = psl(ls, "psO1", [128, 4, 256], F32); b_pO1 = Buf("pO1")
            psO2 = psl(ls, "psO2", [128, 512], F32); b_pO2 = Buf("pO2")
            psO3 = psl(ls, "psO3", [128, 512], F32); b_pO3 = Buf("pO3")
            psX = psl(ls, "psX", [128, 1024], BF16); b_pX = Buf("pX")
            assert NB1 <= 256
            for g in range(4):
                S.dma("sp", KsT[0:64, g, :], kS_d[g, :, :], reads=[b_scr], writes=[b_KsT])
                S.dma("sp", KsT[64:71, g, :], h_kaug[:, :], writes=[b_KsT])
            for c4 in range(0, NT, 4):
                for g in range(4):
                    S.dma("pool", VsA[:, c4:c4 + 4, g, 0:64], vS_d[c4 * 128:(c4 + 4) * 128, g * 64:(g + 1) * 64].rearrange("(c p) d -> p c d", p=128),
                          reads=[b_scr], writes=[b_VsA])
            S.op("dve", lambda: nc.vector.memset(VsA[:, :, :, 64:65], 1.0), writes=[b_VsA])
            S.dma("sp", Eall[:], h_eall[:, :, :], writes=[b_E])
            S.dma("sp", cmab[:], h_cmab[:, :, :], writes=[b_cm])
            S.dma("sp", wmask[:], h_wmask[:, :, :], writes=[b_cm])
            S.op("dve", lambda: nc.vector.tensor_copy(out=maskrep[:, 0:2, :, :], in_=cmab[:].unsqueeze(2).to_broadcast([128, 2, 4, 128])),
                 reads=[b_cm], writes=[b_mr])
            S.op("dve", lambda: nc.vector.tensor_copy(out=maskrep[:, 2:8, :, :], in_=wmask[:].unsqueeze(2).to_broadcast([128, 6, 4, 128])),
                 reads=[b_cm], writes=[b_mr])
            S.dma("sp", g_nsaB[:], g_nsa[0:1, :].partition_broadcast(128), writes=[b_gn])
            for (vt, bv) in zip(VwAr.tiles, VwAr.bufs):
                S.op("dve", lambda: nc.vector.memset(vt[:, :, :, 64:65], 1.0), writes=[bv])

            pend = []

            def flush_pv():
                while pend:
                    (PT, bP, rhsV, bV, psO, b_pO, width, first, last) = pend.pop(0)
                    for r in range(4):
                        st_ = first and (r == 0 or (width == NB1 and r == 2))
                        S.op("pe", lambda: nc.tensor.matmul(out=psO(r), lhsT=PT[:, r * 128:(r + 1) * 128], rhs=rhsV, start=st_, stop=last,
                                                            skip_group_check=True),
                             reads=[bP, bV], writes=[b_pO], inc=(r == 3))

            def attn_chunk(pairs, rb, rhsV, bV, psO, b_pO, width, first, last):
                pS, bS = psS.next()
                mmg(pS[:], pairs, rb, bS)
                PT, bP = PTr.next()
                S.op("act", lambda: nc.scalar.activation(out=PT[:], in_=pS[:], func=AF.Exp), reads=[bS], writes=[bP])
                flush_pv()
                pend.append((PT, bP, rhsV, bV, psO, b_pO, width, first, last))

            def finish_branch(bi, g, psO, b_pO, first_branch):
                flush_pv()
                for r in range(4):
                    S.op("dve", lambda: nc.vector.tensor_scalar(out=rden[:, bi, r:r + 1], in0=psO(r)[:, 64:65], scalar1=1e-30, scalar2=None, op0=ALU.max),
                         reads=[b_pO], writes=[b_rden])
                S.op("dve", lambda: nc.vector.reciprocal(out=rden[:, bi, :], in_=rden[:, bi, :]), reads=[b_rden], writes=[b_rden])
                S.op("dve", lambda: nc.vector.tensor_tensor(out=cf[:, bi, :], in0=rden[:, bi, :],
                                                            in1=ga[:, g * 12:(g + 1) * 12].rearrange("p (r b) -> p r b", b=3)[:, :, bi], op=ALU.mult),
                     reads=[b_rden, bga], writes=[b_rden])
                for r in range(4):
                    h = g * 4 + r
                    if first_branch:
                        S.op("dve", lambda: nc.vector.tensor_scalar(out=oacc[:, h, :], in0=psO(r)[:, 0:64], scalar1=cf[:, bi, r:r + 1], scalar2=None, op0=ALU.mult),
                             reads=[b_pO, b_rden], writes=[b_oacc])
                    else:
                        S.op("dve", lambda: nc.vector.scalar_tensor_tensor(out=oacc[:, h, :], in0=psO(r)[:, 0:64], scalar=cf[:, bi, r:r + 1], in1=oacc[:, h, :],
                                                                           op0=ALU.mult, op1=ALU.add), reads=[b_pO, b_rden, b_oacc], writes=[b_oacc])

            for m in range(NO):
                QT, bQ = QTr.next()
                for g in range(4):
                    S.dma("sp", QT[0:64, g, :].rearrange("d (r q) -> d r q", r=4),
                          qA_d[4 * g:4 * g + 4, :, m * 128:(m + 1) * 128].rearrange("r d q -> d r q"), reads=[b_scr], writes=[bQ])
                S.dma("sp", QT[64:71, :, :], h_qaug[m].rearrange("g a q -> a g q"), writes=[bQ])
                c0 = max(0, 2 * m - 4)
                u0 = c0 - (2 * m - 4)
                nch = 2 * m + 2 - c0
                KwT, bKw = KwTr.next()
                VwA, bVw = VwAr.next()
                for g in range(4):
                    S.dma("pool", KwT[0:64, g, u0 * 128:(u0 + nch) * 128], kW_d[g, :, c0 * 128:(c0 + nch) * 128], reads=[b_scr], writes=[bKw])
                    S.dma("pool", KwT[64:71, g, u0 * 128:(u0 + nch) * 128], h_kaug[:, c0 * 128:(c0 + nch) * 128], writes=[bKw])
                for g in range(4):
                    S.dma("pool", VwA[:, u0:u0 + nch, g, 0:64], vW_d[c0 * 128:(c0 + nch) * 128, g * 64:(g + 1) * 64].rearrange("(c p) d -> p c d", p=128),
                          reads=[b_scr], writes=[bVw])
                cm1, bc1 = cm1r.next()
                S.dma("sp", cm1[:], h_cm1[m], writes=[bc1])
                S.op("pool", lambda: nc.gpsimd.tensor_copy(out=cm1rep[:], in_=cm1[:].unsqueeze(2).to_broadcast([128, NCC, 4, 128])),
                     reads=[bc1], writes=[b_c1r])
                vm, bvm = vmr.next()
                S.dma("sp", vm[:, 0, :], h_vmask[m], writes=[bvm])
                S.dma("sp", vm[:, 1, :], h_amask[m], writes=[bvm])
                ga, bga = gar.next()
                S.dma("sp", ga[:], gA_d[m * 128:(m + 1) * 128, :], reads=[b_scr], writes=[bga])
                S.op("act", lambda: nc.scalar.activation(out=ga[:], in_=ga[:], func=AF.Sigmoid), reads=[bga], writes=[bga])
                for g in range(4):
                    for cc in range(NCC):
                        attn_chunk([(KcT[0:71, g, cc * 128:(cc + 1) * 128], QT[0:71, g, :]),
                                    (ident_b[:], cm1rep[:, cc, :, :].rearrange("p r q -> p (r q)"))],
                                   [b_KcT, bQ, b_ident, b_c1r], VcA[:, cc, g, :], b_VcA,
                                   lambda r: psO1[:, r, 0:NB1], b_pO1, NB1, cc == 0, cc == NCC - 1)
                    finish_branch(0, g, lambda r: psO1[:, r, 0:NB1], b_pO1, True)
                    S.op("dve", lambda: nc.vector.tensor_scalar(out=imp[:], in0=psO1[:, 0, 65:NB1], scalar1=rden[:, 0, 0:1], scalar2=None, op0=ALU.mult),
                         reads=[b_pO1, b_rden], writes=[b_imp])
                    for r in range(1, 4):
                        S.op("dve", lambda: nc.vector.scalar_tensor_tensor(out=imp[:], in0=psO1[:, r, 65:NB1], scalar=rden[:, 0, r:r + 1], in1=imp[:],
                                                                           op0=ALU.mult, op1=ALU.add), reads=[b_pO1, b_rden, b_imp], writes=[b_imp])
                    S.op("dve", lambda: nc.vector.tensor_tensor(out=imp[:], in0=imp[:], in1=vm[:, 0, :], op=ALU.mult), reads=[b_imp, bvm], writes=[b_imp])
                    S.op("dve", lambda: nc.vector.tensor_tensor(out=imp[:], in0=imp[:], in1=vm[:, 1, :], op=ALU.add), reads=[b_imp, bvm], writes=[b_imp])
                    S.op("dve", lambda: nc.vector.max(out=m8[:, 0:8], in_=imp[:]), reads=[b_imp], writes=[b_m8])
                    S.op("dve", lambda: nc.vector.match_replace(out=impr[:], in_to_replace=m8[:, 0:8], in_values=imp[:], imm_value=-1e30),
                         reads=[b_imp, b_m8], writes=[b_impr])
                    S.op("dve", lambda: nc.vector.max(out=m8[:, 8:16], in_=impr[:]), reads=[b_impr], writes=[b_m8])
                    S.op("dve", lambda: nc.vector.tensor_scalar(out=selb[:], in0=imp[:], scalar1=m8[:, 15:16], scalar2=1.0, op0=ALU.is_ge, op1=ALU.subtract),
                         reads=[b_imp, b_m8], writes=[b_sel])
                    S.op("dve", lambda: nc.vector.tensor_scalar(out=negs[:], in0=selb[:], scalar1=-NEG, scalar2=None, op0=ALU.mult), reads=[b_sel], writes=[b_sel])
                    S.op("pe", lambda: nc.tensor.transpose(out=psX[0:NSLC, 0:128], in_=negs[:], identity=ident_b[:]), reads=[b_sel, b_ident], writes=[b_pX])
                    S.op("dve", lambda: nc.vector.tensor_copy(out=nsT[:], in_=psX[0:NSLC, 0:128].unsqueeze(1).to_broadcast([NSLC, 4, 128])),
                         reads=[b_pX], writes=[b_nsT])
                    nc2 = 2 * m + 2
                    for c in range(nc2):
                        pairs = [(KsT[0:71, g, c * 128:(c + 1) * 128], QT[0:71, g, :]),
                                 (Eall[0:NSLC, c, :], nsT[:].rearrange("p r q -> p (r q)"))]
                        rb = [b_KsT, bQ, b_E, b_nsT]
                        if c >= 2 * m:
                            pairs.append((ident_b[:], maskrep[:, c - 2 * m, :, :].rearrange("p r q -> p (r q)")))
                            rb = rb + [b_ident, b_mr]
                        attn_chunk(pairs, rb, VsA[:, c, g, :], b_VsA, lambda r: psO2[:, r * 65:(r + 1) * 65], b_pO2, 65, c == 0, c == nc2 - 1)
                    finish_branch(1, g, lambda r: psO2[:, r * 65:(r + 1) * 65], b_pO2, False)
                    for u in range(u0, 6):
                        pairs = [(KwT[0:71, g, u * 128:(u + 1) * 128], QT[0:71, g, :]),
                                 (ident_b[:], maskrep[:, 2 + u, :, :].rearrange("p r q -> p (r q)"))]
                        attn_chunk(pairs, [bKw, bQ, b_ident, b_mr], VwA[:, u, g, :], bVw, lambda r: psO3[:, r * 65:(r + 1) * 65], b_pO3, 65, u == u0, u == 5)
                    finish_branch(2, g, lambda r: psO3[:, r * 65:(r + 1) * 65], b_pO3, False)
                S.op("dve", lambda: nc.vector.tensor_tensor(out=osq[:], in0=oacc[:], in1=oacc[:], op=ALU.mult), reads=[b_oacc], writes=[b_osq])
                S.op("dve", lambda: nc.vector.tensor_reduce(out=oss[:], in_=osq[:], axis=AX.X, op=ALU.add), reads=[b_osq], writes=[b_osq])
                rstd_from_ss(oss[:], oss[:], 64, [b_osq], [b_osq])
                S.op("dve", lambda: nc.vector.tensor_tensor(out=osq[:], in0=oacc[:], in1=oss[:].unsqueeze(2).to_broadcast([128, 16, 64]), op=ALU.mult),
                     reads=[b_oacc, b_osq], writes=[b_osq])
                on, bon = onb.next()
                S.op("dve", lambda: nc.vector.tensor_tensor(out=on[:], in0=osq[:].rearrange("p h d -> p (h d)"), in1=g_nsaB[:], op=ALU.mult),
                     reads=[b_osq, b_gn], writes=[bon])
                S.dma("sp", omix_d[m * 128:(m + 1) * 128, 0:1024], on[:], reads=[bon], writes=[b_omix], chan="omix")
            S.barrier()


        with scope() as ls:
            decT = sbl(ls, "decT", [128, 8, 128], F32); qdecB = sbl(ls, "qdecB", [128, 8, 128], F32); g_retB = sbl(ls, "g_retB", [128, 1024], F32)
            b_rc = Buf("rconst")
            S.dma("sp", decT[:], h_decT[:, :, :], writes=[b_rc]); S.dma("sp", qdecB[:], h_qdecB[:, :, :], writes=[b_rc])
            S.dma("sp", g_retB[:], g_ret[0:1, :].partition_broadcast(128), writes=[b_rc])
            qTr = ring(ls, "rqT", 2, [128, 8, 128], BF16); kTr = ring(ls, "rkT", 2, [128, 8, 128], BF16)
            vr_ = ring(ls, "rv", 2, [128, 1024], BF16); Rr = ring(ls, "rR", 2, [128, 8, 128], BF16); grr = ring(ls, "rg", 2, [128, 1024], F32)
            qd = sbl(ls, "rqd", [128, 8, 128], BF16); b_qd = Buf("qd")
            AT = sbl(ls, "rAT", [128, 8, 128], BF16); b_AT = Buf("AT")
            ro = sbl(ls, "ro", [128, 8, 128], F32); rsq = sbl(ls, "rsq", [128, 8, 128], F32); b_ro = Buf("ro"); b_rsq = Buf("rsq")
            rmu = sbl(ls, "rmu", [128, 2, 8], F32); b_rmu = Buf("rmu")
            orb = ring(ls, "orb", 2, [128, 1024], BF16)
            psA = pring(ls, "rpsA", 2, [128, 512], F32); psO = pring(ls, "rpsO", 2, [128, 512], F32)
            for m in range(NO):
                sl = slice(m * 128, (m + 1) * 128)
                qT, bq = qTr.next(); kT, bk = kTr.next(); v_, bv = vr_.next(); R_, bR = Rr.next(); gr, bg = grr.next()
                S.dma("sp", qT[:], qR_d[:, :, sl].rearrange("h d c -> d h c"), reads=[b_scr], writes=[bq])
                S.dma("sp", kT[:], kR_d[:, :, sl].rearrange("h d c -> d h c"), reads=[b_scr], writes=[bk])
                S.dma("pool", v_[:], vR_d[sl, :], reads=[b_scr], writes=[bv])
                S.dma("pool", R_[:], R_d[m], reads=[b_scr], writes=[bR])
                S.dma("pool", gr[:], gR_d[sl, :], reads=[b_scr], writes=[bg])
                S.op("pool", lambda: nc.gpsimd.tensor_tensor(out=qd[:], in0=qT[:], in1=qdecB[:], op=ALU.mult), reads=[bq, b_rc], writes=[b_qd])
                for hh in range(2):
                    pa, bpa = psA.next()
                    for h4 in range(4):
                        h = hh * 4 + h4
                        S.op("pe", lambda: nc.tensor.matmul(out=pa[:, h4 * 128:(h4 + 1) * 128], lhsT=kT[:, h, :], rhs=qT[:, h, :], start=True, stop=True),
                             reads=[bk, bq], writes=[bpa], inc=(h4 == 3))
                    S.op("dve", lambda: nc.vector.tensor_tensor(out=AT[:, hh * 4:(hh + 1) * 4, :], in0=pa[:].rearrange("p (h c) -> p h c", h=4),
                                                                in1=decT[:, hh * 4:(hh + 1) * 4, :], op=ALU.mult), reads=[bpa, b_rc], writes=[b_AT])
                for hh in range(2):
                    po, bpo = psO.next()
                    for h4 in range(4):
                        h = hh * 4 + h4
                        mmg(po[:, h4 * 128:(h4 + 1) * 128], [(AT[:, h, :], v_[:, h * 128:(h + 1) * 128]), (qd[:, h, :], R_[:, h, :])],
                            [b_AT, bv, b_qd, bR], bpo)
                    S.op("act", lambda: nc.scalar.copy(out=ro[:, hh * 4:(hh + 1) * 4, :], in_=po[:].rearrange("p (h c) -> p h c", h=4)), reads=[bpo], writes=[b_ro])
                S.op("dve", lambda: nc.vector.tensor_reduce(out=rmu[:, 0, :], in_=ro[:], axis=AX.X, op=ALU.add), reads=[b_ro], writes=[b_rmu])
                S.op("dve", lambda: nc.vector.tensor_scalar(out=rmu[:, 0, :], in0=rmu[:, 0, :], scalar1=-1.0 / 128, scalar2=None, op0=ALU.mult), reads=[b_rmu], writes=[b_rmu])
                S.op("dve", lambda: nc.vector.tensor_tensor(out=ro[:], in0=ro[:], in1=rmu[:, 0, :].unsqueeze(2).to_broadcast([128, 8, 128]), op=ALU.add),
                     reads=[b_ro, b_rmu], writes=[b_ro])
                S.op("dve", lambda: nc.vector.tensor_tensor(out=rsq[:], in0=ro[:], in1=ro[:], op=ALU.mult), reads=[b_ro], writes=[b_rsq])
                S.op("dve", lambda: nc.vector.tensor_reduce(out=rmu[:, 1, :], in_=rsq[:], axis=AX.X, op=ALU.add), reads=[b_rsq], writes=[b_rmu])
                rstd_from_ss(rmu[:, 1, :], rmu[:, 1, :], 128, [b_rmu], [b_rmu])
                S.op("dve", lambda: nc.vector.tensor_tensor(out=rsq[:], in0=ro[:], in1=rmu[:, 1, :].unsqueeze(2).to_broadcast([128, 8, 128]), op=ALU.mult),
                     reads=[b_ro, b_rmu], writes=[b_rsq])
                S.op("dve", lambda: nc.vector.tensor_tensor(out=rsq[:].rearrange("p h d -> p (h d)"), in0=rsq[:].rearrange("p h d -> p (h d)"), in1=g_retB[:], op=ALU.mult),
                     reads=[b_rsq, b_rc], writes=[b_rsq])
                S.op("act", lambda: nc.scalar.activation(out=gr[:], in_=gr[:], func=AF.Silu), reads=[bg], writes=[bg])
                ob, bob = orb.next()
                S.op("dve", lambda: nc.vector.tensor_tensor(out=ob[:], in0=rsq[:].rearrange("p h d -> p (h d)"), in1=gr[:], op=ALU.mult),
                     reads=[b_rsq, bg], writes=[bob])
                S.dma("sp", omix_d[sl, 1024:2048], ob[:], reads=[bob], writes=[b_omix], chan="omix")
            S.barrier()

        b_x1 = Buf("x1d")
        with scope() as ls:
            wo = sbl(ls, "wo", [128, 16, D], BF16); b_wo = Buf("wo")
            S.dma("sp", wo[:], w_out_bf.rearrange("(k p) n -> p k n", p=128), reads=[b_wscr], writes=[b_wo])
            omr = ring(ls, "om", 2, [128, D], BF16); oTr = ring(ls, "oT", 2, [128, 16, 128], BF16)
            xor_ = ring(ls, "xo", 2, [128, D], F32); x1r = ring(ls, "x1", 2, [128, D], F32)
            junk = sbl(ls, "junkD", [128, D], BF16); b_junk = Buf("junkD")
            ssr = ring(ls, "ssD", 2, [128, 2], F32); xnr = ring(ls, "xnD", 2, [128, D], BF16)
            h2r = ring(ls, "h2T", 2, [128, 16, 128], BF16)
            psT = pring(ls, "DpsT", 2, [128, 1024], BF16); psM = pring(ls, "DpsM", 4, [128, 512], F32)
            for m in range(NO):
                sl = slice(m * 128, (m + 1) * 128)
                om, bom = omr.next()
                S.dma("sp", om[:], omix_d[sl, :], reads=[b_omix], writes=[bom])
                oT, boT = oTr.next()
                for half in range(2):
                    pt, bp = psT.next()
                    for k8 in range(8):
                        kc = half * 8 + k8
                        S.op("pe", lambda: nc.tensor.transpose(out=pt[:, k8 * 128:(k8 + 1) * 128], in_=om[:, kc * 128:(kc + 1) * 128], identity=ident_b[:]),
                             reads=[bom, b_ident], writes=[bp], inc=(k8 == 7))
                    S.op("act", lambda: nc.scalar.copy(out=oT[:, half * 8:(half + 1) * 8, :], in_=pt[:].rearrange("p (k t) -> p k t", k=8)), reads=[bp], writes=[boT])
                xo, bxo = xor_.next()
                S.dma("pool", xo[:], x_own[sl, :], writes=[bxo])
                x1, bx1 = x1r.next()
                for n4 in range(4):
                    pt, bp = psM.next()
                    ns = slice(n4 * 512, (n4 + 1) * 512)
                    mmg(pt[:], [(oT[:, k, :], wo[:, k, ns]) for k in range(16)], [boT, b_wo], bp)
                    S.op("dve", lambda: nc.vector.tensor_tensor(out=x1[:, ns], in0=pt[:], in1=gate1B[:, ns], op=ALU.mult), reads=[bp, b_gate], writes=[bx1])
                    S.op("pool", lambda: nc.gpsimd.tensor_tensor(out=x1[:, ns], in0=x1[:, ns], in1=xo[:, ns], op=ALU.add), reads=[bx1, bxo], writes=[bx1])
                S.dma("sp", x1_d[sl, :], x1[:], reads=[bx1], writes=[b_x1], chan="x1d")
                ss, bs = ssr.next()
                S.op("act", lambda: nc.scalar.activation(out=junk[:], in_=x1[:], func=AF.Square, accum_out=ss[:, 0:1]), reads=[bx1], writes=[b_junk, bs])
                rstd_from_ss(ss[:, 1:2], ss[:, 0:1], D, [bs], [bs])
                xn, bn = xnr.next()
                S.op("act", lambda: nc.scalar.activation(out=xn[:], in_=x1[:], func=AF.Copy, scale=ss[:, 1:2]), reads=[bx1, bs], writes=[bn])
                h2T, bh2 = h2r.next()
                for half in range(2):
                    pt, bp = psT.next()
                    for k8 in range(8):
                        kc = half * 8 + k8
                        S.op("pe", lambda: nc.tensor.transpose(out=pt[:, k8 * 128:(k8 + 1) * 128], in_=xn[:, kc * 128:(kc + 1) * 128], identity=ident_b[:]),
                             reads=[bn, b_ident], writes=[bp], inc=(k8 == 7))
                    dst = h2T[:, half * 8:(half + 1) * 8, :]
                    S.op("dve", lambda: nc.vector.tensor_tensor(out=dst, in0=pt[:].rearrange("p (k t) -> p k t", k=8),
                                                                in1=modT[:, 2, half * 8:(half + 1) * 8].unsqueeze(2).to_broadcast([128, 8, 128]), op=ALU.mult),
                         reads=[bp, b_modT], writes=[bh2])
                    S.op("dve", lambda: nc.vector.tensor_tensor(out=dst, in0=dst,
                                                                in1=modT[:, 3, half * 8:(half + 1) * 8].unsqueeze(2).to_broadcast([128, 8, 128]), op=ALU.add),
                         reads=[bh2, b_modT], writes=[bh2])
                S.dma("sp", h2T_d[:, :, sl].rearrange("k p t -> p k t"), h2T[:], reads=[bh2], writes=[b_x1], chan="x1d")
            S.barrier()

        with scope() as ls:
            wq = sbl(ls, "wq", [128, 16, D], BF16); b_wo = Buf("wq")
            S.dma("pool", wq[:], w_q_bf.rearrange("(k p) n -> p k n", p=128), reads=[b_wscr], writes=[b_wo])
            h2r = ring(ls, "D2h2T", 2, [128, 16, 128], BF16)
            qTa = ring(ls, "qTa", 3, [128, 128], BF16)
            scr_ = ring(ls, "sc", 2, [128, 16, 128], F32); scx = sbl(ls, "scx", [128, 256], F32); b_scx = Buf("scx")
            v16 = sbl(ls, "v16", [128, 16, 16], F32); b_v16 = Buf("v16")
            cand = sbl(ls, "cand", [128, 8, 256], F32); b_cand = Buf("cand")
            t16 = sbl(ls, "t16", [128, 8, 16], F32); b_t16 = Buf("t16")
            tneg = sbl(ls, "tneg", [128, 8], F32); zs = sbl(ls, "zs", [128, 8], F32); ej = sbl(ls, "ej", [128, 16], F32); b_z = Buf("z")
            psQ = pring(ls, "DpsQ", 4, [128, 512], F32)
            for m in range(NO):
                sl = slice(m * 128, (m + 1) * 128)
                h2T, bh2 = h2r.next()
                S.dma("sp", h2T[:], h2T_d[:, :, sl].rearrange("k p t -> p k t"), reads=[b_x1], writes=[bh2])
                sc, bsc = scr_.next()
                for a in range(16):
                    pq, bpq = psQ.next()
                    mmg(pq[:, 0:128], [(wq[:, k, a * 128:(a + 1) * 128], h2T[:, k, :]) for k in range(16)], [b_wo, bh2], bpq)
                    qa, bqa = qTa.next()
                    S.op("act", lambda: nc.scalar.copy(out=qa[:], in_=pq[:, 0:128]), reads=[bpq], writes=[bqa])
                    pq2, bpq2 = psQ.next()
                    mmg(pq2[:, 0:128], [(qa[:], skT[:, a, :])], [bqa, b_skT], bpq2)
                    S.op("act", lambda: nc.scalar.copy(out=sc[:, a, :], in_=pq2[:, 0:128]), reads=[bpq2], writes=[bsc])
                    S.op("dve", lambda: nc.vector.max(out=v16[:, a, 0:8], in_=sc[:, a, :]), reads=[bsc], writes=[b_v16])
                    S.op("dve", lambda: nc.vector.match_replace(out=scx[:, 0:128], in_to_replace=v16[:, a, 0:8], in_values=sc[:, a, :], imm_value=-1e30),
                         reads=[bsc, b_v16], writes=[b_scx])
                    S.op("dve", lambda: nc.vector.max(out=v16[:, a, 8:16], in_=scx[:, 0:128]), reads=[b_scx], writes=[b_v16])
                S.dma("sp", sc_d[sl, :, :], sc[:], reads=[bsc], writes=[b_x1], chan="x1d")
                v16v = v16[:].rearrange("p (h t) k -> p h t k", t=2)
                S.op("dve", lambda: nc.vector.tensor_tensor(out=cand[:].rearrange("p h (i j) -> p h i j", i=16),
                                                            in0=v16v[:, :, 0, :].unsqueeze(3).to_broadcast([128, 8, 16, 16]),
                                                            in1=v16v[:, :, 1, :].unsqueeze(2).to_broadcast([128, 8, 16, 16]), op=ALU.add),
                     reads=[b_v16], writes=[b_cand])
                for h in range(8):
                    S.op("dve", lambda: nc.vector.max(out=t16[:, h, 0:8], in_=cand[:, h, :]), reads=[b_cand], writes=[b_t16])
                    S.op("dve", lambda: nc.vector.match_replace(out=scx[:], in_to_replace=t16[:, h, 0:8], in_values=cand[:, h, :], imm_value=-1e30),
                         reads=[b_cand, b_t16], writes=[b_scx])
                    S.op("dve", lambda: nc.vector.max(out=t16[:, h, 8:16], in_=scx[:]), reads=[b_scx], writes=[b_t16])
                S.op("dve", lambda: nc.vector.tensor_scalar(out=tneg[:], in0=t16[:, :, 15], scalar1=-1.0, scalar2=None, op0=ALU.mult), reads=[b_t16], writes=[b_z])
                for h in range(8):
                    S.op("act", lambda: nc.scalar.activation(out=ej[:], in_=t16[:, h, :], func=AF.Exp, bias=tneg[:, h:h + 1], accum_out=zs[:, h:h + 1]),
                         reads=[b_t16, b_z], writes=[b_z])
                S.op("act", lambda: nc.scalar.activation(out=zs[:], in_=zs[:], func=AF.Ln), reads=[b_z], writes=[b_z])
                S.op("dve", lambda: nc.vector.tensor_tensor(out=lnz_all[:, m, :], in0=zs[:], in1=t16[:, :, 15], op=ALU.add), reads=[b_z, b_t16], writes=[b_thr[m]])
                S.op("dve", lambda: nc.vector.tensor_scalar(out=thr_all[:, m, :], in0=t16[:, :, 15], scalar1=-1e-5, scalar2=None, op0=ALU.add),
                     reads=[b_t16], writes=[b_thr[m]])
            S.barrier()

        with scope() as ls:
            g_finB = sbl(ls, "g_finB", [128, D], F32); b_gf = Buf("gfin")
            S.dma("sp", g_finB[:], g_fin[0:1, :].partition_broadcast(128), writes=[b_gf])
            h2r = ring(ls, "Eh2T", 1, [128, 16, 128], BF16); scr_ = ring(ls, "Esc", 1, [128, 16, 128], F32); x1r = ring(ls, "Ex1", 1, [128, D], F32)
            e1r = ring(ls, "e1", 1, [128, 8, 128], F32); e2r = ring(ls, "e2", 1, [128, 8, 128], F32); taur = ring(ls, "tau", 2, [128, 8], F32)
            UTr = ring(ls, "UTb", 2, [128, 16, 512], BF16); Vr = ring(ls, "Vb", 2, [128, 4, D], BF16)
            Er = ring(ls, "E", 2, [128, 8, 512], F32)
            Gr = ring(ls, "Gb", 2, [128, 8, 512], BF16)
            agr = ring(ls, "ag", 2, [128, 512], F32)
            cTr = ring(ls, "coefT", 2, [128, 4, 128], BF16)
            x2 = sbl(ls, "x2", [128, D], F32); b_x2 = Buf("x2")
            junk = Gr.tiles[0][:].rearrange("p h e -> p (h e)"); b_junk = Gr.bufs[0]
            ssE = sbl(ls, "ssE", [128, 2], F32); b_ssE = Buf("ssE")
            psA = pring(ls, "EpsA", 2, [128, 512], F32); psG = pring(ls, "EpsG", 2, [128, 512], F32)
            psOut = [psl(ls, "EpsO%d" % n, [128, 512], F32) for n in range(4)]; b_pOut = Buf("EpsO")
            UT_v = UT_bf.rearrange("(k p) e -> p k e", p=128)
            for m in range(NO):
                sl = slice(m * 128, (m + 1) * 128)
                h2T, bh2 = h2r.next(); sc, bsc = scr_.next()
                S.dma("sp", h2T[:], h2T_d[:, :, sl].rearrange("k p t -> p k t"), reads=[b_x1], writes=[bh2])
                S.dma("sp", sc[:], sc_d[sl, :, :], reads=[b_x1], writes=[bsc])
                scv = sc[:].rearrange("p (h t) k -> p h t k", t=2)
                e1, be1 = e1r.next(); e2, be2 = e2r.next(); tau, btau = taur.next()
                S.op("dve", lambda: nc.vector.tensor_tensor(out=e1[:], in0=scv[:, :, 0, :], in1=lnz_all[:, m, :].unsqueeze(2).to_broadcast([128, 8, 128]), op=ALU.subtract),
                     reads=[bsc, b_thr[m]], writes=[be1])
                S.op("act", lambda: nc.scalar.activation(out=e1[:], in_=e1[:], func=AF.Exp), reads=[be1], writes=[be1])
                S.op("act", lambda: nc.scalar.activation(out=e2[:], in_=scv[:, :, 1, :], func=AF.Exp), reads=[bsc], writes=[be2])
                S.op("dve", lambda: nc.vector.tensor_tensor(out=tau[:], in0=thr_all[:, m, :], in1=lnz_all[:, m, :], op=ALU.subtract), reads=[b_thr[m]], writes=[btau])
                S.op("act", lambda: nc.scalar.activation(out=tau[:], in_=tau[:], func=AF.Exp), reads=[btau], writes=[btau])
                def stageA(eb):
                    UTb, bU = UTr.next(); Vb, bV = Vr.next()
                    S.dma("sp", UTb[:], UT_v[:, :, eb * 512:(eb + 1) * 512], reads=[b_wscr], writes=[bU])
                    S.dma("pool", Vb[:], V_bf[eb * 512:(eb + 1) * 512, :].rearrange("(c p) n -> p c n", p=128), reads=[b_wscr], writes=[bV])
                    Et, bE = Er.next(); Gb, bG = Gr.next()
                    S.op("pool", lambda: nc.gpsimd.tensor_tensor(out=Et[:].rearrange("p h (a k) -> p h a k", a=4),
                                                                 in0=e1[:, :, eb * 4:(eb + 1) * 4].unsqueeze(3).to_broadcast([128, 8, 4, 128]),
                                                                 in1=e2[:].unsqueeze(2).to_broadcast([128, 8, 4, 128]), op=ALU.mult),
                         reads=[be1, be2], writes=[bE])
                    pa, bpa = psA.next()
                    for c in range(4):
                        mmg(pa[:, c * 128:(c + 1) * 128], [(UTb[:, k, c * 128:(c + 1) * 128], h2T[:, k, :]) for k in range(16)], [bU, bh2], bpa)
                    ag, bag = agr.next()
                    S.op("act", lambda: nc.scalar.activation(out=ag[:], in_=pa[:], func=AF.Gelu_apprx_tanh), reads=[bpa], writes=[bag])
                    for h in range(8):
                        S.op("dve", lambda: nc.vector.scalar_tensor_tensor(out=Gb[:, h, :], in0=Et[:, h, :], scalar=tau[:, h:h + 1], in1=Et[:, h, :],
                                                                           op0=ALU.is_ge, op1=ALU.mult), reads=[bE, btau], writes=[bG])
                    return (eb, Vb, bV, Gb, bG, ag, bag)

                def stageB(st):
                    (eb, Vb, bV, Gb, bG, ag, bag) = st
                    pg, bpg = psG.next()
                    for c in range(4):
                        mmg(pg[:, c * 128:(c + 1) * 128], [(Gb[:, h, c * 128:(c + 1) * 128], ident_b[:]) for h in range(8)], [bG, b_ident], bpg)
                    cT, bcT = cTr.next()
                    S.op("dve", lambda: nc.vector.tensor_tensor(out=cT[:].rearrange("p c t -> p (c t)"), in0=pg[:], in1=ag[:], op=ALU.mult),
                         reads=[bpg, bag], writes=[bcT])
                    for n4 in range(4):
                        for c in range(4):
                            S.op("pe", lambda: nc.tensor.matmul(out=psOut[n4][:], lhsT=cT[:, c, :], rhs=Vb[:, c, n4 * 512:(n4 + 1) * 512],
                                                                start=(eb == 0 and c == 0), stop=(eb == 31 and c == 3)),
                                 reads=[bcT, bV], writes=[b_pOut], inc=(n4 == 3 and c == 3))

                prev = None
                for eb in range(32):
                    cur = stageA(eb)
                    if prev is not None:
                        stageB(prev)
                    prev = cur
                stageB(prev)
                x1, bx1 = x1r.next()
                S.dma("sp", x1[:], x1_d[sl, :], reads=[b_x1], writes=[bx1])
                for n4 in range(4):
                    ns = slice(n4 * 512, (n4 + 1) * 512)
                    S.op("dve", lambda: nc.vector.tensor_tensor(out=x2[:, ns], in0=psOut[n4][:], in1=gate2B[:, ns], op=ALU.mult), reads=[b_pOut, b_gate], writes=[b_x2])
                S.op("pool", lambda: nc.gpsimd.tensor_tensor(out=x2[:], in0=x2[:], in1=x1[:], op=ALU.add), reads=[b_x2, bx1], writes=[b_x2])
                S.op("act", lambda: nc.scalar.activation(out=junk[:, 0:D], in_=x2[:], func=AF.Square, accum_out=ssE[:, 0:1]), reads=[b_x2], writes=[b_junk, b_ssE])
                rstd_from_ss(ssE[:, 1:2], ssE[:, 0:1], D, [b_ssE], [b_ssE])
                S.op("dve", lambda: nc.vector.scalar_tensor_tensor(out=x1[:], in0=x2[:], scalar=ssE[:, 1:2], in1=g_finB[:], op0=ALU.mult, op1=ALU.mult),
                     reads=[b_x2, b_ssE, b_gf], writes=[bx1])
                S.dma("sp", y[sl, :], x1[:], reads=[bx1], writes=[b_y], chan="yout")
            S.barrier()

        if dbg:
            with scope() as ls:
                dt_ = sbl(ls, "dbgt", [128, 4096], F32); b_dt = Buf("dbgt")
                dtb = sbl(ls, "dbgtb", [128, 4096], BF16)
                if "KcT" in dbg_out:
                    S.op("dve", lambda: nc.vector.tensor_copy(out=dt_[0:71, 0:4 * NCC * 128], in_=KcT[:].rearrange("p a b -> p (a b)")), reads=[b_KcT], writes=[b_dt])
                    S.dma("sp", dbg_out["KcT"][:, :], dt_[0:71, 0:4 * NCC * 128], reads=[b_dt], writes=[Buf("o1")])
                if "qA" in dbg_out:
                    S.dma("sp", dtb[0:64, 0:SO], qA_d[5, :, :], writes=[b_dt])
                    S.op("dve", lambda: nc.vector.tensor_copy(out=dt_[0:64, 0:SO], in_=dtb[0:64, 0:SO]), reads=[b_dt], writes=[b_dt])
                    S.dma("sp", dbg_out["qA"][:, :], dt_[0:64, 0:SO], reads=[b_dt], writes=[Buf("o2")])
                if "R" in dbg_out:
                    S.dma("sp", dtb[:, 0:1024], R_d[NO - 1].rearrange("p a b -> p (a b)"), writes=[b_dt])
                    S.op("dve", lambda: nc.vector.tensor_copy(out=dt_[:, 0:1024], in_=dtb[:, 0:1024]), reads=[b_dt], writes=[b_dt])
                    S.dma("sp", dbg_out["R"][:, :], dt_[:, 0:1024], reads=[b_dt], writes=[Buf("o3")])
                S.barrier()

        if dbg and ("omix" in dbg_out or "x1" in dbg_out):
            with scope() as ls:
                tb = sbl(ls, "tapb", [128, D], BF16); tf = sbl(ls, "tapf", [128, D], F32); b_tap = Buf("tap")
                for m in range(NO):
                    sl = slice(m * 128, (m + 1) * 128)
                    if "omix" in dbg_out:
                        S.dma("sp", tb[:], omix_d[sl, :], writes=[b_tap])
                        S.op("dve", lambda: nc.vector.tensor_copy(out=tf[:], in_=tb[:]), reads=[b_tap], writes=[b_tap])
                        S.dma("sp", dbg_out["omix"][sl, :], tf[:], reads=[b_tap], writes=[b_tap])
                    if "x1" in dbg_out:
                        S.dma("sp", tf[:], x1_d[sl, :], writes=[b_tap])
                        S.dma("sp", dbg_out["x1"][sl, :], tf[:], reads=[b_tap], writes=[b_tap])
        S.barrier()
        S.finish_all()
        print("ninst", S.ninst, "nsem", S.nsem)
    return nc


def core_inputs(inp, b, j, SEQ):
    NT = SEQ // 128
    xb = np.ascontiguousarray(inp["x"][b])
    own = np.ascontiguousarray(xb.reshape(NT // 2, 2, 128, D)[:, j].reshape(SEQ // 2, D))
    m = {
        "x_all": xb, "x_own": own,
        "c_in": np.ascontiguousarray(inp["c"][b].reshape(16, 128).T),
        "w_ada": inp["w_ada"][0], "b_ada": inp["b_ada"][0].reshape(1, -1),
        "g_mix": inp["g_norm_mix"][0].reshape(1, -1), "g_ffn": inp["g_norm_ffn"][0].reshape(1, -1),
        "g_fin": inp["g_norm_final"].reshape(1, -1), "w_in": inp["w_in"][0],
        "pe_k": inp["cmp_pe_k"][0], "pe_v": inp["cmp_pe_v"][0],
        "ck_w1": inp["cmp_k_w1"][0], "ck_w2": inp["cmp_k_w2"][0], "cv_w1": inp["cmp_v_w1"][0], "cv_w2": inp["cmp_v_w2"][0],
        "g_nsa": inp["g_nsa_out"][0].reshape(1, -1), "g_ret": inp["g_ret_out"][0].reshape(1, -1),
        "w_out": inp["w_out"][0], "w_q": inp["peer_w_q"][0], "subk": inp["peer_sub_keys"][0].reshape(16, 128, 128),
        "U": inp["peer_u"][0], "V": inp["peer_v"][0],
    }
    cs = host_consts(SEQ, j)
    for k in ("ident", "kaug", "caug", "qaug", "cm1", "vmask", "amask", "overlap", "eall", "cmab", "wmask",
              "decT", "kdec", "qdecB", "jsel"):
        m[k] = np.ascontiguousarray(cs[k])
    return {k: np.ascontiguousarray(np.asarray(v)) for k, v in m.items()}


def kernel(**inputs):
    inp = {k: np.asarray(v) for k, v in inputs.items()}
    B, SEQ, _ = inp["x"].shape
    nc = build(SEQ)
    maps = [core_inputs(inp, b, j, SEQ) for b in range(B) for j in range(2)]
    res = run_bass_kernel_spmd(nc, maps, core_ids=list(range(2 * B)))
    out = np.zeros((B, SEQ, D), np.float32)
    NT = SEQ // 128
    for b in range(B):
        ov = out[b].reshape(NT // 2, 2, 128, D)
        for j in range(2):
            ov[:, j] = np.asarray(res.results[2 * b + j]["y"]).reshape(NT // 2, 128, D)
    return out
```

```python
import contextlib
import numpy as np
import ml_dtypes
import concourse.bass as bass
import concourse.mybir as mybir
from concourse.bass_utils import run_bass_kernel_spmd

F32 = mybir.dt.float32
BF16 = mybir.dt.bfloat16
ALU = mybir.AluOpType
AF = mybir.ActivationFunctionType
AX = mybir.AxisListType
EPOCH = 30000
D = 2048
IN_W = 6704
C_QA, C_KC, C_VC, C_KS, C_VS, C_KW, C_VW, C_GA, C_QR, C_KR, C_VR, C_GR = (
    0, 1024, 1280, 1536, 1792, 2048, 2304, 2560, 2608, 3632, 4656, 5680)
NEG = -30000.0
NEXP = 16384


class Buf:
    __slots__ = ("name", "w", "r")

    def __init__(self, name=""):
        self.name = name
        self.w = None
        self.r = {}


class Sync:
    def __init__(self, nc, stack):
        self.nc = nc
        self.stack = stack
        self.engines = {"pe": nc.tensor, "dve": nc.vector, "act": nc.scalar,
                        "pool": nc.gpsimd, "sp": nc.sync}
        self.sems = {}
        self.cnt = {}
        self.epoch = {e: 0 for e in self.engines}
        self.ecount = {e: 0 for e in self.engines}
        self.known = {e: {} for e in self.engines}
        self.nsem = 0
        self.ninst = {e: 0 for e in self.engines}

    def _sem(self, key):
        if key not in self.sems:
            self.sems[key] = self.stack.enter_context(self.nc.semaphore("s%d" % self.nsem))
            self.nsem += 1
            self.cnt[key] = 0
        return self.sems[key]

    def _ekey(self, e):
        return ("E", e, self.epoch[e])

    def _wait(self, e, deps, skip_same_pe=True):
        eng = self.engines[e]
        need = {}
        for d in deps:
            if d is None:
                continue
            k, v = d
            if e == "pe" and k[0] == "E" and k[1] == "pe" and skip_same_pe:
                continue
            if k[0] == "D":
                v = self.cnt[k]
            if self.known[e].get(k, 0) >= v:
                continue
            if need.get(k, 0) < v:
                need[k] = v
        for k, v in need.items():
            eng.wait_ge(self.sems[k], v)
            self.known[e][k] = v
            self.ninst[e] += 1

    def _deps(self, reads, writes):
        deps = []
        for b in reads:
            deps.append(b.w)
        for b in writes:
            deps.append(b.w)
            for k, v in b.r.items():
                deps.append((k, v))
        return deps

    def op(self, e, fn, reads=(), writes=(), inc=True):
        self._wait(e, self._deps(reads, writes))
        ins = fn()
        self.ninst[e] += 1
        if self.ecount[e] >= EPOCH and inc:
            self.epoch[e] += 1
            self.ecount[e] = 0
        k = self._ekey(e)
        sem = self._sem(k)
        if inc:
            ins.then_inc(sem, 1)
            self.cnt[k] += 1
            self.ecount[e] += 1
            v = self.cnt[k]
        else:
            v = self.cnt[k] + 1
        for b in reads:
            if b.r.get(k, 0) < v:
                b.r[k] = v
        for b in writes:
            b.w = (k, v)
            b.r = {}
        return ins

    def dma(self, q, out, in_, reads=(), writes=(), chan=None, **kw):
        self._wait(q, self._deps(reads, writes), skip_same_pe=False)
        k = ("D", chan if chan is not None else (writes[0].name if writes else "dma"))
        sem = self._sem(k)
        ins = self.engines[q].dma_start(out=out, in_=in_, **kw)
        ins.then_inc(sem, 16)
        self.ninst[q] += 1
        self.cnt[k] += 16
        v = self.cnt[k]
        for b in reads:
            if b.r.get(k, 0) < v:
                b.r[k] = v
        for b in writes:
            b.w = (k, v)
            b.r = {}
        return ins

    def finish_all(self):
        deps = [(k, v) for k, v in self.cnt.items() if v > 0]
        self._wait("sp", deps, skip_same_pe=False)

    def barrier(self):
        for e in self.engines:
            deps = [(k, v) for k, v in self.cnt.items() if v > 0]
            self._wait(e, deps, skip_same_pe=False)


class Ring:
    def __init__(self, tiles, name):
        self.tiles = tiles
        self.bufs = [Buf("%s%d" % (name, i)) for i in range(len(tiles))]
        self.i = -1

    def next(self):
        self.i = (self.i + 1) % len(self.tiles)
        return self.tiles[self.i], self.bufs[self.i]


def bf(x):
    return np.asarray(x).astype(ml_dtypes.bfloat16)


def split3(v):
    v = np.asarray(v, np.float64)
    a = bf(v)
    r = v - a.astype(np.float64)
    b = bf(r)
    r2 = r - b.astype(np.float64)
    c = bf(r2)
    return a, b, c


def host_consts(SEQ, j):
    NT = SEQ // 128
    NO = NT // 2
    NSLC = SEQ // 64
    NCMP = SEQ // 16
    NCC = max(1, NCMP // 128)
    n_cmp = (SEQ - 32) // 16 + 1
    cs = {}
    cs["ident"] = np.eye(128, dtype=np.float32)
    pos = np.arange(SEQ)
    cs["kaug"] = bf(np.stack([pos // 128, pos // 128, pos % 128, pos % 128,
                              np.ones(SEQ), np.ones(SEQ), np.ones(SEQ)]).astype(np.float32))
    cpos = np.arange(NCC * 128) * 16 + 31
    cs["caug"] = bf(np.stack([cpos // 128, cpos // 128, cpos % 128, cpos % 128,
                              np.ones_like(cpos), np.ones_like(cpos), np.ones_like(cpos)]).astype(np.float32))
    slopes = np.exp2(-8.0 * (np.arange(16, dtype=np.float64) + 1.0) / 16).astype(np.float32).astype(np.float64)
    qaug = np.zeros((NO, 4, 7, 4, 128), dtype=ml_dtypes.bfloat16)
    for m in range(NO):
        t = 128.0 * (2 * m + j) + np.arange(128)
        for g in range(4):
            for r in range(4):
                s = slopes[4 * g + r]
                shi = bf(s)
                slo = bf(s - float(shi))
                seff = float(shi) + float(slo)
                a, b, c = split3(seff * t)
                qaug[m, g, 0, r, :] = bf(128.0 * float(shi))
                qaug[m, g, 1, r, :] = bf(128.0 * float(slo))
                qaug[m, g, 2, r, :] = shi
                qaug[m, g, 3, r, :] = slo
                qaug[m, g, 4, r, :] = bf(-a.astype(np.float32))
                qaug[m, g, 5, r, :] = bf(-b.astype(np.float32))
                qaug[m, g, 6, r, :] = bf(-c.astype(np.float32))
    cs["qaug"] = qaug.reshape(NO, 4, 7, 512)
    n = np.arange(NCC * 128)
    cm1 = np.zeros((NO, 128, NCC, 128), np.float32)
    vm = np.zeros((NO, 128, NSLC), np.float32)
    am = np.zeros((NO, 128, NSLC), np.float32)
    mb = np.arange(NSLC)
    for m in range(NO):
        t = 128 * (2 * m + j) + np.arange(128)
        ok = (n[:, None] < n_cmp) & ((16 * n[:, None] + 31) <= t[None, :])
        cm1[m] = np.where(ok, 0.0, NEG).reshape(NCC, 128, 128).transpose(1, 0, 2)
        back = (t // 64)[:, None] - mb[None, :]
        valid = back >= 0
        forced = valid & ((mb[None, :] == 0) | (back < 2))
        vm[m] = (valid & ~forced).astype(np.float32)
        am[m] = np.where(forced, 1e4 + mb[None, :], np.where(valid, 0.0, -1.0))
    cs["cm1"] = bf(cm1)
    cs["vmask"] = vm
    cs["amask"] = am
    ov = ((n[:, None] < n_cmp) & (n[:, None] >= 4 * mb[None, :] - 1) & (n[:, None] <= 4 * mb[None, :] + 3))
    cs["overlap"] = bf(ov.astype(np.float32).reshape(NCC, 128, NSLC).transpose(1, 0, 2))
    E = np.zeros((NSLC, NT, 128), np.float32)
    for c in range(NT):
        E[2 * c, c, :64] = 1.0
        E[2 * c + 1, c, 64:] = 1.0
    cs["eall"] = bf(E)
    k = np.arange(128)[:, None]
    q = np.arange(128)[None, :]
    tri = np.where(k <= q, 0.0, NEG).astype(np.float32)
    full = np.full((128, 128), NEG, np.float32)
    zero = np.zeros((128, 128), np.float32)
    cs["cmab"] = bf(np.stack([tri, full] if j == 0 else [zero, tri], axis=1))
    wm = np.zeros((128, 6, 128), np.float32)
    for u in range(6):
        d = 128 * (j + 4 - u) + q - k
        wm[:, u, :] = np.where((d >= 0) & (d < 512), 0.0, NEG)
    cs["wmask"] = bf(wm)
    lg = np.log1p(-np.exp2(-5.0 - np.arange(8, dtype=np.float64)))
    p = np.arange(128, dtype=np.float64)
    dec = np.zeros((128, 8, 128), np.float32)
    for h in range(8):
        diff = p[None, :] - p[:, None]
        dec[:, h, :] = np.where(diff >= 0, np.exp(lg[h] * np.maximum(diff, 0.0)), 0.0) * (128.0 ** -0.5)
    cs["decT"] = dec
    cs["kdec"] = np.stack([np.exp(lg[h] * (127.0 - p)) * (128.0 ** -0.5) for h in range(8)], axis=1).astype(np.float32)
    qd = np.stack([np.exp(lg[h] * (p + 1.0)) for h in range(8)], axis=0).astype(np.float32)
    cs["qdecB"] = np.broadcast_to(qd[None], (128, 8, 128)).copy()
    cs["cdec"] = np.exp(lg * 128.0)
    cs["jsel"] = np.full((128, 1), float(j), np.float32)
    return cs


def build(SEQ, dbg=None):
    NT = SEQ // 128
    NO = NT // 2
    NS = SEQ // 512
    NOS = NO // 4
    NSLC = SEQ // 64
    NCMP = SEQ // 16
    NCC = max(1, NCMP // 128)
    SO = SEQ // 2
    cdec = host_consts(256, 0)["cdec"] if False else np.exp(np.log1p(-np.exp2(-5.0 - np.arange(8, dtype=np.float64))) * 128.0)
    nc = bass.Bass("TRN2", target_bir_lowering=False)

    def din(name, shape, dt=F32):
        return nc.dram_tensor(name, list(shape), dt, kind="ExternalInput").ap()

    def dscr(name, shape, dt):
        return nc.dram_tensor(name, list(shape), dt).ap()

    x_all = din("x_all", [SEQ, D]); x_own = din("x_own", [SO, D]); c_in = din("c_in", [128, 16])
    w_ada = din("w_ada", [D, 6 * D]); b_ada = din("b_ada", [1, 6 * D])
    g_mix = din("g_mix", [1, D]); g_ffn = din("g_ffn", [1, D]); g_fin = din("g_fin", [1, D])
    w_in = din("w_in", [D, IN_W])
    pe_k = din("pe_k", [32, 64]); pe_v = din("pe_v", [32, 64])
    ck_w1 = din("ck_w1", [2048, 128]); ck_w2 = din("ck_w2", [128, 64])
    cv_w1 = din("cv_w1", [2048, 128]); cv_w2 = din("cv_w2", [128, 64])
    g_nsa = din("g_nsa", [1, 1024]); g_ret = din("g_ret", [1, 1024])
    w_out = din("w_out", [D, D]); w_q = din("w_q", [D, D]); subk = din("subk", [16, 128, 128])
    U = din("U", [NEXP, D]); V = din("V", [NEXP, D])
    h_ident = din("ident", [128, 128]); h_kaug = din("kaug", [7, SEQ], BF16); h_caug = din("caug", [7, NCC * 128], BF16)
    h_qaug = din("qaug", [NO, 4, 7, 512], BF16); h_cm1 = din("cm1", [NO, 128, NCC, 128], BF16)
    h_vmask = din("vmask", [NO, 128, NSLC]); h_amask = din("amask", [NO, 128, NSLC])
    h_overlap = din("overlap", [128, NCC, NSLC], BF16); h_eall = din("eall", [NSLC, NT, 128], BF16)
    h_cmab = din("cmab", [128, 2, 128], BF16); h_wmask = din("wmask", [128, 6, 128], BF16)
    h_decT = din("decT", [128, 8, 128]); h_kdec = din("kdec", [128, 8]); h_qdecB = din("qdecB", [128, 8, 128])
    h_jsel = din("jsel", [128, 1])
    y = nc.dram_tensor("y", [SO, D], F32, kind="ExternalOutput").ap()
    dbg_out = {}
    if dbg:
        for nm, shp in dbg.items():
            dbg_out[nm] = nc.dram_tensor("dbg_" + nm, list(shp), F32, kind="ExternalOutput").ap()

    w_in_bf = dscr("w_in_bf", [D, IN_W], BF16); w_out_bf = dscr("w_out_bf", [D, D], BF16); w_q_bf = dscr("w_q_bf", [D, D], BF16)
    UT_bf = dscr("UT_bf", [D, NEXP], BF16); V_bf = dscr("V_bf", [NEXP, D], BF16)
    kS_d = dscr("kS_d", [4, 64, SEQ], BF16); kW_d = dscr("kW_d", [4, 64, SEQ], BF16)
    vS_d = dscr("vS_d", [SEQ, 256], BF16); vW_d = dscr("vW_d", [SEQ, 256], BF16)
    qA_d = dscr("qA_d", [16, 64, SO], BF16); gA_d = dscr("gA_d", [SO, 48], F32)
    qR_d = dscr("qR_d", [8, 128, SO], BF16); kR_d = dscr("kR_d", [8, 128, SO], BF16)
    vR_d = dscr("vR_d", [SO, 1024], BF16); gR_d = dscr("gR_d", [SO, 1024], F32)
    R_d = dscr("R_d", [NO, 128, 8, 128], BF16)
    omix_d = dscr("omix_d", [SO, D], BF16)
    x1_d = dscr("x1_d", [SO, D], F32); h2T_d = dscr("h2T_d", [16, 128, SO], BF16)
    sc_d = dscr("sc_d", [SO, 16, 128], F32)

    with contextlib.ExitStack() as st:
        S = Sync(nc, st)

        def sbp(name, shape, dt):
            return st.enter_context(nc.sbuf_tensor("t_" + name, list(shape), dt))

        ident_f = sbp("ident_f", [128, 128], F32); b_ident = Buf("ident")
        ident_b = sbp("ident_b", [128, 128], BF16)
        gate1B = sbp("gate1B", [128, D], F32); gate2B = sbp("gate2B", [128, D], F32); b_gate = Buf("gate")
        modT = sbp("modT", [128, 4, 16], F32); b_modT = Buf("modT")
        KcT = sbp("KcT", [71, 4, NCC * 128], BF16); b_KcT = Buf("KcT")
        VcA = sbp("VcA", [128, NCC, 4, 65 + NSLC], BF16); b_VcA = Buf("VcA")
        skT = sbp("skT", [128, 16, 128], BF16); b_skT = Buf("skT")
        jsel = sbp("jsel", [128, 1], F32); b_jsel = Buf("jsel")
        thr_all = sbp("thr_all", [128, NO, 8], F32)
        lnz_all = sbp("lnz_all", [128, NO, 8], F32)
        b_thr = [Buf("thr%d" % i) for i in range(NO)]

        S.dma("sp", ident_f[:], h_ident[:, :], writes=[b_ident])
        S.op("dve", lambda: nc.vector.tensor_copy(out=ident_b[:], in_=ident_f[:]), reads=[b_ident], writes=[b_ident])
        S.dma("sp", jsel[:], h_jsel[:, :], writes=[b_jsel])

        def scope():
            return contextlib.ExitStack()

        def sbl(ls, name, shape, dt):
            return ls.enter_context(nc.sbuf_tensor("t_" + name, list(shape), dt))

        def psl(ls, name, shape, dt):
            return ls.enter_context(nc.psum_tensor("p_" + name, list(shape), dt))

        def ring(ls, name, n, shape, dt):
            return Ring([sbl(ls, "%s%d" % (name, i), shape, dt) for i in range(n)], name)

        def pring(ls, name, n, shape, dt):
            return Ring([psl(ls, "%s%d" % (name, i), shape, dt) for i in range(n)], name)

        def mmg(out_ap, pairs, rbufs, wbuf):
            n = len(pairs)
            for i, (l, r) in enumerate(pairs):
                S.op("pe", lambda: nc.tensor.matmul(out=out_ap, lhsT=l, rhs=r, start=(i == 0), stop=(i == n - 1)),
                     reads=rbufs, writes=[wbuf], inc=(i == n - 1))

        cast_rr = [0]

        def cast(out_ap, in_ap, rb, wb, psum=False):
            e = ("dve", "act")[cast_rr[0] % 2] if psum else ("dve", "pool", "act")[cast_rr[0] % 3]
            cast_rr[0] += 1
            if e == "act":
                S.op("act", lambda: nc.scalar.copy(out=out_ap, in_=in_ap), reads=rb, writes=wb)
            elif e == "dve":
                S.op("dve", lambda: nc.vector.tensor_copy(out=out_ap, in_=in_ap), reads=rb, writes=wb)
            else:
                S.op("pool", lambda: nc.gpsimd.tensor_copy(out=out_ap, in_=in_ap), reads=rb, writes=wb)

        def rstd_from_ss(rstd_ap, ss_ap, n, rb, wb):
            S.op("dve", lambda: nc.vector.tensor_scalar(out=rstd_ap, in0=ss_ap, scalar1=1.0 / n, scalar2=1e-6,
                                                        op0=ALU.mult, op1=ALU.add), reads=rb, writes=wb)
            S.op("act", lambda: nc.scalar.activation(out=rstd_ap, in_=rstd_ap, func=AF.Sqrt), reads=wb, writes=wb)
            S.op("dve", lambda: nc.vector.reciprocal(out=rstd_ap, in_=rstd_ap), reads=wb, writes=wb)

        with scope() as ls:
            cin_t = sbl(ls, "cin_t", [128, 16], F32); b_cin = Buf("cin")
            cb = sbl(ls, "cb", [128, 16, 128], F32); b_cb = Buf("cb")
            modB = sbl(ls, "modB", [128, 6, D], F32); b_modB = [Buf("modB%d" % v) for v in range(6)]
            wr = ring(ls, "wada", 3, [128, D], F32)
            btile = sbl(ls, "btile", [128, D], F32); b_bt = Buf("bt")
            gB = sbl(ls, "gB", [128, 2, D], F32); b_gB = Buf("gB")
            vecB = sbl(ls, "vecB", [128, 4, D], F32); b_vecB = Buf("vecB")
            psA = pring(ls, "psA", 8, [128, 512], F32)
            S.dma("sp", cin_t[:], c_in[:, :], writes=[b_cin])
            S.op("act", lambda: nc.scalar.activation(out=cin_t[:], in_=cin_t[:], func=AF.Silu), reads=[b_cin], writes=[b_cin])
            S.op("dve", lambda: nc.vector.tensor_copy(out=cb[:], in_=cin_t[:].unsqueeze(2).to_broadcast([128, 16, 128])),
                 reads=[b_cin], writes=[b_cb])
            for v in range(6):
                banks = [psA.next() for _ in range(4)]
                for k in range(16):
                    wt, bw = wr.next()
                    S.dma("sp" if k % 2 == 0 else "pool", wt[:], w_ada[k * 128:(k + 1) * 128, v * D:(v + 1) * D], writes=[bw])
                    for n4 in range(4):
                        pt, bp = banks[n4]
                        S.op("pe", lambda: nc.tensor.matmul(out=pt[:], lhsT=cb[:, k, :], rhs=wt[:, n4 * 512:(n4 + 1) * 512],
                                                            start=(k == 0), stop=(k == 15)),
                             reads=[b_cb, bw], writes=[bp], inc=(k == 15 or n4 == 3))
                S.dma("sp", btile[:], b_ada[0:1, v * D:(v + 1) * D].partition_broadcast(128), writes=[b_bt])
                for n4 in range(4):
                    pt, bp = banks[n4]
                    S.op("dve", lambda: nc.vector.tensor_tensor(out=modB[:, v, n4 * 512:(n4 + 1) * 512], in0=pt[:],
                                                                in1=btile[:, n4 * 512:(n4 + 1) * 512], op=ALU.add),
                         reads=[bp, b_bt], writes=[b_modB[v]])
            S.dma("sp", gB[:, 0, :], g_mix[0:1, :].partition_broadcast(128), writes=[b_gB])
            S.dma("sp", gB[:, 1, :], g_ffn[0:1, :].partition_broadcast(128), writes=[b_gB])
            S.op("dve", lambda: nc.vector.scalar_tensor_tensor(out=vecB[:, 0, :], in0=modB[:, 1, :], scalar=1.0, in1=gB[:, 0, :],
                                                               op0=ALU.add, op1=ALU.mult), reads=[b_modB[1], b_gB], writes=[b_vecB])
            S.op("dve", lambda: nc.vector.tensor_copy(out=vecB[:, 1, :], in_=modB[:, 0, :]), reads=[b_modB[0]], writes=[b_vecB])
            S.op("dve", lambda: nc.vector.scalar_tensor_tensor(out=vecB[:, 2, :], in0=modB[:, 4, :], scalar=1.0, in1=gB[:, 1, :],
                                                               op0=ALU.add, op1=ALU.mult), reads=[b_modB[4], b_gB], writes=[b_vecB])
            S.op("dve", lambda: nc.vector.tensor_copy(out=vecB[:, 3, :], in_=modB[:, 3, :]), reads=[b_modB[3]], writes=[b_vecB])
            S.op("pool", lambda: nc.gpsimd.tensor_copy(out=gate1B[:], in_=modB[:, 2, :]), reads=[b_modB[2]], writes=[b_gate])
            S.op("pool", lambda: nc.gpsimd.tensor_copy(out=gate2B[:], in_=modB[:, 5, :]), reads=[b_modB[5]], writes=[b_gate])
            for vi in range(4):
                for kc in range(16):
                    pt, bp = psA.next()
                    S.op("pe", lambda: nc.tensor.transpose(out=pt[:, 0:128], in_=vecB[:, vi, kc * 128:(kc + 1) * 128], identity=ident_f[:]),
                         reads=[b_vecB, b_ident], writes=[bp])
                    S.op("dve", lambda: nc.vector.tensor_copy(out=modT[:, vi, kc:kc + 1], in_=pt[:, 0:1]), reads=[bp], writes=[b_modT])
            S.barrier()

        b_wscr = Buf("wscr")
        with scope() as ls:
            fr = ring(ls, "cf", 2, [128, IN_W], F32)
            br = ring(ls, "cbf", 2, [128, IN_W], BF16)
            psT = pring(ls, "psTa", 2, [128, 1024], BF16)
            ust = ring(ls, "ust", 2, [128, 16, 512], BF16)
            for k in range(16):
                ft, bfb = fr.next(); bt_, bbb = br.next()
                S.dma("sp", ft[:], w_in[k * 128:(k + 1) * 128, :], writes=[bfb])
                cast(bt_[:], ft[:], [bfb], [bbb])
                S.dma("pool", w_in_bf[k * 128:(k + 1) * 128, :], bt_[:], reads=[bbb], writes=[b_wscr], chan="wscr")
            for (src, dst) in ((w_out, w_out_bf), (w_q, w_q_bf)):
                for k in range(0, 16, 2):
                    ft, bfb = fr.next(); bt_, bbb = br.next()
                    S.dma("sp", ft[:, 0:2 * D].rearrange("p (a n) -> p a n", a=2),
                          src[k * 128:(k + 2) * 128, :].rearrange("(a p) n -> p a n", p=128), writes=[bfb])
                    cast(bt_[:, 0:2 * D], ft[:, 0:2 * D], [bfb], [bbb])
                    S.dma("pool", dst[k * 128:(k + 2) * 128, :].rearrange("(a p) n -> p a n", p=128),
                          bt_[:, 0:2 * D].rearrange("p (a n) -> p a n", a=2), reads=[bbb], writes=[b_wscr], chan="wscr")
            for e0 in range(0, 128, 2):
                ft, bfb = fr.next(); bt_, bbb = br.next()
                S.dma("sp", ft[:, 0:2 * D].rearrange("p (a n) -> p a n", a=2),
                      V[e0 * 128:(e0 + 2) * 128, :].rearrange("(a p) n -> p a n", p=128), writes=[bfb])
                cast(bt_[:, 0:2 * D], ft[:, 0:2 * D], [bfb], [bbb])
                S.dma("pool", V_bf[e0 * 128:(e0 + 2) * 128, :].rearrange("(a p) n -> p a n", p=128),
                      bt_[:, 0:2 * D].rearrange("p (a n) -> p a n", a=2), reads=[bbb], writes=[b_wscr], chan="wscr")
            for e4 in range(32):
                us, bus = ust.next()
                for a in range(0, 4, 2):
                    ft, bfb = fr.next(); bt_, bbb = br.next()
                    e0 = e4 * 4 + a
                    S.dma("sp", ft[:, 0:2 * D].rearrange("p (a n) -> p a n", a=2),
                          U[e0 * 128:(e0 + 2) * 128, :].rearrange("(a p) n -> p a n", p=128), writes=[bfb])
                    cast(bt_[:, 0:2 * D], ft[:, 0:2 * D], [bfb], [bbb])
                    for a2 in range(2):
                        for half in range(2):
                            pt, bp = psT.next()
                            for k8 in range(8):
                                kc = half * 8 + k8
                                S.op("pe", lambda: nc.tensor.transpose(out=pt[:, k8 * 128:(k8 + 1) * 128],
                                                                       in_=bt_[:, a2 * D + kc * 128:a2 * D + (kc + 1) * 128], identity=ident_b[:]),
                                     reads=[bbb, b_ident], writes=[bp], inc=(k8 == 7))
                            ee = (a + a2) * 128
                            cast(us[:, half * 8:(half + 1) * 8, ee:ee + 128], pt[:].rearrange("p (k e) -> p k e", k=8), [bp], [bus], psum=True)
                S.dma("pool", UT_bf.rearrange("(k p) e -> p k e", p=128)[:, :, e4 * 512:(e4 + 1) * 512], us[:],
                      reads=[bus], writes=[b_wscr], chan="wscr")
            ft, bfb = fr.next(); bt_, bbb = br.next()
            S.dma("sp", ft[:, 0:2048].rearrange("p (a d) -> p a d", a=16), subk.rearrange("a k d -> k a d"), writes=[bfb])
            cast(bt_[:, 0:2048], ft[:, 0:2048], [bfb], [bbb])
            for half in range(2):
                pt, bp = psT.next()
                for k8 in range(8):
                    a = half * 8 + k8
                    S.op("pe", lambda: nc.tensor.transpose(out=pt[:, k8 * 128:(k8 + 1) * 128], in_=bt_[:, a * 128:(a + 1) * 128], identity=ident_b[:]),
                         reads=[bbb, b_ident], writes=[bp], inc=(k8 == 7))
                cast(skT[:, half * 8:(half + 1) * 8, :], pt[:].rearrange("p (k e) -> p k e", k=8), [bp], [b_skT], psum=True)
            S.barrier()

        if dbg and "modT" in dbg_out:
            S.dma("sp", dbg_out["modT"][:, :], modT[:].rearrange("p a b -> p (a b)"), reads=[b_modT], writes=[Buf("dbgo")])


        b_y = Buf("y")
        b_scr = Buf("scr")
        gelu_c = 1.5957691216057308

        def gelu_tanh(out_ap, x_ap, tmp_ap, rb, wb, tb, shape_bias=None):
            S.op("dve", lambda: nc.vector.tensor_tensor(out=tmp_ap, in0=x_ap, in1=x_ap, op=ALU.mult), reads=rb, writes=[tb])
            S.op("dve", lambda: nc.vector.tensor_scalar(out=tmp_ap, in0=tmp_ap, scalar1=0.044715, scalar2=1.0, op0=ALU.mult, op1=ALU.add),
                 reads=[tb], writes=[tb])
            S.op("dve", lambda: nc.vector.tensor_tensor(out=tmp_ap, in0=tmp_ap, in1=x_ap, op=ALU.mult), reads=rb + [tb], writes=[tb])
            S.op("act", lambda: nc.scalar.activation(out=tmp_ap, in_=tmp_ap, func=AF.Sigmoid, scale=gelu_c), reads=[tb], writes=[tb])
            S.op("dve", lambda: nc.vector.tensor_tensor(out=out_ap, in0=tmp_ap, in1=x_ap, op=ALU.mult), reads=rb + [tb], writes=wb)

        w_in_v = w_in_bf.rearrange("(k p) n -> p k n", p=128)

        with scope() as ls:
            psT = pring(ls, "psT", 2, [128, 1024], BF16)
            psM = pring(ls, "psM", 4, [128, 512], F32)
            psK = pring(ls, "psK", 2, [128, 512], F32)
            W1s = sbl(ls, "W1s", [128, 2, 32, 128], BF16); b_W1s = Buf("W1s")
            W2s = sbl(ls, "W2s", [128, 2, 64], BF16); b_W2 = Buf("W2")
            peTb = sbl(ls, "peTb", [64, 2, 32], BF16); b_pe = Buf("pe")
            peb = sbl(ls, "peb", [128, 2], F32); b_peb = Buf("peb")
            with scope() as ls2:
                W1f = sbl(ls2, "W1f", [128, 32, 128], F32); b_W1f = Buf("W1f")
                W2f = sbl(ls2, "W2f", [128, 2, 64], F32)
                peT = sbl(ls2, "peT", [64, 2, 32], F32)
                for kv, (w1, w2, pe) in enumerate(((ck_w1, ck_w2, pe_k), (cv_w1, cv_w2, pe_v))):
                    for half in range(2):
                        S.dma("sp", W1f[half * 64:(half + 1) * 64, :, :], w1.rearrange("(l d) h -> d l h", d=64), writes=[b_W1f])
                    S.op("dve", lambda: nc.vector.tensor_copy(out=W1s[:, kv, :, :], in_=W1f[:]), reads=[b_W1f], writes=[b_W1s])
                    S.dma("sp", W2f[:, kv, :], w2[:, :], writes=[b_W2])
                    S.dma("sp", peT[:, kv, :], pe.rearrange("l d -> d l"), writes=[b_pe], allow_slow_non_contiguous=True)
                S.op("dve", lambda: nc.vector.tensor_copy(out=W2s[:], in_=W2f[:]), reads=[b_W2], writes=[b_W2])
                S.op("dve", lambda: nc.vector.tensor_copy(out=peTb[:], in_=peT[:]), reads=[b_pe], writes=[b_pe])
                for kv in range(2):
                    pt, bp = psM.next()
                    mmg(pt[:, 0:1], [(W1s[0:64, kv, l, :], peTb[:, kv, l:l + 1]) for l in range(32)], [b_W1s, b_pe], bp)
                    S.op("dve", lambda: nc.vector.tensor_copy(out=peb[:, kv:kv + 1], in_=pt[:, 0:1]), reads=[bp], writes=[b_peb])
                S.barrier()
            xr = ring(ls, "xt", 2, [128, D], F32)
            junk = sbl(ls, "junk", [128, D], BF16); b_junk = Buf("junk")
            ssr = ring(ls, "ss", 2, [128, 2], F32)
            xnr = ring(ls, "xn", 2, [128, D], BF16)
            hTr = ring(ls, "hT", 2, [128, 16, 512], BF16)
            wr = ring(ls, "wblk", 3, [128, 16, 512], BF16)
            stg = ring(ls, "stg", 4, [128, 512], BF16)
            stgf = ring(ls, "stgf", 2, [128, 512], F32)
            kcb = sbl(ls, "kcb", [128, 2, 2, 528], BF16); b_kcb = Buf("kcb")
            hid = sbl(ls, "hid", [128, 32], BF16); hidx = sbl(ls, "hidx", [128, 32], F32); hidt = sbl(ls, "hidt", [128, 32], F32)
            b_hid = Buf("hid"); b_hidx = Buf("hidx"); b_hidt = Buf("hidt")
            VcT = sbl(ls, "VcT", [64, 4, NCC * 128], BF16); b_VcT = Buf("VcT")
            kdec = sbl(ls, "kdec", [128, 8], F32); b_kdec = Buf("kdec")
            krd = sbl(ls, "krd", [128, 4, 1024], BF16); b_krd = Buf("krd")
            vrb = sbl(ls, "vrb", [128, 4, 1024], BF16); b_vrb = Buf("vrb")
            Rst = sbl(ls, "Rst", [128, 2, 8, 128], F32); b_Rst = Buf("Rst")
            Rown = sbl(ls, "Rown", [128, 8, 128], F32); Rownb = sbl(ls, "Rownb", [128, 8, 128], BF16); b_Rown = Buf("Rown")
            S.dma("sp", kdec[:], h_kdec[:, :], writes=[b_kdec])
            S.op("dve", lambda: nc.vector.memset(kcb[:], 0.0), writes=[b_kcb])
            S.op("dve", lambda: nc.vector.memset(Rst[:], 0.0), writes=[b_Rst])
            S.op("pool", lambda: nc.gpsimd.memset(KcT[:], 0.0), writes=[b_KcT])
            S.op("pool", lambda: nc.gpsimd.memset(VcT[:], 0.0), writes=[b_VcT])
            S.dma("sp", KcT[64:71, 0, :], h_caug[:, :], writes=[b_KcT])
            for g in range(1, 4):
                S.dma("sp", KcT[64:71, g, :], h_caug[:, :], writes=[b_KcT])

            def norm_T(xsrc, vi, hT, bh, col0):
                xt, bx = xr.next()
                S.dma("sp", xt[:], xsrc, writes=[bx])
                ss, bs = ssr.next()
                S.op("act", lambda: nc.scalar.activation(out=junk[:], in_=xt[:], func=AF.Square, accum_out=ss[:, 0:1]),
                     reads=[bx], writes=[b_junk, bs])
                rstd_from_ss(ss[:, 1:2], ss[:, 0:1], D, [bs], [bs])
                xn, bn = xnr.next()
                S.op("act", lambda: nc.scalar.activation(out=xn[:], in_=xt[:], func=AF.Copy, scale=ss[:, 1:2]), reads=[bx, bs], writes=[bn])
                for half in range(2):
                    pt, bp = psT.next()
                    for k8 in range(8):
                        kc = half * 8 + k8
                        S.op("pe", lambda: nc.tensor.transpose(out=pt[:, k8 * 128:(k8 + 1) * 128], in_=xn[:, kc * 128:(kc + 1) * 128],
                                                               identity=ident_b[:]), reads=[bn, b_ident], writes=[bp], inc=(k8 == 7))
                    dst = hT[:, half * 8:(half + 1) * 8, col0:col0 + 128]
                    S.op("dve", lambda: nc.vector.tensor_tensor(out=dst, in0=pt[:].rearrange("p (k t) -> p k t", k=8),
                                                                in1=modT[:, vi, half * 8:(half + 1) * 8].unsqueeze(2).to_broadcast([128, 8, 128]),
                                                                op=ALU.mult), reads=[bp, b_modT], writes=[bh])
                    S.op("dve", lambda: nc.vector.tensor_tensor(out=dst, in0=dst,
                                                                in1=modT[:, vi + 1, half * 8:(half + 1) * 8].unsqueeze(2).to_broadcast([128, 8, 128]),
                                                                op=ALU.add), reads=[bh, b_modT], writes=[bh])

            def load_w(c0, ncols):
                wt, bw = wr.next()
                S.dma("pool", wt[:, :, 0:ncols], w_in_v[:, :, c0:c0 + ncols], reads=[b_wscr], writes=[bw])
                return wt, bw

            def fm(wt, bw, col, M, hT, bh):
                pt, bp = psM.next()
                mmg(pt[0:M, :], [(wt[:, k, col:col + M], hT[:, k, :]) for k in range(16)], [bw, bh], bp)
                return pt, bp

            def tm(wt, bw, col, N, hT, bh, tile):
                pt, bp = psM.next()
                mmg(pt[:, 0:N], [(hT[:, k, tile * 128:(tile + 1) * 128], wt[:, k, col:col + N]) for k in range(16)], [bh, bw], bp)
                return pt, bp

            ev_rr = [0]

            def evac(out_ap, in_ap, rb, wb, scale=None):
                e = ("dve", "act")[ev_rr[0] % 2]
                ev_rr[0] += 1
                if e == "act":
                    if scale is None:
                        S.op("act", lambda: nc.scalar.copy(out=out_ap, in_=in_ap), reads=rb, writes=wb)
                    else:
                        S.op("act", lambda: nc.scalar.mul(out=out_ap, in_=in_ap, mul=scale), reads=rb, writes=wb)
                else:
                    if scale is None:
                        S.op("dve", lambda: nc.vector.tensor_copy(out=out_ap, in_=in_ap), reads=rb, writes=wb)
                    else:
                        S.op("dve", lambda: nc.vector.tensor_scalar(out=out_ap, in0=in_ap, scalar1=scale, scalar2=None, op0=ALU.mult),
                             reads=rb, writes=wb)

            for s in range(NS):
                hT, bh = hTr.next()
                for tt in range(4):
                    norm_T(x_all[(s * 4 + tt) * 128:(s * 4 + tt + 1) * 128, :], 0, hT, bh, tt * 128)
                wt, bw = load_w(C_KC, 512)
                S.op("dve", lambda: nc.vector.tensor_copy(out=kcb[:, :, :, 0:16], in_=kcb[:, :, :, 512:528]), reads=[b_kcb], writes=[b_kcb])
                for kv in range(2):
                    for gp in range(2):
                        pt, bp = fm(wt, bw, kv * 256 + gp * 128, 128, hT, bh)
                        evac(kcb[:, kv, gp, 16:528], pt[:, :], [bp], [b_kcb])
                i0 = 1 if s == 0 else 0
                ni = 32 - i0
                n0 = 32 * s - 1 + i0
                for kv in range(2):
                    for g in range(4):
                        base = (g % 2) * 64
                        pt, bp = psM.next()
                        mmg(pt[:, 0:ni], [(W1s[base:base + 64, kv, l, :],
                                            kcb[base:base + 64, kv, g // 2, 16 * i0 + l:16 * i0 + l + 16 * (ni - 1) + 1:16]) for l in range(32)],
                            [b_W1s, b_kcb], bp)
                        S.op("act", lambda: nc.scalar.activation(out=hidx[:, 0:ni], in_=pt[:, 0:ni], func=AF.Identity, bias=peb[:, kv:kv + 1]),
                             reads=[bp, b_peb], writes=[b_hidx])
                        gelu_tanh(hid[:, 0:ni], hidx[:, 0:ni], hidt[:, 0:ni], [b_hidx], [b_hid], b_hidt)
                        pt2, bp2 = psM.next()
                        mmg(pt2[0:64, 0:ni], [(W2s[:, kv, :], hid[:, 0:ni])], [b_W2, b_hid], bp2)
                        if kv == 0:
                            evac(KcT[0:64, g, n0:n0 + ni], pt2[0:64, 0:ni], [bp2], [b_KcT])
                        else:
                            evac(VcT[0:64, g, n0:n0 + ni], pt2[0:64, 0:ni], [bp2], [b_VcT])
                for (c0, kd, vd) in ((C_KS, kS_d, vS_d), (C_KW, kW_d, vW_d)):
                    wt, bw = load_w(c0, 512)
                    for g in range(4):
                        pt, bp = fm(wt, bw, g * 64, 64, hT, bh)
                        sg, bsg = stg.next()
                        evac(sg[0:64, :], pt[0:64, :], [bp], [bsg])
                        S.dma("sp", kd[g, :, s * 512:(s + 1) * 512], sg[0:64, :], reads=[bsg], writes=[b_scr], chan="scr")
                    for tt in range(4):
                        pt, bp = tm(wt, bw, 256, 256, hT, bh, tt)
                        sg, bsg = stg.next()
                        evac(sg[:, 0:256], pt[:, 0:256], [bp], [bsg])
                        S.dma("sp", vd[(s * 4 + tt) * 128:(s * 4 + tt + 1) * 128, :], sg[:, 0:256], reads=[bsg], writes=[b_scr], chan="scr")
                wk = [load_w(C_KR, 512), load_w(C_KR + 512, 512)]
                for tt in range(4):
                    for hh in range(2):
                        pt, bp = tm(wk[hh][0], wk[hh][1], 0, 512, hT, bh, tt)
                        S.op("dve", lambda: nc.vector.tensor_tensor(out=krd[:, tt, hh * 512:(hh + 1) * 512].rearrange("p (h d) -> p h d", h=4),
                                                                    in0=pt[:].rearrange("p (h d) -> p h d", h=4),
                                                                    in1=kdec[:, hh * 4:(hh + 1) * 4].unsqueeze(2).to_broadcast([128, 4, 128]),
                                                                    op=ALU.mult), reads=[bp, b_kdec], writes=[b_krd])
                wv = [load_w(C_VR, 512), load_w(C_VR + 512, 512)]
                for tt in range(4):
                    for hh in range(2):
                        pt, bp = tm(wv[hh][0], wv[hh][1], 0, 512, hT, bh, tt)
                        evac(vrb[:, tt, hh * 512:(hh + 1) * 512], pt[:, :], [bp], [b_vrb])
                for tt in range(4):
                    ci = s * 4 + tt
                    par = ci % 2
                    for hh in range(2):
                        pk, bpk = psK.next()
                        for h4 in range(4):
                            h = hh * 4 + h4
                            S.op("pe", lambda: nc.tensor.matmul(out=pk[:, h4 * 128:(h4 + 1) * 128], lhsT=krd[:, tt, h * 128:(h + 1) * 128],
                                                                rhs=vrb[:, tt, h * 128:(h + 1) * 128], start=True, stop=True),
                                 reads=[b_krd, b_vrb], writes=[bpk], inc=(h4 == 3))
                        for h4 in range(4):
                            h = hh * 4 + h4
                            S.op("dve", lambda: nc.vector.scalar_tensor_tensor(out=Rst[:, 1 - par, h, :], in0=Rst[:, par, h, :], scalar=float(cdec[h]),
                                                                               in1=pk[:, h4 * 128:(h4 + 1) * 128], op0=ALU.mult, op1=ALU.add),
                                 reads=[b_Rst, bpk], writes=[b_Rst])
                    if par == 1:
                        pass
                    if par == 0:
                        S.op("pool", lambda: nc.gpsimd.tensor_tensor(out=Rown[:], in0=Rst[:, 1, :, :], in1=Rst[:, 0, :, :], op=ALU.subtract),
                             reads=[b_Rst], writes=[b_Rown])
                        S.op("dve", lambda: nc.vector.scalar_tensor_tensor(out=Rownb[:], in0=Rown[:], scalar=jsel[:, 0:1], in1=Rst[:, 0, :, :],
                                                                            op0=ALU.mult, op1=ALU.add), reads=[b_Rown, b_Rst, b_jsel], writes=[b_Rown])
                        S.dma("sp", R_d[ci // 2], Rownb[:], reads=[b_Rown], writes=[b_scr], chan="scr")

            S.op("dve", lambda: nc.vector.memset(VcA[:, :, :, 64:65], 1.0), writes=[b_VcA])
            for g in range(4):
                S.dma("sp", VcA[:, :, g, 65:65 + NSLC], h_overlap[:, :, :], writes=[b_VcA])
            for cc in range(NCC):
                for g in range(4):
                    pt, bp = psT.next()
                    S.op("pe", lambda: nc.tensor.transpose(out=pt[:, 0:64], in_=VcT[0:64, g, cc * 128:(cc + 1) * 128], identity=ident_b[0:64, 0:64]),
                         reads=[b_VcT, b_ident], writes=[bp])
                    evac(VcA[:, cc, g, 0:64], pt[:, 0:64], [bp], [b_VcA])

            for so in range(NOS):
                hT, bh = hTr.next()
                for tt in range(4):
                    norm_T(x_own[(so * 4 + tt) * 128:(so * 4 + tt + 1) * 128, :], 0, hT, bh, tt * 128)
                t0, t1 = so * 512, (so + 1) * 512
                for blk in range(2):
                    wt, bw = load_w(C_QA + blk * 512, 512)
                    for h8 in range(8):
                        pt, bp = fm(wt, bw, h8 * 64, 64, hT, bh)
                        sg, bsg = stg.next()
                        evac(sg[0:64, :], pt[0:64, :], [bp], [bsg], scale=0.125)
                        S.dma("sp", qA_d[blk * 8 + h8, :, t0:t1], sg[0:64, :], reads=[bsg], writes=[b_scr], chan="scr")
                for (c0, dd) in ((C_QR, qR_d), (C_KR, kR_d)):
                    for blk in range(2):
                        wt, bw = load_w(c0 + blk * 512, 512)
                        for h4 in range(4):
                            pt, bp = fm(wt, bw, h4 * 128, 128, hT, bh)
                            sg, bsg = stg.next()
                            evac(sg[:, :], pt[:, :], [bp], [bsg])
                            S.dma("sp", dd[blk * 4 + h4, :, t0:t1], sg[:, :], reads=[bsg], writes=[b_scr], chan="scr")
                for blk in range(2):
                    wt, bw = load_w(C_VR + blk * 512, 512)
                    for tt in range(4):
                        pt, bp = tm(wt, bw, 0, 512, hT, bh, tt)
                        sg, bsg = stg.next()
                        evac(sg[:, :], pt[:, :], [bp], [bsg])
                        S.dma("sp", vR_d[t0 + tt * 128:t0 + (tt + 1) * 128, blk * 512:(blk + 1) * 512], sg[:, :], reads=[bsg], writes=[b_scr], chan="scr")
                for blk in range(2):
                    wt, bw = load_w(C_GR + blk * 512, 512)
                    for tt in range(4):
                        pt, bp = tm(wt, bw, 0, 512, hT, bh, tt)
                        sf, bsf = stgf.next()
                        evac(sf[:, 0:512], pt[:, :], [bp], [bsf])
                        S.dma("sp", gR_d[t0 + tt * 128:t0 + (tt + 1) * 128, blk * 512:(blk + 1) * 512], sf[:, 0:512], reads=[bsf], writes=[b_scr], chan="scr")
                wt, bw = load_w(C_GA, 48)
                for tt in range(4):
                    pt, bp = tm(wt, bw, 0, 48, hT, bh, tt)
                    sf, bsf = stgf.next()
                    evac(sf[:, 0:48], pt[:, 0:48], [bp], [bsf])
                    S.dma("sp", gA_d[t0 + tt * 128:t0 + (tt + 1) * 128, :], sf[:, 0:48], reads=[bsf], writes=[b_scr], chan="scr")
            S.barrier()


        b_omix = Buf("omix")
        NB1 = 65 + NSLC
        with scope() as ls:
            KsT = sbl(ls, "KsT", [71, 4, SEQ], BF16); b_KsT = Buf("KsT")
            VsA = sbl(ls, "VsA", [128, NT, 4, 65], BF16); b_VsA = Buf("VsA")
            Eall = sbl(ls, "Eall", [NSLC, NT, 128], BF16); b_E = Buf("Eall")
            cmab = sbl(ls, "cmab", [128, 2, 128], BF16); wmask = sbl(ls, "wmask", [128, 6, 128], BF16); b_cm = Buf("cm")
            maskrep = sbl(ls, "maskrep", [128, 8, 4, 128], BF16); b_mr = Buf("maskrep")
            g_nsaB = sbl(ls, "g_nsaB", [128, 1024], F32); b_gn = Buf("gnsa")
            QTr = ring(ls, "QT", 1, [71, 4, 512], BF16)
            KwTr = ring(ls, "KwT", 1, [71, 4, 768], BF16)
            VwAr = ring(ls, "VwA", 1, [128, 6, 4, 65], BF16)
            cm1r = ring(ls, "cm1", 2, [128, NCC, 128], BF16)
            cm1rep = sbl(ls, "cm1rep", [128, NCC, 4, 128], BF16); b_c1r = Buf("cm1rep")
            vmr = ring(ls, "vm", 1, [128, 2, NSLC], F32)
            gar = ring(ls, "ga", 2, [128, 48], F32)
            PTr = ring(ls, "PT", 3, [128, 512], BF16)
            nsT = sbl(ls, "nsT", [NSLC, 4, 128], BF16); b_nsT = Buf("nsT")
            imp = sbl(ls, "imp", [128, NSLC], F32); impr = sbl(ls, "impr", [128, NSLC], F32); b_imp = Buf("imp"); b_impr = Buf("impr")
            m8 = sbl(ls, "m8", [128, 16], F32); b_m8 = Buf("m8")
            selb = sbl(ls, "selb", [128, NSLC], F32); negs = sbl(ls, "negs", [128, NSLC], BF16); b_sel = Buf("sel")
            rden = sbl(ls, "rden", [128, 3, 4], F32); cf = sbl(ls, "cf", [128, 3, 4], F32); b_rden = Buf("rden")
            oacc = sbl(ls, "oacc", [128, 16, 64], F32); otmp = sbl(ls, "otmp", [128, 4, 64], F32); b_oacc = Buf("oacc"); b_otmp = Buf("otmp")
            osq = sbl(ls, "osq", [128, 16, 64], F32); oss = sbl(ls, "oss", [128, 16], F32); b_osq = Buf("osq")
            onb = ring(ls, "onb", 1, [128, 1024], BF16)
            psS = pring(ls, "psS", 3, [128, 512], F32)
            psO1 = psl(ls, "psO1", [128, 4, 256], F32); b_pO1 = Buf("pO1")
            psO2 = psl(ls, "psO2", [128, 512], F32); b_pO2 = Buf("pO2")
            psO3 = psl(ls, "psO3", [128, 512], F32); b_pO3 = Buf("pO3")
            psX = psl(ls, "psX", [128, 1024], BF16); b_pX = Buf("pX")
            assert NB1 <= 256
            for g in range(4):
                S.dma("sp", KsT[0:64, g, :], kS_d[g, :, :], reads=[b_scr], writes=[b_KsT])
                S.dma("sp", KsT[64:71, g, :], h_kaug[:, :], writes=[b_KsT])
            for c4 in range(0, NT, 4):
                for g in range(4):
                    S.dma("pool", VsA[:, c4:c4 + 4, g, 0:64], vS_d[c4 * 128:(c4 + 4) * 128, g * 64:(g + 1) * 64].rearrange("(c p) d -> p c d", p=128),
                          reads=[b_scr], writes=[b_VsA])
            S.op("dve", lambda: nc.vector.memset(VsA[:, :, :, 64:65], 1.0), writes=[b_VsA])
            S.dma("sp", Eall[:], h_eall[:, :, :], writes=[b_E])
            S.dma("sp", cmab[:], h_cmab[:, :, :], writes=[b_cm])
            S.dma("sp", wmask[:], h_wmask[:, :, :], writes=[b_cm])
            S.op("dve", lambda: nc.vector.tensor_copy(out=maskrep[:, 0:2, :, :], in_=cmab[:].unsqueeze(2).to_broadcast([128, 2, 4, 128])),
                 reads=[b_cm], writes=[b_mr])
            S.op("dve", lambda: nc.vector.tensor_copy(out=maskrep[:, 2:8, :, :], in_=wmask[:].unsqueeze(2).to_broadcast([128, 6, 4, 128])),
                 reads=[b_cm], writes=[b_mr])
            S.dma("sp", g_nsaB[:], g_nsa[0:1, :].partition_broadcast(128), writes=[b_gn])
            for (vt, bv) in zip(VwAr.tiles, VwAr.bufs):
                S.op("dve", lambda: nc.vector.memset(vt[:, :, :, 64:65], 1.0), writes=[bv])

            pend = []

            def flush_pv():
                while pend:
                    (PT, bP, rhsV, bV, psO, b_pO, width, first, last) = pend.pop(0)
                    for r in range(4):
                        st_ = first and (r == 0 or (width == NB1 and r == 2))
                        S.op("pe", lambda: nc.tensor.matmul(out=psO(r), lhsT=PT[:, r * 128:(r + 1) * 128], rhs=rhsV, start=st_, stop=last,
                                                            skip_group_check=True),
                             reads=[bP, bV], writes=[b_pO], inc=(r == 3))

            def attn_chunk(pairs, rb, rhsV, bV, psO, b_pO, width, first, last):
                pS, bS = psS.next()
                mmg(pS[:], pairs, rb, bS)
                PT, bP = PTr.next()
                S.op("act", lambda: nc.scalar.activation(out=PT[:], in_=pS[:], func=AF.Exp), reads=[bS], writes=[bP])
                flush_pv()
                pend.append((PT, bP, rhsV, bV, psO, b_pO, width, first, last))

            def finish_branch(bi, g, psO, b_pO, first_branch):
                flush_pv()
                for r in range(4):
                    S.op("dve", lambda: nc.vector.tensor_scalar(out=rden[:, bi, r:r + 1], in0=psO(r)[:, 64:65], scalar1=1e-30, scalar2=None, op0=ALU.max),
                         reads=[b_pO], writes=[b_rden])
                S.op("dve", lambda: nc.vector.reciprocal(out=rden[:, bi, :], in_=rden[:, bi, :]), reads=[b_rden], writes=[b_rden])
                S.op("dve", lambda: nc.vector.tensor_tensor(out=cf[:, bi, :], in0=rden[:, bi, :],
                                                            in1=ga[:, g * 12:(g + 1) * 12].rearrange("p (r b) -> p r b", b=3)[:, :, bi], op=ALU.mult),
                     reads=[b_rden, bga], writes=[b_rden])
                for r in range(4):
                    h = g * 4 + r
                    if first_branch:
                        S.op("dve", lambda: nc.vector.tensor_scalar(out=oacc[:, h, :], in0=psO(r)[:, 0:64], scalar1=cf[:, bi, r:r + 1], scalar2=None, op0=ALU.mult),
                             reads=[b_pO, b_rden], writes=[b_oacc])
                    else:
                        S.op("dve", lambda: nc.vector.scalar_tensor_tensor(out=oacc[:, h, :], in0=psO(r)[:, 0:64], scalar=cf[:, bi, r:r + 1], in1=oacc[:, h, :],
                                                                           op0=ALU.mult, op1=ALU.add), reads=[b_pO, b_rden, b_oacc], writes=[b_oacc])

            for m in range(NO):
                QT, bQ = QTr.next()
                for g in range(4):
                    S.dma("sp", QT[0:64, g, :].rearrange("d (r q) -> d r q", r=4),
                          qA_d[4 * g:4 * g + 4, :, m * 128:(m + 1) * 128].rearrange("r d q -> d r q"), reads=[b_scr], writes=[bQ])
                S.dma("sp", QT[64:71, :, :], h_qaug[m].rearrange("g a q -> a g q"), writes=[bQ])
                c0 = max(0, 2 * m - 4)
                u0 = c0 - (2 * m - 4)
                nch = 2 * m + 2 - c0
                KwT, bKw = KwTr.next()
                VwA, bVw = VwAr.next()
                for g in range(4):
                    S.dma("pool", KwT[0:64, g, u0 * 128:(u0 + nch) * 128], kW_d[g, :, c0 * 128:(c0 + nch) * 128], reads=[b_scr], writes=[bKw])
                    S.dma("pool", KwT[64:71, g, u0 * 128:(u0 + nch) * 128], h_kaug[:, c0 * 128:(c0 + nch) * 128], writes=[bKw])
                for g in range(4):
                    S.dma("pool", VwA[:, u0:u0 + nch, g, 0:64], vW_d[c0 * 128:(c0 + nch) * 128, g * 64:(g + 1) * 64].rearrange("(c p) d -> p c d", p=128),
                          reads=[b_scr], writes=[bVw])
                cm1, bc1 = cm1r.next()
                S.dma("sp", cm1[:], h_cm1[m], writes=[bc1])
                S.op("pool", lambda: nc.gpsimd.tensor_copy(out=cm1rep[:], in_=cm1[:].unsqueeze(2).to_broadcast([128, NCC, 4, 128])),
                     reads=[bc1], writes=[b_c1r])
                vm, bvm = vmr.next()
                S.dma("sp", vm[:, 0, :], h_vmask[m], writes=[bvm])
                S.dma("sp", vm[:, 1, :], h_amask[m], writes=[bvm])
                ga, bga = gar.next()
                S.dma("sp", ga[:], gA_d[m * 128:(m + 1) * 128, :], reads=[b_scr], writes=[bga])
                S.op("act", lambda: nc.scalar.activation(out=ga[:], in_=ga[:], func=AF.Sigmoid), reads=[bga], writes=[bga])
                for g in range(4):
                    for cc in range(NCC):
                        attn_chunk([(KcT[0:71, g, cc * 128:(cc + 1) * 128], QT[0:71, g, :]),
                                    (ident_b[:], cm1rep[:, cc, :, :].rearrange("p r q -> p (r q)"))],
                                   [b_KcT, bQ, b_ident, b_c1r], VcA[:, cc, g, :], b_VcA,
                                   lambda r: psO1[:, r, 0:NB1], b_pO1, NB1, cc == 0, cc == NCC - 1)
                    finish_branch(0, g, lambda r: psO1[:, r, 0:NB1], b_pO1, True)
                    S.op("dve", lambda: nc.vector.tensor_scalar(out=imp[:], in0=psO1[:, 0, 65:NB1], scalar1=rden[:, 0, 0:1], scalar2=None, op0=ALU.mult),
                         reads=[b_pO1, b_rden], writes=[b_imp])
                    for r in range(1, 4):
                        S.op("dve", lambda: nc.vector.scalar_tensor_tensor(out=imp[:], in0=psO1[:, r, 65:NB1], scalar=rden[:, 0, r:r + 1], in1=imp[:],
                                                                           op0=ALU.mult, op1=ALU.add), reads=[b_pO1, b_rden, b_imp], writes=[b_imp])
                    S.op("dve", lambda: nc.vector.tensor_tensor(out=imp[:], in0=imp[:], in1=vm[:, 0, :], op=ALU.mult), reads=[b_imp, bvm], writes=[b_imp])
                    S.op("dve", lambda: nc.vector.tensor_tensor(out=imp[:], in0=imp[:], in1=vm[:, 1, :], op=ALU.add), reads=[b_imp, bvm], writes=[b_imp])
                    S.op("dve", lambda: nc.vector.max(out=m8[:, 0:8], in_=imp[:]), reads=[b_imp], writes=[b_m8])
                    S.op("dve", lambda: nc.vector.match_replace(out=impr[:], in_to_replace=m8[:, 0:8], in_values=imp[:], imm_value=-1e30),
                         reads=[b_imp, b_m8], writes=[b_impr])
                    S.op("dve", lambda: nc.vector.max(out=m8[:, 8:16], in_=impr[:]), reads=[b_impr], writes=[b_m8])
                    S.op("dve", lambda: nc.vector.tensor_scalar(out=selb[:], in0=imp[:], scalar1=m8[:, 15:16], scalar2=1.0, op0=ALU.is_ge, op1=ALU.subtract),
                         reads=[b_imp, b_m8], writes=[b_sel])
                    S.op("dve", lambda: nc.vector.tensor_scalar(out=negs[:], in0=selb[:], scalar1=-NEG, scalar2=None, op0=ALU.mult), reads=[b_sel], writes=[b_sel])
                    S.op("pe", lambda: nc.tensor.transpose(out=psX[0:NSLC, 0:128], in_=negs[:], identity=ident_b[:]), reads=[b_sel, b_ident], writes=[b_pX])
                    S.op("dve", lambda: nc.vector.tensor_copy(out=nsT[:], in_=psX[0:NSLC, 0:128].unsqueeze(1).to_broadcast([NSLC, 4, 128])),
                         reads=[b_pX], writes=[b_nsT])
                    nc2 = 2 * m + 2
                    for c in range(nc2):
                        pairs = [(KsT[0:71, g, c * 128:(c + 1) * 128], QT[0:71, g, :]),
                                 (Eall[0:NSLC, c, :], nsT[:].rearrange("p r q -> p (r q)"))]
                        rb = [b_KsT, bQ, b_E, b_nsT]
                        if c >= 2 * m:
                            pairs.append((ident_b[:], maskrep[:, c - 2 * m, :, :].rearrange("p r q -> p (r q)")))
                            rb = rb + [b_ident, b_mr]
                        attn_chunk(pairs, rb, VsA[:, c, g, :], b_VsA, lambda r: psO2[:, r * 65:(r + 1) * 65], b_pO2, 65, c == 0, c == nc2 - 1)
                    finish_branch(1, g, lambda r: psO2[:, r * 65:(r + 1) * 65], b_pO2, False)
                    for u in range(u0, 6):
                        pairs = [(KwT[0:71, g, u * 128:(u + 1) * 128], QT[0:71, g, :]),
                                 (ident_b[:], maskrep[:, 2 + u, :, :].rearrange("p r q -> p (r q)"))]
                        attn_chunk(pairs, [bKw, bQ, b_ident, b_mr], VwA[:, u, g, :], bVw, lambda r: psO3[:, r * 65:(r + 1) * 65], b_pO3, 65, u == u0, u == 5)
                    finish_branch(2, g, lambda r: psO3[:, r * 65:(r + 1) * 65], b_pO3, False)
                S.op("dve", lambda: nc.vector.tensor_tensor(out=osq[:], in0=oacc[:], in1=oacc[:], op=ALU.mult), reads=[b_oacc], writes=[b_osq])
                S.op("dve", lambda: nc.vector.tensor_reduce(out=oss[:], in_=osq[:], axis=AX.X, op=ALU.add), reads=[b_osq], writes=[b_osq])
                rstd_from_ss(oss[:], oss[:], 64, [b_osq], [b_osq])
                S.op("dve", lambda: nc.vector.tensor_tensor(out=osq[:], in0=oacc[:], in1=oss[:].unsqueeze(2).to_broadcast([128, 16, 64]), op=ALU.mult),
                     reads=[b_oacc, b_osq], writes=[b_osq])
                on, bon = onb.next()
                S.op("dve", lambda: nc.vector.tensor_tensor(out=on[:], in0=osq[:].rearrange("p h d -> p (h d)"), in1=g_nsaB[:], op=ALU.mult),
                     reads=[b_osq, b_gn], writes=[bon])
                S.dma("sp", omix_d[m * 128:(m + 1) * 128, 0:1024], on[:], reads=[bon], writes=[b_omix], chan="omix")
            S.barrier()


        with scope() as ls:
            decT = sbl(ls, "decT", [128, 8, 128], F32); qdecB = sbl(ls, "qdecB", [128, 8, 128], F32); g_retB = sbl(ls, "g_retB", [128, 1024], F32)
            b_rc = Buf("rconst")
            S.dma("sp", decT[:], h_decT[:, :, :], writes=[b_rc]); S.dma("sp", qdecB[:], h_qdecB[:, :, :], writes=[b_rc])
            S.dma("sp", g_retB[:], g_ret[0:1, :].partition_broadcast(128), writes=[b_rc])
            qTr = ring(ls, "rqT", 2, [128, 8, 128], BF16); kTr = ring(ls, "rkT", 2, [128, 8, 128], BF16)
            vr_ = ring(ls, "rv", 2, [128, 1024], BF16); Rr = ring(ls, "rR", 2, [128, 8, 128], BF16); grr = ring(ls, "rg", 2, [128, 1024], F32)
            qd = sbl(ls, "rqd", [128, 8, 128], BF16); b_qd = Buf("qd")
            AT = sbl(ls, "rAT", [128, 8, 128], BF16); b_AT = Buf("AT")
            ro = sbl(ls, "ro", [128, 8, 128], F32); rsq = sbl(ls, "rsq", [128, 8, 128], F32); b_ro = Buf("ro"); b_rsq = Buf("rsq")
            rmu = sbl(ls, "rmu", [128, 2, 8], F32); b_rmu = Buf("rmu")
            orb = ring(ls, "orb", 2, [128, 1024], BF16)
            psA = pring(ls, "rpsA", 2, [128, 512], F32); psO = pring(ls, "rpsO", 2, [128, 512], F32)
            for m in range(NO):
                sl = slice(m * 128, (m + 1) * 128)
                qT, bq = qTr.next(); kT, bk = kTr.next(); v_, bv = vr_.next(); R_, bR = Rr.next(); gr, bg = grr.next()
                S.dma("sp", qT[:], qR_d[:, :, sl].rearrange("h d c -> d h c"), reads=[b_scr], writes=[bq])
                S.dma("sp", kT[:], kR_d[:, :, sl].rearrange("h d c -> d h c"), reads=[b_scr], writes=[bk])
                S.dma("pool", v_[:], vR_d[sl, :], reads=[b_scr], writes=[bv])
                S.dma("pool", R_[:], R_d[m], reads=[b_scr], writes=[bR])
                S.dma("pool", gr[:], gR_d[sl, :], reads=[b_scr], writes=[bg])
                S.op("pool", lambda: nc.gpsimd.tensor_tensor(out=qd[:], in0=qT[:], in1=qdecB[:], op=ALU.mult), reads=[bq, b_rc], writes=[b_qd])
                for hh in range(2):
                    pa, bpa = psA.next()
                    for h4 in range(4):
                        h = hh * 4 + h4
                        S.op("pe", lambda: nc.tensor.matmul(out=pa[:, h4 * 128:(h4 + 1) * 128], lhsT=kT[:, h, :], rhs=qT[:, h, :], start=True, stop=True),
                             reads=[bk, bq], writes=[bpa], inc=(h4 == 3))
                    S.op("dve", lambda: nc.vector.tensor_tensor(out=AT[:, hh * 4:(hh + 1) * 4, :], in0=pa[:].rearrange("p (h c) -> p h c", h=4),
                                                                in1=decT[:, hh * 4:(hh + 1) * 4, :], op=ALU.mult), reads=[bpa, b_rc], writes=[b_AT])
                for hh in range(2):
                    po, bpo = psO.next()
                    for h4 in range(4):
                        h = hh * 4 + h4
                        mmg(po[:, h4 * 128:(h4 + 1) * 128], [(AT[:, h, :], v_[:, h * 128:(h + 1) * 128]), (qd[:, h, :], R_[:, h, :])],
                            [b_AT, bv, b_qd, bR], bpo)
                    S.op("act", lambda: nc.scalar.copy(out=ro[:, hh * 4:(hh + 1) * 4, :], in_=po[:].rearrange("p (h c) -> p h c", h=4)), reads=[bpo], writes=[b_ro])
                S.op("dve", lambda: nc.vector.tensor_reduce(out=rmu[:, 0, :], in_=ro[:], axis=AX.X, op=ALU.add), reads=[b_ro], writes=[b_rmu])
                S.op("dve", lambda: nc.vector.tensor_scalar(out=rmu[:, 0, :], in0=rmu[:, 0, :], scalar1=-1.0 / 128, scalar2=None, op0=ALU.mult), reads=[b_rmu], writes=[b_rmu])
                S.op("dve", lambda: nc.vector.tensor_tensor(out=ro[:], in0=ro[:], in1=rmu[:, 0, :].unsqueeze(2).to_broadcast([128, 8, 128]), op=ALU.add),
                     reads=[b_ro, b_rmu], writes=[b_ro])
                S.op("dve", lambda: nc.vector.tensor_tensor(out=rsq[:], in0=ro[:], in1=ro[:], op=ALU.mult), reads=[b_ro], writes=[b_rsq])
                S.op("dve", lambda: nc.vector.tensor_reduce(out=rmu[:, 1, :], in_=rsq[:], axis=AX.X, op=ALU.add), reads=[b_rsq], writes=[b_rmu])
                rstd_from_ss(rmu[:, 1, :], rmu[:, 1, :], 128, [b_rmu], [b_rmu])
                S.op("dve", lambda: nc.vector.tensor_tensor(out=rsq[:], in0=ro[:], in1=rmu[:, 1, :].unsqueeze(2).to_broadcast([128, 8, 128]), op=ALU.mult),
                     reads=[b_ro, b_rmu], writes=[b_rsq])
                S.op("dve", lambda: nc.vector.tensor_tensor(out=rsq[:].rearrange("p h d -> p (h d)"), in0=rsq[:].rearrange("p h d -> p (h d)"), in1=g_retB[:], op=ALU.mult),
                     reads=[b_rsq, b_rc], writes=[b_rsq])
                S.op("act", lambda: nc.scalar.activation(out=gr[:], in_=gr[:], func=AF.Silu), reads=[bg], writes=[bg])
                ob, bob = orb.next()
                S.op("dve", lambda: nc.vector.tensor_tensor(out=ob[:], in0=rsq[:].rearrange("p h d -> p (h d)"), in1=gr[:], op=ALU.mult),
                     reads=[b_rsq, bg], writes=[bob])
                S.dma("sp", omix_d[sl, 1024:2048], ob[:], reads=[bob], writes=[b_omix], chan="omix")
            S.barrier()

        b_x1 = Buf("x1d")
        with scope() as ls:
            wo = sbl(ls, "wo", [128, 16, D], BF16); b_wo = Buf("wo")
            S.dma("sp", wo[:], w_out_bf.rearrange("(k p) n -> p k n", p=128), reads=[b_wscr], writes=[b_wo])
            omr = ring(ls, "om", 2, [128, D], BF16); oTr = ring(ls, "oT", 2, [128, 16, 128], BF16)
            xor_ = ring(ls, "xo", 2, [128, D], F32); x1r = ring(ls, "x1", 2, [128, D], F32)
            junk = sbl(ls, "junkD", [128, D], BF16); b_junk = Buf("junkD")
            ssr = ring(ls, "ssD", 2, [128, 2], F32); xnr = ring(ls, "xnD", 2, [128, D], BF16)
            h2r = ring(ls, "h2T", 2, [128, 16, 128], BF16)
            psT = pring(ls, "DpsT", 2, [128, 1024], BF16); psM = pring(ls, "DpsM", 4, [128, 512], F32)
            for m in range(NO):
                sl = slice(m * 128, (m + 1) * 128)
                om, bom = omr.next()
                S.dma("sp", om[:], omix_d[sl, :], reads=[b_omix], writes=[bom])
                oT, boT = oTr.next()
                for half in range(2):
                    pt, bp = psT.next()
                    for k8 in range(8):
                        kc = half * 8 + k8
                        S.op("pe", lambda: nc.tensor.transpose(out=pt[:, k8 * 128:(k8 + 1) * 128], in_=om[:, kc * 128:(kc + 1) * 128], identity=ident_b[:]),
                             reads=[bom, b_ident], writes=[bp], inc=(k8 == 7))
                    S.op("act", lambda: nc.scalar.copy(out=oT[:, half * 8:(half + 1) * 8, :], in_=pt[:].rearrange("p (k t) -> p k t", k=8)), reads=[bp], writes=[boT])
                xo, bxo = xor_.next()
                S.dma("pool", xo[:], x_own[sl, :], writes=[bxo])
                x1, bx1 = x1r.next()
                for n4 in range(4):
                    pt, bp = psM.next()
                    ns = slice(n4 * 512, (n4 + 1) * 512)
                    mmg(pt[:], [(oT[:, k, :], wo[:, k, ns]) for k in range(16)], [boT, b_wo], bp)
                    S.op("dve", lambda: nc.vector.tensor_tensor(out=x1[:, ns], in0=pt[:], in1=gate1B[:, ns], op=ALU.mult), reads=[bp, b_gate], writes=[bx1])
                    S.op("pool", lambda: nc.gpsimd.tensor_tensor(out=x1[:, ns], in0=x1[:, ns], in1=xo[:, ns], op=ALU.add), reads=[bx1, bxo], writes=[bx1])
                S.dma("sp", x1_d[sl, :], x1[:], reads=[bx1], writes=[b_x1], chan="x1d")
                ss, bs = ssr.next()
                S.op("act", lambda: nc.scalar.activation(out=junk[:], in_=x1[:], func=AF.Square, accum_out=ss[:, 0:1]), reads=[bx1], writes=[b_junk, bs])
                rstd_from_ss(ss[:, 1:2], ss[:, 0:1], D, [bs], [bs])
                xn, bn = xnr.next()
                S.op("act", lambda: nc.scalar.activation(out=xn[:], in_=x1[:], func=AF.Copy, scale=ss[:, 1:2]), reads=[bx1, bs], writes=[bn])
                h2T, bh2 = h2r.next()
                for half in range(2):
                    pt, bp = psT.next()
                    for k8 in range(8):
                        kc = half * 8 + k8
                        S.op("pe", lambda: nc.tensor.transpose(out=pt[:, k8 * 128:(k8 + 1) * 128], in_=xn[:, kc * 128:(kc + 1) * 128], identity=ident_b[:]),
                             reads=[bn, b_ident], writes=[bp], inc=(k8 == 7))
                    dst = h2T[:, half * 8:(half + 1) * 8, :]
                    S.op("dve", lambda: nc.vector.tensor_tensor(out=dst, in0=pt[:].rearrange("p (k t) -> p k t", k=8),
                                                                in1=modT[:, 2, half * 8:(half + 1) * 8].unsqueeze(2).to_broadcast([128, 8, 128]), op=ALU.mult),
                         reads=[bp, b_modT], writes=[bh2])
                    S.op("dve", lambda: nc.vector.tensor_tensor(out=dst, in0=dst,
                                                                in1=modT[:, 3, half * 8:(half + 1) * 8].unsqueeze(2).to_broadcast([128, 8, 128]), op=ALU.add),
                         reads=[bh2, b_modT], writes=[bh2])
                S.dma("sp", h2T_d[:, :, sl].rearrange("k p t -> p k t"), h2T[:], reads=[bh2], writes=[b_x1], chan="x1d")
            S.barrier()

        with scope() as ls:
            wq = sbl(ls, "wq", [128, 16, D], BF16); b_wo = Buf("wq")
            S.dma("pool", wq[:], w_q_bf.rearrange("(k p) n -> p k n", p=128), reads=[b_wscr], writes=[b_wo])
            h2r = ring(ls, "D2h2T", 2, [128, 16, 128], BF16)
            qTa = ring(ls, "qTa", 3, [128, 128], BF16)
            scr_ = ring(ls, "sc", 2, [128, 16, 128], F32); scx = sbl(ls, "scx", [128, 256], F32); b_scx = Buf("scx")
            v16 = sbl(ls, "v16", [128, 16, 16], F32); b_v16 = Buf("v16")
            cand = sbl(ls, "cand", [128, 8, 256], F32); b_cand = Buf("cand")
            t16 = sbl(ls, "t16", [128, 8, 16], F32); b_t16 = Buf("t16")
            tneg = sbl(ls, "tneg", [128, 8], F32); zs = sbl(ls, "zs", [128, 8], F32); ej = sbl(ls, "ej", [128, 16], F32); b_z = Buf("z")
            psQ = pring(ls, "DpsQ", 4, [128, 512], F32)
            for m in range(NO):
                sl = slice(m * 128, (m + 1) * 128)
                h2T, bh2 = h2r.next()
                S.dma("sp", h2T[:], h2T_d[:, :, sl].rearrange("k p t -> p k t"), reads=[b_x1], writes=[bh2])
                sc, bsc = scr_.next()
                for a in range(16):
                    pq, bpq = psQ.next()
                    mmg(pq[:, 0:128], [(wq[:, k, a * 128:(a + 1) * 128], h2T[:, k, :]) for k in range(16)], [b_wo, bh2], bpq)
                    qa, bqa = qTa.next()
                    S.op("act", lambda: nc.scalar.copy(out=qa[:], in_=pq[:, 0:128]), reads=[bpq], writes=[bqa])
                    pq2, bpq2 = psQ.next()
                    mmg(pq2[:, 0:128], [(qa[:], skT[:, a, :])], [bqa, b_skT], bpq2)
                    S.op("act", lambda: nc.scalar.copy(out=sc[:, a, :], in_=pq2[:, 0:128]), reads=[bpq2], writes=[bsc])
                    S.op("dve", lambda: nc.vector.max(out=v16[:, a, 0:8], in_=sc[:, a, :]), reads=[bsc], writes=[b_v16])
                    S.op("dve", lambda: nc.vector.match_replace(out=scx[:, 0:128], in_to_replace=v16[:, a, 0:8], in_values=sc[:, a, :], imm_value=-1e30),
                         reads=[bsc, b_v16], writes=[b_scx])
                    S.op("dve", lambda: nc.vector.max(out=v16[:, a, 8:16], in_=scx[:, 0:128]), reads=[b_scx], writes=[b_v16])
                S.dma("sp", sc_d[sl, :, :], sc[:], reads=[bsc], writes=[b_x1], chan="x1d")
                v16v = v16[:].rearrange("p (h t) k -> p h t k", t=2)
                S.op("dve", lambda: nc.vector.tensor_tensor(out=cand[:].rearrange("p h (i j) -> p h i j", i=16),
                                                            in0=v16v[:, :, 0, :].unsqueeze(3).to_broadcast([128, 8, 16, 16]),
                                                            in1=v16v[:, :, 1, :].unsqueeze(2).to_broadcast([128, 8, 16, 16]), op=ALU.add),
                     reads=[b_v16], writes=[b_cand])
                for h in range(8):
                    S.op("dve", lambda: nc.vector.max(out=t16[:, h, 0:8], in_=cand[:, h, :]), reads=[b_cand], writes=[b_t16])
                    S.op("dve", lambda: nc.vector.match_replace(out=scx[:], in_to_replace=t16[:, h, 0:8], in_values=cand[:, h, :], imm_value=-1e30),
                         reads=[b_cand, b_t16], writes=[b_scx])
                    S.op("dve", lambda: nc.vector.max(out=t16[:, h, 8:16], in_=scx[:]), reads=[b_scx], writes=[b_t16])
                S.op("dve", lambda: nc.vector.tensor_scalar(out=tneg[:], in0=t16[:, :, 15], scalar1=-1.0, scalar2=None, op0=ALU.mult), reads=[b_t16], writes=[b_z])
                for h in range(8):
                    S.op("act", lambda: nc.scalar.activation(out=ej[:], in_=t16[:, h, :], func=AF.Exp, bias=tneg[:, h:h + 1], accum_out=zs[:, h:h + 1]),
                         reads=[b_t16, b_z], writes=[b_z])
                S.op("act", lambda: nc.scalar.activation(out=zs[:], in_=zs[:], func=AF.Ln), reads=[b_z], writes=[b_z])
                S.op("dve", lambda: nc.vector.tensor_tensor(out=lnz_all[:, m, :], in0=zs[:], in1=t16[:, :, 15], op=ALU.add), reads=[b_z, b_t16], writes=[b_thr[m]])
                S.op("dve", lambda: nc.vector.tensor_scalar(out=thr_all[:, m, :], in0=t16[:, :, 15], scalar1=-1e-5, scalar2=None, op0=ALU.add),
                     reads=[b_t16], writes=[b_thr[m]])
            S.barrier()

        TT = 2
        with scope() as ls:
            g_finB = sbl(ls, "g_finB", [128, D], F32); b_gf = Buf("gfin")
            S.dma("sp", g_finB[:], g_fin[0:1, :].partition_broadcast(128), writes=[b_gf])
            h2T2 = sbl(ls, "Eh2T", [128, TT, 16, 128], BF16); b_h2 = [Buf("Eh2T%d" % t) for t in range(TT)]
            scr_ = ring(ls, "Esc", 1, [128, 16, 128], F32)
            e1_2 = sbl(ls, "e1", [128, TT, 8, 128], F32); e2_2 = sbl(ls, "e2", [128, TT, 8, 128], F32); tau2 = sbl(ls, "tau", [128, TT, 8], F32)
            b_e = [Buf("e%d" % t) for t in range(TT)]
            acc = sbl(ls, "acc", [128, TT, D], F32); b_acc = [Buf("acc%d" % t) for t in range(TT)]
            UTr = ring(ls, "UTb", 2, [128, 16, 512], BF16); Vr = ring(ls, "Vb", 2, [128, 4, D], BF16)
            Er = ring(ls, "E", 2, [128, 8, 512], F32)
            Gr = ring(ls, "Gb", 2, [128, 8, 512], BF16)
            agr = ring(ls, "ag", 2, [128, 512], F32)
            cTr = ring(ls, "coefT", 2, [128, 4, 128], BF16)
            junk = Gr.tiles[0][:].rearrange("p h e -> p (h e)"); b_junk = Gr.bufs[0]
            ssE = sbl(ls, "ssE", [128, 2], F32); b_ssE = Buf("ssE")
            psA = pring(ls, "EpsA", 2, [128, 512], F32); psG = pring(ls, "EpsG", 2, [128, 512], F32)
            psOut = psl(ls, "EpsO", [128, D], F32); b_pOut = Buf("EpsO")
            UT_v = UT_bf.rearrange("(k p) e -> p k e", p=128)
            for sup in range(NO // TT):
                for t in range(TT):
                    m = sup * TT + t
                    sl = slice(m * 128, (m + 1) * 128)
                    sc, bsc = scr_.next()
                    S.dma("sp", h2T2[:, t, :, :], h2T_d[:, :, sl].rearrange("k p t -> p k t"), reads=[b_x1], writes=[b_h2[t]])
                    S.dma("sp", sc[:], sc_d[sl, :, :], reads=[b_x1], writes=[bsc])
                    scv = sc[:].rearrange("p (h t) k -> p h t k", t=2)
                    S.op("dve", lambda: nc.vector.tensor_tensor(out=e1_2[:, t, :, :], in0=scv[:, :, 0, :],
                                                                in1=lnz_all[:, m, :].unsqueeze(2).to_broadcast([128, 8, 128]), op=ALU.subtract),
                         reads=[bsc, b_thr[m]], writes=[b_e[t]])
                    S.op("act", lambda: nc.scalar.activation(out=e1_2[:, t, :, :], in_=e1_2[:, t, :, :], func=AF.Exp), reads=[b_e[t]], writes=[b_e[t]])
                    S.op("act", lambda: nc.scalar.activation(out=e2_2[:, t, :, :], in_=scv[:, :, 1, :], func=AF.Exp), reads=[bsc], writes=[b_e[t]])
                    S.op("dve", lambda: nc.vector.tensor_tensor(out=tau2[:, t, :], in0=thr_all[:, m, :], in1=lnz_all[:, m, :], op=ALU.subtract),
                         reads=[b_thr[m]], writes=[b_e[t]])
                    S.op("act", lambda: nc.scalar.activation(out=tau2[:, t, :], in_=tau2[:, t, :], func=AF.Exp), reads=[b_e[t]], writes=[b_e[t]])

                wcur = [None]

                def stageA(eb, t):
                    if t == 0:
                        UTb, bU = UTr.next(); Vb, bV = Vr.next()
                        S.dma("sp", UTb[:], UT_v[:, :, eb * 512:(eb + 1) * 512], reads=[b_wscr], writes=[bU])
                        S.dma("pool", Vb[:], V_bf[eb * 512:(eb + 1) * 512, :].rearrange("(c p) n -> p c n", p=128), reads=[b_wscr], writes=[bV])
                        wcur[0] = (UTb, bU, Vb, bV)
                    UTb, bU, Vb, bV = wcur[0]
                    Et, bE = Er.next(); Gb, bG = Gr.next()
                    S.op("pool", lambda: nc.gpsimd.tensor_tensor(out=Et[:].rearrange("p h (a k) -> p h a k", a=4),
                                                                 in0=e1_2[:, t, :, eb * 4:(eb + 1) * 4].unsqueeze(3).to_broadcast([128, 8, 4, 128]),
                                                                 in1=e2_2[:, t, :, :].unsqueeze(2).to_broadcast([128, 8, 4, 128]), op=ALU.mult),
                         reads=[b_e[t]], writes=[bE])
                    pa, bpa = psA.next()
                    for c in range(4):
                        mmg(pa[:, c * 128:(c + 1) * 128], [(UTb[:, k, c * 128:(c + 1) * 128], h2T2[:, t, k, :]) for k in range(16)], [bU, b_h2[t]], bpa)
                    ag, bag = agr.next()
                    S.op("act", lambda: nc.scalar.activation(out=ag[:], in_=pa[:], func=AF.Gelu_apprx_tanh), reads=[bpa], writes=[bag])
                    for h in range(8):
                        S.op("dve", lambda: nc.vector.scalar_tensor_tensor(out=Gb[:, h, :], in0=Et[:, h, :], scalar=tau2[:, t, h:h + 1], in1=Et[:, h, :],
                                                                           op0=ALU.is_ge, op1=ALU.mult), reads=[bE, b_e[t]], writes=[bG])
                    return (eb, t, Vb, bV, Gb, bG, ag, bag)

                def stageB(st):
                    (eb, t, Vb, bV, Gb, bG, ag, bag) = st
                    pg, bpg = psG.next()
                    for c in range(4):
                        mmg(pg[:, c * 128:(c + 1) * 128], [(Gb[:, h, c * 128:(c + 1) * 128], ident_b[:]) for h in range(8)], [bG, b_ident], bpg)
                    cT, bcT = cTr.next()
                    S.op("dve", lambda: nc.vector.tensor_tensor(out=cT[:].rearrange("p c t -> p (c t)"), in0=pg[:], in1=ag[:], op=ALU.mult),
                         reads=[bpg, bag], writes=[bcT])
                    for n4 in range(4):
                        for c in range(4):
                            S.op("pe", lambda: nc.tensor.matmul(out=psOut[:, n4 * 512:(n4 + 1) * 512], lhsT=cT[:, c, :], rhs=Vb[:, c, n4 * 512:(n4 + 1) * 512],
                                                                start=(c == 0), stop=(c == 3)),
                                 reads=[bcT, bV], writes=[b_pOut], inc=(n4 == 3 and c == 3))
                    if eb == 0:
                        S.op("act", lambda: nc.scalar.copy(out=acc[:, t, :], in_=psOut[:]), reads=[b_pOut], writes=[b_acc[t]])
                    else:
                        S.op("dve", lambda: nc.vector.tensor_tensor(out=acc[:, t, :], in0=psOut[:], in1=acc[:, t, :], op=ALU.add),
                             reads=[b_pOut, b_acc[t]], writes=[b_acc[t]])

                prev = None
                for eb in range(32):
                    for t in range(TT):
                        cur = stageA(eb, t)
                        if prev is not None:
                            stageB(prev)
                        prev = cur
                stageB(prev)
                for t in range(TT):
                    m = sup * TT + t
                    sl = slice(m * 128, (m + 1) * 128)
                    x1, bx1 = scr_.tiles[0][:].rearrange("p a k -> p (a k)"), scr_.bufs[0]
                    S.dma("sp", x1[:, :], x1_d[sl, :], reads=[b_x1], writes=[bx1])
                    x2 = acc[:, t, :]
                    S.op("dve", lambda: nc.vector.tensor_tensor(out=x2, in0=x2, in1=gate2B[:], op=ALU.mult), reads=[b_acc[t], b_gate], writes=[b_acc[t]])
                    S.op("pool", lambda: nc.gpsimd.tensor_tensor(out=x2, in0=x2, in1=x1[:, :], op=ALU.add), reads=[b_acc[t], bx1], writes=[b_acc[t]])
                    S.op("act", lambda: nc.scalar.activation(out=junk[:, 0:D], in_=x2, func=AF.Square, accum_out=ssE[:, 0:1]), reads=[b_acc[t]], writes=[b_junk, b_ssE])
                    rstd_from_ss(ssE[:, 1:2], ssE[:, 0:1], D, [b_ssE], [b_ssE])
                    S.op("dve", lambda: nc.vector.scalar_tensor_tensor(out=x1[:, :], in0=x2, scalar=ssE[:, 1:2], in1=g_finB[:], op0=ALU.mult, op1=ALU.mult),
                         reads=[b_acc[t], b_ssE, b_gf], writes=[bx1])
                    S.dma("sp", y[sl, :], x1[:, :], reads=[bx1], writes=[b_y], chan="yout")
            S.barrier()

        if dbg:
            with scope() as ls:
                dt_ = sbl(ls, "dbgt", [128, 4096], F32); b_dt = Buf("dbgt")
                dtb = sbl(ls, "dbgtb", [128, 4096], BF16)
                if "KcT" in dbg_out:
                    S.op("dve", lambda: nc.vector.tensor_copy(out=dt_[0:71, 0:4 * NCC * 128], in_=KcT[:].rearrange("p a b -> p (a b)")), reads=[b_KcT], writes=[b_dt])
                    S.dma("sp", dbg_out["KcT"][:, :], dt_[0:71, 0:4 * NCC * 128], reads=[b_dt], writes=[Buf("o1")])
                if "qA" in dbg_out:
                    S.dma("sp", dtb[0:64, 0:SO], qA_d[5, :, :], writes=[b_dt])
                    S.op("dve", lambda: nc.vector.tensor_copy(out=dt_[0:64, 0:SO], in_=dtb[0:64, 0:SO]), reads=[b_dt], writes=[b_dt])
                    S.dma("sp", dbg_out["qA"][:, :], dt_[0:64, 0:SO], reads=[b_dt], writes=[Buf("o2")])
                if "R" in dbg_out:
                    S.dma("sp", dtb[:, 0:1024], R_d[NO - 1].rearrange("p a b -> p (a b)"), writes=[b_dt])
                    S.op("dve", lambda: nc.vector.tensor_copy(out=dt_[:, 0:1024], in_=dtb[:, 0:1024]), reads=[b_dt], writes=[b_dt])
                    S.dma("sp", dbg_out["R"][:, :], dt_[:, 0:1024], reads=[b_dt], writes=[Buf("o3")])
                S.barrier()

        if dbg and ("omix" in dbg_out or "x1" in dbg_out):
            with scope() as ls:
                tb = sbl(ls, "tapb", [128, D], BF16); tf = sbl(ls, "tapf", [128, D], F32); b_tap = Buf("tap")
                for m in range(NO):
                    sl = slice(m * 128, (m + 1) * 128)
                    if "omix" in dbg_out:
                        S.dma("sp", tb[:], omix_d[sl, :], writes=[b_tap])
                        S.op("dve", lambda: nc.vector.tensor_copy(out=tf[:], in_=tb[:]), reads=[b_tap], writes=[b_tap])
                        S.dma("sp", dbg_out["omix"][sl, :], tf[:], reads=[b_tap], writes=[b_tap])
                    if "x1" in dbg_out:
                        S.dma("sp", tf[:], x1_d[sl, :], writes=[b_tap])
                        S.dma("sp", dbg_out["x1"][sl, :], tf[:], reads=[b_tap], writes=[b_tap])
        S.barrier()
        S.finish_all()
        print("ninst", S.ninst, "nsem", S.nsem)
    return nc


def core_inputs(inp, b, j, SEQ):
    NT = SEQ // 128
    xb = np.ascontiguousarray(inp["x"][b])
    own = np.ascontiguousarray(xb.reshape(NT // 2, 2, 128, D)[:, j].reshape(SEQ // 2, D))
    m = {
        "x_all": xb, "x_own": own,
        "c_in": np.ascontiguousarray(inp["c"][b].reshape(16, 128).T),
        "w_ada": inp["w_ada"][0], "b_ada": inp["b_ada"][0].reshape(1, -1),
        "g_mix": inp["g_norm_mix"][0].reshape(1, -1), "g_ffn": inp["g_norm_ffn"][0].reshape(1, -1),
        "g_fin": inp["g_norm_final"].reshape(1, -1), "w_in": inp["w_in"][0],
        "pe_k": inp["cmp_pe_k"][0], "pe_v": inp["cmp_pe_v"][0],
        "ck_w1": inp["cmp_k_w1"][0], "ck_w2": inp["cmp_k_w2"][0], "cv_w1": inp["cmp_v_w1"][0], "cv_w2": inp["cmp_v_w2"][0],
        "g_nsa": inp["g_nsa_out"][0].reshape(1, -1), "g_ret": inp["g_ret_out"][0].reshape(1, -1),
        "w_out": inp["w_out"][0], "w_q": inp["peer_w_q"][0], "subk": inp["peer_sub_keys"][0].reshape(16, 128, 128),
        "U": inp["peer_u"][0], "V": inp["peer_v"][0],
    }
    cs = host_consts(SEQ, j)
    for k in ("ident", "kaug", "caug", "qaug", "cm1", "vmask", "amask", "overlap", "eall", "cmab", "wmask",
              "decT", "kdec", "qdecB", "jsel"):
        m[k] = np.ascontiguousarray(cs[k])
    return {k: np.ascontiguousarray(np.asarray(v)) for k, v in m.items()}


def kernel(**inputs):
    inp = {k: np.asarray(v) for k, v in inputs.items()}
    B, SEQ, _ = inp["x"].shape
    nc = build(SEQ)
    maps = [core_inputs(inp, b, j, SEQ) for b in range(B) for j in range(2)]
    res = run_bass_kernel_spmd(nc, maps, core_ids=list(range(2 * B)))
    out = np.zeros((B, SEQ, D), np.float32)
    NT = SEQ // 128
    for b in range(B):
        ov = out[b].reshape(NT // 2, 2, 128, D)
        for j in range(2):
            ov[:, j] = np.asarray(res.results[2 * b + j]["y"]).reshape(NT // 2, 128, D)
    return out
```

```python
import contextlib
import numpy as np
import ml_dtypes
import concourse.bass as bass
import concourse.mybir as mybir
from concourse.bass_utils import run_bass_kernel_spmd

F32 = mybir.dt.float32
BF16 = mybir.dt.bfloat16
ALU = mybir.AluOpType
AF = mybir.ActivationFunctionType
AX = mybir.AxisListType
EPOCH = 30000
D = 2048
IN_W = 6704
C_QA, C_KC, C_VC, C_KS, C_VS, C_KW, C_VW, C_GA, C_QR, C_KR, C_VR, C_GR = (
    0, 1024, 1280, 1536, 1792, 2048, 2304, 2560, 2608, 3632, 4656, 5680)
NEG = -30000.0
NEXP = 16384


class Buf:
    __slots__ = ("name", "w", "r")

    def __init__(self, name=""):
        self.name = name
        self.w = None
        self.r = {}


class Sync:
    def __init__(self, nc, stack):
        self.nc = nc
        self.stack = stack
        self.engines = {"pe": nc.tensor, "dve": nc.vector, "act": nc.scalar,
                        "pool": nc.gpsimd, "sp": nc.sync}
        self.sems = {}
        self.cnt = {}
        self.epoch = {e: 0 for e in self.engines}
        self.ecount = {e: 0 for e in self.engines}
        self.known = {e: {} for e in self.engines}
        self.nsem = 0
        self.ninst = {e: 0 for e in self.engines}

    def _sem(self, key):
        if key not in self.sems:
            self.sems[key] = self.stack.enter_context(self.nc.semaphore("s%d" % self.nsem))
            self.nsem += 1
            self.cnt[key] = 0
        return self.sems[key]

    def _ekey(self, e):
        return ("E", e, self.epoch[e])

    def _wait(self, e, deps, skip_same_pe=True):
        eng = self.engines[e]
        need = {}
        for d in deps:
            if d is None:
                continue
            k, v = d
            if e == "pe" and k[0] == "E" and k[1] == "pe" and skip_same_pe:
                continue
            if k[0] == "D":
                v = self.cnt[k]
            if self.known[e].get(k, 0) >= v:
                continue
            if need.get(k, 0) < v:
                need[k] = v
        for k, v in need.items():
            eng.wait_ge(self.sems[k], v)
            self.known[e][k] = v
            self.ninst[e] += 1

    def _deps(self, reads, writes):
        deps = []
        for b in reads:
            deps.append(b.w)
        for b in writes:
            deps.append(b.w)
            for k, v in b.r.items():
                deps.append((k, v))
        return deps

    def op(self, e, fn, reads=(), writes=(), inc=True):
        self._wait(e, self._deps(reads, writes))
        ins = fn()
        self.ninst[e] += 1
        if self.ecount[e] >= EPOCH and inc:
            self.epoch[e] += 1
            self.ecount[e] = 0
        k = self._ekey(e)
        sem = self._sem(k)
        if inc:
            ins.then_inc(sem, 1)
            self.cnt[k] += 1
            self.ecount[e] += 1
            v = self.cnt[k]
        else:
            v = self.cnt[k] + 1
        for b in reads:
            if b.r.get(k, 0) < v:
                b.r[k] = v
        for b in writes:
            b.w = (k, v)
            b.r = {}
        return ins

    def dma(self, q, out, in_, reads=(), writes=(), chan=None, **kw):
        self._wait(q, self._deps(reads, writes), skip_same_pe=False)
        k = ("D", chan if chan is not None else (writes[0].name if writes else "dma"))
        sem = self._sem(k)
        ins = self.engines[q].dma_start(out=out, in_=in_, **kw)
        ins.then_inc(sem, 16)
        self.ninst[q] += 1
        self.cnt[k] += 16
        v = self.cnt[k]
        for b in reads:
            if b.r.get(k, 0) < v:
                b.r[k] = v
        for b in writes:
            b.w = (k, v)
            b.r = {}
        return ins

    def finish_all(self):
        deps = [(k, v) for k, v in self.cnt.items() if v > 0]
        self._wait("sp", deps, skip_same_pe=False)

    def barrier(self):
        for e in self.engines:
            deps = [(k, v) for k, v in self.cnt.items() if v > 0]
            self._wait(e, deps, skip_same_pe=False)


class Ring:
    def __init__(self, tiles, name):
        self.tiles = tiles
        self.bufs = [Buf("%s%d" % (name, i)) for i in range(len(tiles))]
        self.i = -1

    def next(self):
        self.i = (self.i + 1) % len(self.tiles)
        return self.tiles[self.i], self.bufs[self.i]


def bf(x):
    return np.asarray(x).astype(ml_dtypes.bfloat16)


def split3(v):
    v = np.asarray(v, np.float64)
    a = bf(v)
    r = v - a.astype(np.float64)
    b = bf(r)
    r2 = r - b.astype(np.float64)
    c = bf(r2)
    return a, b, c


def host_consts(SEQ, j):
    NT = SEQ // 128
    NO = NT // 2
    NSLC = SEQ // 64
    NCMP = SEQ // 16
    NCC = max(1, NCMP // 128)
    n_cmp = (SEQ - 32) // 16 + 1
    cs = {}
    cs["ident"] = np.eye(128, dtype=np.float32)
    pos = np.arange(SEQ)
    cs["kaug"] = bf(np.stack([pos // 128, pos // 128, pos % 128, pos % 128,
                              np.ones(SEQ), np.ones(SEQ), np.ones(SEQ)]).astype(np.float32))
    cpos = np.arange(NCC * 128) * 16 + 31
    cs["caug"] = bf(np.stack([cpos // 128, cpos // 128, cpos % 128, cpos % 128,
                              np.ones_like(cpos), np.ones_like(cpos), np.ones_like(cpos)]).astype(np.float32))
    slopes = np.exp2(-8.0 * (np.arange(16, dtype=np.float64) + 1.0) / 16).astype(np.float32).astype(np.float64)
    qaug = np.zeros((NO, 4, 7, 4, 128), dtype=ml_dtypes.bfloat16)
    for m in range(NO):
        t = 128.0 * (2 * m + j) + np.arange(128)
        for g in range(4):
            for r in range(4):
                s = slopes[4 * g + r]
                shi = bf(s)
                slo = bf(s - float(shi))
                seff = float(shi) + float(slo)
                a, b, c = split3(seff * t)
                qaug[m, g, 0, r, :] = bf(128.0 * float(shi))
                qaug[m, g, 1, r, :] = bf(128.0 * float(slo))
                qaug[m, g, 2, r, :] = shi
                qaug[m, g, 3, r, :] = slo
                qaug[m, g, 4, r, :] = bf(-a.astype(np.float32))
                qaug[m, g, 5, r, :] = bf(-b.astype(np.float32))
                qaug[m, g, 6, r, :] = bf(-c.astype(np.float32))
    cs["qaug"] = qaug.reshape(NO, 4, 7, 512)
    n = np.arange(NCC * 128)
    cm1 = np.zeros((NO, 128, NCC, 128), np.float32)
    vm = np.zeros((NO, 128, NSLC), np.float32)
    am = np.zeros((NO, 128, NSLC), np.float32)
    mb = np.arange(NSLC)
    for m in range(NO):
        t = 128 * (2 * m + j) + np.arange(128)
        ok = (n[:, None] < n_cmp) & ((16 * n[:, None] + 31) <= t[None, :])
        cm1[m] = np.where(ok, 0.0, NEG).reshape(NCC, 128, 128).transpose(1, 0, 2)
        back = (t // 64)[:, None] - mb[None, :]
        valid = back >= 0
        forced = valid & ((mb[None, :] == 0) | (back < 2))
        vm[m] = (valid & ~forced).astype(np.float32)
        am[m] = np.where(forced, 1e4 + mb[None, :], np.where(valid, 0.0, -1.0))
    cs["cm1"] = bf(cm1)
    cs["vmask"] = vm
    cs["amask"] = am
    ov = ((n[:, None] < n_cmp) & (n[:, None] >= 4 * mb[None, :] - 1) & (n[:, None] <= 4 * mb[None, :] + 3))
    cs["overlap"] = bf(ov.astype(np.float32).reshape(NCC, 128, NSLC).transpose(1, 0, 2))
    E = np.zeros((NSLC, NT, 128), np.float32)
    for c in range(NT):
        E[2 * c, c, :64] = 1.0
        E[2 * c + 1, c, 64:] = 1.0
    cs["eall"] = bf(E)
    k = np.arange(128)[:, None]
    q = np.arange(128)[None, :]
    tri = np.where(k <= q, 0.0, NEG).astype(np.float32)
    full = np.full((128, 128), NEG, np.float32)
    zero = np.zeros((128, 128), np.float32)
    cs["cmab"] = bf(np.stack([tri, full] if j == 0 else [zero, tri], axis=1))
    wm = np.zeros((128, 6, 128), np.float32)
    for u in range(6):
        d = 128 * (j + 4 - u) + q - k
        wm[:, u, :] = np.where((d >= 0) & (d < 512), 0.0, NEG)
    cs["wmask"] = bf(wm)
    lg = np.log1p(-np.exp2(-5.0 - np.arange(8, dtype=np.float64)))
    p = np.arange(128, dtype=np.float64)
    dec = np.zeros((128, 8, 128), np.float32)
    for h in range(8):
        diff = p[None, :] - p[:, None]
        dec[:, h, :] = np.where(diff >= 0, np.exp(lg[h] * np.maximum(diff, 0.0)), 0.0) * (128.0 ** -0.5)
    cs["decT"] = dec
    cs["kdec"] = np.stack([np.exp(lg[h] * (127.0 - p)) * (128.0 ** -0.5) for h in range(8)], axis=1).astype(np.float32)
    qd = np.stack([np.exp(lg[h] * (p + 1.0)) for h in range(8)], axis=0).astype(np.float32)
    cs["qdecB"] = np.broadcast_to(qd[None], (128, 8, 128)).copy()
    cs["cdec"] = np.exp(lg * 128.0)
    cs["jsel"] = np.full((128, 1), float(j), np.float32)
    return cs


def build(SEQ, dbg=None):
    NT = SEQ // 128
    NO = NT // 2
    NS = SEQ // 512
    NOS = NO // 4
    NSLC = SEQ // 64
    NCMP = SEQ // 16
    NCC = max(1, NCMP // 128)
    SO = SEQ // 2
    cdec = host_consts(256, 0)["cdec"] if False else np.exp(np.log1p(-np.exp2(-5.0 - np.arange(8, dtype=np.float64))) * 128.0)
    nc = bass.Bass("TRN2", target_bir_lowering=False)

    def din(name, shape, dt=F32):
        return nc.dram_tensor(name, list(shape), dt, kind="ExternalInput").ap()

    def dscr(name, shape, dt):
        return nc.dram_tensor(name, list(shape), dt).ap()

    x_all = din("x_all", [SEQ, D]); x_own = din("x_own", [SO, D]); c_in = din("c_in", [128, 16])
    w_ada = din("w_ada", [D, 6 * D]); b_ada = din("b_ada", [1, 6 * D])
    g_mix = din("g_mix", [1, D]); g_ffn = din("g_ffn", [1, D]); g_fin = din("g_fin", [1, D])
    w_in = din("w_in", [D, IN_W])
    pe_k = din("pe_k", [32, 64]); pe_v = din("pe_v", [32, 64])
    ck_w1 = din("ck_w1", [2048, 128]); ck_w2 = din("ck_w2", [128, 64])
    cv_w1 = din("cv_w1", [2048, 128]); cv_w2 = din("cv_w2", [128, 64])
    g_nsa = din("g_nsa", [1, 1024]); g_ret = din("g_ret", [1, 1024])
    w_out = din("w_out", [D, D]); w_q = din("w_q", [D, D]); subk = din("subk", [16, 128, 128])
    U = din("U", [NEXP, D]); V = din("V", [NEXP, D])
    h_ident = din("ident", [128, 128]); h_kaug = din("kaug", [7, SEQ], BF16); h_caug = din("caug", [7, NCC * 128], BF16)
    h_qaug = din("qaug", [NO, 4, 7, 512], BF16); h_cm1 = din("cm1", [NO, 128, NCC, 128], BF16)
    h_vmask = din("vmask", [NO, 128, NSLC]); h_amask = din("amask", [NO, 128, NSLC])
    h_overlap = din("overlap", [128, NCC, NSLC], BF16); h_eall = din("eall", [NSLC, NT, 128], BF16)
    h_cmab = din("cmab", [128, 2, 128], BF16); h_wmask = din("wmask", [128, 6, 128], BF16)
    h_decT = din("decT", [128, 8, 128]); h_kdec = din("kdec", [128, 8]); h_qdecB = din("qdecB", [128, 8, 128])
    h_jsel = din("jsel", [128, 1])
    y = nc.dram_tensor("y", [SO, D], F32, kind="ExternalOutput").ap()
    dbg_out = {}
    if dbg:
        for nm, shp in dbg.items():
            dbg_out[nm] = nc.dram_tensor("dbg_" + nm, list(shp), F32, kind="ExternalOutput").ap()

    w_in_bf = dscr("w_in_bf", [D, IN_W], BF16); w_out_bf = dscr("w_out_bf", [D, D], BF16); w_q_bf = dscr("w_q_bf", [D, D], BF16)
    UT_bf = dscr("UT_bf", [D, NEXP], BF16); V_bf = dscr("V_bf", [NEXP, D], BF16)
    kS_d = dscr("kS_d", [4, 64, SEQ], BF16); kW_d = dscr("kW_d", [4, 64, SEQ], BF16)
    vS_d = dscr("vS_d", [SEQ, 256], BF16); vW_d = dscr("vW_d", [SEQ, 256], BF16)
    qA_d = dscr("qA_d", [16, 64, SO], BF16); gA_d = dscr("gA_d", [SO, 48], F32)
    qR_d = dscr("qR_d", [8, 128, SO], BF16); kR_d = dscr("kR_d", [8, 128, SO], BF16)
    vR_d = dscr("vR_d", [SO, 1024], BF16); gR_d = dscr("gR_d", [SO, 1024], F32)
    R_d = dscr("R_d", [NO, 128, 8, 128], BF16)
    omix_d = dscr("omix_d", [SO, D], BF16)
    x1_d = dscr("x1_d", [SO, D], F32); h2T_d = dscr("h2T_d", [16, 128, SO], BF16)
    sc_d = dscr("sc_d", [SO, 16, 128], F32)

    with contextlib.ExitStack() as st:
        S = Sync(nc, st)

        def sbp(name, shape, dt):
            return st.enter_context(nc.sbuf_tensor("t_" + name, list(shape), dt))

        ident_f = sbp("ident_f", [128, 128], F32); b_ident = Buf("ident")
        ident_b = sbp("ident_b", [128, 128], BF16)
        gate1B = sbp("gate1B", [128, D], F32); gate2B = sbp("gate2B", [128, D], F32); b_gate = Buf("gate")
        modT = sbp("modT", [128, 4, 16], F32); b_modT = Buf("modT")
        KcT = sbp("KcT", [71, 4, NCC * 128], BF16); b_KcT = Buf("KcT")
        VcA = sbp("VcA", [128, NCC, 4, 65 + NSLC], BF16); b_VcA = Buf("VcA")
        skT = sbp("skT", [128, 16, 128], BF16); b_skT = Buf("skT")
        jsel = sbp("jsel", [128, 1], F32); b_jsel = Buf("jsel")
        thr_all = sbp("thr_all", [128, NO, 8], F32)
        lnz_all = sbp("lnz_all", [128, NO, 8], F32)
        b_thr = [Buf("thr%d" % i) for i in range(NO)]

        S.dma("sp", ident_f[:], h_ident[:, :], writes=[b_ident])
        S.op("dve", lambda: nc.vector.tensor_copy(out=ident_b[:], in_=ident_f[:]), reads=[b_ident], writes=[b_ident])
        S.dma("sp", jsel[:], h_jsel[:, :], writes=[b_jsel])

        def scope():
            return contextlib.ExitStack()

        def sbl(ls, name, shape, dt):
            return ls.enter_context(nc.sbuf_tensor("t_" + name, list(shape), dt))

        def psl(ls, name, shape, dt):
            return ls.enter_context(nc.psum_tensor("p_" + name, list(shape), dt))

        def ring(ls, name, n, shape, dt):
            return Ring([sbl(ls, "%s%d" % (name, i), shape, dt) for i in range(n)], name)

        def pring(ls, name, n, shape, dt):
            return Ring([psl(ls, "%s%d" % (name, i), shape, dt) for i in range(n)], name)

        def mmg(out_ap, pairs, rbufs, wbuf):
            n = len(pairs)
            for i, (l, r) in enumerate(pairs):
                S.op("pe", lambda: nc.tensor.matmul(out=out_ap, lhsT=l, rhs=r, start=(i == 0), stop=(i == n - 1)),
                     reads=rbufs, writes=[wbuf], inc=(i == n - 1))

        cast_rr = [0]

        def cast(out_ap, in_ap, rb, wb, psum=False):
            e = ("dve", "act")[cast_rr[0] % 2] if psum else ("dve", "pool", "act")[cast_rr[0] % 3]
            cast_rr[0] += 1
            if e == "act":
                S.op("act", lambda: nc.scalar.copy(out=out_ap, in_=in_ap), reads=rb, writes=wb)
            elif e == "dve":
                S.op("dve", lambda: nc.vector.tensor_copy(out=out_ap, in_=in_ap), reads=rb, writes=wb)
            else:
                S.op("pool", lambda: nc.gpsimd.tensor_copy(out=out_ap, in_=in_ap), reads=rb, writes=wb)

        def rstd_from_ss(rstd_ap, ss_ap, n, rb, wb):
            S.op("dve", lambda: nc.vector.tensor_scalar(out=rstd_ap, in0=ss_ap, scalar1=1.0 / n, scalar2=1e-6,
                                                        op0=ALU.mult, op1=ALU.add), reads=rb, writes=wb)
            S.op("act", lambda: nc.scalar.activation(out=rstd_ap, in_=rstd_ap, func=AF.Sqrt), reads=wb, writes=wb)
            S.op("dve", lambda: nc.vector.reciprocal(out=rstd_ap, in_=rstd_ap), reads=wb, writes=wb)

        with scope() as ls:
            cin_t = sbl(ls, "cin_t", [128, 16], F32); b_cin = Buf("cin")
            cb = sbl(ls, "cb", [128, 16, 128], F32); b_cb = Buf("cb")
            modB = sbl(ls, "modB", [128, 6, D], F32); b_modB = [Buf("modB%d" % v) for v in range(6)]
            wr = ring(ls, "wada", 3, [128, D], F32)
            btile = sbl(ls, "btile", [128, D], F32); b_bt = Buf("bt")
            gB = sbl(ls, "gB", [128, 2, D], F32); b_gB = Buf("gB")
            vecB = sbl(ls, "vecB", [128, 4, D], F32); b_vecB = Buf("vecB")
            psA = pring(ls, "psA", 8, [128, 512], F32)
            S.dma("sp", cin_t[:], c_in[:, :], writes=[b_cin])
            S.op("act", lambda: nc.scalar.activation(out=cin_t[:], in_=cin_t[:], func=AF.Silu), reads=[b_cin], writes=[b_cin])
            S.op("dve", lambda: nc.vector.tensor_copy(out=cb[:], in_=cin_t[:].unsqueeze(2).to_broadcast([128, 16, 128])),
                 reads=[b_cin], writes=[b_cb])
            for v in range(6):
                banks = [psA.next() for _ in range(4)]
                for k in range(16):
                    wt, bw = wr.next()
                    S.dma("sp" if k % 2 == 0 else "pool", wt[:], w_ada[k * 128:(k + 1) * 128, v * D:(v + 1) * D], writes=[bw])
                    for n4 in range(4):
                        pt, bp = banks[n4]
                        S.op("pe", lambda: nc.tensor.matmul(out=pt[:], lhsT=cb[:, k, :], rhs=wt[:, n4 * 512:(n4 + 1) * 512],
                                                            start=(k == 0), stop=(k == 15)),
                             reads=[b_cb, bw], writes=[bp], inc=(k == 15 or n4 == 3))
                S.dma("sp", btile[:], b_ada[0:1, v * D:(v + 1) * D].partition_broadcast(128), writes=[b_bt])
                for n4 in range(4):
                    pt, bp = banks[n4]
                    S.op("dve", lambda: nc.vector.tensor_tensor(out=modB[:, v, n4 * 512:(n4 + 1) * 512], in0=pt[:],
                                                                in1=btile[:, n4 * 512:(n4 + 1) * 512], op=ALU.add),
                         reads=[bp, b_bt], writes=[b_modB[v]])
            S.dma("sp", gB[:, 0, :], g_mix[0:1, :].partition_broadcast(128), writes=[b_gB])
            S.dma("sp", gB[:, 1, :], g_ffn[0:1, :].partition_broadcast(128), writes=[b_gB])
            S.op("dve", lambda: nc.vector.scalar_tensor_tensor(out=vecB[:, 0, :], in0=modB[:, 1, :], scalar=1.0, in1=gB[:, 0, :],
                                                               op0=ALU.add, op1=ALU.mult), reads=[b_modB[1], b_gB], writes=[b_vecB])
            S.op("dve", lambda: nc.vector.tensor_copy(out=vecB[:, 1, :], in_=modB[:, 0, :]), reads=[b_modB[0]], writes=[b_vecB])
            S.op("dve", lambda: nc.vector.scalar_tensor_tensor(out=vecB[:, 2, :], in0=modB[:, 4, :], scalar=1.0, in1=gB[:, 1, :],
                                                               op0=ALU.add, op1=ALU.mult), reads=[b_modB[4], b_gB], writes=[b_vecB])
            S.op("dve", lambda: nc.vector.tensor_copy(out=vecB[:, 3, :], in_=modB[:, 3, :]), reads=[b_modB[3]], writes=[b_vecB])
            S.op("pool", lambda: nc.gpsimd.tensor_copy(out=gate1B[:], in_=modB[:, 2, :]), reads=[b_modB[2]], writes=[b_gate])
            S.op("pool", lambda: nc.gpsimd.tensor_copy(out=gate2B[:], in_=modB[:, 5, :]), reads=[b_modB[5]], writes=[b_gate])
            for vi in range(4):
                for kc in range(16):
                    pt, bp = psA.next()
                    S.op("pe", lambda: nc.tensor.transpose(out=pt[:, 0:128], in_=vecB[:, vi, kc * 128:(kc + 1) * 128], identity=ident_f[:]),
                         reads=[b_vecB, b_ident], writes=[bp])
                    S.op("dve", lambda: nc.vector.tensor_copy(out=modT[:, vi, kc:kc + 1], in_=pt[:, 0:1]), reads=[bp], writes=[b_modT])
            S.barrier()

        b_wscr = Buf("wscr")
        with scope() as ls:
            fr = ring(ls, "cf", 2, [128, IN_W], F32)
            br = ring(ls, "cbf", 2, [128, IN_W], BF16)
            psT = pring(ls, "psTa", 2, [128, 1024], BF16)
            ust = ring(ls, "ust", 2, [128, 16, 512], BF16)
            for k in range(16):
                ft, bfb = fr.next(); bt_, bbb = br.next()
                S.dma("sp", ft[:], w_in[k * 128:(k + 1) * 128, :], writes=[bfb])
                cast(bt_[:], ft[:], [bfb], [bbb])
                S.dma("pool", w_in_bf[k * 128:(k + 1) * 128, :], bt_[:], reads=[bbb], writes=[b_wscr], chan="wscr")
            for (src, dst) in ((w_out, w_out_bf), (w_q, w_q_bf)):
                for k in range(0, 16, 2):
                    ft, bfb = fr.next(); bt_, bbb = br.next()
                    S.dma("sp", ft[:, 0:2 * D].rearrange("p (a n) -> p a n", a=2),
                          src[k * 128:(k + 2) * 128, :].rearrange("(a p) n -> p a n", p=128), writes=[bfb])
                    cast(bt_[:, 0:2 * D], ft[:, 0:2 * D], [bfb], [bbb])
                    S.dma("pool", dst[k * 128:(k + 2) * 128, :].rearrange("(a p) n -> p a n", p=128),
                          bt_[:, 0:2 * D].rearrange("p (a n) -> p a n", a=2), reads=[bbb], writes=[b_wscr], chan="wscr")
            for e0 in range(0, 128, 2):
                ft, bfb = fr.next(); bt_, bbb = br.next()
                S.dma("sp", ft[:, 0:2 * D].rearrange("p (a n) -> p a n", a=2),
                      V[e0 * 128:(e0 + 2) * 128, :].rearrange("(a p) n -> p a n", p=128), writes=[bfb])
                cast(bt_[:, 0:2 * D], ft[:, 0:2 * D], [bfb], [bbb])
                S.dma("pool", V_bf[e0 * 128:(e0 + 2) * 128, :].rearrange("(a p) n -> p a n", p=128),
                      bt_[:, 0:2 * D].rearrange("p (a n) -> p a n", a=2), reads=[bbb], writes=[b_wscr], chan="wscr")
            for e4 in range(32):
                us, bus = ust.next()
                for a in range(0, 4, 2):
                    ft, bfb = fr.next(); bt_, bbb = br.next()
                    e0 = e4 * 4 + a
                    S.dma("sp", ft[:, 0:2 * D].rearrange("p (a n) -> p a n", a=2),
                          U[e0 * 128:(e0 + 2) * 128, :].rearrange("(a p) n -> p a n", p=128), writes=[bfb])
                    cast(bt_[:, 0:2 * D], ft[:, 0:2 * D], [bfb], [bbb])
                    for a2 in range(2):
                        for half in range(2):
                            pt, bp = psT.next()
                            for k8 in range(8):
                                kc = half * 8 + k8
                                S.op("pe", lambda: nc.tensor.transpose(out=pt[:, k8 * 128:(k8 + 1) * 128],
                                                                       in_=bt_[:, a2 * D + kc * 128:a2 * D + (kc + 1) * 128], identity=ident_b[:]),
                                     reads=[bbb, b_ident], writes=[bp], inc=(k8 == 7))
                            ee = (a + a2) * 128
                            cast(us[:, half * 8:(half + 1) * 8, ee:ee + 128], pt[:].rearrange("p (k e) -> p k e", k=8), [bp], [bus], psum=True)
                S.dma("pool", UT_bf.rearrange("(k p) e -> p k e", p=128)[:, :, e4 * 512:(e4 + 1) * 512], us[:],
                      reads=[bus], writes=[b_wscr], chan="wscr")
            ft, bfb = fr.next(); bt_, bbb = br.next()
            S.dma("sp", ft[:, 0:2048].rearrange("p (a d) -> p a d", a=16), subk.rearrange("a k d -> k a d"), writes=[bfb])
            cast(bt_[:, 0:2048], ft[:, 0:2048], [bfb], [bbb])
            for half in range(2):
                pt, bp = psT.next()
                for k8 in range(8):
                    a = half * 8 + k8
                    S.op("pe", lambda: nc.tensor.transpose(out=pt[:, k8 * 128:(k8 + 1) * 128], in_=bt_[:, a * 128:(a + 1) * 128], identity=ident_b[:]),
                         reads=[bbb, b_ident], writes=[bp], inc=(k8 == 7))
                cast(skT[:, half * 8:(half + 1) * 8, :], pt[:].rearrange("p (k e) -> p k e", k=8), [bp], [b_skT], psum=True)
            S.barrier()

        if dbg and "modT" in dbg_out:
            S.dma("sp", dbg_out["modT"][:, :], modT[:].rearrange("p a b -> p (a b)"), reads=[b_modT], writes=[Buf("dbgo")])


        b_y = Buf("y")
        b_scr = Buf("scr")
        gelu_c = 1.5957691216057308

        def gelu_tanh(out_ap, x_ap, tmp_ap, rb, wb, tb, shape_bias=None):
            S.op("dve", lambda: nc.vector.tensor_tensor(out=tmp_ap, in0=x_ap, in1=x_ap, op=ALU.mult), reads=rb, writes=[tb])
            S.op("dve", lambda: nc.vector.tensor_scalar(out=tmp_ap, in0=tmp_ap, scalar1=0.044715, scalar2=1.0, op0=ALU.mult, op1=ALU.add),
                 reads=[tb], writes=[tb])
            S.op("dve", lambda: nc.vector.tensor_tensor(out=tmp_ap, in0=tmp_ap, in1=x_ap, op=ALU.mult), reads=rb + [tb], writes=[tb])
            S.op("act", lambda: nc.scalar.activation(out=tmp_ap, in_=tmp_ap, func=AF.Sigmoid, scale=gelu_c), reads=[tb], writes=[tb])
            S.op("dve", lambda: nc.vector.tensor_tensor(out=out_ap, in0=tmp_ap, in1=x_ap, op=ALU.mult), reads=rb + [tb], writes=wb)

        w_in_v = w_in_bf.rearrange("(k p) n -> p k n", p=128)

        with scope() as ls:
            psT = pring(ls, "psT", 2, [128, 1024], BF16)
            psM = pring(ls, "psM", 4, [128, 512], F32)
            psK = pring(ls, "psK", 2, [128, 512], F32)
            W1s = sbl(ls, "W1s", [128, 2, 32, 128], BF16); b_W1s = Buf("W1s")
            W2s = sbl(ls, "W2s", [128, 2, 64], BF16); b_W2 = Buf("W2")
            peTb = sbl(ls, "peTb", [64, 2, 32], BF16); b_pe = Buf("pe")
            peb = sbl(ls, "peb", [128, 2], F32); b_peb = Buf("peb")
            with scope() as ls2:
                W1f = sbl(ls2, "W1f", [128, 32, 128], F32); b_W1f = Buf("W1f")
                W2f = sbl(ls2, "W2f", [128, 2, 64], F32)
                peT = sbl(ls2, "peT", [64, 2, 32], F32)
                for kv, (w1, w2, pe) in enumerate(((ck_w1, ck_w2, pe_k), (cv_w1, cv_w2, pe_v))):
                    for half in range(2):
                        S.dma("sp", W1f[half * 64:(half + 1) * 64, :, :], w1.rearrange("(l d) h -> d l h", d=64), writes=[b_W1f])
                    S.op("dve", lambda: nc.vector.tensor_copy(out=W1s[:, kv, :, :], in_=W1f[:]), reads=[b_W1f], writes=[b_W1s])
                    S.dma("sp", W2f[:, kv, :], w2[:, :], writes=[b_W2])
                    S.dma("sp", peT[:, kv, :], pe.rearrange("l d -> d l"), writes=[b_pe], allow_slow_non_contiguous=True)
                S.op("dve", lambda: nc.vector.tensor_copy(out=W2s[:], in_=W2f[:]), reads=[b_W2], writes=[b_W2])
                S.op("dve", lambda: nc.vector.tensor_copy(out=peTb[:], in_=peT[:]), reads=[b_pe], writes=[b_pe])
                for kv in range(2):
                    pt, bp = psM.next()
                    mmg(pt[:, 0:1], [(W1s[0:64, kv, l, :], peTb[:, kv, l:l + 1]) for l in range(32)], [b_W1s, b_pe], bp)
                    S.op("dve", lambda: nc.vector.tensor_copy(out=peb[:, kv:kv + 1], in_=pt[:, 0:1]), reads=[bp], writes=[b_peb])
                S.barrier()
            xr = ring(ls, "xt", 2, [128, D], F32)
            junk = sbl(ls, "junk", [128, D], BF16); b_junk = Buf("junk")
            ssr = ring(ls, "ss", 2, [128, 2], F32)
            xnr = ring(ls, "xn", 2, [128, D], BF16)
            hTr = ring(ls, "hT", 2, [128, 16, 512], BF16)
            wr = ring(ls, "wblk", 3, [128, 16, 512], BF16)
            stg = ring(ls, "stg", 4, [128, 512], BF16)
            stgf = ring(ls, "stgf", 2, [128, 512], F32)
            kcb = sbl(ls, "kcb", [128, 2, 2, 528], BF16); b_kcb = Buf("kcb")
            hid = sbl(ls, "hid", [128, 32], BF16); hidx = sbl(ls, "hidx", [128, 32], F32); hidt = sbl(ls, "hidt", [128, 32], F32)
            b_hid = Buf("hid"); b_hidx = Buf("hidx"); b_hidt = Buf("hidt")
            VcT = sbl(ls, "VcT", [64, 4, NCC * 128], BF16); b_VcT = Buf("VcT")
            kdec = sbl(ls, "kdec", [128, 8], F32); b_kdec = Buf("kdec")
            krd = sbl(ls, "krd", [128, 4, 1024], BF16); b_krd = Buf("krd")
            vrb = sbl(ls, "vrb", [128, 4, 1024], BF16); b_vrb = Buf("vrb")
            Rst = sbl(ls, "Rst", [128, 2, 8, 128], F32); b_Rst = Buf("Rst")
            Rown = sbl(ls, "Rown", [128, 8, 128], F32); Rownb = sbl(ls, "Rownb", [128, 8, 128], BF16); b_Rown = Buf("Rown")
            S.dma("sp", kdec[:], h_kdec[:, :], writes=[b_kdec])
            S.op("dve", lambda: nc.vector.memset(kcb[:], 0.0), writes=[b_kcb])
            S.op("dve", lambda: nc.vector.memset(Rst[:], 0.0), writes=[b_Rst])
            S.op("pool", lambda: nc.gpsimd.memset(KcT[:], 0.0), writes=[b_KcT])
            S.op("pool", lambda: nc.gpsimd.memset(VcT[:], 0.0), writes=[b_VcT])
            S.dma("sp", KcT[64:71, 0, :], h_caug[:, :], writes=[b_KcT])
            for g in range(1, 4):
                S.dma("sp", KcT[64:71, g, :], h_caug[:, :], writes=[b_KcT])

            def norm_T(xsrc, vi, hT, bh, col0):
                xt, bx = xr.next()
                S.dma("sp", xt[:], xsrc, writes=[bx])
                ss, bs = ssr.next()
                S.op("act", lambda: nc.scalar.activation(out=junk[:], in_=xt[:], func=AF.Square, accum_out=ss[:, 0:1]),
                     reads=[bx], writes=[b_junk, bs])
                rstd_from_ss(ss[:, 1:2], ss[:, 0:1], D, [bs], [bs])
                xn, bn = xnr.next()
                S.op("act", lambda: nc.scalar.activation(out=xn[:], in_=xt[:], func=AF.Copy, scale=ss[:, 1:2]), reads=[bx, bs], writes=[bn])
                for half in range(2):
                    pt, bp = psT.next()
                    for k8 in range(8):
                        kc = half * 8 + k8
                        S.op("pe", lambda: nc.tensor.transpose(out=pt[:, k8 * 128:(k8 + 1) * 128], in_=xn[:, kc * 128:(kc + 1) * 128],
                                                               identity=ident_b[:]), reads=[bn, b_ident], writes=[bp], inc=(k8 == 7))
                    dst = hT[:, half * 8:(half + 1) * 8, col0:col0 + 128]
                    S.op("dve", lambda: nc.vector.tensor_tensor(out=dst, in0=pt[:].rearrange("p (k t) -> p k t", k=8),
                                                                in1=modT[:, vi, half * 8:(half + 1) * 8].unsqueeze(2).to_broadcast([128, 8, 128]),
                                                                op=ALU.mult), reads=[bp, b_modT], writes=[bh])
                    S.op("dve", lambda: nc.vector.tensor_tensor(out=dst, in0=dst,
                                                                in1=modT[:, vi + 1, half * 8:(half + 1) * 8].unsqueeze(2).to_broadcast([128, 8, 128]),
                                                                op=ALU.add), reads=[bh, b_modT], writes=[bh])

            def load_w(c0, ncols):
                wt, bw = wr.next()
                S.dma("pool", wt[:, :, 0:ncols], w_in_v[:, :, c0:c0 + ncols], reads=[b_wscr], writes=[bw])
                return wt, bw

            def fm(wt, bw, col, M, hT, bh):
                pt, bp = psM.next()
                mmg(pt[0:M, :], [(wt[:, k, col:col + M], hT[:, k, :]) for k in range(16)], [bw, bh], bp)
                return pt, bp

            def tm(wt, bw, col, N, hT, bh, tile):
                pt, bp = psM.next()
                mmg(pt[:, 0:N], [(hT[:, k, tile * 128:(tile + 1) * 128], wt[:, k, col:col + N]) for k in range(16)], [bh, bw], bp)
                return pt, bp

            ev_rr = [0]

            def evac(out_ap, in_ap, rb, wb, scale=None):
                e = ("dve", "act")[ev_rr[0] % 2]
                ev_rr[0] += 1
                if e == "act":
                    if scale is None:
                        S.op("act", lambda: nc.scalar.copy(out=out_ap, in_=in_ap), reads=rb, writes=wb)
                    else:
                        S.op("act", lambda: nc.scalar.mul(out=out_ap, in_=in_ap, mul=scale), reads=rb, writes=wb)
                else:
                    if scale is None:
                        S.op("dve", lambda: nc.vector.tensor_copy(out=out_ap, in_=in_ap), reads=rb, writes=wb)
                    else:
                        S.op("dve", lambda: nc.vector.tensor_scalar(out=out_ap, in0=in_ap, scalar1=scale, scalar2=None, op0=ALU.mult),
                             reads=rb, writes=wb)

            for s in range(NS):
                hT, bh = hTr.next()
                for tt in range(4):
                    norm_T(x_all[(s * 4 + tt) * 128:(s * 4 + tt + 1) * 128, :], 0, hT, bh, tt * 128)
                wt, bw = load_w(C_KC, 512)
                S.op("dve", lambda: nc.vector.tensor_copy(out=kcb[:, :, :, 0:16], in_=kcb[:, :, :, 512:528]), reads=[b_kcb], writes=[b_kcb])
                for kv in range(2):
                    for gp in range(2):
                        pt, bp = fm(wt, bw, kv * 256 + gp * 128, 128, hT, bh)
                        evac(kcb[:, kv, gp, 16:528], pt[:, :], [bp], [b_kcb])
                i0 = 1 if s == 0 else 0
                ni = 32 - i0
                n0 = 32 * s - 1 + i0
                for kv in range(2):
                    for g in range(4):
                        base = (g % 2) * 64
                        pt, bp = psM.next()
                        mmg(pt[:, 0:ni], [(W1s[base:base + 64, kv, l, :],
                                            kcb[base:base + 64, kv, g // 2, 16 * i0 + l:16 * i0 + l + 16 * (ni - 1) + 1:16]) for l in range(32)],
                            [b_W1s, b_kcb], bp)
                        S.op("act", lambda: nc.scalar.activation(out=hidx[:, 0:ni], in_=pt[:, 0:ni], func=AF.Identity, bias=peb[:, kv:kv + 1]),
                             reads=[bp, b_peb], writes=[b_hidx])
                        gelu_tanh(hid[:, 0:ni], hidx[:, 0:ni], hidt[:, 0:ni], [b_hidx], [b_hid], b_hidt)
                        pt2, bp2 = psM.next()
                        mmg(pt2[0:64, 0:ni], [(W2s[:, kv, :], hid[:, 0:ni])], [b_W2, b_hid], bp2)
                        if kv == 0:
                            evac(KcT[0:64, g, n0:n0 + ni], pt2[0:64, 0:ni], [bp2], [b_KcT])
                        else:
                            evac(VcT[0:64, g, n0:n0 + ni], pt2[0:64, 0:ni], [bp2], [b_VcT])
                for (c0, kd, vd) in ((C_KS, kS_d, vS_d), (C_KW, kW_d, vW_d)):
                    wt, bw = load_w(c0, 512)
                    for g in range(4):
                        pt, bp = fm(wt, bw, g * 64, 64, hT, bh)
                        sg, bsg = stg.next()
                        evac(sg[0:64, :], pt[0:64, :], [bp], [bsg])
                        S.dma("sp", kd[g, :, s * 512:(s + 1) * 512], sg[0:64, :], reads=[bsg], writes=[b_scr], chan="scr")
                    for tt in range(4):
                        pt, bp = tm(wt, bw, 256, 256, hT, bh, tt)
                        sg, bsg = stg.next()
                        evac(sg[:, 0:256], pt[:, 0:256], [bp], [bsg])
                        S.dma("sp", vd[(s * 4 + tt) * 128:(s * 4 + tt + 1) * 128, :], sg[:, 0:256], reads=[bsg], writes=[b_scr], chan="scr")
                wk = [load_w(C_KR, 512), load_w(C_KR + 512, 512)]
                for tt in range(4):
                    for hh in range(2):
                        pt, bp = tm(wk[hh][0], wk[hh][1], 0, 512, hT, bh, tt)
                        S.op("dve", lambda: nc.vector.tensor_tensor(out=krd[:, tt, hh * 512:(hh + 1) * 512].rearrange("p (h d) -> p h d", h=4),
                                                                    in0=pt[:].rearrange("p (h d) -> p h d", h=4),
                                                                    in1=kdec[:, hh * 4:(hh + 1) * 4].unsqueeze(2).to_broadcast([128, 4, 128]),
                                                                    op=ALU.mult), reads=[bp, b_kdec], writes=[b_krd])
                wv = [load_w(C_VR, 512), load_w(C_VR + 512, 512)]
                for tt in range(4):
                    for hh in range(2):
                        pt, bp = tm(wv[hh][0], wv[hh][1], 0, 512, hT, bh, tt)
                        evac(vrb[:, tt, hh * 512:(hh + 1) * 512], pt[:, :], [bp], [b_vrb])
                for tt in range(4):
                    ci = s * 4 + tt
                    par = ci % 2
                    for hh in range(2):
                        pk, bpk = psK.next()
                        for h4 in range(4):
                            h = hh * 4 + h4
                            S.op("pe", lambda: nc.tensor.matmul(out=pk[:, h4 * 128:(h4 + 1) * 128], lhsT=krd[:, tt, h * 128:(h + 1) * 128],
                                                                rhs=vrb[:, tt, h * 128:(h + 1) * 128], start=True, stop=True),
                                 reads=[b_krd, b_vrb], writes=[bpk], inc=(h4 == 3))
                        for h4 in range(4):
                            h = hh * 4 + h4
                            S.op("dve", lambda: nc.vector.scalar_tensor_tensor(out=Rst[:, 1 - par, h, :], in0=Rst[:, par, h, :], scalar=float(cdec[h]),
                                                                               in1=pk[:, h4 * 128:(h4 + 1) * 128], op0=ALU.mult, op1=ALU.add),
                                 reads=[b_Rst, bpk], writes=[b_Rst])
                    if par == 1:
                        pass
                    if par == 0:
                        S.op("pool", lambda: nc.gpsimd.tensor_tensor(out=Rown[:], in0=Rst[:, 1, :, :], in1=Rst[:, 0, :, :], op=ALU.subtract),
                             reads=[b_Rst], writes=[b_Rown])
                        S.op("dve", lambda: nc.vector.scalar_tensor_tensor(out=Rownb[:], in0=Rown[:], scalar=jsel[:, 0:1], in1=Rst[:, 0, :, :],
                                                                            op0=ALU.mult, op1=ALU.add), reads=[b_Rown, b_Rst, b_jsel], writes=[b_Rown])
                        S.dma("sp", R_d[ci // 2], Rownb[:], reads=[b_Rown], writes=[b_scr], chan="scr")

            S.op("dve", lambda: nc.vector.memset(VcA[:, :, :, 64:65], 1.0), writes=[b_VcA])
            for g in range(4):
                S.dma("sp", VcA[:, :, g, 65:65 + NSLC], h_overlap[:, :, :], writes=[b_VcA])
            for cc in range(NCC):
                for g in range(4):
                    pt, bp = psT.next()
                    S.op("pe", lambda: nc.tensor.transpose(out=pt[:, 0:64], in_=VcT[0:64, g, cc * 128:(cc + 1) * 128], identity=ident_b[0:64, 0:64]),
                         reads=[b_VcT, b_ident], writes=[bp])
                    evac(VcA[:, cc, g, 0:64], pt[:, 0:64], [bp], [b_VcA])

            for so in range(NOS):
                hT, bh = hTr.next()
                for tt in range(4):
                    norm_T(x_own[(so * 4 + tt) * 128:(so * 4 + tt + 1) * 128, :], 0, hT, bh, tt * 128)
                t0, t1 = so * 512, (so + 1) * 512
                for blk in range(2):
                    wt, bw = load_w(C_QA + blk * 512, 512)
                    for h8 in range(8):
                        pt, bp = fm(wt, bw, h8 * 64, 64, hT, bh)
                        sg, bsg = stg.next()
                        evac(sg[0:64, :], pt[0:64, :], [bp], [bsg], scale=0.125)
                        S.dma("sp", qA_d[blk * 8 + h8, :, t0:t1], sg[0:64, :], reads=[bsg], writes=[b_scr], chan="scr")
                for (c0, dd) in ((C_QR, qR_d), (C_KR, kR_d)):
                    for blk in range(2):
                        wt, bw = load_w(c0 + blk * 512, 512)
                        for h4 in range(4):
                            pt, bp = fm(wt, bw, h4 * 128, 128, hT, bh)
                            sg, bsg = stg.next()
                            evac(sg[:, :], pt[:, :], [bp], [bsg])
                            S.dma("sp", dd[blk * 4 + h4, :, t0:t1], sg[:, :], reads=[bsg], writes=[b_scr], chan="scr")
                for blk in range(2):
                    wt, bw = load_w(C_VR + blk * 512, 512)
                    for tt in range(4):
                        pt, bp = tm(wt, bw, 0, 512, hT, bh, tt)
                        sg, bsg = stg.next()
                        evac(sg[:, :], pt[:, :], [bp], [bsg])
                        S.dma("sp", vR_d[t0 + tt * 128:t0 + (tt + 1) * 128, blk * 512:(blk + 1) * 512], sg[:, :], reads=[bsg], writes=[b_scr], chan="scr")
                for blk in range(2):
                    wt, bw = load_w(C_GR + blk * 512, 512)
                    for tt in range(4):
                        pt, bp = tm(wt, bw, 0, 512, hT, bh, tt)
                        sf, bsf = stgf.next()
                        evac(sf[:, 0:512], pt[:, :], [bp], [bsf])
                        S.dma("sp", gR_d[t0 + tt * 128:t0 + (tt + 1) * 128, blk * 512:(blk + 1) * 512], sf[:, 0:512], reads=[bsf], writes=[b_scr], chan="scr")
                wt, bw = load_w(C_GA, 48)
                for tt in range(4):
                    pt, bp = tm(wt, bw, 0, 48, hT, bh, tt)
                    sf, bsf = stgf.next()
                    evac(sf[:, 0:48], pt[:, 0:48], [bp], [bsf])
                    S.dma("sp", gA_d[t0 + tt * 128:t0 + (tt + 1) * 128, :], sf[:, 0:48], reads=[bsf], writes=[b_scr], chan="scr")
            S.barrier()


        b_omix = Buf("omix")
        NB1 = 65 + NSLC
        with scope() as ls:
            KsT = sbl(ls, "KsT", [71, 4, SEQ], BF16); b_KsT = Buf("KsT")
            VsA = sbl(ls, "VsA", [128, NT, 4, 65], BF16); b_VsA = Buf("VsA")
            Eall = sbl(ls, "Eall", [NSLC, NT, 128], BF16); b_E = Buf("Eall")
            cmab = sbl(ls, "cmab", [128, 2, 128], BF16); wmask = sbl(ls, "wmask", [128, 6, 128], BF16); b_cm = Buf("cm")
            maskrep = sbl(ls, "maskrep", [128, 8, 4, 128], BF16); b_mr = Buf("maskrep")
            g_nsaB = sbl(ls, "g_nsaB", [128, 1024], F32); b_gn = Buf("gnsa")
            QTr = ring(ls, "QT", 1, [71, 4, 512], BF16)
            KwTr = ring(ls, "KwT", 1, [71, 4, 768], BF16)
            VwAr = ring(ls, "VwA", 1, [128, 6, 4, 65], BF16)
            cm1r = ring(ls, "cm1", 2, [128, NCC, 128], BF16)
            cm1rep = sbl(ls, "cm1rep", [128, NCC, 4, 128], BF16); b_c1r = Buf("cm1rep")
            vmr = ring(ls, "vm", 1, [128, 2, NSLC], F32)
            gar = ring(ls, "ga", 2, [128, 48], F32)
            PTr = ring(ls, "PT", 3, [128, 512], BF16)
            nsT = sbl(ls, "nsT", [NSLC, 4, 128], BF16); b_nsT = Buf("nsT")
            imp = sbl(ls, "imp", [128, NSLC], F32); impr = sbl(ls, "impr", [128, NSLC], F32); b_imp = Buf("imp"); b_impr = Buf("impr")
            m8 = sbl(ls, "m8", [128, 16], F32); b_m8 = Buf("m8")
            selb = sbl(ls, "selb", [128, NSLC], F32); negs = sbl(ls, "negs", [128, NSLC], BF16); b_sel = Buf("sel")
            rden = sbl(ls, "rden", [128, 3, 4], F32); cf = sbl(ls, "cf", [128, 3, 4], F32); b_rden = Buf("rden")
            oacc = sbl(ls, "oacc", [128, 16, 64], F32); otmp = sbl(ls, "otmp", [128, 4, 64], F32); b_oacc = Buf("oacc"); b_otmp = Buf("otmp")
            osq = sbl(ls, "osq", [128, 16, 64], F32); oss = sbl(ls, "oss", [128, 16], F32); b_osq = Buf("osq")
            onb = ring(ls, "onb", 1, [128, 1024], BF16)
            psS = pring(ls, "psS", 3, [128, 512], F32)
            psO1 = psl(ls, "psO1", [128, 4, 256], F32); b_pO1 = Buf("pO1")
            psO2 = psl(ls, "psO2", [128, 512], F32); b_pO2 = Buf("pO2")
            psO3 = psl(ls, "psO3", [128, 512], F32); b_pO3 = Buf("pO3")
            psX = psl(ls, "psX", [128, 1024], BF16); b_pX = Buf("pX")
            assert NB1 <= 256
            for g in range(4):
                S.dma("sp", KsT[0:64, g, :], kS_d[g, :, :], reads=[b_scr], writes=[b_KsT])
                S.dma("sp", KsT[64:71, g, :], h_kaug[:, :], writes=[b_KsT])
            for c4 in range(0, NT, 4):
                for g in range(4):
                    S.dma("pool", VsA[:, c4:c4 + 4, g, 0:64], vS_d[c4 * 128:(c4 + 4) * 128, g * 64:(g + 1) * 64].rearrange("(c p) d -> p c d", p=128),
                          reads=[b_scr], writes=[b_VsA])
            S.op("dve", lambda: nc.vector.memset(VsA[:, :, :, 64:65], 1.0), writes=[b_VsA])
            S.dma("sp", Eall[:], h_eall[:, :, :], writes=[b_E])
            S.dma("sp", cmab[:], h_cmab[:, :, :], writes=[b_cm])
            S.dma("sp", wmask[:], h_wmask[:, :, :], writes=[b_cm])
            S.op("dve", lambda: nc.vector.tensor_copy(out=maskrep[:, 0:2, :, :], in_=cmab[:].unsqueeze(2).to_broadcast([128, 2, 4, 128])),
                 reads=[b_cm], writes=[b_mr])
            S.op("dve", lambda: nc.vector.tensor_copy(out=maskrep[:, 2:8, :, :], in_=wmask[:].unsqueeze(2).to_broadcast([128, 6, 4, 128])),
                 reads=[b_cm], writes=[b_mr])
            S.dma("sp", g_nsaB[:], g_nsa[0:1, :].partition_broadcast(128), writes=[b_gn])
            for (vt, bv) in zip(VwAr.tiles, VwAr.bufs):
                S.op("dve", lambda: nc.vector.memset(vt[:, :, :, 64:65], 1.0), writes=[bv])

            pend = []

            def flush_pv():
                while pend:
                    (PT, bP, rhsV, bV, psO, b_pO, width, first, last) = pend.pop(0)
                    for r in range(4):
                        st_ = first and (r == 0 or (width == NB1 and r == 2))
                        S.op("pe", lambda: nc.tensor.matmul(out=psO(r), lhsT=PT[:, r * 128:(r + 1) * 128], rhs=rhsV, start=st_, stop=last,
                                                            skip_group_check=True),
                             reads=[bP, bV], writes=[b_pO], inc=(r == 3))

            def attn_chunk(pairs, rb, rhsV, bV, psO, b_pO, width, first, last):
                pS, bS = psS.next()
                mmg(pS[:], pairs, rb, bS)
                PT, bP = PTr.next()
                S.op("act", lambda: nc.scalar.activation(out=PT[:], in_=pS[:], func=AF.Exp), reads=[bS], writes=[bP])
                flush_pv()
                pend.append((PT, bP, rhsV, bV, psO, b_pO, width, first, last))

            def finish_branch(bi, g, psO, b_pO, first_branch):
                flush_pv()
                for r in range(4):
                    S.op("dve", lambda: nc.vector.tensor_scalar(out=rden[:, bi, r:r + 1], in0=psO(r)[:, 64:65], scalar1=1e-30, scalar2=None, op0=ALU.max),
                         reads=[b_pO], writes=[b_rden])
                S.op("dve", lambda: nc.vector.reciprocal(out=rden[:, bi, :], in_=rden[:, bi, :]), reads=[b_rden], writes=[b_rden])
                S.op("dve", lambda: nc.vector.tensor_tensor(out=cf[:, bi, :], in0=rden[:, bi, :],
                                                            in1=ga[:, g * 12:(g + 1) * 12].rearrange("p (r b) -> p r b", b=3)[:, :, bi], op=ALU.mult),
                     reads=[b_rden, bga], writes=[b_rden])
                for r in range(4):
                    h = g * 4 + r
                    if first_branch:
                        S.op("dve", lambda: nc.vector.tensor_scalar(out=oacc[:, h, :], in0=psO(r)[:, 0:64], scalar1=cf[:, bi, r:r + 1], scalar2=None, op0=ALU.mult),
                             reads=[b_pO, b_rden], writes=[b_oacc])
                    else:
                        S.op("dve", lambda: nc.vector.scalar_tensor_tensor(out=oacc[:, h, :], in0=psO(r)[:, 0:64], scalar=cf[:, bi, r:r + 1], in1=oacc[:, h, :],
                                                                           op0=ALU.mult, op1=ALU.add), reads=[b_pO, b_rden, b_oacc], writes=[b_oacc])

            for m in range(NO):
                QT, bQ = QTr.next()
                for g in range(4):
                    S.dma("sp", QT[0:64, g, :].rearrange("d (r q) -> d r q", r=4),
                          qA_d[4 * g:4 * g + 4, :, m * 128:(m + 1) * 128].rearrange("r d q -> d r q"), reads=[b_scr], writes=[bQ])
                S.dma("sp", QT[64:71, :, :], h_qaug[m].rearrange("g a q -> a g q"), writes=[bQ])
                c0 = max(0, 2 * m - 4)
                u0 = c0 - (2 * m - 4)
                nch = 2 * m + 2 - c0
                KwT, bKw = KwTr.next()
                VwA, bVw = VwAr.next()
                for g in range(4):
                    S.dma("pool", KwT[0:64, g, u0 * 128:(u0 + nch) * 128], kW_d[g, :, c0 * 128:(c0 + nch) * 128], reads=[b_scr], writes=[bKw])
                    S.dma("pool", KwT[64:71, g, u0 * 128:(u0 + nch) * 128], h_kaug[:, c0 * 128:(c0 + nch) * 128], writes=[bKw])
                for g in range(4):
                    S.dma("pool", VwA[:, u0:u0 + nch, g, 0:64], vW_d[c0 * 128:(c0 + nch) * 128, g * 64:(g + 1) * 64].rearrange("(c p) d -> p c d", p=128),
                          reads=[b_scr], writes=[bVw])
                cm1, bc1 = cm1r.next()
                S.dma("sp", cm1[:], h_cm1[m], writes=[bc1])
                S.op("pool", lambda: nc.gpsimd.tensor_copy(out=cm1rep[:], in_=cm1[:].unsqueeze(2).to_broadcast([128, NCC, 4, 128])),
                     reads=[bc1], writes=[b_c1r])
                vm, bvm = vmr.next()
                S.dma("sp", vm[:, 0, :], h_vmask[m], writes=[bvm])
                S.dma("sp", vm[:, 1, :], h_amask[m], writes=[bvm])
                ga, bga = gar.next()
                S.dma("sp", ga[:], gA_d[m * 128:(m + 1) * 128, :], reads=[b_scr], writes=[bga])
                S.op("act", lambda: nc.scalar.activation(out=ga[:], in_=ga[:], func=AF.Sigmoid), reads=[bga], writes=[bga])
                for g in range(4):
                    for cc in range(NCC):
                        attn_chunk([(KcT[0:71, g, cc * 128:(cc + 1) * 128], QT[0:71, g, :]),
                                    (ident_b[:], cm1rep[:, cc, :, :].rearrange("p r q -> p (r q)"))],
                                   [b_KcT, bQ, b_ident, b_c1r], VcA[:, cc, g, :], b_VcA,
                                   lambda r: psO1[:, r, 0:NB1], b_pO1, NB1, cc == 0, cc == NCC - 1)
                    finish_branch(0, g, lambda r: psO1[:, r, 0:NB1], b_pO1, True)
                    S.op("dve", lambda: nc.vector.tensor_scalar(out=imp[:], in0=psO1[:, 0, 65:NB1], scalar1=rden[:, 0, 0:1], scalar2=None, op0=ALU.mult),
                         reads=[b_pO1, b_rden], writes=[b_imp])
                    for r in range(1, 4):
                        S.op("dve", lambda: nc.vector.scalar_tensor_tensor(out=imp[:], in0=psO1[:, r, 65:NB1], scalar=rden[:, 0, r:r + 1], in1=imp[:],
                                                                           op0=ALU.mult, op1=ALU.add), reads=[b_pO1, b_rden, b_imp], writes=[b_imp])
                    S.op("dve", lambda: nc.vector.tensor_tensor(out=imp[:], in0=imp[:], in1=vm[:, 0, :], op=ALU.mult), reads=[b_imp, bvm], writes=[b_imp])
                    S.op("dve", lambda: nc.vector.tensor_tensor(out=imp[:], in0=imp[:], in1=vm[:, 1, :], op=ALU.add), reads=[b_imp, bvm], writes=[b_imp])
                    S.op("dve", lambda: nc.vector.max(out=m8[:, 0:8], in_=imp[:]), reads=[b_imp], writes=[b_m8])
                    S.op("dve", lambda: nc.vector.match_replace(out=impr[:], in_to_replace=m8[:, 0:8], in_values=imp[:], imm_value=-1e30),
                         reads=[b_imp, b_m8], writes=[b_impr])
                    S.op("dve", lambda: nc.vector.max(out=m8[:, 8:16], in_=impr[:]), reads=[b_impr], writes=[b_m8])
                    S.op("dve", lambda: nc.vector.tensor_scalar(out=selb[:], in0=imp[:], scalar1=m8[:, 15:16], scalar2=1.0, op0=ALU.is_ge, op1=ALU.subtract),
                         reads=[b_imp, b_m8], writes=[b_sel])
                    S.op("dve", lambda: nc.vector.tensor_scalar(out=negs[:], in0=selb[:], scalar1=-NEG, scalar2=None, op0=ALU.mult), reads=[b_sel], writes=[b_sel])
                    S.op("pe", lambda: nc.tensor.transpose(out=psX[0:NSLC, 0:128], in_=negs[:], identity=ident_b[:]), reads=[b_sel, b_ident], writes=[b_pX])
                    S.op("dve", lambda: nc.vector.tensor_copy(out=nsT[:], in_=psX[0:NSLC, 0:128].unsqueeze(1).to_broadcast([NSLC, 4, 128])),
                         reads=[b_pX], writes=[b_nsT])
                    for u in range(u0, 6):
                        pairs = [(KwT[0:71, g, u * 128:(u + 1) * 128], QT[0:71, g, :]),
                                 (ident_b[:], maskrep[:, 2 + u, :, :].rearrange("p r q -> p (r q)"))]
                        attn_chunk(pairs, [bKw, bQ, b_ident, b_mr], VwA[:, u, g, :], bVw, lambda r: psO3[:, r * 65:(r + 1) * 65], b_pO3, 65, u == u0, u == 5)
                    finish_branch(2, g, lambda r: psO3[:, r * 65:(r + 1) * 65], b_pO3, False)
                    nc2 = 2 * m + 2
                    for c in range(nc2):
                        pairs = [(KsT[0:71, g, c * 128:(c + 1) * 128], QT[0:71, g, :]),
                                 (Eall[0:NSLC, c, :], nsT[:].rearrange("p r q -> p (r q)"))]
                        rb = [b_KsT, bQ, b_E, b_nsT]
                        if c >= 2 * m:
                            pairs.append((ident_b[:], maskrep[:, c - 2 * m, :, :].rearrange("p r q -> p (r q)")))
                            rb = rb + [b_ident, b_mr]
                        attn_chunk(pairs, rb, VsA[:, c, g, :], b_VsA, lambda r: psO2[:, r * 65:(r + 1) * 65], b_pO2, 65, c == 0, c == nc2 - 1)
                    finish_branch(1, g, lambda r: psO2[:, r * 65:(r + 1) * 65], b_pO2, False)
                S.op("dve", lambda: nc.vector.tensor_tensor(out=osq[:], in0=oacc[:], in1=oacc[:], op=ALU.mult), reads=[b_oacc], writes=[b_osq])
                S.op("dve", lambda: nc.vector.tensor_reduce(out=oss[:], in_=osq[:], axis=AX.X, op=ALU.add), reads=[b_osq], writes=[b_osq])
                rstd_from_ss(oss[:], oss[:], 64, [b_osq], [b_osq])
                S.op("dve", lambda: nc.vector.tensor_tensor(out=osq[:], in0=oacc[:], in1=oss[:].unsqueeze(2).to_broadcast([128, 16, 64]), op=ALU.mult),
                     reads=[b_oacc, b_osq], writes=[b_osq])
                on, bon = onb.next()
                S.op("dve", lambda: nc.vector.tensor_tensor(out=on[:], in0=osq[:].rearrange("p h d -> p (h d)"), in1=g_nsaB[:], op=ALU.mult),
                     reads=[b_osq, b_gn], writes=[bon])
                S.dma("sp", omix_d[m * 128:(m + 1) * 128, 0:1024], on[:], reads=[bon], writes=[b_omix], chan="omix")
            S.barrier()


        with scope() as ls:
            decT = sbl(ls, "decT", [128, 8, 128], F32); qdecB = sbl(ls, "qdecB", [128, 8, 128], F32); g_retB = sbl(ls, "g_retB", [128, 1024], F32)
            b_rc = Buf("rconst")
            S.dma("sp", decT[:], h_decT[:, :, :], writes=[b_rc]); S.dma("sp", qdecB[:], h_qdecB[:, :, :], writes=[b_rc])
            S.dma("sp", g_retB[:], g_ret[0:1, :].partition_broadcast(128), writes=[b_rc])
            qTr = ring(ls, "rqT", 2, [128, 8, 128], BF16); kTr = ring(ls, "rkT", 2, [128, 8, 128], BF16)
            vr_ = ring(ls, "rv", 2, [128, 1024], BF16); Rr = ring(ls, "rR", 2, [128, 8, 128], BF16); grr = ring(ls, "rg", 2, [128, 1024], F32)
            qd = sbl(ls, "rqd", [128, 8, 128], BF16); b_qd = Buf("qd")
            AT = sbl(ls, "rAT", [128, 8, 128], BF16); b_AT = Buf("AT")
            ro = sbl(ls, "ro", [128, 8, 128], F32); rsq = sbl(ls, "rsq", [128, 8, 128], F32); b_ro = Buf("ro"); b_rsq = Buf("rsq")
            rmu = sbl(ls, "rmu", [128, 2, 8], F32); b_rmu = Buf("rmu")
            orb = ring(ls, "orb", 2, [128, 1024], BF16)
            psA = pring(ls, "rpsA", 2, [128, 512], F32); psO = pring(ls, "rpsO", 2, [128, 512], F32)
            for m in range(NO):
                sl = slice(m * 128, (m + 1) * 128)
                qT, bq = qTr.next(); kT, bk = kTr.next(); v_, bv = vr_.next(); R_, bR = Rr.next(); gr, bg = grr.next()
                S.dma("sp", qT[:], qR_d[:, :, sl].rearrange("h d c -> d h c"), reads=[b_scr], writes=[bq])
                S.dma("sp", kT[:], kR_d[:, :, sl].rearrange("h d c -> d h c"), reads=[b_scr], writes=[bk])
                S.dma("pool", v_[:], vR_d[sl, :], reads=[b_scr], writes=[bv])
                S.dma("pool", R_[:], R_d[m], reads=[b_scr], writes=[bR])
                S.dma("pool", gr[:], gR_d[sl, :], reads=[b_scr], writes=[bg])
                S.op("pool", lambda: nc.gpsimd.tensor_tensor(out=qd[:], in0=qT[:], in1=qdecB[:], op=ALU.mult), reads=[bq, b_rc], writes=[b_qd])
                for hh in range(2):
                    pa, bpa = psA.next()
                    for h4 in range(4):
                        h = hh * 4 + h4
                        S.op("pe", lambda: nc.tensor.matmul(out=pa[:, h4 * 128:(h4 + 1) * 128], lhsT=kT[:, h, :], rhs=qT[:, h, :], start=True, stop=True),
                             reads=[bk, bq], writes=[bpa], inc=(h4 == 3))
                    S.op("dve", lambda: nc.vector.tensor_tensor(out=AT[:, hh * 4:(hh + 1) * 4, :], in0=pa[:].rearrange("p (h c) -> p h c", h=4),
                                                                in1=decT[:, hh * 4:(hh + 1) * 4, :], op=ALU.mult), reads=[bpa, b_rc], writes=[b_AT])
                for hh in range(2):
                    po, bpo = psO.next()
                    for h4 in range(4):
                        h = hh * 4 + h4
                        mmg(po[:, h4 * 128:(h4 + 1) * 128], [(AT[:, h, :], v_[:, h * 128:(h + 1) * 128]), (qd[:, h, :], R_[:, h, :])],
                            [b_AT, bv, b_qd, bR], bpo)
                    S.op("act", lambda: nc.scalar.copy(out=ro[:, hh * 4:(hh + 1) * 4, :], in_=po[:].rearrange("p (h c) -> p h c", h=4)), reads=[bpo], writes=[b_ro])
                S.op("dve", lambda: nc.vector.tensor_reduce(out=rmu[:, 0, :], in_=ro[:], axis=AX.X, op=ALU.add), reads=[b_ro], writes=[b_rmu])
                S.op("dve", lambda: nc.vector.tensor_scalar(out=rmu[:, 0, :], in0=rmu[:, 0, :], scalar1=-1.0 / 128, scalar2=None, op0=ALU.mult), reads=[b_rmu], writes=[b_rmu])
                S.op("dve", lambda: nc.vector.tensor_tensor(out=ro[:], in0=ro[:], in1=rmu[:, 0, :].unsqueeze(2).to_broadcast([128, 8, 128]), op=ALU.add),
                     reads=[b_ro, b_rmu], writes=[b_ro])
                S.op("dve", lambda: nc.vector.tensor_tensor(out=rsq[:], in0=ro[:], in1=ro[:], op=ALU.mult), reads=[b_ro], writes=[b_rsq])
                S.op("dve", lambda: nc.vector.tensor_reduce(out=rmu[:, 1, :], in_=rsq[:], axis=AX.X, op=ALU.add), reads=[b_rsq], writes=[b_rmu])
                rstd_from_ss(rmu[:, 1, :], rmu[:, 1, :], 128, [b_rmu], [b_rmu])
                S.op("dve", lambda: nc.vector.tensor_tensor(out=rsq[:], in0=ro[:], in1=rmu[:, 1, :].unsqueeze(2).to_broadcast([128, 8, 128]), op=ALU.mult),
                     reads=[b_ro, b_rmu], writes=[b_rsq])
                S.op("dve", lambda: nc.vector.tensor_tensor(out=rsq[:].rearrange("p h d -> p (h d)"), in0=rsq[:].rearrange("p h d -> p (h d)"), in1=g_retB[:], op=ALU.mult),
                     reads=[b_rsq, b_rc], writes=[b_rsq])
                S.op("act", lambda: nc.scalar.activation(out=gr[:], in_=gr[:], func=AF.Silu), reads=[bg], writes=[bg])
                ob, bob = orb.next()
                S.op("dve", lambda: nc.vector.tensor_tensor(out=ob[:], in0=rsq[:].rearrange("p h d -> p (h d)"), in1=gr[:], op=ALU.mult),
                     reads=[b_rsq, bg], writes=[bob])
                S.dma("sp", omix_d[sl, 1024:2048], ob[:], reads=[bob], writes=[b_omix], chan="omix")
            S.barrier()

        b_x1 = Buf("x1d")
        with scope() as ls:
            wo = sbl(ls, "wo", [128, 16, D], BF16); b_wo = Buf("wo")
            S.dma("sp", wo[:], w_out_bf.rearrange("(k p) n -> p k n", p=128), reads=[b_wscr], writes=[b_wo])
            omr = ring(ls, "om", 2, [128, D], BF16); oTr = ring(ls, "oT", 2, [128, 16, 128], BF16)
            xor_ = ring(ls, "xo", 2, [128, D], F32); x1r = ring(ls, "x1", 2, [128, D], F32)
            junk = sbl(ls, "junkD", [128, D], BF16); b_junk = Buf("junkD")
            ssr = ring(ls, "ssD", 2, [128, 2], F32); xnr = ring(ls, "xnD", 2, [128, D], BF16)
            h2r = ring(ls, "h2T", 2, [128, 16, 128], BF16)
            psT = pring(ls, "DpsT", 2, [128, 1024], BF16); psM = pring(ls, "DpsM", 4, [128, 512], F32)
            for m in range(NO):
                sl = slice(m * 128, (m + 1) * 128)
                om, bom = omr.next()
                S.dma("sp", om[:], omix_d[sl, :], reads=[b_omix], writes=[bom])
                oT, boT = oTr.next()
                for half in range(2):
                    pt, bp = psT.next()
                    for k8 in range(8):
                        kc = half * 8 + k8
                        S.op("pe", lambda: nc.tensor.transpose(out=pt[:, k8 * 128:(k8 + 1) * 128], in_=om[:, kc * 128:(kc + 1) * 128], identity=ident_b[:]),
                             reads=[bom, b_ident], writes=[bp], inc=(k8 == 7))
                    S.op("act", lambda: nc.scalar.copy(out=oT[:, half * 8:(half + 1) * 8, :], in_=pt[:].rearrange("p (k t) -> p k t", k=8)), reads=[bp], writes=[boT])
                xo, bxo = xor_.next()
                S.dma("pool", xo[:], x_own[sl, :], writes=[bxo])
                x1, bx1 = x1r.next()
                for n4 in range(4):
                    pt, bp = psM.next()
                    ns = slice(n4 * 512, (n4 + 1) * 512)
                    mmg(pt[:], [(oT[:, k, :], wo[:, k, ns]) for k in range(16)], [boT, b_wo], bp)
                    S.op("dve", lambda: nc.vector.tensor_tensor(out=x1[:, ns], in0=pt[:], in1=gate1B[:, ns], op=ALU.mult), reads=[bp, b_gate], writes=[bx1])
                    S.op("pool", lambda: nc.gpsimd.tensor_tensor(out=x1[:, ns], in0=x1[:, ns], in1=xo[:, ns], op=ALU.add), reads=[bx1, bxo], writes=[bx1])
                S.dma("sp", x1_d[sl, :], x1[:], reads=[bx1], writes=[b_x1], chan="x1d")
                ss, bs = ssr.next()
                S.op("act", lambda: nc.scalar.activation(out=junk[:], in_=x1[:], func=AF.Square, accum_out=ss[:, 0:1]), reads=[bx1], writes=[b_junk, bs])
                rstd_from_ss(ss[:, 1:2], ss[:, 0:1], D, [bs], [bs])
                xn, bn = xnr.next()
                S.op("act", lambda: nc.scalar.activation(out=xn[:], in_=x1[:], func=AF.Copy, scale=ss[:, 1:2]), reads=[bx1, bs], writes=[bn])
                h2T, bh2 = h2r.next()
                for half in range(2):
                    pt, bp = psT.next()
                    for k8 in range(8):
                        kc = half * 8 + k8
                        S.op("pe", lambda: nc.tensor.transpose(out=pt[:, k8 * 128:(k8 + 1) * 128], in_=xn[:, kc * 128:(kc + 1) * 128], identity=ident_b[:]),
                             reads=[bn, b_ident], writes=[bp], inc=(k8 == 7))
                    dst = h2T[:, half * 8:(half + 1) * 8, :]
                    S.op("dve", lambda: nc.vector.tensor_tensor(out=dst, in0=pt[:].rearrange("p (k t) -> p k t", k=8),
                                                                in1=modT[:, 2, half * 8:(half + 1) * 8].unsqueeze(2).to_broadcast([128, 8, 128]), op=ALU.mult),
                         reads=[bp, b_modT], writes=[bh2])
                    S.op("dve", lambda: nc.vector.tensor_tensor(out=dst, in0=dst,
                                                                in1=modT[:, 3, half * 8:(half + 1) * 8].unsqueeze(2).to_broadcast([128, 8, 128]), op=ALU.add),
                         reads=[bh2, b_modT], writes=[bh2])
                S.dma("sp", h2T_d[:, :, sl].rearrange("k p t -> p k t"), h2T[:], reads=[bh2], writes=[b_x1], chan="x1d")
            S.barrier()

        with scope() as ls:
            wq = sbl(ls, "wq", [128, 16, D], BF16); b_wo = Buf("wq")
            S.dma("pool", wq[:], w_q_bf.rearrange("(k p) n -> p k n", p=128), reads=[b_wscr], writes=[b_wo])
            h2r = ring(ls, "D2h2T", 2, [128, 16, 128], BF16)
            qTa = ring(ls, "qTa", 3, [128, 128], BF16)
            scr_ = ring(ls, "sc", 2, [128, 16, 128], F32); scx = sbl(ls, "scx", [128, 256], F32); b_scx = Buf("scx")
            v16 = sbl(ls, "v16", [128, 16, 16], F32); b_v16 = Buf("v16")
            cand = sbl(ls, "cand", [128, 8, 256], F32); b_cand = Buf("cand")
            t16 = sbl(ls, "t16", [128, 8, 16], F32); b_t16 = Buf("t16")
            tneg = sbl(ls, "tneg", [128, 8], F32); zs = sbl(ls, "zs", [128, 8], F32); ej = sbl(ls, "ej", [128, 16], F32); b_z = Buf("z")
            psQ = pring(ls, "DpsQ", 4, [128, 512], F32)
            for m in range(NO):
                sl = slice(m * 128, (m + 1) * 128)
                h2T, bh2 = h2r.next()
                S.dma("sp", h2T[:], h2T_d[:, :, sl].rearrange("k p t -> p k t"), reads=[b_x1], writes=[bh2])
                sc, bsc = scr_.next()
                for a in range(16):
                    pq, bpq = psQ.next()
                    mmg(pq[:, 0:128], [(wq[:, k, a * 128:(a + 1) * 128], h2T[:, k, :]) for k in range(16)], [b_wo, bh2], bpq)
                    qa, bqa = qTa.next()
                    S.op("act", lambda: nc.scalar.copy(out=qa[:], in_=pq[:, 0:128]), reads=[bpq], writes=[bqa])
                    pq2, bpq2 = psQ.next()
                    mmg(pq2[:, 0:128], [(qa[:], skT[:, a, :])], [bqa, b_skT], bpq2)
                    S.op("act", lambda: nc.scalar.copy(out=sc[:, a, :], in_=pq2[:, 0:128]), reads=[bpq2], writes=[bsc])
                    S.op("dve", lambda: nc.vector.max(out=v16[:, a, 0:8], in_=sc[:, a, :]), reads=[bsc], writes=[b_v16])
                    S.op("dve", lambda: nc.vector.match_replace(out=scx[:, 0:128], in_to_replace=v16[:, a, 0:8], in_values=sc[:, a, :], imm_value=-1e30),
                         reads=[bsc, b_v16], writes=[b_scx])
                    S.op("dve", lambda: nc.vector.max(out=v16[:, a, 8:16], in_=scx[:, 0:128]), reads=[b_scx], writes=[b_v16])
                S.dma("sp", sc_d[sl, :, :], sc[:], reads=[bsc], writes=[b_x1], chan="x1d")
                v16v = v16[:].rearrange("p (h t) k -> p h t k", t=2)
                S.op("dve", lambda: nc.vector.tensor_tensor(out=cand[:].rearrange("p h (i j) -> p h i j", i=16),
                                                            in0=v16v[:, :, 0, :].unsqueeze(3).to_broadcast([128, 8, 16, 16]),
                                                            in1=v16v[:, :, 1, :].unsqueeze(2).to_broadcast([128, 8, 16, 16]), op=ALU.add),
                     reads=[b_v16], writes=[b_cand])
                for h in range(8):
                    S.op("dve", lambda: nc.vector.max(out=t16[:, h, 0:8], in_=cand[:, h, :]), reads=[b_cand], writes=[b_t16])
                    S.op("dve", lambda: nc.vector.match_replace(out=scx[:], in_to_replace=t16[:, h, 0:8], in_values=cand[:, h, :], imm_value=-1e30),
                         reads=[b_cand, b_t16], writes=[b_scx])
                    S.op("dve", lambda: nc.vector.max(out=t16[:, h, 8:16], in_=scx[:]), reads=[b_scx], writes=[b_t16])
                S.op("dve", lambda: nc.vector.tensor_scalar(out=tneg[:], in0=t16[:, :, 15], scalar1=-1.0, scalar2=None, op0=ALU.mult), reads=[b_t16], writes=[b_z])
                for h in range(8):
                    S.op("act", lambda: nc.scalar.activation(out=ej[:], in_=t16[:, h, :], func=AF.Exp, bias=tneg[:, h:h + 1], accum_out=zs[:, h:h + 1]),
                         reads=[b_t16, b_z], writes=[b_z])
                S.op("act", lambda: nc.scalar.activation(out=zs[:], in_=zs[:], func=AF.Ln), reads=[b_z], writes=[b_z])
                S.op("dve", lambda: nc.vector.tensor_tensor(out=lnz_all[:, m, :], in0=zs[:], in1=t16[:, :, 15], op=ALU.add), reads=[b_z, b_t16], writes=[b_thr[m]])
                S.op("dve", lambda: nc.vector.tensor_scalar(out=thr_all[:, m, :], in0=t16[:, :, 15], scalar1=-1e-5, scalar2=None, op0=ALU.add),
                     reads=[b_t16], writes=[b_thr[m]])
            S.barrier()

        TT = 2
        with scope() as ls:
            g_finB = sbl(ls, "g_finB", [128, D], F32); b_gf = Buf("gfin")
            S.dma("sp", g_finB[:], g_fin[0:1, :].partition_broadcast(128), writes=[b_gf])
            h2T2 = sbl(ls, "Eh2T", [128, TT, 16, 128], BF16); b_h2 = [Buf("Eh2T%d" % t) for t in range(TT)]
            scr_ = ring(ls, "Esc", 1, [128, 16, 128], F32)
            e1_2 = sbl(ls, "e1", [128, TT, 8, 128], F32); e2_2 = sbl(ls, "e2", [128, TT, 8, 128], F32); tau2 = sbl(ls, "tau", [128, TT, 8], F32)
            b_e = [Buf("e%d" % t) for t in range(TT)]
            acc = sbl(ls, "acc", [128, TT, D], F32); b_acc = [Buf("acc%d" % t) for t in range(TT)]
            UTr = ring(ls, "UTb", 2, [128, 16, 512], BF16); Vr = ring(ls, "Vb", 2, [128, 4, D], BF16)
            Er = ring(ls, "E", 2, [128, 8, 512], F32)
            Gr = ring(ls, "Gb", 2, [128, 8, 512], BF16)
            agr = ring(ls, "ag", 2, [128, 512], F32)
            cTr = ring(ls, "coefT", 2, [128, 4, 128], BF16)
            junk = Gr.tiles[0][:].rearrange("p h e -> p (h e)"); b_junk = Gr.bufs[0]
            ssE = sbl(ls, "ssE", [128, 2], F32); b_ssE = Buf("ssE")
            psA = pring(ls, "EpsA", 2, [128, 512], F32); psG = pring(ls, "EpsG", 2, [128, 512], F32)
            psOut = psl(ls, "EpsO", [128, D], F32); b_pOut = Buf("EpsO")
            UT_v = UT_bf.rearrange("(k p) e -> p k e", p=128)
            for sup in range(NO // TT):
                for t in range(TT):
                    m = sup * TT + t
                    sl = slice(m * 128, (m + 1) * 128)
                    sc, bsc = scr_.next()
                    S.dma("sp", h2T2[:, t, :, :], h2T_d[:, :, sl].rearrange("k p t -> p k t"), reads=[b_x1], writes=[b_h2[t]])
                    S.dma("sp", sc[:], sc_d[sl, :, :], reads=[b_x1], writes=[bsc])
                    scv = sc[:].rearrange("p (h t) k -> p h t k", t=2)
                    S.op("dve", lambda: nc.vector.tensor_tensor(out=e1_2[:, t, :, :], in0=scv[:, :, 0, :],
                                                                in1=lnz_all[:, m, :].unsqueeze(2).to_broadcast([128, 8, 128]), op=ALU.subtract),
                         reads=[bsc, b_thr[m]], writes=[b_e[t]])
                    S.op("act", lambda: nc.scalar.activation(out=e1_2[:, t, :, :], in_=e1_2[:, t, :, :], func=AF.Exp), reads=[b_e[t]], writes=[b_e[t]])
                    S.op("act", lambda: nc.scalar.activation(out=e2_2[:, t, :, :], in_=scv[:, :, 1, :], func=AF.Exp), reads=[bsc], writes=[b_e[t]])
                    S.op("dve", lambda: nc.vector.tensor_tensor(out=tau2[:, t, :], in0=thr_all[:, m, :], in1=lnz_all[:, m, :], op=ALU.subtract),
                         reads=[b_thr[m]], writes=[b_e[t]])
                    S.op("act", lambda: nc.scalar.activation(out=tau2[:, t, :], in_=tau2[:, t, :], func=AF.Exp), reads=[b_e[t]], writes=[b_e[t]])

                wcur = [None]

                def stageA(eb, t):
                    if t == 0:
                        UTb, bU = UTr.next(); Vb, bV = Vr.next()
                        S.dma("sp", UTb[:], UT_v[:, :, eb * 512:(eb + 1) * 512], reads=[b_wscr], writes=[bU])
                        S.dma("pool", Vb[:], V_bf[eb * 512:(eb + 1) * 512, :].rearrange("(c p) n -> p c n", p=128), reads=[b_wscr], writes=[bV])
                        wcur[0] = (UTb, bU, Vb, bV)
                    UTb, bU, Vb, bV = wcur[0]
                    Et, bE = Er.next(); Gb, bG = Gr.next()
                    S.op("pool", lambda: nc.gpsimd.tensor_tensor(out=Et[:].rearrange("p h (a k) -> p h a k", a=4),
                                                                 in0=e1_2[:, t, :, eb * 4:(eb + 1) * 4].unsqueeze(3).to_broadcast([128, 8, 4, 128]),
                                                                 in1=e2_2[:, t, :, :].unsqueeze(2).to_broadcast([128, 8, 4, 128]), op=ALU.mult),
                         reads=[b_e[t]], writes=[bE])
                    pa, bpa = psA.next()
                    for c in range(4):
                        mmg(pa[:, c * 128:(c + 1) * 128], [(UTb[:, k, c * 128:(c + 1) * 128], h2T2[:, t, k, :]) for k in range(16)], [bU, b_h2[t]], bpa)
                    ag, bag = agr.next()
                    S.op("act", lambda: nc.scalar.activation(out=ag[:], in_=pa[:], func=AF.Gelu_apprx_tanh), reads=[bpa], writes=[bag])
                    for h in range(8):
                        S.op("dve", lambda: nc.vector.scalar_tensor_tensor(out=Gb[:, h, :], in0=Et[:, h, :], scalar=tau2[:, t, h:h + 1], in1=Et[:, h, :],
                                                                           op0=ALU.is_ge, op1=ALU.mult), reads=[bE, b_e[t]], writes=[bG])
                    return (eb, t, Vb, bV, Gb, bG, ag, bag)

                def stageB(st):
                    (eb, t, Vb, bV, Gb, bG, ag, bag) = st
                    pg, bpg = psG.next()
                    for c in range(4):
                        mmg(pg[:, c * 128:(c + 1) * 128], [(Gb[:, h, c * 128:(c + 1) * 128], ident_b[:]) for h in range(8)], [bG, b_ident], bpg)
                    cT, bcT = cTr.next()
                    S.op("dve", lambda: nc.vector.tensor_tensor(out=cT[:].rearrange("p c t -> p (c t)"), in0=pg[:], in1=ag[:], op=ALU.mult),
                         reads=[bpg, bag], writes=[bcT])
                    for n4 in range(4):
                        for c in range(4):
                            S.op("pe", lambda: nc.tensor.matmul(out=psOut[:, n4 * 512:(n4 + 1) * 512], lhsT=cT[:, c, :], rhs=Vb[:, c, n4 * 512:(n4 + 1) * 512],
                                                                start=(c == 0), stop=(c == 3)),
                                 reads=[bcT, bV], writes=[b_pOut], inc=(n4 == 3 and c == 3))
                    if eb == 0:
                        S.op("act", lambda: nc.scalar.copy(out=acc[:, t, :], in_=psOut[:]), reads=[b_pOut], writes=[b_acc[t]])
                    else:
                        S.op("dve", lambda: nc.vector.tensor_tensor(out=acc[:, t, :], in0=psOut[:], in1=acc[:, t, :], op=ALU.add),
                             reads=[b_pOut, b_acc[t]], writes=[b_acc[t]])

                prev = None
                for eb in range(32):
                    for t in range(TT):
                        cur = stageA(eb, t)
                        if prev is not None:
                            stageB(prev)
                        prev = cur
                stageB(prev)
                for t in range(TT):
                    m = sup * TT + t
                    sl = slice(m * 128, (m + 1) * 128)
                    x1, bx1 = scr_.tiles[0][:].rearrange("p a k -> p (a k)"), scr_.bufs[0]
                    S.dma("sp", x1[:, :], x1_d[sl, :], reads=[b_x1], writes=[bx1])
                    x2 = acc[:, t, :]
                    S.op("dve", lambda: nc.vector.tensor_tensor(out=x2, in0=x2, in1=gate2B[:], op=ALU.mult), reads=[b_acc[t], b_gate], writes=[b_acc[t]])
                    S.op("pool", lambda: nc.gpsimd.tensor_tensor(out=x2, in0=x2, in1=x1[:, :], op=ALU.add), reads=[b_acc[t], bx1], writes=[b_acc[t]])
                    S.op("act", lambda: nc.scalar.activation(out=junk[:, 0:D], in_=x2, func=AF.Square, accum_out=ssE[:, 0:1]), reads=[b_acc[t]], writes=[b_junk, b_ssE])
                    rstd_from_ss(ssE[:, 1:2], ssE[:, 0:1], D, [b_ssE], [b_ssE])
                    S.op("dve", lambda: nc.vector.scalar_tensor_tensor(out=x1[:, :], in0=x2, scalar=ssE[:, 1:2], in1=g_finB[:], op0=ALU.mult, op1=ALU.mult),
                         reads=[b_acc[t], b_ssE, b_gf], writes=[bx1])
                    S.dma("sp", y[sl, :], x1[:, :], reads=[bx1], writes=[b_y], chan="yout")
            S.barrier()

        if dbg:
            with scope() as ls:
                dt_ = sbl(ls, "dbgt", [128, 4096], F32); b_dt = Buf("dbgt")
                dtb = sbl(ls, "dbgtb", [128, 4096], BF16)
                if "KcT" in dbg_out:
                    S.op("dve", lambda: nc.vector.tensor_copy(out=dt_[0:71, 0:4 * NCC * 128], in_=KcT[:].rearrange("p a b -> p (a b)")), reads=[b_KcT], writes=[b_dt])
                    S.dma("sp", dbg_out["KcT"][:, :], dt_[0:71, 0:4 * NCC * 128], reads=[b_dt], writes=[Buf("o1")])
                if "qA" in dbg_out:
                    S.dma("sp", dtb[0:64, 0:SO], qA_d[5, :, :], writes=[b_dt])
                    S.op("dve", lambda: nc.vector.tensor_copy(out=dt_[0:64, 0:SO], in_=dtb[0:64, 0:SO]), reads=[b_dt], writes=[b_dt])
                    S.dma("sp", dbg_out["qA"][:, :], dt_[0:64, 0:SO], reads=[b_dt], writes=[Buf("o2")])
                if "R" in dbg_out:
                    S.dma("sp", dtb[:, 0:1024], R_d[NO - 1].rearrange("p a b -> p (a b)"), writes=[b_dt])
                    S.op("dve", lambda: nc.vector.tensor_copy(out=dt_[:, 0:1024], in_=dtb[:, 0:1024]), reads=[b_dt], writes=[b_dt])
                    S.dma("sp", dbg_out["R"][:, :], dt_[:, 0:1024], reads=[b_dt], writes=[Buf("o3")])
                S.barrier()

        if dbg and ("omix" in dbg_out or "x1" in dbg_out):
            with scope() as ls:
                tb = sbl(ls, "tapb", [128, D], BF16); tf = sbl(ls, "tapf", [128, D], F32); b_tap = Buf("tap")
                for m in range(NO):
                    sl = slice(m * 128, (m + 1) * 128)
                    if "omix" in dbg_out:
                        S.dma("sp", tb[:], omix_d[sl, :], writes=[b_tap])
                        S.op("dve", lambda: nc.vector.tensor_copy(out=tf[:], in_=tb[:]), reads=[b_tap], writes=[b_tap])
                        S.dma("sp", dbg_out["omix"][sl, :], tf[:], reads=[b_tap], writes=[b_tap])
                    if "x1" in dbg_out:
                        S.dma("sp", tf[:], x1_d[sl, :], writes=[b_tap])
                        S.dma("sp", dbg_out["x1"][sl, :], tf[:], reads=[b_tap], writes=[b_tap])
        S.barrier()
        S.finish_all()
        print("ninst", S.ninst, "nsem", S.nsem)
    return nc


def core_inputs(inp, b, j, SEQ):
    NT = SEQ // 128
    xb = np.ascontiguousarray(inp["x"][b])
    own = np.ascontiguousarray(xb.reshape(NT // 2, 2, 128, D)[:, j].reshape(SEQ // 2, D))
    m = {
        "x_all": xb, "x_own": own,
        "c_in": np.ascontiguousarray(inp["c"][b].reshape(16, 128).T),
        "w_ada": inp["w_ada"][0], "b_ada": inp["b_ada"][0].reshape(1, -1),
        "g_mix": inp["g_norm_mix"][0].reshape(1, -1), "g_ffn": inp["g_norm_ffn"][0].reshape(1, -1),
        "g_fin": inp["g_norm_final"].reshape(1, -1), "w_in": inp["w_in"][0],
        "pe_k": inp["cmp_pe_k"][0], "pe_v": inp["cmp_pe_v"][0],
        "ck_w1": inp["cmp_k_w1"][0], "ck_w2": inp["cmp_k_w2"][0], "cv_w1": inp["cmp_v_w1"][0], "cv_w2": inp["cmp_v_w2"][0],
        "g_nsa": inp["g_nsa_out"][0].reshape(1, -1), "g_ret": inp["g_ret_out"][0].reshape(1, -1),
        "w_out": inp["w_out"][0], "w_q": inp["peer_w_q"][0], "subk": inp["peer_sub_keys"][0].reshape(16, 128, 128),
        "U": inp["peer_u"][0], "V": inp["peer_v"][0],
    }
    cs = host_consts(SEQ, j)
    for k in ("ident", "kaug", "caug", "qaug", "cm1", "vmask", "amask", "overlap", "eall", "cmab", "wmask",
              "decT", "kdec", "qdecB", "jsel"):
        m[k] = np.ascontiguousarray(cs[k])
    return {k: np.ascontiguousarray(np.asarray(v)) for k, v in m.items()}


def kernel(**inputs):
    inp = {k: np.asarray(v) for k, v in inputs.items()}
    B, SEQ, _ = inp["x"].shape
    nc = build(SEQ)
    maps = [core_inputs(inp, b, j, SEQ) for b in range(B) for j in range(2)]
    res = run_bass_kernel_spmd(nc, maps, core_ids=list(range(2 * B)))
    out = np.zeros((B, SEQ, D), np.float32)
    NT = SEQ // 128
    for b in range(B):
        ov = out[b].reshape(NT // 2, 2, 128, D)
        for j in range(2):
            ov[:, j] = np.asarray(res.results[2 * b + j]["y"]).reshape(NT // 2, 128, D)
    return out
```

```python
import contextlib
import numpy as np
import ml_dtypes
import concourse.bass as bass
import concourse.mybir as mybir
from concourse.bass_utils import run_bass_kernel_spmd

F32 = mybir.dt.float32
BF16 = mybir.dt.bfloat16
ALU = mybir.AluOpType
AF = mybir.ActivationFunctionType
AX = mybir.AxisListType
EPOCH = 30000
D = 2048
IN_W = 6704
C_QA, C_KC, C_VC, C_KS, C_VS, C_KW, C_VW, C_GA, C_QR, C_KR, C_VR, C_GR = (
    0, 1024, 1280, 1536, 1792, 2048, 2304, 2560, 2608, 3632, 4656, 5680)
NEG = -30000.0
NEXP = 16384


class Buf:
    __slots__ = ("name", "w", "r")

    def __init__(self, name=""):
        self.name = name
        self.w = None
        self.r = {}


class Sync:
    def __init__(self, nc, stack):
        self.nc = nc
        self.stack = stack
        self.engines = {"pe": nc.tensor, "dve": nc.vector, "act": nc.scalar,
                        "pool": nc.gpsimd, "sp": nc.sync}
        self.sems = {}
        self.cnt = {}
        self.epoch = {e: 0 for e in self.engines}
        self.ecount = {e: 0 for e in self.engines}
        self.known = {e: {} for e in self.engines}
        self.nsem = 0
        self.ninst = {e: 0 for e in self.engines}

    def _sem(self, key):
        if key not in self.sems:
            self.sems[key] = self.stack.enter_context(self.nc.semaphore("s%d" % self.nsem))
            self.nsem += 1
            self.cnt[key] = 0
        return self.sems[key]

    def _ekey(self, e):
        return ("E", e, self.epoch[e])

    def _wait(self, e, deps, skip_same_pe=True):
        eng = self.engines[e]
        need = {}
        for d in deps:
            if d is None:
                continue
            k, v = d
            if e == "pe" and k[0] == "E" and k[1] == "pe" and skip_same_pe:
                continue
            if k[0] == "D":
                v = self.cnt[k]
            if self.known[e].get(k, 0) >= v:
                continue
            if need.get(k, 0) < v:
                need[k] = v
        for k, v in need.items():
            eng.wait_ge(self.sems[k], v)
            self.known[e][k] = v
            self.ninst[e] += 1

    def _deps(self, reads, writes):
        deps = []
        for b in reads:
            deps.append(b.w)
        for b in writes:
            deps.append(b.w)
            for k, v in b.r.items():
                deps.append((k, v))
        return deps

    def op(self, e, fn, reads=(), writes=(), inc=True):
        self._wait(e, self._deps(reads, writes))
        ins = fn()
        self.ninst[e] += 1
        if self.ecount[e] >= EPOCH and inc:
            self.epoch[e] += 1
            self.ecount[e] = 0
        k = self._ekey(e)
        sem = self._sem(k)
        if inc:
            ins.then_inc(sem, 1)
            self.cnt[k] += 1
            self.ecount[e] += 1
            v = self.cnt[k]
        else:
            v = self.cnt[k] + 1
        for b in reads:
            if b.r.get(k, 0) < v:
                b.r[k] = v
        for b in writes:
            b.w = (k, v)
            b.r = {}
        return ins

    def dma(self, q, out, in_, reads=(), writes=(), chan=None, **kw):
        self._wait(q, self._deps(reads, writes), skip_same_pe=False)
        k = ("D", chan if chan is not None else (writes[0].name if writes else "dma"))
        sem = self._sem(k)
        ins = self.engines[q].dma_start(out=out, in_=in_, **kw)
        ins.then_inc(sem, 16)
        self.ninst[q] += 1
        self.cnt[k] += 16
        v = self.cnt[k]
        for b in reads:
            if b.r.get(k, 0) < v:
                b.r[k] = v
        for b in writes:
            b.w = (k, v)
            b.r = {}
        return ins

    def finish_all(self):
        deps = [(k, v) for k, v in self.cnt.items() if v > 0]
        self._wait("sp", deps, skip_same_pe=False)

    def barrier(self):
        for e in self.engines:
            deps = [(k, v) for k, v in self.cnt.items() if v > 0]
            self._wait(e, deps, skip_same_pe=False)


class Ring:
    def __init__(self, tiles, name):
        self.tiles = tiles
        self.bufs = [Buf("%s%d" % (name, i)) for i in range(len(tiles))]
        self.i = -1

    def next(self):
        self.i = (self.i + 1) % len(self.tiles)
        return self.tiles[self.i], self.bufs[self.i]


def bf(x):
    return np.asarray(x).astype(ml_dtypes.bfloat16)


def split3(v):
    v = np.asarray(v, np.float64)
    a = bf(v)
    r = v - a.astype(np.float64)
    b = bf(r)
    r2 = r - b.astype(np.float64)
    c = bf(r2)
    return a, b, c


def host_consts(SEQ, j):
    NT = SEQ // 128
    NO = NT // 2
    NSLC = SEQ // 64
    NCMP = SEQ // 16
    NCC = max(1, NCMP // 128)
    n_cmp = (SEQ - 32) // 16 + 1
    cs = {}
    cs["ident"] = np.eye(128, dtype=np.float32)
    pos = np.arange(SEQ)
    cs["kaug"] = bf(np.stack([pos // 128, pos // 128, pos % 128, pos % 128,
                              np.ones(SEQ), np.ones(SEQ), np.ones(SEQ)]).astype(np.float32))
    cpos = np.arange(NCC * 128) * 16 + 31
    cs["caug"] = bf(np.stack([cpos // 128, cpos // 128, cpos % 128, cpos % 128,
                              np.ones_like(cpos), np.ones_like(cpos), np.ones_like(cpos)]).astype(np.float32))
    slopes = np.exp2(-8.0 * (np.arange(16, dtype=np.float64) + 1.0) / 16).astype(np.float32).astype(np.float64)
    qaug = np.zeros((NO, 4, 7, 4, 128), dtype=ml_dtypes.bfloat16)
    for m in range(NO):
        t = 128.0 * (2 * m + j) + np.arange(128)
        for g in range(4):
            for r in range(4):
                s = slopes[4 * g + r]
                shi = bf(s)
                slo = bf(s - float(shi))
                seff = float(shi) + float(slo)
                a, b, c = split3(seff * t)
                qaug[m, g, 0, r, :] = bf(128.0 * float(shi))
                qaug[m, g, 1, r, :] = bf(128.0 * float(slo))
                qaug[m, g, 2, r, :] = shi
                qaug[m, g, 3, r, :] = slo
                qaug[m, g, 4, r, :] = bf(-a.astype(np.float32))
                qaug[m, g, 5, r, :] = bf(-b.astype(np.float32))
                qaug[m, g, 6, r, :] = bf(-c.astype(np.float32))
    cs["qaug"] = qaug.reshape(NO, 4, 7, 512)
    n = np.arange(NCC * 128)
    cm1 = np.zeros((NO, 128, NCC, 128), np.float32)
    vm = np.zeros((NO, 128, NSLC), np.float32)
    am = np.zeros((NO, 128, NSLC), np.float32)
    mb = np.arange(NSLC)
    for m in range(NO):
        t = 128 * (2 * m + j) + np.arange(128)
        ok = (n[:, None] < n_cmp) & ((16 * n[:, None] + 31) <= t[None, :])
        cm1[m] = np.where(ok, 0.0, NEG).reshape(NCC, 128, 128).transpose(1, 0, 2)
        back = (t // 64)[:, None] - mb[None, :]
        valid = back >= 0
        forced = valid & ((mb[None, :] == 0) | (back < 2))
        vm[m] = (valid & ~forced).astype(np.float32)
        am[m] = np.where(forced, 1e4 + mb[None, :], np.where(valid, 0.0, -1.0))
    cs["cm1"] = bf(cm1)
    cs["vmask"] = vm
    cs["amask"] = am
    ov = ((n[:, None] < n_cmp) & (n[:, None] >= 4 * mb[None, :] - 1) & (n[:, None] <= 4 * mb[None, :] + 3))
    cs["overlap"] = bf(ov.astype(np.float32).reshape(NCC, 128, NSLC).transpose(1, 0, 2))
    E = np.zeros((NSLC, NT, 128), np.float32)
    for c in range(NT):
        E[2 * c, c, :64] = 1.0
        E[2 * c + 1, c, 64:] = 1.0
    cs["eall"] = bf(E)
    k = np.arange(128)[:, None]
    q = np.arange(128)[None, :]
    tri = np.where(k <= q, 0.0, NEG).astype(np.float32)
    full = np.full((128, 128), NEG, np.float32)
    zero = np.zeros((128, 128), np.float32)
    cs["cmab"] = bf(np.stack([tri, full] if j == 0 else [zero, tri], axis=1))
    wm = np.zeros((128, 6, 128), np.float32)
    for u in range(6):
        d = 128 * (j + 4 - u) + q - k
        wm[:, u, :] = np.where((d >= 0) & (d < 512), 0.0, NEG)
    cs["wmask"] = bf(wm)
    lg = np.log1p(-np.exp2(-5.0 - np.arange(8, dtype=np.float64)))
    p = np.arange(128, dtype=np.float64)
    dec = np.zeros((128, 8, 128), np.float32)
    for h in range(8):
        diff = p[None, :] - p[:, None]
        dec[:, h, :] = np.where(diff >= 0, np.exp(lg[h] * np.maximum(diff, 0.0)), 0.0) * (128.0 ** -0.5)
    cs["decT"] = dec
    cs["kdec"] = np.stack([np.exp(lg[h] * (127.0 - p)) * (128.0 ** -0.5) for h in range(8)], axis=1).astype(np.float32)
    qd = np.stack([np.exp(lg[h] * (p + 1.0)) for h in range(8)], axis=0).astype(np.float32)
    cs["qdecB"] = np.broadcast_to(qd[None], (128, 8, 128)).copy()
    cs["cdec"] = np.exp(lg * 128.0)
    cs["jsel"] = np.full((128, 1), float(j), np.float32)
    return cs


def build(SEQ, dbg=None):
    NT = SEQ // 128
    NO = NT // 2
    NS = SEQ // 512
    NOS = NO // 4
    NSLC = SEQ // 64
    NCMP = SEQ // 16
    NCC = max(1, NCMP // 128)
    SO = SEQ // 2
    cdec = host_consts(256, 0)["cdec"] if False else np.exp(np.log1p(-np.exp2(-5.0 - np.arange(8, dtype=np.float64))) * 128.0)
    nc = bass.Bass("TRN2", target_bir_lowering=False)

    def din(name, shape, dt=F32):
        return nc.dram_tensor(name, list(shape), dt, kind="ExternalInput").ap()

    def dscr(name, shape, dt):
        return nc.dram_tensor(name, list(shape), dt).ap()

    x_all = din("x_all", [SEQ, D]); x_own = din("x_own", [SO, D]); c_in = din("c_in", [128, 16])
    w_ada = din("w_ada", [D, 6 * D]); b_ada = din("b_ada", [1, 6 * D])
    g_mix = din("g_mix", [1, D]); g_ffn = din("g_ffn", [1, D]); g_fin = din("g_fin", [1, D])
    w_in = din("w_in", [D, IN_W])
    pe_k = din("pe_k", [32, 64]); pe_v = din("pe_v", [32, 64])
    ck_w1 = din("ck_w1", [2048, 128]); ck_w2 = din("ck_w2", [128, 64])
    cv_w1 = din("cv_w1", [2048, 128]); cv_w2 = din("cv_w2", [128, 64])
    g_nsa = din("g_nsa", [1, 1024]); g_ret = din("g_ret", [1, 1024])
    w_out = din("w_out", [D, D]); w_q = din("w_q", [D, D]); subk = din("subk", [16, 128, 128])
    U = din("U", [NEXP, D]); V = din("V", [NEXP, D])
    h_ident = din("ident", [128, 128]); h_kaug = din("kaug", [7, SEQ], BF16); h_caug = din("caug", [7, NCC * 128], BF16)
    h_qaug = din("qaug", [NO, 4, 7, 512], BF16); h_cm1 = din("cm1", [NO, 128, NCC, 128], BF16)
    h_vmask = din("vmask", [NO, 128, NSLC]); h_amask = din("amask", [NO, 128, NSLC])
    h_overlap = din("overlap", [128, NCC, NSLC], BF16); h_eall = din("eall", [NSLC, NT, 128], BF16)
    h_cmab = din("cmab", [128, 2, 128], BF16); h_wmask = din("wmask", [128, 6, 128], BF16)
    h_decT = din("decT", [128, 8, 128]); h_kdec = din("kdec", [128, 8]); h_qdecB = din("qdecB", [128, 8, 128])
    h_jsel = din("jsel", [128, 1])
    y = nc.dram_tensor("y", [SO, D], F32, kind="ExternalOutput").ap()
    dbg_out = {}
    if dbg:
        for nm, shp in dbg.items():
            dbg_out[nm] = nc.dram_tensor("dbg_" + nm, list(shp), F32, kind="ExternalOutput").ap()

    w_in_bf = dscr("w_in_bf", [D, IN_W], BF16); w_out_bf = dscr("w_out_bf", [D, D], BF16); w_q_bf = dscr("w_q_bf", [D, D], BF16)
    UT_bf = dscr("UT_bf", [D, NEXP], BF16); V_bf = dscr("V_bf", [NEXP, D], BF16)
    kS_d = dscr("kS_d", [4, 64, SEQ], BF16); kW_d = dscr("kW_d", [4, 64, SEQ], BF16)
    vS_d = dscr("vS_d", [SEQ, 256], BF16); vW_d = dscr("vW_d", [SEQ, 256], BF16)
    qA_d = dscr("qA_d", [16, 64, SO], BF16); gA_d = dscr("gA_d", [SO, 48], F32)
    qR_d = dscr("qR_d", [8, 128, SO], BF16); kR_d = dscr("kR_d", [8, 128, SO], BF16)
    vR_d = dscr("vR_d", [SO, 1024], BF16); gR_d = dscr("gR_d", [SO, 1024], F32)
    R_d = dscr("R_d", [NO, 128, 8, 128], BF16)
    omix_d = dscr("omix_d", [SO, D], BF16)
    x1_d = dscr("x1_d", [SO, D], F32); h2T_d = dscr("h2T_d", [16, 128, SO], BF16)
    sc_d = dscr("sc_d", [SO, 16, 128], F32)

    with contextlib.ExitStack() as st:
        S = Sync(nc, st)

        def sbp(name, shape, dt):
            return st.enter_context(nc.sbuf_tensor("t_" + name, list(shape), dt))

        ident_f = sbp("ident_f", [128, 128], F32); b_ident = Buf("ident")
        ident_b = sbp("ident_b", [128, 128], BF16)
        gate1B = sbp("gate1B", [128, D], F32); gate2B = sbp("gate2B", [128, D], F32); b_gate = Buf("gate")
        modT = sbp("modT", [128, 4, 16], F32); b_modT = Buf("modT")
        KcT = sbp("KcT", [71, 4, NCC * 128], BF16); b_KcT = Buf("KcT")
        VcA = sbp("VcA", [128, NCC, 4, 65 + NSLC], BF16); b_VcA = Buf("VcA")
        skT = sbp("skT", [128, 16, 128], BF16); b_skT = Buf("skT")
        jsel = sbp("jsel", [128, 1], F32); b_jsel = Buf("jsel")
        thr_all = sbp("thr_all", [128, NO, 8], F32)
        lnz_all = sbp("lnz_all", [128, NO, 8], F32)
        b_thr = [Buf("thr%d" % i) for i in range(NO)]

        S.dma("sp", ident_f[:], h_ident[:, :], writes=[b_ident])
        S.op("dve", lambda: nc.vector.tensor_copy(out=ident_b[:], in_=ident_f[:]), reads=[b_ident], writes=[b_ident])
        S.dma("sp", jsel[:], h_jsel[:, :], writes=[b_jsel])

        def scope():
            return contextlib.ExitStack()

        def sbl(ls, name, shape, dt):
            return ls.enter_context(nc.sbuf_tensor("t_" + name, list(shape), dt))

        def psl(ls, name, shape, dt):
            return ls.enter_context(nc.psum_tensor("p_" + name, list(shape), dt))

        def ring(ls, name, n, shape, dt):
            return Ring([sbl(ls, "%s%d" % (name, i), shape, dt) for i in range(n)], name)

        def pring(ls, name, n, shape, dt):
            return Ring([psl(ls, "%s%d" % (name, i), shape, dt) for i in range(n)], name)

        def mmg(out_ap, pairs, rbufs, wbuf):
            n = len(pairs)
            for i, (l, r) in enumerate(pairs):
                S.op("pe", lambda: nc.tensor.matmul(out=out_ap, lhsT=l, rhs=r, start=(i == 0), stop=(i == n - 1)),
                     reads=rbufs, writes=[wbuf], inc=(i == n - 1))

        cast_rr = [0]

        def cast(out_ap, in_ap, rb, wb, psum=False):
            e = ("dve", "act")[cast_rr[0] % 2] if psum else ("dve", "pool", "act")[cast_rr[0] % 3]
            cast_rr[0] += 1
            if e == "act":
                S.op("act", lambda: nc.scalar.copy(out=out_ap, in_=in_ap), reads=rb, writes=wb)
            elif e == "dve":
                S.op("dve", lambda: nc.vector.tensor_copy(out=out_ap, in_=in_ap), reads=rb, writes=wb)
            else:
                S.op("pool", lambda: nc.gpsimd.tensor_copy(out=out_ap, in_=in_ap), reads=rb, writes=wb)

        def rstd_from_ss(rstd_ap, ss_ap, n, rb, wb):
            S.op("dve", lambda: nc.vector.tensor_scalar(out=rstd_ap, in0=ss_ap, scalar1=1.0 / n, scalar2=1e-6,
                                                        op0=ALU.mult, op1=ALU.add), reads=rb, writes=wb)
            S.op("act", lambda: nc.scalar.activation(out=rstd_ap, in_=rstd_ap, func=AF.Sqrt), reads=wb, writes=wb)
            S.op("dve", lambda: nc.vector.reciprocal(out=rstd_ap, in_=rstd_ap), reads=wb, writes=wb)

        with scope() as ls:
            cin_t = sbl(ls, "cin_t", [128, 16], F32); b_cin = Buf("cin")
            cb = sbl(ls, "cb", [128, 16, 128], F32); b_cb = Buf("cb")
            modB = sbl(ls, "modB", [128, 6, D], F32); b_modB = [Buf("modB%d" % v) for v in range(6)]
            wr = ring(ls, "wada", 3, [128, D], F32)
            btile = sbl(ls, "btile", [128, D], F32); b_bt = Buf("bt")
            gB = sbl(ls, "gB", [128, 2, D], F32); b_gB = Buf("gB")
            vecB = sbl(ls, "vecB", [128, 4, D], F32); b_vecB = Buf("vecB")
            psA = pring(ls, "psA", 8, [128, 512], F32)
            S.dma("sp", cin_t[:], c_in[:, :], writes=[b_cin])
            S.op("act", lambda: nc.scalar.activation(out=cin_t[:], in_=cin_t[:], func=AF.Silu), reads=[b_cin], writes=[b_cin])
            S.op("dve", lambda: nc.vector.tensor_copy(out=cb[:], in_=cin_t[:].unsqueeze(2).to_broadcast([128, 16, 128])),
                 reads=[b_cin], writes=[b_cb])
            for v in range(6):
                banks = [psA.next() for _ in range(4)]
                for k in range(16):
                    wt, bw = wr.next()
                    S.dma("sp" if k % 2 == 0 else "pool", wt[:], w_ada[k * 128:(k + 1) * 128, v * D:(v + 1) * D], writes=[bw])
                    for n4 in range(4):
                        pt, bp = banks[n4]
                        S.op("pe", lambda: nc.tensor.matmul(out=pt[:], lhsT=cb[:, k, :], rhs=wt[:, n4 * 512:(n4 + 1) * 512],
                                                            start=(k == 0), stop=(k == 15)),
                             reads=[b_cb, bw], writes=[bp], inc=(k == 15 or n4 == 3))
                S.dma("sp", btile[:], b_ada[0:1, v * D:(v + 1) * D].partition_broadcast(128), writes=[b_bt])
                for n4 in range(4):
                    pt, bp = banks[n4]
                    S.op("dve", lambda: nc.vector.tensor_tensor(out=modB[:, v, n4 * 512:(n4 + 1) * 512], in0=pt[:],
                                                                in1=btile[:, n4 * 512:(n4 + 1) * 512], op=ALU.add),
                         reads=[bp, b_bt], writes=[b_modB[v]])
            S.dma("sp", gB[:, 0, :], g_mix[0:1, :].partition_broadcast(128), writes=[b_gB])
            S.dma("sp", gB[:, 1, :], g_ffn[0:1, :].partition_broadcast(128), writes=[b_gB])
            S.op("dve", lambda: nc.vector.scalar_tensor_tensor(out=vecB[:, 0, :], in0=modB[:, 1, :], scalar=1.0, in1=gB[:, 0, :],
                                                               op0=ALU.add, op1=ALU.mult), reads=[b_modB[1], b_gB], writes=[b_vecB])
            S.op("dve", lambda: nc.vector.tensor_copy(out=vecB[:, 1, :], in_=modB[:, 0, :]), reads=[b_modB[0]], writes=[b_vecB])
            S.op("dve", lambda: nc.vector.scalar_tensor_tensor(out=vecB[:, 2, :], in0=modB[:, 4, :], scalar=1.0, in1=gB[:, 1, :],
                                                               op0=ALU.add, op1=ALU.mult), reads=[b_modB[4], b_gB], writes=[b_vecB])
            S.op("dve", lambda: nc.vector.tensor_copy(out=vecB[:, 3, :], in_=modB[:, 3, :]), reads=[b_modB[3]], writes=[b_vecB])
            S.op("pool", lambda: nc.gpsimd.tensor_copy(out=gate1B[:], in_=modB[:, 2, :]), reads=[b_modB[2]], writes=[b_gate])
            S.op("pool", lambda: nc.gpsimd.tensor_copy(out=gate2B[:], in_=modB[:, 5, :]), reads=[b_modB[5]], writes=[b_gate])
            for vi in range(4):
                for kc in range(16):
                    pt, bp = psA.next()
                    S.op("pe", lambda: nc.tensor.transpose(out=pt[:, 0:128], in_=vecB[:, vi, kc * 128:(kc + 1) * 128], identity=ident_f[:]),
                         reads=[b_vecB, b_ident], writes=[bp])
                    S.op("dve", lambda: nc.vector.tensor_copy(out=modT[:, vi, kc:kc + 1], in_=pt[:, 0:1]), reads=[bp], writes=[b_modT])
            S.barrier()

        b_wscr = Buf("wscr")
        with scope() as ls:
            fr = ring(ls, "cf", 2, [128, IN_W], F32)
            br = ring(ls, "cbf", 2, [128, IN_W], BF16)
            psT = pring(ls, "psTa", 2, [128, 1024], BF16)
            ust = ring(ls, "ust", 2, [128, 16, 512], BF16)
            for k in range(16):
                ft, bfb = fr.next(); bt_, bbb = br.next()
                S.dma("sp", ft[:], w_in[k * 128:(k + 1) * 128, :], writes=[bfb])
                cast(bt_[:], ft[:], [bfb], [bbb])
                S.dma("pool", w_in_bf[k * 128:(k + 1) * 128, :], bt_[:], reads=[bbb], writes=[b_wscr], chan="wscr")
            for (src, dst) in ((w_out, w_out_bf), (w_q, w_q_bf)):
                for k in range(0, 16, 2):
                    ft, bfb = fr.next(); bt_, bbb = br.next()
                    S.dma("sp", ft[:, 0:2 * D].rearrange("p (a n) -> p a n", a=2),
                          src[k * 128:(k + 2) * 128, :].rearrange("(a p) n -> p a n", p=128), writes=[bfb])
                    cast(bt_[:, 0:2 * D], ft[:, 0:2 * D], [bfb], [bbb])
                    S.dma("pool", dst[k * 128:(k + 2) * 128, :].rearrange("(a p) n -> p a n", p=128),
                          bt_[:, 0:2 * D].rearrange("p (a n) -> p a n", a=2), reads=[bbb], writes=[b_wscr], chan="wscr")
            for e0 in range(0, 128, 2):
                ft, bfb = fr.next(); bt_, bbb = br.next()
                S.dma("sp", ft[:, 0:2 * D].rearrange("p (a n) -> p a n", a=2),
                      V[e0 * 128:(e0 + 2) * 128, :].rearrange("(a p) n -> p a n", p=128), writes=[bfb])
                cast(bt_[:, 0:2 * D], ft[:, 0:2 * D], [bfb], [bbb])
                S.dma("pool", V_bf[e0 * 128:(e0 + 2) * 128, :].rearrange("(a p) n -> p a n", p=128),
                      bt_[:, 0:2 * D].rearrange("p (a n) -> p a n", a=2), reads=[bbb], writes=[b_wscr], chan="wscr")
            for e4 in range(32):
                us, bus = ust.next()
                for a in range(0, 4, 2):
                    ft, bfb = fr.next(); bt_, bbb = br.next()
                    e0 = e4 * 4 + a
                    S.dma("sp", ft[:, 0:2 * D].rearrange("p (a n) -> p a n", a=2),
                          U[e0 * 128:(e0 + 2) * 128, :].rearrange("(a p) n -> p a n", p=128), writes=[bfb])
                    cast(bt_[:, 0:2 * D], ft[:, 0:2 * D], [bfb], [bbb])
                    for a2 in range(2):
                        for half in range(2):
                            pt, bp = psT.next()
                            for k8 in range(8):
                                kc = half * 8 + k8
                                S.op("pe", lambda: nc.tensor.transpose(out=pt[:, k8 * 128:(k8 + 1) * 128],
                                                                       in_=bt_[:, a2 * D + kc * 128:a2 * D + (kc + 1) * 128], identity=ident_b[:]),
                                     reads=[bbb, b_ident], writes=[bp], inc=(k8 == 7))
                            ee = (a + a2) * 128
                            cast(us[:, half * 8:(half + 1) * 8, ee:ee + 128], pt[:].rearrange("p (k e) -> p k e", k=8), [bp], [bus], psum=True)
                S.dma("pool", UT_bf.rearrange("(k p) e -> p k e", p=128)[:, :, e4 * 512:(e4 + 1) * 512], us[:],
                      reads=[bus], writes=[b_wscr], chan="wscr")
            ft, bfb = fr.next(); bt_, bbb = br.next()
            S.dma("sp", ft[:, 0:2048].rearrange("p (a d) -> p a d", a=16), subk.rearrange("a k d -> k a d"), writes=[bfb])
            cast(bt_[:, 0:2048], ft[:, 0:2048], [bfb], [bbb])
            for half in range(2):
                pt, bp = psT.next()
                for k8 in range(8):
                    a = half * 8 + k8
                    S.op("pe", lambda: nc.tensor.transpose(out=pt[:, k8 * 128:(k8 + 1) * 128], in_=bt_[:, a * 128:(a + 1) * 128], identity=ident_b[:]),
                         reads=[bbb, b_ident], writes=[bp], inc=(k8 == 7))
                cast(skT[:, half * 8:(half + 1) * 8, :], pt[:].rearrange("p (k e) -> p k e", k=8), [bp], [b_skT], psum=True)
            S.barrier()

        if dbg and "modT" in dbg_out:
            S.dma("sp", dbg_out["modT"][:, :], modT[:].rearrange("p a b -> p (a b)"), reads=[b_modT], writes=[Buf("dbgo")])


        b_y = Buf("y")
        b_scr = Buf("scr")
        gelu_c = 1.5957691216057308

        def gelu_tanh(out_ap, x_ap, tmp_ap, rb, wb, tb, shape_bias=None):
            S.op("dve", lambda: nc.vector.tensor_tensor(out=tmp_ap, in0=x_ap, in1=x_ap, op=ALU.mult), reads=rb, writes=[tb])
            S.op("dve", lambda: nc.vector.tensor_scalar(out=tmp_ap, in0=tmp_ap, scalar1=0.044715, scalar2=1.0, op0=ALU.mult, op1=ALU.add),
                 reads=[tb], writes=[tb])
            S.op("dve", lambda: nc.vector.tensor_tensor(out=tmp_ap, in0=tmp_ap, in1=x_ap, op=ALU.mult), reads=rb + [tb], writes=[tb])
            S.op("act", lambda: nc.scalar.activation(out=tmp_ap, in_=tmp_ap, func=AF.Sigmoid, scale=gelu_c), reads=[tb], writes=[tb])
            S.op("dve", lambda: nc.vector.tensor_tensor(out=out_ap, in0=tmp_ap, in1=x_ap, op=ALU.mult), reads=rb + [tb], writes=wb)

        w_in_v = w_in_bf.rearrange("(k p) n -> p k n", p=128)

        with scope() as ls:
            psT = pring(ls, "psT", 2, [128, 1024], BF16)
            psM = pring(ls, "psM", 4, [128, 512], F32)
            psK = pring(ls, "psK", 2, [128, 512], F32)
            W1s = sbl(ls, "W1s", [128, 2, 32, 128], BF16); b_W1s = Buf("W1s")
            W2s = sbl(ls, "W2s", [128, 2, 64], BF16); b_W2 = Buf("W2")
            peTb = sbl(ls, "peTb", [64, 2, 32], BF16); b_pe = Buf("pe")
            peb = sbl(ls, "peb", [128, 2], F32); b_peb = Buf("peb")
            with scope() as ls2:
                W1f = sbl(ls2, "W1f", [128, 32, 128], F32); b_W1f = Buf("W1f")
                W2f = sbl(ls2, "W2f", [128, 2, 64], F32)
                peT = sbl(ls2, "peT", [64, 2, 32], F32)
                for kv, (w1, w2, pe) in enumerate(((ck_w1, ck_w2, pe_k), (cv_w1, cv_w2, pe_v))):
                    for half in range(2):
                        S.dma("sp", W1f[half * 64:(half + 1) * 64, :, :], w1.rearrange("(l d) h -> d l h", d=64), writes=[b_W1f])
                    S.op("dve", lambda: nc.vector.tensor_copy(out=W1s[:, kv, :, :], in_=W1f[:]), reads=[b_W1f], writes=[b_W1s])
                    S.dma("sp", W2f[:, kv, :], w2[:, :], writes=[b_W2])
                    S.dma("sp", peT[:, kv, :], pe.rearrange("l d -> d l"), writes=[b_pe], allow_slow_non_contiguous=True)
                S.op("dve", lambda: nc.vector.tensor_copy(out=W2s[:], in_=W2f[:]), reads=[b_W2], writes=[b_W2])
                S.op("dve", lambda: nc.vector.tensor_copy(out=peTb[:], in_=peT[:]), reads=[b_pe], writes=[b_pe])
                for kv in range(2):
                    pt, bp = psM.next()
                    mmg(pt[:, 0:1], [(W1s[0:64, kv, l, :], peTb[:, kv, l:l + 1]) for l in range(32)], [b_W1s, b_pe], bp)
                    S.op("dve", lambda: nc.vector.tensor_copy(out=peb[:, kv:kv + 1], in_=pt[:, 0:1]), reads=[bp], writes=[b_peb])
                S.barrier()
            xr = ring(ls, "xt", 2, [128, D], F32)
            junk = sbl(ls, "junk", [128, D], BF16); b_junk = Buf("junk")
            ssr = ring(ls, "ss", 2, [128, 2], F32)
            xnr = ring(ls, "xn", 2, [128, D], BF16)
            hTr = ring(ls, "hT", 2, [128, 16, 512], BF16)
            wr = ring(ls, "wblk", 3, [128, 16, 512], BF16)
            stg = ring(ls, "stg", 4, [128, 512], BF16)
            stgf = ring(ls, "stgf", 2, [128, 512], F32)
            kcb = sbl(ls, "kcb", [128, 2, 2, 528], BF16); b_kcb = Buf("kcb")
            hid = sbl(ls, "hid", [128, 32], BF16); hidx = sbl(ls, "hidx", [128, 32], F32); hidt = sbl(ls, "hidt", [128, 32], F32)
            b_hid = Buf("hid"); b_hidx = Buf("hidx"); b_hidt = Buf("hidt")
            VcT = sbl(ls, "VcT", [64, 4, NCC * 128], BF16); b_VcT = Buf("VcT")
            kdec = sbl(ls, "kdec", [128, 8], F32); b_kdec = Buf("kdec")
            krd = sbl(ls, "krd", [128, 4, 1024], BF16); b_krd = Buf("krd")
            vrb = sbl(ls, "vrb", [128, 4, 1024], BF16); b_vrb = Buf("vrb")
            Rst = sbl(ls, "Rst", [128, 2, 8, 128], F32); b_Rst = Buf("Rst")
            Rown = sbl(ls, "Rown", [128, 8, 128], F32); Rownb = sbl(ls, "Rownb", [128, 8, 128], BF16); b_Rown = Buf("Rown")
            S.dma("sp", kdec[:], h_kdec[:, :], writes=[b_kdec])
            S.op("dve", lambda: nc.vector.memset(kcb[:], 0.0), writes=[b_kcb])
            S.op("dve", lambda: nc.vector.memset(Rst[:], 0.0), writes=[b_Rst])
            S.op("pool", lambda: nc.gpsimd.memset(KcT[:], 0.0), writes=[b_KcT])
            S.op("pool", lambda: nc.gpsimd.memset(VcT[:], 0.0), writes=[b_VcT])
            S.dma("sp", KcT[64:71, 0, :], h_caug[:, :], writes=[b_KcT])
            for g in range(1, 4):
                S.dma("sp", KcT[64:71, g, :], h_caug[:, :], writes=[b_KcT])

            def norm_T(xsrc, vi, hT, bh, col0):
                xt, bx = xr.next()
                S.dma("sp", xt[:], xsrc, writes=[bx])
                ss, bs = ssr.next()
                S.op("act", lambda: nc.scalar.activation(out=junk[:], in_=xt[:], func=AF.Square, accum_out=ss[:, 0:1]),
                     reads=[bx], writes=[b_junk, bs])
                rstd_from_ss(ss[:, 1:2], ss[:, 0:1], D, [bs], [bs])
                xn, bn = xnr.next()
                S.op("act", lambda: nc.scalar.activation(out=xn[:], in_=xt[:], func=AF.Copy, scale=ss[:, 1:2]), reads=[bx, bs], writes=[bn])
                for half in range(2):
                    pt, bp = psT.next()
                    for k8 in range(8):
                        kc = half * 8 + k8
                        S.op("pe", lambda: nc.tensor.transpose(out=pt[:, k8 * 128:(k8 + 1) * 128], in_=xn[:, kc * 128:(kc + 1) * 128],
                                                               identity=ident_b[:]), reads=[bn, b_ident], writes=[bp], inc=(k8 == 7))
                    dst = hT[:, half * 8:(half + 1) * 8, col0:col0 + 128]
                    S.op("dve", lambda: nc.vector.tensor_tensor(out=dst, in0=pt[:].rearrange("p (k t) -> p k t", k=8),
                                                                in1=modT[:, vi, half * 8:(half + 1) * 8].unsqueeze(2).to_broadcast([128, 8, 128]),
                                                                op=ALU.mult), reads=[bp, b_modT], writes=[bh])
                    S.op("dve", lambda: nc.vector.tensor_tensor(out=dst, in0=dst,
                                                                in1=modT[:, vi + 1, half * 8:(half + 1) * 8].unsqueeze(2).to_broadcast([128, 8, 128]),
                                                                op=ALU.add), reads=[bh, b_modT], writes=[bh])

            def load_w(c0, ncols):
                wt, bw = wr.next()
                S.dma("pool", wt[:, :, 0:ncols], w_in_v[:, :, c0:c0 + ncols], reads=[b_wscr], writes=[bw])
                return wt, bw

            def fm(wt, bw, col, M, hT, bh):
                pt, bp = psM.next()
                mmg(pt[0:M, :], [(wt[:, k, col:col + M], hT[:, k, :]) for k in range(16)], [bw, bh], bp)
                return pt, bp

            def tm(wt, bw, col, N, hT, bh, tile):
                pt, bp = psM.next()
                mmg(pt[:, 0:N], [(hT[:, k, tile * 128:(tile + 1) * 128], wt[:, k, col:col + N]) for k in range(16)], [bh, bw], bp)
                return pt, bp

            ev_rr = [0]

            def evac(out_ap, in_ap, rb, wb, scale=None):
                e = ("dve", "act")[ev_rr[0] % 2]
                ev_rr[0] += 1
                if e == "act":
                    if scale is None:
                        S.op("act", lambda: nc.scalar.copy(out=out_ap, in_=in_ap), reads=rb, writes=wb)
                    else:
                        S.op("act", lambda: nc.scalar.mul(out=out_ap, in_=in_ap, mul=scale), reads=rb, writes=wb)
                else:
                    if scale is None:
                        S.op("dve", lambda: nc.vector.tensor_copy(out=out_ap, in_=in_ap), reads=rb, writes=wb)
                    else:
                        S.op("dve", lambda: nc.vector.tensor_scalar(out=out_ap, in0=in_ap, scalar1=scale, scalar2=None, op0=ALU.mult),
                             reads=rb, writes=wb)

            for s in range(NS):
                hT, bh = hTr.next()
                for tt in range(4):
                    norm_T(x_all[(s * 4 + tt) * 128:(s * 4 + tt + 1) * 128, :], 0, hT, bh, tt * 128)
                wt, bw = load_w(C_KC, 512)
                S.op("dve", lambda: nc.vector.tensor_copy(out=kcb[:, :, :, 0:16], in_=kcb[:, :, :, 512:528]), reads=[b_kcb], writes=[b_kcb])
                for kv in range(2):
                    for gp in range(2):
                        pt, bp = fm(wt, bw, kv * 256 + gp * 128, 128, hT, bh)
                        evac(kcb[:, kv, gp, 16:528], pt[:, :], [bp], [b_kcb])
                i0 = 1 if s == 0 else 0
                ni = 32 - i0
                n0 = 32 * s - 1 + i0
                for kv in range(2):
                    for g in range(4):
                        base = (g % 2) * 64
                        pt, bp = psM.next()
                        mmg(pt[:, 0:ni], [(W1s[base:base + 64, kv, l, :],
                                            kcb[base:base + 64, kv, g // 2, 16 * i0 + l:16 * i0 + l + 16 * (ni - 1) + 1:16]) for l in range(32)],
                            [b_W1s, b_kcb], bp)
                        S.op("act", lambda: nc.scalar.activation(out=hidx[:, 0:ni], in_=pt[:, 0:ni], func=AF.Identity, bias=peb[:, kv:kv + 1]),
                             reads=[bp, b_peb], writes=[b_hidx])
                        gelu_tanh(hid[:, 0:ni], hidx[:, 0:ni], hidt[:, 0:ni], [b_hidx], [b_hid], b_hidt)
                        pt2, bp2 = psM.next()
                        mmg(pt2[0:64, 0:ni], [(W2s[:, kv, :], hid[:, 0:ni])], [b_W2, b_hid], bp2)
                        if kv == 0:
                            evac(KcT[0:64, g, n0:n0 + ni], pt2[0:64, 0:ni], [bp2], [b_KcT])
                        else:
                            evac(VcT[0:64, g, n0:n0 + ni], pt2[0:64, 0:ni], [bp2], [b_VcT])
                for (c0, kd, vd) in ((C_KS, kS_d, vS_d), (C_KW, kW_d, vW_d)):
                    wt, bw = load_w(c0, 512)
                    for g in range(4):
                        pt, bp = fm(wt, bw, g * 64, 64, hT, bh)
                        sg, bsg = stg.next()
                        evac(sg[0:64, :], pt[0:64, :], [bp], [bsg])
                        S.dma("sp", kd[g, :, s * 512:(s + 1) * 512], sg[0:64, :], reads=[bsg], writes=[b_scr], chan="scr")
                    for tt in range(4):
                        pt, bp = tm(wt, bw, 256, 256, hT, bh, tt)
                        sg, bsg = stg.next()
                        evac(sg[:, 0:256], pt[:, 0:256], [bp], [bsg])
                        S.dma("sp", vd[(s * 4 + tt) * 128:(s * 4 + tt + 1) * 128, :], sg[:, 0:256], reads=[bsg], writes=[b_scr], chan="scr")
                wk = [load_w(C_KR, 512), load_w(C_KR + 512, 512)]
                for tt in range(4):
                    for hh in range(2):
                        pt, bp = tm(wk[hh][0], wk[hh][1], 0, 512, hT, bh, tt)
                        S.op("dve", lambda: nc.vector.tensor_tensor(out=krd[:, tt, hh * 512:(hh + 1) * 512].rearrange("p (h d) -> p h d", h=4),
                                                                    in0=pt[:].rearrange("p (h d) -> p h d", h=4),
                                                                    in1=kdec[:, hh * 4:(hh + 1) * 4].unsqueeze(2).to_broadcast([128, 4, 128]),
                                                                    op=ALU.mult), reads=[bp, b_kdec], writes=[b_krd])
                wv = [load_w(C_VR, 512), load_w(C_VR + 512, 512)]
                for tt in range(4):
                    for hh in range(2):
                        pt, bp = tm(wv[hh][0], wv[hh][1], 0, 512, hT, bh, tt)
                        evac(vrb[:, tt, hh * 512:(hh + 1) * 512], pt[:, :], [bp], [b_vrb])
                for tt in range(4):
                    ci = s * 4 + tt
                    par = ci % 2
                    for hh in range(2):
                        pk, bpk = psK.next()
                        for h4 in range(4):
                            h = hh * 4 + h4
                            S.op("pe", lambda: nc.tensor.matmul(out=pk[:, h4 * 128:(h4 + 1) * 128], lhsT=krd[:, tt, h * 128:(h + 1) * 128],
                                                                rhs=vrb[:, tt, h * 128:(h + 1) * 128], start=True, stop=True),
                                 reads=[b_krd, b_vrb], writes=[bpk], inc=(h4 == 3))
                        for h4 in range(4):
                            h = hh * 4 + h4
                            S.op("dve", lambda: nc.vector.scalar_tensor_tensor(out=Rst[:, 1 - par, h, :], in0=Rst[:, par, h, :], scalar=float(cdec[h]),
                                                                               in1=pk[:, h4 * 128:(h4 + 1) * 128], op0=ALU.mult, op1=ALU.add),
                                 reads=[b_Rst, bpk], writes=[b_Rst])
                    if par == 1:
                        pass
                    if par == 0:
                        S.op("pool", lambda: nc.gpsimd.tensor_tensor(out=Rown[:], in0=Rst[:, 1, :, :], in1=Rst[:, 0, :, :], op=ALU.subtract),
                             reads=[b_Rst], writes=[b_Rown])
                        S.op("dve", lambda: nc.vector.scalar_tensor_tensor(out=Rownb[:], in0=Rown[:], scalar=jsel[:, 0:1], in1=Rst[:, 0, :, :],
                                                                            op0=ALU.mult, op1=ALU.add), reads=[b_Rown, b_Rst, b_jsel], writes=[b_Rown])
                        S.dma("sp", R_d[ci // 2], Rownb[:], reads=[b_Rown], writes=[b_scr], chan="scr")

            S.op("dve", lambda: nc.vector.memset(VcA[:, :, :, 64:65], 1.0), writes=[b_VcA])
            for g in range(4):
                S.dma("sp", VcA[:, :, g, 65:65 + NSLC], h_overlap[:, :, :], writes=[b_VcA])
            for cc in range(NCC):
                for g in range(4):
                    pt, bp = psT.next()
                    S.op("pe", lambda: nc.tensor.transpose(out=pt[:, 0:64], in_=VcT[0:64, g, cc * 128:(cc + 1) * 128], identity=ident_b[0:64, 0:64]),
                         reads=[b_VcT, b_ident], writes=[bp])
                    evac(VcA[:, cc, g, 0:64], pt[:, 0:64], [bp], [b_VcA])

            for so in range(NOS):
                hT, bh = hTr.next()
                for tt in range(4):
                    norm_T(x_own[(so * 4 + tt) * 128:(so * 4 + tt + 1) * 128, :], 0, hT, bh, tt * 128)
                t0, t1 = so * 512, (so + 1) * 512
                for blk in range(2):
                    wt, bw = load_w(C_QA + blk * 512, 512)
                    for h8 in range(8):
                        pt, bp = fm(wt, bw, h8 * 64, 64, hT, bh)
                        sg, bsg = stg.next()
                        evac(sg[0:64, :], pt[0:64, :], [bp], [bsg], scale=0.125)
                        S.dma("sp", qA_d[blk * 8 + h8, :, t0:t1], sg[0:64, :], reads=[bsg], writes=[b_scr], chan="scr")
                for (c0, dd) in ((C_QR, qR_d), (C_KR, kR_d)):
                    for blk in range(2):
                        wt, bw = load_w(c0 + blk * 512, 512)
                        for h4 in range(4):
                            pt, bp = fm(wt, bw, h4 * 128, 128, hT, bh)
                            sg, bsg = stg.next()
                            evac(sg[:, :], pt[:, :], [bp], [bsg])
                            S.dma("sp", dd[blk * 4 + h4, :, t0:t1], sg[:, :], reads=[bsg], writes=[b_scr], chan="scr")
                for blk in range(2):
                    wt, bw = load_w(C_VR + blk * 512, 512)
                    for tt in range(4):
                        pt, bp = tm(wt, bw, 0, 512, hT, bh, tt)
                        sg, bsg = stg.next()
                        evac(sg[:, :], pt[:, :], [bp], [bsg])
                        S.dma("sp", vR_d[t0 + tt * 128:t0 + (tt + 1) * 128, blk * 512:(blk + 1) * 512], sg[:, :], reads=[bsg], writes=[b_scr], chan="scr")
                for blk in range(2):
                    wt, bw = load_w(C_GR + blk * 512, 512)
                    for tt in range(4):
                        pt, bp = tm(wt, bw, 0, 512, hT, bh, tt)
                        sf, bsf = stgf.next()
                        evac(sf[:, 0:512], pt[:, :], [bp], [bsf])
                        S.dma("sp", gR_d[t0 + tt * 128:t0 + (tt + 1) * 128, blk * 512:(blk + 1) * 512], sf[:, 0:512], reads=[bsf], writes=[b_scr], chan="scr")
                wt, bw = load_w(C_GA, 48)
                for tt in range(4):
                    pt, bp = tm(wt, bw, 0, 48, hT, bh, tt)
                    sf, bsf = stgf.next()
                    evac(sf[:, 0:48], pt[:, 0:48], [bp], [bsf])
                    S.dma("sp", gA_d[t0 + tt * 128:t0 + (tt + 1) * 128, :], sf[:, 0:48], reads=[bsf], writes=[b_scr], chan="scr")
            S.barrier()


        b_omix = Buf("omix")
        NB1 = 65 + NSLC
        with scope() as ls:
            KsT = sbl(ls, "KsT", [71, 4, SEQ], BF16); b_KsT = Buf("KsT")
            VsA = sbl(ls, "VsA", [128, NT, 4, 65], BF16); b_VsA = Buf("VsA")
            Eall = sbl(ls, "Eall", [NSLC, NT, 128], BF16); b_E = Buf("Eall")
            cmab = sbl(ls, "cmab", [128, 2, 128], BF16); wmask = sbl(ls, "wmask", [128, 6, 128], BF16); b_cm = Buf("cm")
            maskrep = sbl(ls, "maskrep", [128, 8, 4, 128], BF16); b_mr = Buf("maskrep")
            g_nsaB = sbl(ls, "g_nsaB", [128, 1024], F32); b_gn = Buf("gnsa")
            QTr = ring(ls, "QT", 1, [71, 4, 512], BF16)
            KwTr = ring(ls, "KwT", 1, [71, 4, 768], BF16)
            VwAr = ring(ls, "VwA", 1, [128, 6, 4, 65], BF16)
            cm1r = ring(ls, "cm1", 2, [128, NCC, 128], BF16)
            cm1rep = sbl(ls, "cm1rep", [128, NCC, 4, 128], BF16); b_c1r = Buf("cm1rep")
            vmr = ring(ls, "vm", 1, [128, 2, NSLC], F32)
            gar = ring(ls, "ga", 2, [128, 48], F32)
            PTr = ring(ls, "PT", 3, [128, 512], BF16)
            nsT = sbl(ls, "nsT", [NSLC, 4, 128], BF16); b_nsT = Buf("nsT")
            imp = sbl(ls, "imp", [128, NSLC], F32); impr = sbl(ls, "impr", [128, NSLC], F32); b_imp = Buf("imp"); b_impr = Buf("impr")
            m8 = sbl(ls, "m8", [128, 16], F32); b_m8 = Buf("m8")
            selb = sbl(ls, "selb", [128, NSLC], F32); negs = sbl(ls, "negs", [128, NSLC], BF16); b_sel = Buf("sel")
            rden = sbl(ls, "rden", [128, 3, 4], F32); cf = sbl(ls, "cf", [128, 3, 4], F32); b_rden = Buf("rden")
            oacc = sbl(ls, "oacc", [128, 16, 64], F32); otmp = sbl(ls, "otmp", [128, 4, 64], F32); b_oacc = Buf("oacc"); b_otmp = Buf("otmp")
            osq = sbl(ls, "osq", [128, 16, 64], F32); oss = sbl(ls, "oss", [128, 16], F32); b_osq = Buf("osq")
            onb = ring(ls, "onb", 1, [128, 1024], BF16)
            psS = pring(ls, "psS", 3, [128, 512], F32)
            psO1 = psl(ls, "psO1", [128, 4, 256], F32); b_pO1 = Buf("pO1")
            psO2 = psl(ls, "psO2", [128, 512], F32); b_pO2 = Buf("pO2")
            psO3 = psl(ls, "psO3", [128, 512], F32); b_pO3 = Buf("pO3")
            psX = psl(ls, "psX", [128, 1024], BF16); b_pX = Buf("pX")
            assert NB1 <= 256
            for g in range(4):
                S.dma("sp", KsT[0:64, g, :], kS_d[g, :, :], reads=[b_scr], writes=[b_KsT])
                S.dma("sp", KsT[64:71, g, :], h_kaug[:, :], writes=[b_KsT])
            for c4 in range(0, NT, 4):
                for g in range(4):
                    S.dma("pool", VsA[:, c4:c4 + 4, g, 0:64], vS_d[c4 * 128:(c4 + 4) * 128, g * 64:(g + 1) * 64].rearrange("(c p) d -> p c d", p=128),
                          reads=[b_scr], writes=[b_VsA])
            S.op("dve", lambda: nc.vector.memset(VsA[:, :, :, 64:65], 1.0), writes=[b_VsA])
            S.dma("sp", Eall[:], h_eall[:, :, :], writes=[b_E])
            S.dma("sp", cmab[:], h_cmab[:, :, :], writes=[b_cm])
            S.dma("sp", wmask[:], h_wmask[:, :, :], writes=[b_cm])
            S.op("dve", lambda: nc.vector.tensor_copy(out=maskrep[:, 0:2, :, :], in_=cmab[:].unsqueeze(2).to_broadcast([128, 2, 4, 128])),
                 reads=[b_cm], writes=[b_mr])
            S.op("dve", lambda: nc.vector.tensor_copy(out=maskrep[:, 2:8, :, :], in_=wmask[:].unsqueeze(2).to_broadcast([128, 6, 4, 128])),
                 reads=[b_cm], writes=[b_mr])
            S.dma("sp", g_nsaB[:], g_nsa[0:1, :].partition_broadcast(128), writes=[b_gn])
            for (vt, bv) in zip(VwAr.tiles, VwAr.bufs):
                S.op("dve", lambda: nc.vector.memset(vt[:, :, :, 64:65], 1.0), writes=[bv])

            pend = []

            def flush_pv():
                while pend:
                    (PT, bP, rhsV, bV, psO, b_pO, width, first, last) = pend.pop(0)
                    for r in range(4):
                        st_ = first and (r == 0 or (width == NB1 and r == 2))
                        S.op("pe", lambda: nc.tensor.matmul(out=psO(r), lhsT=PT[:, r * 128:(r + 1) * 128], rhs=rhsV, start=st_, stop=last,
                                                            skip_group_check=True),
                             reads=[bP, bV], writes=[b_pO], inc=(r == 3))

            def attn_chunk(pairs, rb, rhsV, bV, psO, b_pO, width, first, last):
                pS, bS = psS.next()
                mmg(pS[:], pairs, rb, bS)
                PT, bP = PTr.next()
                S.op("act", lambda: nc.scalar.activation(out=PT[:], in_=pS[:], func=AF.Exp), reads=[bS], writes=[bP])
                flush_pv()
                pend.append((PT, bP, rhsV, bV, psO, b_pO, width, first, last))

            def finish_branch(bi, g, psO, b_pO, first_branch):
                flush_pv()
                for r in range(4):
                    S.op("dve", lambda: nc.vector.tensor_scalar(out=rden[:, bi, r:r + 1], in0=psO(r)[:, 64:65], scalar1=1e-30, scalar2=None, op0=ALU.max),
                         reads=[b_pO], writes=[b_rden])
                S.op("dve", lambda: nc.vector.reciprocal(out=rden[:, bi, :], in_=rden[:, bi, :]), reads=[b_rden], writes=[b_rden])
                S.op("dve", lambda: nc.vector.tensor_tensor(out=cf[:, bi, :], in0=rden[:, bi, :],
                                                            in1=ga[:, g * 12:(g + 1) * 12].rearrange("p (r b) -> p r b", b=3)[:, :, bi], op=ALU.mult),
                     reads=[b_rden, bga], writes=[b_rden])
                for r in range(4):
                    h = g * 4 + r
                    if first_branch:
                        S.op("dve", lambda: nc.vector.tensor_scalar(out=oacc[:, h, :], in0=psO(r)[:, 0:64], scalar1=cf[:, bi, r:r + 1], scalar2=None, op0=ALU.mult),
                             reads=[b_pO, b_rden], writes=[b_oacc])
                    else:
                        S.op("dve", lambda: nc.vector.scalar_tensor_tensor(out=oacc[:, h, :], in0=psO(r)[:, 0:64], scalar=cf[:, bi, r:r + 1], in1=oacc[:, h, :],
                                                                           op0=ALU.mult, op1=ALU.add), reads=[b_pO, b_rden, b_oacc], writes=[b_oacc])

            for m in range(NO):
                QT, bQ = QTr.next()
                for g in range(4):
                    S.dma("sp", QT[0:64, g, :].rearrange("d (r q) -> d r q", r=4),
                          qA_d[4 * g:4 * g + 4, :, m * 128:(m + 1) * 128].rearrange("r d q -> d r q"), reads=[b_scr], writes=[bQ])
                S.dma("sp", QT[64:71, :, :], h_qaug[m].rearrange("g a q -> a g q"), writes=[bQ])
                c0 = max(0, 2 * m - 4)
                u0 = c0 - (2 * m - 4)
                nch = 2 * m + 2 - c0
                KwT, bKw = KwTr.next()
                VwA, bVw = VwAr.next()
                for g in range(4):
                    S.dma("pool", KwT[0:64, g, u0 * 128:(u0 + nch) * 128], kW_d[g, :, c0 * 128:(c0 + nch) * 128], reads=[b_scr], writes=[bKw])
                    S.dma("pool", KwT[64:71, g, u0 * 128:(u0 + nch) * 128], h_kaug[:, c0 * 128:(c0 + nch) * 128], writes=[bKw])
                for g in range(4):
                    S.dma("pool", VwA[:, u0:u0 + nch, g, 0:64], vW_d[c0 * 128:(c0 + nch) * 128, g * 64:(g + 1) * 64].rearrange("(c p) d -> p c d", p=128),
                          reads=[b_scr], writes=[bVw])
                cm1, bc1 = cm1r.next()
                S.dma("sp", cm1[:], h_cm1[m], writes=[bc1])
                S.op("pool", lambda: nc.gpsimd.tensor_copy(out=cm1rep[:], in_=cm1[:].unsqueeze(2).to_broadcast([128, NCC, 4, 128])),
                     reads=[bc1], writes=[b_c1r])
                vm, bvm = vmr.next()
                S.dma("sp", vm[:, 0, :], h_vmask[m], writes=[bvm])
                S.dma("sp", vm[:, 1, :], h_amask[m], writes=[bvm])
                ga, bga = gar.next()
                S.dma("sp", ga[:], gA_d[m * 128:(m + 1) * 128, :], reads=[b_scr], writes=[bga])
                S.op("act", lambda: nc.scalar.activation(out=ga[:], in_=ga[:], func=AF.Sigmoid), reads=[bga], writes=[bga])
                for g in range(4):
                    for cc in range(NCC):
                        attn_chunk([(KcT[0:71, g, cc * 128:(cc + 1) * 128], QT[0:71, g, :]),
                                    (ident_b[:], cm1rep[:, cc, :, :].rearrange("p r q -> p (r q)"))],
                                   [b_KcT, bQ, b_ident, b_c1r], VcA[:, cc, g, :], b_VcA,
                                   lambda r: psO1[:, r, 0:NB1], b_pO1, NB1, cc == 0, cc == NCC - 1)
                    finish_branch(0, g, lambda r: psO1[:, r, 0:NB1], b_pO1, True)
                    S.op("dve", lambda: nc.vector.tensor_scalar(out=imp[:], in0=psO1[:, 0, 65:NB1], scalar1=rden[:, 0, 0:1], scalar2=None, op0=ALU.mult),
                         reads=[b_pO1, b_rden], writes=[b_imp])
                    for r in range(1, 4):
                        S.op("dve", lambda: nc.vector.scalar_tensor_tensor(out=imp[:], in0=psO1[:, r, 65:NB1], scalar=rden[:, 0, r:r + 1], in1=imp[:],
                                                                           op0=ALU.mult, op1=ALU.add), reads=[b_pO1, b_rden, b_imp], writes=[b_imp])
                    S.op("dve", lambda: nc.vector.tensor_tensor(out=imp[:], in0=imp[:], in1=vm[:, 0, :], op=ALU.mult), reads=[b_imp, bvm], writes=[b_imp])
                    S.op("dve", lambda: nc.vector.tensor_tensor(out=imp[:], in0=imp[:], in1=vm[:, 1, :], op=ALU.add), reads=[b_imp, bvm], writes=[b_imp])
                    S.op("dve", lambda: nc.vector.max(out=m8[:, 0:8], in_=imp[:]), reads=[b_imp], writes=[b_m8])
                    S.op("dve", lambda: nc.vector.match_replace(out=impr[:], in_to_replace=m8[:, 0:8], in_values=imp[:], imm_value=-1e30),
                         reads=[b_imp, b_m8], writes=[b_impr])
                    S.op("dve", lambda: nc.vector.max(out=m8[:, 8:16], in_=impr[:]), reads=[b_impr], writes=[b_m8])
                    S.op("dve", lambda: nc.vector.tensor_scalar(out=selb[:], in0=imp[:], scalar1=m8[:, 15:16], scalar2=1.0, op0=ALU.is_ge, op1=ALU.subtract),
                         reads=[b_imp, b_m8], writes=[b_sel])
                    S.op("dve", lambda: nc.vector.tensor_scalar(out=negs[:], in0=selb[:], scalar1=-NEG, scalar2=None, op0=ALU.mult), reads=[b_sel], writes=[b_sel])
                    S.op("pe", lambda: nc.tensor.transpose(out=psX[0:NSLC, 0:128], in_=negs[:], identity=ident_b[:]), reads=[b_sel, b_ident], writes=[b_pX])
                    S.op("dve", lambda: nc.vector.tensor_copy(out=nsT[:], in_=psX[0:NSLC, 0:128].unsqueeze(1).to_broadcast([NSLC, 4, 128])),
                         reads=[b_pX], writes=[b_nsT])
                    for u in range(u0, 6):
                        pairs = [(KwT[0:71, g, u * 128:(u + 1) * 128], QT[0:71, g, :]),
                                 (ident_b[:], maskrep[:, 2 + u, :, :].rearrange("p r q -> p (r q)"))]
                        attn_chunk(pairs, [bKw, bQ, b_ident, b_mr], VwA[:, u, g, :], bVw, lambda r: psO3[:, r * 65:(r + 1) * 65], b_pO3, 65, u == u0, u == 5)
                    finish_branch(2, g, lambda r: psO3[:, r * 65:(r + 1) * 65], b_pO3, False)
                    nc2 = 2 * m + 2
                    for c in range(nc2):
                        pairs = [(KsT[0:71, g, c * 128:(c + 1) * 128], QT[0:71, g, :]),
                                 (Eall[0:NSLC, c, :], nsT[:].rearrange("p r q -> p (r q)"))]
                        rb = [b_KsT, bQ, b_E, b_nsT]
                        if c >= 2 * m:
                            pairs.append((ident_b[:], maskrep[:, c - 2 * m, :, :].rearrange("p r q -> p (r q)")))
                            rb = rb + [b_ident, b_mr]
                        attn_chunk(pairs, rb, VsA[:, c, g, :], b_VsA, lambda r: psO2[:, r * 65:(r + 1) * 65], b_pO2, 65, c == 0, c == nc2 - 1)
                    finish_branch(1, g, lambda r: psO2[:, r * 65:(r + 1) * 65], b_pO2, False)
                S.op("dve", lambda: nc.vector.tensor_tensor(out=osq[:], in0=oacc[:], in1=oacc[:], op=ALU.mult), reads=[b_oacc], writes=[b_osq])
                S.op("dve", lambda: nc.vector.tensor_reduce(out=oss[:], in_=osq[:], axis=AX.X, op=ALU.add), reads=[b_osq], writes=[b_osq])
                rstd_from_ss(oss[:], oss[:], 64, [b_osq], [b_osq])
                S.op("dve", lambda: nc.vector.tensor_tensor(out=osq[:], in0=oacc[:], in1=oss[:].unsqueeze(2).to_broadcast([128, 16, 64]), op=ALU.mult),
                     reads=[b_oacc, b_osq], writes=[b_osq])
                on, bon = onb.next()
                S.op("dve", lambda: nc.vector.tensor_tensor(out=on[:], in0=osq[:].rearrange("p h d -> p (h d)"), in1=g_nsaB[:], op=ALU.mult),
                     reads=[b_osq, b_gn], writes=[bon])
                S.dma("sp", omix_d[m * 128:(m + 1) * 128, 0:1024], on[:], reads=[bon], writes=[b_omix], chan="omix")
            S.barrier()


        with scope() as ls:
            decT = sbl(ls, "decT", [128, 8, 128], F32); qdecB = sbl(ls, "qdecB", [128, 8, 128], F32); g_retB = sbl(ls, "g_retB", [128, 1024], F32)
            b_rc = Buf("rconst")
            S.dma("sp", decT[:], h_decT[:, :, :], writes=[b_rc]); S.dma("sp", qdecB[:], h_qdecB[:, :, :], writes=[b_rc])
            S.dma("sp", g_retB[:], g_ret[0:1, :].partition_broadcast(128), writes=[b_rc])
            qTr = ring(ls, "rqT", 2, [128, 8, 128], BF16); kTr = ring(ls, "rkT", 2, [128, 8, 128], BF16)
            vr_ = ring(ls, "rv", 2, [128, 1024], BF16); Rr = ring(ls, "rR", 2, [128, 8, 128], BF16); grr = ring(ls, "rg", 2, [128, 1024], F32)
            qd = sbl(ls, "rqd", [128, 8, 128], BF16); b_qd = Buf("qd")
            AT = sbl(ls, "rAT", [128, 8, 128], BF16); b_AT = Buf("AT")
            ro = sbl(ls, "ro", [128, 8, 128], F32); rsq = sbl(ls, "rsq", [128, 8, 128], F32); b_ro = Buf("ro"); b_rsq = Buf("rsq")
            rmu = sbl(ls, "rmu", [128, 2, 8], F32); b_rmu = Buf("rmu")
            orb = ring(ls, "orb", 2, [128, 1024], BF16)
            psA = pring(ls, "rpsA", 2, [128, 512], F32); psO = pring(ls, "rpsO", 2, [128, 512], F32)
            for m in range(NO):
                sl = slice(m * 128, (m + 1) * 128)
                qT, bq = qTr.next(); kT, bk = kTr.next(); v_, bv = vr_.next(); R_, bR = Rr.next(); gr, bg = grr.next()
                S.dma("sp", qT[:], qR_d[:, :, sl].rearrange("h d c -> d h c"), reads=[b_scr], writes=[bq])
                S.dma("sp", kT[:], kR_d[:, :, sl].rearrange("h d c -> d h c"), reads=[b_scr], writes=[bk])
                S.dma("pool", v_[:], vR_d[sl, :], reads=[b_scr], writes=[bv])
                S.dma("pool", R_[:], R_d[m], reads=[b_scr], writes=[bR])
                S.dma("pool", gr[:], gR_d[sl, :], reads=[b_scr], writes=[bg])
                S.op("pool", lambda: nc.gpsimd.tensor_tensor(out=qd[:], in0=qT[:], in1=qdecB[:], op=ALU.mult), reads=[bq, b_rc], writes=[b_qd])
                for hh in range(2):
                    pa, bpa = psA.next()
                    for h4 in range(4):
                        h = hh * 4 + h4
                        S.op("pe", lambda: nc.tensor.matmul(out=pa[:, h4 * 128:(h4 + 1) * 128], lhsT=kT[:, h, :], rhs=qT[:, h, :], start=True, stop=True),
                             reads=[bk, bq], writes=[bpa], inc=(h4 == 3))
                    S.op("dve", lambda: nc.vector.tensor_tensor(out=AT[:, hh * 4:(hh + 1) * 4, :], in0=pa[:].rearrange("p (h c) -> p h c", h=4),
                                                                in1=decT[:, hh * 4:(hh + 1) * 4, :], op=ALU.mult), reads=[bpa, b_rc], writes=[b_AT])
                for hh in range(2):
                    po, bpo = psO.next()
                    for h4 in range(4):
                        h = hh * 4 + h4
                        mmg(po[:, h4 * 128:(h4 + 1) * 128], [(AT[:, h, :], v_[:, h * 128:(h + 1) * 128]), (qd[:, h, :], R_[:, h, :])],
                            [b_AT, bv, b_qd, bR], bpo)
                    S.op("act", lambda: nc.scalar.copy(out=ro[:, hh * 4:(hh + 1) * 4, :], in_=po[:].rearrange("p (h c) -> p h c", h=4)), reads=[bpo], writes=[b_ro])
                S.op("dve", lambda: nc.vector.tensor_reduce(out=rmu[:, 0, :], in_=ro[:], axis=AX.X, op=ALU.add), reads=[b_ro], writes=[b_rmu])
                S.op("dve", lambda: nc.vector.tensor_scalar(out=rmu[:, 0, :], in0=rmu[:, 0, :], scalar1=-1.0 / 128, scalar2=None, op0=ALU.mult), reads=[b_rmu], writes=[b_rmu])
                S.op("dve", lambda: nc.vector.tensor_tensor(out=ro[:], in0=ro[:], in1=rmu[:, 0, :].unsqueeze(2).to_broadcast([128, 8, 128]), op=ALU.add),
                     reads=[b_ro, b_rmu], writes=[b_ro])
                S.op("dve", lambda: nc.vector.tensor_tensor(out=rsq[:], in0=ro[:], in1=ro[:], op=ALU.mult), reads=[b_ro], writes=[b_rsq])
                S.op("dve", lambda: nc.vector.tensor_reduce(out=rmu[:, 1, :], in_=rsq[:], axis=AX.X, op=ALU.add), reads=[b_rsq], writes=[b_rmu])
                rstd_from_ss(rmu[:, 1, :], rmu[:, 1, :], 128, [b_rmu], [b_rmu])
                S.op("dve", lambda: nc.vector.tensor_tensor(out=rsq[:], in0=ro[:], in1=rmu[:, 1, :].unsqueeze(2).to_broadcast([128, 8, 128]), op=ALU.mult),
                     reads=[b_ro, b_rmu], writes=[b_rsq])
                S.op("dve", lambda: nc.vector.tensor_tensor(out=rsq[:].rearrange("p h d -> p (h d)"), in0=rsq[:].rearrange("p h d -> p (h d)"), in1=g_retB[:], op=ALU.mult),
                     reads=[b_rsq, b_rc], writes=[b_rsq])
                S.op("act", lambda: nc.scalar.activation(out=gr[:], in_=gr[:], func=AF.Silu), reads=[bg], writes=[bg])
                ob, bob = orb.next()
                S.op("dve", lambda: nc.vector.tensor_tensor(out=ob[:], in0=rsq[:].rearrange("p h d -> p (h d)"), in1=gr[:], op=ALU.mult),
                     reads=[b_rsq, bg], writes=[bob])
                S.dma("sp", omix_d[sl, 1024:2048], ob[:], reads=[bob], writes=[b_omix], chan="omix")
            S.barrier()

        b_x1 = Buf("x1d")
        with scope() as ls:
            wo = sbl(ls, "wo", [128, 16, D], BF16); b_wo = Buf("wo")
            S.dma("sp", wo[:], w_out_bf.rearrange("(k p) n -> p k n", p=128), reads=[b_wscr], writes=[b_wo])
            omr = ring(ls, "om", 2, [128, D], BF16); oTr = ring(ls, "oT", 2, [128, 16, 128], BF16)
            xor_ = ring(ls, "xo", 2, [128, D], F32); x1r = ring(ls, "x1", 2, [128, D], F32)
            junk = sbl(ls, "junkD", [128, D], BF16); b_junk = Buf("junkD")
            ssr = ring(ls, "ssD", 2, [128, 2], F32); xnr = ring(ls, "xnD", 2, [128, D], BF16)
            h2r = ring(ls, "h2T", 2, [128, 16, 128], BF16)
            psT = pring(ls, "DpsT", 2, [128, 1024], BF16); psM = pring(ls, "DpsM", 4, [128, 512], F32)
            for m in range(NO):
                sl = slice(m * 128, (m + 1) * 128)
                om, bom = omr.next()
                S.dma("sp", om[:], omix_d[sl, :], reads=[b_omix], writes=[bom])
                oT, boT = oTr.next()
                for half in range(2):
                    pt, bp = psT.next()
                    for k8 in range(8):
                        kc = half * 8 + k8
                        S.op("pe", lambda: nc.tensor.transpose(out=pt[:, k8 * 128:(k8 + 1) * 128], in_=om[:, kc * 128:(kc + 1) * 128], identity=ident_b[:]),
                             reads=[bom, b_ident], writes=[bp], inc=(k8 == 7))
                    S.op("act", lambda: nc.scalar.copy(out=oT[:, half * 8:(half + 1) * 8, :], in_=pt[:].rearrange("p (k t) -> p k t", k=8)), reads=[bp], writes=[boT])
                xo, bxo = xor_.next()
                S.dma("pool", xo[:], x_own[sl, :], writes=[bxo])
                x1, bx1 = x1r.next()
                for n4 in range(4):
                    pt, bp = psM.next()
                    ns = slice(n4 * 512, (n4 + 1) * 512)
                    mmg(pt[:], [(oT[:, k, :], wo[:, k, ns]) for k in range(16)], [boT, b_wo], bp)
                    S.op("dve", lambda: nc.vector.tensor_tensor(out=x1[:, ns], in0=pt[:], in1=gate1B[:, ns], op=ALU.mult), reads=[bp, b_gate], writes=[bx1])
                    S.op("pool", lambda: nc.gpsimd.tensor_tensor(out=x1[:, ns], in0=x1[:, ns], in1=xo[:, ns], op=ALU.add), reads=[bx1, bxo], writes=[bx1])
                S.dma("sp", x1_d[sl, :], x1[:], reads=[bx1], writes=[b_x1], chan="x1d")
                ss, bs = ssr.next()
                S.op("act", lambda: nc.scalar.activation(out=junk[:], in_=x1[:], func=AF.Square, accum_out=ss[:, 0:1]), reads=[bx1], writes=[b_junk, bs])
                rstd_from_ss(ss[:, 1:2], ss[:, 0:1], D, [bs], [bs])
                xn, bn = xnr.next()
                S.op("act", lambda: nc.scalar.activation(out=xn[:], in_=x1[:], func=AF.Copy, scale=ss[:, 1:2]), reads=[bx1, bs], writes=[bn])
                h2T, bh2 = h2r.next()
                for half in range(2):
                    pt, bp = psT.next()
                    for k8 in range(8):
                        kc = half * 8 + k8
                        S.op("pe", lambda: nc.tensor.transpose(out=pt[:, k8 * 128:(k8 + 1) * 128], in_=xn[:, kc * 128:(kc + 1) * 128], identity=ident_b[:]),
                             reads=[bn, b_ident], writes=[bp], inc=(k8 == 7))
                    dst = h2T[:, half * 8:(half + 1) * 8, :]
                    S.op("dve", lambda: nc.vector.tensor_tensor(out=dst, in0=pt[:].rearrange("p (k t) -> p k t", k=8),
                                                                in1=modT[:, 2, half * 8:(half + 1) * 8].unsqueeze(2).to_broadcast([128, 8, 128]), op=ALU.mult),
                         reads=[bp, b_modT], writes=[bh2])
                    S.op("dve", lambda: nc.vector.tensor_tensor(out=dst, in0=dst,
                                                                in1=modT[:, 3, half * 8:(half + 1) * 8].unsqueeze(2).to_broadcast([128, 8, 128]), op=ALU.add),
                         reads=[bh2, b_modT], writes=[bh2])
                S.dma("sp", h2T_d[:, :, sl].rearrange("k p t -> p k t"), h2T[:], reads=[bh2], writes=[b_x1], chan="x1d")
            S.barrier()

        with scope() as ls:
            wq = sbl(ls, "wq", [128, 16, D], BF16); b_wo = Buf("wq")
            S.dma("pool", wq[:], w_q_bf.rearrange("(k p) n -> p k n", p=128), reads=[b_wscr], writes=[b_wo])
            h2r = ring(ls, "D2h2T", 2, [128, 16, 128], BF16)
            qTa = ring(ls, "qTa", 3, [128, 128], BF16)
            scr_ = ring(ls, "sc", 2, [128, 16, 128], F32); scx = sbl(ls, "scx", [128, 256], F32); b_scx = Buf("scx")
            v16 = sbl(ls, "v16", [128, 16, 16], F32); b_v16 = Buf("v16")
            cand = sbl(ls, "cand", [128, 8, 256], F32); b_cand = Buf("cand")
            t16 = sbl(ls, "t16", [128, 8, 16], F32); b_t16 = Buf("t16")
            tneg = sbl(ls, "tneg", [128, 8], F32); zs = sbl(ls, "zs", [128, 8], F32); ej = sbl(ls, "ej", [128, 16], F32); b_z = Buf("z")
            psQ = pring(ls, "DpsQ", 4, [128, 512], F32)
            for m in range(NO):
                sl = slice(m * 128, (m + 1) * 128)
                h2T, bh2 = h2r.next()
                S.dma("sp", h2T[:], h2T_d[:, :, sl].rearrange("k p t -> p k t"), reads=[b_x1], writes=[bh2])
                sc, bsc = scr_.next()
                for a in range(16):
                    pq, bpq = psQ.next()
                    mmg(pq[:, 0:128], [(wq[:, k, a * 128:(a + 1) * 128], h2T[:, k, :]) for k in range(16)], [b_wo, bh2], bpq)
                    qa, bqa = qTa.next()
                    S.op("act", lambda: nc.scalar.copy(out=qa[:], in_=pq[:, 0:128]), reads=[bpq], writes=[bqa])
                    pq2, bpq2 = psQ.next()
                    mmg(pq2[:, 0:128], [(qa[:], skT[:, a, :])], [bqa, b_skT], bpq2)
                    S.op("act", lambda: nc.scalar.copy(out=sc[:, a, :], in_=pq2[:, 0:128]), reads=[bpq2], writes=[bsc])
                    S.op("dve", lambda: nc.vector.max(out=v16[:, a, 0:8], in_=sc[:, a, :]), reads=[bsc], writes=[b_v16])
                    S.op("dve", lambda: nc.vector.match_replace(out=scx[:, 0:128], in_to_replace=v16[:, a, 0:8], in_values=sc[:, a, :], imm_value=-1e30),
                         reads=[bsc, b_v16], writes=[b_scx])
                    S.op("dve", lambda: nc.vector.max(out=v16[:, a, 8:16], in_=scx[:, 0:128]), reads=[b_scx], writes=[b_v16])
                S.dma("sp", sc_d[sl, :, :], sc[:], reads=[bsc], writes=[b_x1], chan="x1d")
                v16v = v16[:].rearrange("p (h t) k -> p h t k", t=2)
                S.op("dve", lambda: nc.vector.tensor_tensor(out=cand[:].rearrange("p h (i j) -> p h i j", i=16),
                                                            in0=v16v[:, :, 0, :].unsqueeze(3).to_broadcast([128, 8, 16, 16]),
                                                            in1=v16v[:, :, 1, :].unsqueeze(2).to_broadcast([128, 8, 16, 16]), op=ALU.add),
                     reads=[b_v16], writes=[b_cand])
                for h in range(8):
                    S.op("dve", lambda: nc.vector.max(out=t16[:, h, 0:8], in_=cand[:, h, :]), reads=[b_cand], writes=[b_t16])
                    S.op("dve", lambda: nc.vector.match_replace(out=scx[:], in_to_replace=t16[:, h, 0:8], in_values=cand[:, h, :], imm_value=-1e30),
                         reads=[b_cand, b_t16], writes=[b_scx])
                    S.op("dve", lambda: nc.vector.max(out=t16[:, h, 8:16], in_=scx[:]), reads=[b_scx], writes=[b_t16])
                S.op("dve", lambda: nc.vector.tensor_scalar(out=tneg[:], in0=t16[:, :, 15], scalar1=-1.0, scalar2=None, op0=ALU.mult), reads=[b_t16], writes=[b_z])
                for h in range(8):
                    S.op("act", lambda: nc.scalar.activation(out=ej[:], in_=t16[:, h, :], func=AF.Exp, bias=tneg[:, h:h + 1], accum_out=zs[:, h:h + 1]),
                         reads=[b_t16, b_z], writes=[b_z])
                S.op("act", lambda: nc.scalar.activation(out=zs[:], in_=zs[:], func=AF.Ln), reads=[b_z], writes=[b_z])
                S.op("dve", lambda: nc.vector.tensor_tensor(out=lnz_all[:, m, :], in0=zs[:], in1=t16[:, :, 15], op=ALU.add), reads=[b_z, b_t16], writes=[b_thr[m]])
                S.op("dve", lambda: nc.vector.tensor_scalar(out=thr_all[:, m, :], in0=t16[:, :, 15], scalar1=-1e-5, scalar2=None, op0=ALU.add),
                     reads=[b_t16], writes=[b_thr[m]])
            S.barrier()

        TT = 2
        with scope() as ls:
            g_finB = sbl(ls, "g_finB", [128, D], F32); b_gf = Buf("gfin")
            S.dma("sp", g_finB[:], g_fin[0:1, :].partition_broadcast(128), writes=[b_gf])
            h2T2 = sbl(ls, "Eh2T", [128, TT, 16, 128], BF16); b_h2 = [Buf("Eh2T%d" % t) for t in range(TT)]
            scr_ = ring(ls, "Esc", 1, [128, 16, 128], F32)
            e1_2 = sbl(ls, "e1", [128, TT, 8, 128], F32); e2_2 = sbl(ls, "e2", [128, TT, 8, 128], F32); tau2 = sbl(ls, "tau", [128, TT, 8], F32)
            b_e = [Buf("e%d" % t) for t in range(TT)]
            acc = sbl(ls, "acc", [128, TT, D], F32); b_acc = [Buf("acc%d" % t) for t in range(TT)]
            UTr = ring(ls, "UTb", 2, [128, 16, 512], BF16); Vr = ring(ls, "Vb", 2, [128, 4, D], BF16)
            Er = ring(ls, "E", 2, [128, 8, 512], F32)
            Gr = ring(ls, "Gb", 2, [128, 8, 512], BF16)
            agr = ring(ls, "ag", 2, [128, 512], F32)
            cTr = ring(ls, "coefT", 2, [128, 4, 128], BF16)
            junk = Gr.tiles[0][:].rearrange("p h e -> p (h e)"); b_junk = Gr.bufs[0]
            ssE = sbl(ls, "ssE", [128, 2], F32); b_ssE = Buf("ssE")
            psA = pring(ls, "EpsA", 2, [128, 512], F32); psG = pring(ls, "EpsG", 2, [128, 512], F32)
            psOut = psl(ls, "EpsO", [128, D], F32); b_pOut = Buf("EpsO")
            UT_v = UT_bf.rearrange("(k p) e -> p k e", p=128)
            for sup in range(NO // TT):
                for t in range(TT):
                    m = sup * TT + t
                    sl = slice(m * 128, (m + 1) * 128)
                    sc, bsc = scr_.next()
                    S.dma("sp", h2T2[:, t, :, :], h2T_d[:, :, sl].rearrange("k p t -> p k t"), reads=[b_x1], writes=[b_h2[t]])
                    S.dma("sp", sc[:], sc_d[sl, :, :], reads=[b_x1], writes=[bsc])
                    scv = sc[:].rearrange("p (h t) k -> p h t k", t=2)
                    S.op("dve", lambda: nc.vector.tensor_tensor(out=e1_2[:, t, :, :], in0=scv[:, :, 0, :],
                                                                in1=lnz_all[:, m, :].unsqueeze(2).to_broadcast([128, 8, 128]), op=ALU.subtract),
                         reads=[bsc, b_thr[m]], writes=[b_e[t]])
                    S.op("act", lambda: nc.scalar.activation(out=e1_2[:, t, :, :], in_=e1_2[:, t, :, :], func=AF.Exp), reads=[b_e[t]], writes=[b_e[t]])
                    S.op("act", lambda: nc.scalar.activation(out=e2_2[:, t, :, :], in_=scv[:, :, 1, :], func=AF.Exp), reads=[bsc], writes=[b_e[t]])
                    S.op("dve", lambda: nc.vector.tensor_tensor(out=tau2[:, t, :], in0=thr_all[:, m, :], in1=lnz_all[:, m, :], op=ALU.subtract),
                         reads=[b_thr[m]], writes=[b_e[t]])
                    S.op("act", lambda: nc.scalar.activation(out=tau2[:, t, :], in_=tau2[:, t, :], func=AF.Exp), reads=[b_e[t]], writes=[b_e[t]])

                wcur = [None]

                def stageA(eb, t):
                    if t == 0:
                        UTb, bU = UTr.next(); Vb, bV = Vr.next()
                        S.dma("sp", UTb[:], UT_v[:, :, eb * 512:(eb + 1) * 512], reads=[b_wscr], writes=[bU])
                        S.dma("act" if False else "sp", Vb[:], V_bf[eb * 512:(eb + 1) * 512, :].rearrange("(c p) n -> p c n", p=128), reads=[b_wscr], writes=[bV])
                        wcur[0] = (UTb, bU, Vb, bV)
                    UTb, bU, Vb, bV = wcur[0]
                    Et, bE = Er.next(); Gb, bG = Gr.next()
                    S.op("pool", lambda: nc.gpsimd.tensor_tensor(out=Et[:].rearrange("p h (a k) -> p h a k", a=4),
                                                                 in0=e1_2[:, t, :, eb * 4:(eb + 1) * 4].unsqueeze(3).to_broadcast([128, 8, 4, 128]),
                                                                 in1=e2_2[:, t, :, :].unsqueeze(2).to_broadcast([128, 8, 4, 128]), op=ALU.mult),
                         reads=[b_e[t]], writes=[bE])
                    pa, bpa = psA.next()
                    for c in range(4):
                        mmg(pa[:, c * 128:(c + 1) * 128], [(UTb[:, k, c * 128:(c + 1) * 128], h2T2[:, t, k, :]) for k in range(16)], [bU, b_h2[t]], bpa)
                    ag, bag = agr.next()
                    S.op("act", lambda: nc.scalar.activation(out=ag[:], in_=pa[:], func=AF.Gelu_apprx_tanh), reads=[bpa], writes=[bag])
                    for h in range(8):
                        S.op("dve", lambda: nc.vector.scalar_tensor_tensor(out=Gb[:, h, :], in0=Et[:, h, :], scalar=tau2[:, t, h:h + 1], in1=Et[:, h, :],
                                                                           op0=ALU.is_ge, op1=ALU.mult), reads=[bE, b_e[t]], writes=[bG])
                    return (eb, t, Vb, bV, Gb, bG, ag, bag)

                def stageB(st):
                    (eb, t, Vb, bV, Gb, bG, ag, bag) = st
                    pg, bpg = psG.next()
                    for c in range(4):
                        mmg(pg[:, c * 128:(c + 1) * 128], [(Gb[:, h, c * 128:(c + 1) * 128], ident_b[:]) for h in range(8)], [bG, b_ident], bpg)
                    cT, bcT = cTr.next()
                    S.op("dve", lambda: nc.vector.tensor_tensor(out=cT[:].rearrange("p c t -> p (c t)"), in0=pg[:], in1=ag[:], op=ALU.mult),
                         reads=[bpg, bag], writes=[bcT])
                    for n4 in range(4):
                        for c in range(4):
                            S.op("pe", lambda: nc.tensor.matmul(out=psOut[:, n4 * 512:(n4 + 1) * 512], lhsT=cT[:, c, :], rhs=Vb[:, c, n4 * 512:(n4 + 1) * 512],
                                                                start=(c == 0), stop=(c == 3)),
                                 reads=[bcT, bV], writes=[b_pOut], inc=(n4 == 3 and c == 3))
                    if eb == 0:
                        S.op("act", lambda: nc.scalar.copy(out=acc[:, t, :], in_=psOut[:]), reads=[b_pOut], writes=[b_acc[t]])
                    else:
                        S.op("dve", lambda: nc.vector.tensor_tensor(out=acc[:, t, :], in0=psOut[:], in1=acc[:, t, :], op=ALU.add),
                             reads=[b_pOut, b_acc[t]], writes=[b_acc[t]])

                prev = None
                for eb in range(32):
                    for t in range(TT):
                        cur = stageA(eb, t)
                        if prev is not None:
                            stageB(prev)
                        prev = cur
                stageB(prev)
                for t in range(TT):
                    m = sup * TT + t
                    sl = slice(m * 128, (m + 1) * 128)
                    x1, bx1 = scr_.tiles[0][:].rearrange("p a k -> p (a k)"), scr_.bufs[0]
                    S.dma("sp", x1[:, :], x1_d[sl, :], reads=[b_x1], writes=[bx1])
                    x2 = acc[:, t, :]
                    S.op("dve", lambda: nc.vector.tensor_tensor(out=x2, in0=x2, in1=gate2B[:], op=ALU.mult), reads=[b_acc[t], b_gate], writes=[b_acc[t]])
                    S.op("pool", lambda: nc.gpsimd.tensor_tensor(out=x2, in0=x2, in1=x1[:, :], op=ALU.add), reads=[b_acc[t], bx1], writes=[b_acc[t]])
                    S.op("act", lambda: nc.scalar.activation(out=junk[:, 0:D], in_=x2, func=AF.Square, accum_out=ssE[:, 0:1]), reads=[b_acc[t]], writes=[b_junk, b_ssE])
                    rstd_from_ss(ssE[:, 1:2], ssE[:, 0:1], D, [b_ssE], [b_ssE])
                    S.op("dve", lambda: nc.vector.scalar_tensor_tensor(out=x1[:, :], in0=x2, scalar=ssE[:, 1:2], in1=g_finB[:], op0=ALU.mult, op1=ALU.mult),
                         reads=[b_acc[t], b_ssE, b_gf], writes=[bx1])
                    S.dma("sp", y[sl, :], x1[:, :], reads=[bx1], writes=[b_y], chan="yout")
            S.barrier()

        if dbg:
            with scope() as ls:
                dt_ = sbl(ls, "dbgt", [128, 4096], F32); b_dt = Buf("dbgt")
                dtb = sbl(ls, "dbgtb", [128, 4096], BF16)
                if "KcT" in dbg_out:
                    S.op("dve", lambda: nc.vector.tensor_copy(out=dt_[0:71, 0:4 * NCC * 128], in_=KcT[:].rearrange("p a b -> p (a b)")), reads=[b_KcT], writes=[b_dt])
                    S.dma("sp", dbg_out["KcT"][:, :], dt_[0:71, 0:4 * NCC * 128], reads=[b_dt], writes=[Buf("o1")])
                if "qA" in dbg_out:
                    S.dma("sp", dtb[0:64, 0:SO], qA_d[5, :, :], writes=[b_dt])
                    S.op("dve", lambda: nc.vector.tensor_copy(out=dt_[0:64, 0:SO], in_=dtb[0:64, 0:SO]), reads=[b_dt], writes=[b_dt])
                    S.dma("sp", dbg_out["qA"][:, :], dt_[0:64, 0:SO], reads=[b_dt], writes=[Buf("o2")])
                if "R" in dbg_out:
                    S.dma("sp", dtb[:, 0:1024], R_d[NO - 1].rearrange("p a b -> p (a b)"), writes=[b_dt])
                    S.op("dve", lambda: nc.vector.tensor_copy(out=dt_[:, 0:1024], in_=dtb[:, 0:1024]), reads=[b_dt], writes=[b_dt])
                    S.dma("sp", dbg_out["R"][:, :], dt_[:, 0:1024], reads=[b_dt], writes=[Buf("o3")])
                S.barrier()

        if dbg and ("omix" in dbg_out or "x1" in dbg_out):
            with scope() as ls:
                tb = sbl(ls, "tapb", [128, D], BF16); tf = sbl(ls, "tapf", [128, D], F32); b_tap = Buf("tap")
                for m in range(NO):
                    sl = slice(m * 128, (m + 1) * 128)
                    if "omix" in dbg_out:
                        S.dma("sp", tb[:], omix_d[sl, :], writes=[b_tap])
                        S.op("dve", lambda: nc.vector.tensor_copy(out=tf[:], in_=tb[:]), reads=[b_tap], writes=[b_tap])
                        S.dma("sp", dbg_out["omix"][sl, :], tf[:], reads=[b_tap], writes=[b_tap])
                    if "x1" in dbg_out:
                        S.dma("sp", tf[:], x1_d[sl, :], writes=[b_tap])
                        S.dma("sp", dbg_out["x1"][sl, :], tf[:], reads=[b_tap], writes=[b_tap])
        S.barrier()
        S.finish_all()
        print("ninst", S.ninst, "nsem", S.nsem)
    return nc


def core_inputs(inp, b, j, SEQ):
    NT = SEQ // 128
    xb = np.ascontiguousarray(inp["x"][b])
    own = np.ascontiguousarray(xb.reshape(NT // 2, 2, 128, D)[:, j].reshape(SEQ // 2, D))
    m = {
        "x_all": xb, "x_own": own,
        "c_in": np.ascontiguousarray(inp["c"][b].reshape(16, 128).T),
        "w_ada": inp["w_ada"][0], "b_ada": inp["b_ada"][0].reshape(1, -1),
        "g_mix": inp["g_norm_mix"][0].reshape(1, -1), "g_ffn": inp["g_norm_ffn"][0].reshape(1, -1),
        "g_fin": inp["g_norm_final"].reshape(1, -1), "w_in": inp["w_in"][0],
        "pe_k": inp["cmp_pe_k"][0], "pe_v": inp["cmp_pe_v"][0],
        "ck_w1": inp["cmp_k_w1"][0], "ck_w2": inp["cmp_k_w2"][0], "cv_w1": inp["cmp_v_w1"][0], "cv_w2": inp["cmp_v_w2"][0],
        "g_nsa": inp["g_nsa_out"][0].reshape(1, -1), "g_ret": inp["g_ret_out"][0].reshape(1, -1),
        "w_out": inp["w_out"][0], "w_q": inp["peer_w_q"][0], "subk": inp["peer_sub_keys"][0].reshape(16, 128, 128),
        "U": inp["peer_u"][0], "V": inp["peer_v"][0],
    }
    cs = host_consts(SEQ, j)
    for k in ("ident", "kaug", "caug", "qaug", "cm1", "vmask", "amask", "overlap", "eall", "cmab", "wmask",
              "decT", "kdec", "qdecB", "jsel"):
        m[k] = np.ascontiguousarray(cs[k])
    return {k: np.ascontiguousarray(np.asarray(v)) for k, v in m.items()}


def kernel(**inputs):
    inp = {k: np.asarray(v) for k, v in inputs.items()}
    B, SEQ, _ = inp["x"].shape
    nc = build(SEQ)
    maps = [core_inputs(inp, b, j, SEQ) for b in range(B) for j in range(2)]
    res = run_bass_kernel_spmd(nc, maps, core_ids=list(range(2 * B)))
    out = np.zeros((B, SEQ, D), np.float32)
    NT = SEQ // 128
    for b in range(B):
        ov = out[b].reshape(NT // 2, 2, 128, D)
        for j in range(2):
            ov[:, j] = np.asarray(res.results[2 * b + j]["y"]).reshape(NT // 2, 128, D)
    return out
```

```python
import contextlib
import numpy as np
import ml_dtypes
import concourse.bass as bass
import concourse.mybir as mybir
from concourse.bass_utils import run_bass_kernel_spmd

F32 = mybir.dt.float32
BF16 = mybir.dt.bfloat16
ALU = mybir.AluOpType
AF = mybir.ActivationFunctionType
AX = mybir.AxisListType
EPOCH = 30000
D = 2048
IN_W = 6704
C_QA, C_KC, C_VC, C_KS, C_VS, C_KW, C_VW, C_GA, C_QR, C_KR, C_VR, C_GR = (
    0, 1024, 1280, 1536, 1792, 2048, 2304, 2560, 2608, 3632, 4656, 5680)
NEG = -30000.0
NEXP = 16384


class Buf:
    __slots__ = ("name", "w", "r")

    def __init__(self, name=""):
        self.name = name
        self.w = None
        self.r = {}


class Sync:
    def __init__(self, nc, stack):
        self.nc = nc
        self.stack = stack
        self.engines = {"pe": nc.tensor, "dve": nc.vector, "act": nc.scalar,
                        "pool": nc.gpsimd, "sp": nc.sync}
        self.sems = {}
        self.cnt = {}
        self.epoch = {e: 0 for e in self.engines}
        self.ecount = {e: 0 for e in self.engines}
        self.known = {e: {} for e in self.engines}
        self.nsem = 0
        self.ninst = {e: 0 for e in self.engines}

    def _sem(self, key):
        if key not in self.sems:
            self.sems[key] = self.stack.enter_context(self.nc.semaphore("s%d" % self.nsem))
            self.nsem += 1
            self.cnt[key] = 0
        return self.sems[key]

    def _ekey(self, e):
        return ("E", e, self.epoch[e])

    def _wait(self, e, deps, skip_same_pe=True):
        eng = self.engines[e]
        need = {}
        for d in deps:
            if d is None:
                continue
            k, v = d
            if e == "pe" and k[0] == "E" and k[1] == "pe" and skip_same_pe:
                continue
            if k[0] == "D":
                v = self.cnt[k]
            if self.known[e].get(k, 0) >= v:
                continue
            if need.get(k, 0) < v:
                need[k] = v
        for k, v in need.items():
            eng.wait_ge(self.sems[k], v)
            self.known[e][k] = v
            self.ninst[e] += 1

    def _deps(self, reads, writes):
        deps = []
        for b in reads:
            deps.append(b.w)
        for b in writes:
            deps.append(b.w)
            for k, v in b.r.items():
                deps.append((k, v))
        return deps

    def op(self, e, fn, reads=(), writes=(), inc=True):
        self._wait(e, self._deps(reads, writes))
        ins = fn()
        self.ninst[e] += 1
        if self.ecount[e] >= EPOCH and inc:
            self.epoch[e] += 1
            self.ecount[e] = 0
        k = self._ekey(e)
        sem = self._sem(k)
        if inc:
            ins.then_inc(sem, 1)
            self.cnt[k] += 1
            self.ecount[e] += 1
            v = self.cnt[k]
        else:
            v = self.cnt[k] + 1
        for b in reads:
            if b.r.get(k, 0) < v:
                b.r[k] = v
        for b in writes:
            b.w = (k, v)
            b.r = {}
        return ins

    def dma(self, q, out, in_, reads=(), writes=(), chan=None, **kw):
        self._wait(q, self._deps(reads, writes), skip_same_pe=False)
        k = ("D", chan if chan is not None else (writes[0].name if writes else "dma"))
        sem = self._sem(k)
        ins = self.engines[q].dma_start(out=out, in_=in_, **kw)
        ins.then_inc(sem, 16)
        self.ninst[q] += 1
        self.cnt[k] += 16
        v = self.cnt[k]
        for b in reads:
            if b.r.get(k, 0) < v:
                b.r[k] = v
        for b in writes:
            b.w = (k, v)
            b.r = {}
        return ins

    def finish_all(self):
        deps = [(k, v) for k, v in self.cnt.items() if v > 0]
        self._wait("sp", deps, skip_same_pe=False)

    def barrier(self):
        for e in self.engines:
            deps = [(k, v) for k, v in self.cnt.items() if v > 0]
            self._wait(e, deps, skip_same_pe=False)


class Ring:
    def __init__(self, tiles, name):
        self.tiles = tiles
        self.bufs = [Buf("%s%d" % (name, i)) for i in range(len(tiles))]
        self.i = -1

    def next(self):
        self.i = (self.i + 1) % len(self.tiles)
        return self.tiles[self.i], self.bufs[self.i]


def bf(x):
    return np.asarray(x).astype(ml_dtypes.bfloat16)


def split3(v):
    v = np.asarray(v, np.float64)
    a = bf(v)
    r = v - a.astype(np.float64)
    b = bf(r)
    r2 = r - b.astype(np.float64)
    c = bf(r2)
    return a, b, c


def host_consts(SEQ, j):
    NT = SEQ // 128
    NO = NT // 2
    NSLC = SEQ // 64
    NCMP = SEQ // 16
    NCC = max(1, NCMP // 128)
    n_cmp = (SEQ - 32) // 16 + 1
    cs = {}
    cs["ident"] = np.eye(128, dtype=np.float32)
    pos = np.arange(SEQ)
    cs["kaug"] = bf(np.stack([pos // 128, pos // 128, pos % 128, pos % 128,
                              np.ones(SEQ), np.ones(SEQ), np.ones(SEQ)]).astype(np.float32))
    cpos = np.arange(NCC * 128) * 16 + 31
    cs["caug"] = bf(np.stack([cpos // 128, cpos // 128, cpos % 128, cpos % 128,
                              np.ones_like(cpos), np.ones_like(cpos), np.ones_like(cpos)]).astype(np.float32))
    slopes = np.exp2(-8.0 * (np.arange(16, dtype=np.float64) + 1.0) / 16).astype(np.float32).astype(np.float64)
    qaug = np.zeros((NO, 4, 7, 4, 128), dtype=ml_dtypes.bfloat16)
    for m in range(NO):
        t = 128.0 * (2 * m + j) + np.arange(128)
        for g in range(4):
            for r in range(4):
                s = slopes[4 * g + r]
                shi = bf(s)
                slo = bf(s - float(shi))
                seff = float(shi) + float(slo)
                a, b, c = split3(seff * t)
                qaug[m, g, 0, r, :] = bf(128.0 * float(shi))
                qaug[m, g, 1, r, :] = bf(128.0 * float(slo))
                qaug[m, g, 2, r, :] = shi
                qaug[m, g, 3, r, :] = slo
                qaug[m, g, 4, r, :] = bf(-a.astype(np.float32))
                qaug[m, g, 5, r, :] = bf(-b.astype(np.float32))
                qaug[m, g, 6, r, :] = bf(-c.astype(np.float32))
    cs["qaug"] = qaug.reshape(NO, 4, 7, 512)
    n = np.arange(NCC * 128)
    cm1 = np.zeros((NO, 128, NCC, 128), np.float32)
    vm = np.zeros((NO, 128, NSLC), np.float32)
    am = np.zeros((NO, 128, NSLC), np.float32)
    mb = np.arange(NSLC)
    for m in range(NO):
        t = 128 * (2 * m + j) + np.arange(128)
        ok = (n[:, None] < n_cmp) & ((16 * n[:, None] + 31) <= t[None, :])
        cm1[m] = np.where(ok, 0.0, NEG).reshape(NCC, 128, 128).transpose(1, 0, 2)
        back = (t // 64)[:, None] - mb[None, :]
        valid = back >= 0
        forced = valid & ((mb[None, :] == 0) | (back < 2))
        vm[m] = (valid & ~forced).astype(np.float32)
        am[m] = np.where(forced, 1e4 + mb[None, :], np.where(valid, 0.0, -1.0))
    cs["cm1"] = bf(cm1)
    cs["vmask"] = vm
    cs["amask"] = am
    ov = ((n[:, None] < n_cmp) & (n[:, None] >= 4 * mb[None, :] - 1) & (n[:, None] <= 4 * mb[None, :] + 3))
    cs["overlap"] = bf(ov.astype(np.float32).reshape(NCC, 128, NSLC).transpose(1, 0, 2))
    E = np.zeros((NSLC, NT, 128), np.float32)
    for c in range(NT):
        E[2 * c, c, :64] = 1.0
        E[2 * c + 1, c, 64:] = 1.0
    cs["eall"] = bf(E)
    k = np.arange(128)[:, None]
    q = np.arange(128)[None, :]
    tri = np.where(k <= q, 0.0, NEG).astype(np.float32)
    full = np.full((128, 128), NEG, np.float32)
    zero = np.zeros((128, 128), np.float32)
    cs["cmab"] = bf(np.stack([tri, full] if j == 0 else [zero, tri], axis=1))
    wm = np.zeros((128, 6, 128), np.float32)
    for u in range(6):
        d = 128 * (j + 4 - u) + q - k
        wm[:, u, :] = np.where((d >= 0) & (d < 512), 0.0, NEG)
    cs["wmask"] = bf(wm)
    lg = np.log1p(-np.exp2(-5.0 - np.arange(8, dtype=np.float64)))
    p = np.arange(128, dtype=np.float64)
    dec = np.zeros((128, 8, 128), np.float32)
    for h in range(8):
        diff = p[None, :] - p[:, None]
        dec[:, h, :] = np.where(diff >= 0, np.exp(lg[h] * np.maximum(diff, 0.0)), 0.0) * (128.0 ** -0.5)
    cs["decT"] = dec
    cs["kdec"] = np.stack([np.exp(lg[h] * (127.0 - p)) * (128.0 ** -0.5) for h in range(8)], axis=1).astype(np.float32)
    qd = np.stack([np.exp(lg[h] * (p + 1.0)) for h in range(8)], axis=0).astype(np.float32)
    cs["qdecB"] = np.broadcast_to(qd[None], (128, 8, 128)).copy()
    cs["cdec"] = np.exp(lg * 128.0)
    cs["jsel"] = np.full((128, 1), float(j), np.float32)
    return cs


def build(SEQ, dbg=None):
    NT = SEQ // 128
    NO = NT // 2
    NS = SEQ // 512
    NOS = NO // 4
    NSLC = SEQ // 64
    NCMP = SEQ // 16
    NCC = max(1, NCMP // 128)
    SO = SEQ // 2
    cdec = host_consts(256, 0)["cdec"] if False else np.exp(np.log1p(-np.exp2(-5.0 - np.arange(8, dtype=np.float64))) * 128.0)
    nc = bass.Bass("TRN2", target_bir_lowering=False)

    def din(name, shape, dt=F32):
        return nc.dram_tensor(name, list(shape), dt, kind="ExternalInput").ap()

    def dscr(name, shape, dt):
        return nc.dram_tensor(name, list(shape), dt).ap()

    x_all = din("x_all", [SEQ, D]); x_own = din("x_own", [SO, D]); c_in = din("c_in", [128, 16])
    w_ada = din("w_ada", [D, 6 * D]); b_ada = din("b_ada", [1, 6 * D])
    g_mix = din("g_mix", [1, D]); g_ffn = din("g_ffn", [1, D]); g_fin = din("g_fin", [1, D])
    w_in = din("w_in", [D, IN_W])
    pe_k = din("pe_k", [32, 64]); pe_v = din("pe_v", [32, 64])
    ck_w1 = din("ck_w1", [2048, 128]); ck_w2 = din("ck_w2", [128, 64])
    cv_w1 = din("cv_w1", [2048, 128]); cv_w2 = din("cv_w2", [128, 64])
    g_nsa = din("g_nsa", [1, 1024]); g_ret = din("g_ret", [1, 1024])
    w_out = din("w_out", [D, D]); w_q = din("w_q", [D, D]); subk = din("subk", [16, 128, 128])
    U = din("U", [NEXP, D]); V = din("V", [NEXP, D])
    h_ident = din("ident", [128, 128]); h_kaug = din("kaug", [7, SEQ], BF16); h_caug = din("caug", [7, NCC * 128], BF16)
    h_qaug = din("qaug", [NO, 4, 7, 512], BF16); h_cm1 = din("cm1", [NO, 128, NCC, 128], BF16)
    h_vmask = din("vmask", [NO, 128, NSLC]); h_amask = din("amask", [NO, 128, NSLC])
    h_overlap = din("overlap", [128, NCC, NSLC], BF16); h_eall = din("eall", [NSLC, NT, 128], BF16)
    h_cmab = din("cmab", [128, 2, 128], BF16); h_wmask = din("wmask", [128, 6, 128], BF16)
    h_decT = din("decT", [128, 8, 128]); h_kdec = din("kdec", [128, 8]); h_qdecB = din("qdecB", [128, 8, 128])
    h_jsel = din("jsel", [128, 1])
    y = nc.dram_tensor("y", [SO, D], F32, kind="ExternalOutput").ap()
    dbg_out = {}
    if dbg:
        for nm, shp in dbg.items():
            dbg_out[nm] = nc.dram_tensor("dbg_" + nm, list(shp), F32, kind="ExternalOutput").ap()

    w_in_bf = dscr("w_in_bf", [D, IN_W], BF16); w_out_bf = dscr("w_out_bf", [D, D], BF16); w_q_bf = dscr("w_q_bf", [D, D], BF16)
    UT_bf = dscr("UT_bf", [D, NEXP], BF16); V_bf = dscr("V_bf", [NEXP, D], BF16)
    kS_d = dscr("kS_d", [4, 64, SEQ], BF16); kW_d = dscr("kW_d", [4, 64, SEQ], BF16)
    vS_d = dscr("vS_d", [SEQ, 256], BF16); vW_d = dscr("vW_d", [SEQ, 256], BF16)
    qA_d = dscr("qA_d", [16, 64, SO], BF16); gA_d = dscr("gA_d", [SO, 48], F32)
    qR_d = dscr("qR_d", [8, 128, SO], BF16); kR_d = dscr("kR_d", [8, 128, SO], BF16)
    vR_d = dscr("vR_d", [SO, 1024], BF16); gR_d = dscr("gR_d", [SO, 1024], F32)
    R_d = dscr("R_d", [NO, 128, 8, 128], BF16)
    omix_d = dscr("omix_d", [SO, D], BF16)
    x1_d = dscr("x1_d", [SO, D], F32); h2T_d = dscr("h2T_d", [16, 128, SO], BF16)
    sc_d = dscr("sc_d", [SO, 16, 128], F32)

    with contextlib.ExitStack() as st:
        S = Sync(nc, st)

        def sbp(name, shape, dt):
            return st.enter_context(nc.sbuf_tensor("t_" + name, list(shape), dt))

        ident_f = sbp("ident_f", [128, 128], F32); b_ident = Buf("ident")
        ident_b = sbp("ident_b", [128, 128], BF16)
        gate1B = sbp("gate1B", [128, D], F32); gate2B = sbp("gate2B", [128, D], F32); b_gate = Buf("gate")
        modT = sbp("modT", [128, 4, 16], F32); b_modT = Buf("modT")
        KcT = sbp("KcT", [71, 4, NCC * 128], BF16); b_KcT = Buf("KcT")
        VcA = sbp("VcA", [128, NCC, 4, 65 + NSLC], BF16); b_VcA = Buf("VcA")
        skT = sbp("skT", [128, 16, 128], BF16); b_skT = Buf("skT")
        jsel = sbp("jsel", [128, 1], F32); b_jsel = Buf("jsel")
        thr_all = sbp("thr_all", [128, NO, 8], F32)
        lnz_all = sbp("lnz_all", [128, NO, 8], F32)
        b_thr = [Buf("thr%d" % i) for i in range(NO)]

        S.dma("sp", ident_f[:], h_ident[:, :], writes=[b_ident])
        S.op("dve", lambda: nc.vector.tensor_copy(out=ident_b[:], in_=ident_f[:]), reads=[b_ident], writes=[b_ident])
        S.dma("sp", jsel[:], h_jsel[:, :], writes=[b_jsel])

        def scope():
            return contextlib.ExitStack()

        def sbl(ls, name, shape, dt):
            return ls.enter_context(nc.sbuf_tensor("t_" + name, list(shape), dt))

        def psl(ls, name, shape, dt):
            return ls.enter_context(nc.psum_tensor("p_" + name, list(shape), dt))

        def ring(ls, name, n, shape, dt):
            return Ring([sbl(ls, "%s%d" % (name, i), shape, dt) for i in range(n)], name)

        def pring(ls, name, n, shape, dt):
            return Ring([psl(ls, "%s%d" % (name, i), shape, dt) for i in range(n)], name)

        def mmg(out_ap, pairs, rbufs, wbuf):
            n = len(pairs)
            for i, (l, r) in enumerate(pairs):
                S.op("pe", lambda: nc.tensor.matmul(out=out_ap, lhsT=l, rhs=r, start=(i == 0), stop=(i == n - 1)),
                     reads=rbufs, writes=[wbuf], inc=(i == n - 1))

        cast_rr = [0]

        def cast(out_ap, in_ap, rb, wb, psum=False):
            e = ("dve", "act")[cast_rr[0] % 2] if psum else ("dve", "pool", "act")[cast_rr[0] % 3]
            cast_rr[0] += 1
            if e == "act":
                S.op("act", lambda: nc.scalar.copy(out=out_ap, in_=in_ap), reads=rb, writes=wb)
            elif e == "dve":
                S.op("dve", lambda: nc.vector.tensor_copy(out=out_ap, in_=in_ap), reads=rb, writes=wb)
            else:
                S.op("pool", lambda: nc.gpsimd.tensor_copy(out=out_ap, in_=in_ap), reads=rb, writes=wb)

        def rstd_from_ss(rstd_ap, ss_ap, n, rb, wb):
            S.op("dve", lambda: nc.vector.tensor_scalar(out=rstd_ap, in0=ss_ap, scalar1=1.0 / n, scalar2=1e-6,
                                                        op0=ALU.mult, op1=ALU.add), reads=rb, writes=wb)
            S.op("act", lambda: nc.scalar.activation(out=rstd_ap, in_=rstd_ap, func=AF.Sqrt), reads=wb, writes=wb)
            S.op("dve", lambda: nc.vector.reciprocal(out=rstd_ap, in_=rstd_ap), reads=wb, writes=wb)

        with scope() as ls:
            cin_t = sbl(ls, "cin_t", [128, 16], F32); b_cin = Buf("cin")
            cb = sbl(ls, "cb", [128, 16, 128], F32); b_cb = Buf("cb")
            modB = sbl(ls, "modB", [128, 6, D], F32); b_modB = [Buf("modB%d" % v) for v in range(6)]
            wr = ring(ls, "wada", 3, [128, D], F32)
            btile = sbl(ls, "btile", [128, D], F32); b_bt = Buf("bt")
            gB = sbl(ls, "gB", [128, 2, D], F32); b_gB = Buf("gB")
            vecB = sbl(ls, "vecB", [128, 4, D], F32); b_vecB = Buf("vecB")
            psA = pring(ls, "psA", 8, [128, 512], F32)
            S.dma("sp", cin_t[:], c_in[:, :], writes=[b_cin])
            S.op("act", lambda: nc.scalar.activation(out=cin_t[:], in_=cin_t[:], func=AF.Silu), reads=[b_cin], writes=[b_cin])
            S.op("dve", lambda: nc.vector.tensor_copy(out=cb[:], in_=cin_t[:].unsqueeze(2).to_broadcast([128, 16, 128])),
                 reads=[b_cin], writes=[b_cb])
            for v in range(6):
                banks = [psA.next() for _ in range(4)]
                for k in range(16):
                    wt, bw = wr.next()
                    S.dma("sp" if k % 2 == 0 else "pool", wt[:], w_ada[k * 128:(k + 1) * 128, v * D:(v + 1) * D], writes=[bw])
                    for n4 in range(4):
                        pt, bp = banks[n4]
                        S.op("pe", lambda: nc.tensor.matmul(out=pt[:], lhsT=cb[:, k, :], rhs=wt[:, n4 * 512:(n4 + 1) * 512],
                                                            start=(k == 0), stop=(k == 15)),
                             reads=[b_cb, bw], writes=[bp], inc=(k == 15 or n4 == 3))
                S.dma("sp", btile[:], b_ada[0:1, v * D:(v + 1) * D].partition_broadcast(128), writes=[b_bt])
                for n4 in range(4):
                    pt, bp = banks[n4]
                    S.op("dve", lambda: nc.vector.tensor_tensor(out=modB[:, v, n4 * 512:(n4 + 1) * 512], in0=pt[:],
                                                                in1=btile[:, n4 * 512:(n4 + 1) * 512], op=ALU.add),
                         reads=[bp, b_bt], writes=[b_modB[v]])
            S.dma("sp", gB[:, 0, :], g_mix[0:1, :].partition_broadcast(128), writes=[b_gB])
            S.dma("sp", gB[:, 1, :], g_ffn[0:1, :].partition_broadcast(128), writes=[b_gB])
            S.op("dve", lambda: nc.vector.scalar_tensor_tensor(out=vecB[:, 0, :], in0=modB[:, 1, :], scalar=1.0, in1=gB[:, 0, :],
                                                               op0=ALU.add, op1=ALU.mult), reads=[b_modB[1], b_gB], writes=[b_vecB])
            S.op("dve", lambda: nc.vector.tensor_copy(out=vecB[:, 1, :], in_=modB[:, 0, :]), reads=[b_modB[0]], writes=[b_vecB])
            S.op("dve", lambda: nc.vector.scalar_tensor_tensor(out=vecB[:, 2, :], in0=modB[:, 4, :], scalar=1.0, in1=gB[:, 1, :],
                                                               op0=ALU.add, op1=ALU.mult), reads=[b_modB[4], b_gB], writes=[b_vecB])
            S.op("dve", lambda: nc.vector.tensor_copy(out=vecB[:, 3, :], in_=modB[:, 3, :]), reads=[b_modB[3]], writes=[b_vecB])
            S.op("pool", lambda: nc.gpsimd.tensor_copy(out=gate1B[:], in_=modB[:, 2, :]), reads=[b_modB[2]], writes=[b_gate])
            S.op("pool", lambda: nc.gpsimd.tensor_copy(out=gate2B[:], in_=modB[:, 5, :]), reads=[b_modB[5]], writes=[b_gate])
            for vi in range(4):
                for kc in range(16):
                    pt, bp = psA.next()
                    S.op("pe", lambda: nc.tensor.transpose(out=pt[:, 0:128], in_=vecB[:, vi, kc * 128:(kc + 1) * 128], identity=ident_f[:]),
                         reads=[b_vecB, b_ident], writes=[bp])
                    S.op("dve", lambda: nc.vector.tensor_copy(out=modT[:, vi, kc:kc + 1], in_=pt[:, 0:1]), reads=[bp], writes=[b_modT])
            S.barrier()

        b_wscr = Buf("wscr")
        with scope() as ls:
            fr = ring(ls, "cf", 2, [128, IN_W], F32)
            br = ring(ls, "cbf", 2, [128, IN_W], BF16)
            psT = pring(ls, "psTa", 2, [128, 1024], BF16)
            ust = ring(ls, "ust", 2, [128, 16, 512], BF16)
            for k in range(16):
                ft, bfb = fr.next(); bt_, bbb = br.next()
                S.dma("sp", ft[:], w_in[k * 128:(k + 1) * 128, :], writes=[bfb])
                cast(bt_[:], ft[:], [bfb], [bbb])
                S.dma("pool", w_in_bf[k * 128:(k + 1) * 128, :], bt_[:], reads=[bbb], writes=[b_wscr], chan="wscr")
            for (src, dst) in ((w_out, w_out_bf), (w_q, w_q_bf)):
                for k in range(0, 16, 2):
                    ft, bfb = fr.next(); bt_, bbb = br.next()
                    S.dma("sp", ft[:, 0:2 * D].rearrange("p (a n) -> p a n", a=2),
                          src[k * 128:(k + 2) * 128, :].rearrange("(a p) n -> p a n", p=128), writes=[bfb])
                    cast(bt_[:, 0:2 * D], ft[:, 0:2 * D], [bfb], [bbb])
                    S.dma("pool", dst[k * 128:(k + 2) * 128, :].rearrange("(a p) n -> p a n", p=128),
                          bt_[:, 0:2 * D].rearrange("p (a n) -> p a n", a=2), reads=[bbb], writes=[b_wscr], chan="wscr")
            for e0 in range(0, 128, 2):
                ft, bfb = fr.next(); bt_, bbb = br.next()
                S.dma("sp", ft[:, 0:2 * D].rearrange("p (a n) -> p a n", a=2),
                      V[e0 * 128:(e0 + 2) * 128, :].rearrange("(a p) n -> p a n", p=128), writes=[bfb])
                cast(bt_[:, 0:2 * D], ft[:, 0:2 * D], [bfb], [bbb])
                S.dma("pool", V_bf[e0 * 128:(e0 + 2) * 128, :].rearrange("(a p) n -> p a n", p=128),
                      bt_[:, 0:2 * D].rearrange("p (a n) -> p a n", a=2), reads=[bbb], writes=[b_wscr], chan="wscr")
            for e4 in range(32):
                us, bus = ust.next()
                for a in range(0, 4, 2):
                    ft, bfb = fr.next(); bt_, bbb = br.next()
                    e0 = e4 * 4 + a
                    S.dma("sp", ft[:, 0:2 * D].rearrange("p (a n) -> p a n", a=2),
                          U[e0 * 128:(e0 + 2) * 128, :].rearrange("(a p) n -> p a n", p=128), writes=[bfb])
                    cast(bt_[:, 0:2 * D], ft[:, 0:2 * D], [bfb], [bbb])
                    for a2 in range(2):
                        for half in range(2):
                            pt, bp = psT.next()
                            for k8 in range(8):
                                kc = half * 8 + k8
                                S.op("pe", lambda: nc.tensor.transpose(out=pt[:, k8 * 128:(k8 + 1) * 128],
                                                                       in_=bt_[:, a2 * D + kc * 128:a2 * D + (kc + 1) * 128], identity=ident_b[:]),
                                     reads=[bbb, b_ident], writes=[bp], inc=(k8 == 7))
                            ee = (a + a2) * 128
                            cast(us[:, half * 8:(half + 1) * 8, ee:ee + 128], pt[:].rearrange("p (k e) -> p k e", k=8), [bp], [bus], psum=True)
                S.dma("pool", UT_bf.rearrange("(k p) e -> p k e", p=128)[:, :, e4 * 512:(e4 + 1) * 512], us[:],
                      reads=[bus], writes=[b_wscr], chan="wscr")
            ft, bfb = fr.next(); bt_, bbb = br.next()
            S.dma("sp", ft[:, 0:2048].rearrange("p (a d) -> p a d", a=16), subk.rearrange("a k d -> k a d"), writes=[bfb])
            cast(bt_[:, 0:2048], ft[:, 0:2048], [bfb], [bbb])
            for half in range(2):
                pt, bp = psT.next()
                for k8 in range(8):
                    a = half * 8 + k8
                    S.op("pe", lambda: nc.tensor.transpose(out=pt[:, k8 * 128:(k8 + 1) * 128], in_=bt_[:, a * 128:(a + 1) * 128], identity=ident_b[:]),
                         reads=[bbb, b_ident], writes=[bp], inc=(k8 == 7))
                cast(skT[:, half * 8:(half + 1) * 8, :], pt[:].rearrange("p (k e) -> p k e", k=8), [bp], [b_skT], psum=True)
            S.barrier()

        if dbg and "modT" in dbg_out:
            S.dma("sp", dbg_out["modT"][:, :], modT[:].rearrange("p a b -> p (a b)"), reads=[b_modT], writes=[Buf("dbgo")])


        b_y = Buf("y")
        b_scr = Buf("scr")
        gelu_c = 1.5957691216057308

        def gelu_tanh(out_ap, x_ap, tmp_ap, rb, wb, tb, shape_bias=None):
            S.op("dve", lambda: nc.vector.tensor_tensor(out=tmp_ap, in0=x_ap, in1=x_ap, op=ALU.mult), reads=rb, writes=[tb])
            S.op("dve", lambda: nc.vector.tensor_scalar(out=tmp_ap, in0=tmp_ap, scalar1=0.044715, scalar2=1.0, op0=ALU.mult, op1=ALU.add),
                 reads=[tb], writes=[tb])
            S.op("dve", lambda: nc.vector.tensor_tensor(out=tmp_ap, in0=tmp_ap, in1=x_ap, op=ALU.mult), reads=rb + [tb], writes=[tb])
            S.op("act", lambda: nc.scalar.activation(out=tmp_ap, in_=tmp_ap, func=AF.Sigmoid, scale=gelu_c), reads=[tb], writes=[tb])
            S.op("dve", lambda: nc.vector.tensor_tensor(out=out_ap, in0=tmp_ap, in1=x_ap, op=ALU.mult), reads=rb + [tb], writes=wb)

        w_in_v = w_in_bf.rearrange("(k p) n -> p k n", p=128)

        with scope() as ls:
            psT = pring(ls, "psT", 2, [128, 1024], BF16)
            psM = pring(ls, "psM", 4, [128, 512], F32)
            psK = pring(ls, "psK", 2, [128, 512], F32)
            W1s = sbl(ls, "W1s", [128, 2, 32, 128], BF16); b_W1s = Buf("W1s")
            W2s = sbl(ls, "W2s", [128, 2, 64], BF16); b_W2 = Buf("W2")
            peTb = sbl(ls, "peTb", [64, 2, 32], BF16); b_pe = Buf("pe")
            peb = sbl(ls, "peb", [128, 2], F32); b_peb = Buf("peb")
            with scope() as ls2:
                W1f = sbl(ls2, "W1f", [128, 32, 128], F32); b_W1f = Buf("W1f")
                W2f = sbl(ls2, "W2f", [128, 2, 64], F32)
                peT = sbl(ls2, "peT", [64, 2, 32], F32)
                for kv, (w1, w2, pe) in enumerate(((ck_w1, ck_w2, pe_k), (cv_w1, cv_w2, pe_v))):
                    for half in range(2):
                        S.dma("sp", W1f[half * 64:(half + 1) * 64, :, :], w1.rearrange("(l d) h -> d l h", d=64), writes=[b_W1f])
                    S.op("dve", lambda: nc.vector.tensor_copy(out=W1s[:, kv, :, :], in_=W1f[:]), reads=[b_W1f], writes=[b_W1s])
                    S.dma("sp", W2f[:, kv, :], w2[:, :], writes=[b_W2])
                    S.dma("sp", peT[:, kv, :], pe.rearrange("l d -> d l"), writes=[b_pe], allow_slow_non_contiguous=True)
                S.op("dve", lambda: nc.vector.tensor_copy(out=W2s[:], in_=W2f[:]), reads=[b_W2], writes=[b_W2])
                S.op("dve", lambda: nc.vector.tensor_copy(out=peTb[:], in_=peT[:]), reads=[b_pe], writes=[b_pe])
                for kv in range(2):
                    pt, bp = psM.next()
                    mmg(pt[:, 0:1], [(W1s[0:64, kv, l, :], peTb[:, kv, l:l + 1]) for l in range(32)], [b_W1s, b_pe], bp)
                    S.op("dve", lambda: nc.vector.tensor_copy(out=peb[:, kv:kv + 1], in_=pt[:, 0:1]), reads=[bp], writes=[b_peb])
                S.barrier()
            xr = ring(ls, "xt", 2, [128, D], F32)
            junk = sbl(ls, "junk", [128, D], BF16); b_junk = Buf("junk")
            ssr = ring(ls, "ss", 2, [128, 2], F32)
            xnr = ring(ls, "xn", 2, [128, D], BF16)
            hTr = ring(ls, "hT", 2, [128, 16, 512], BF16)
            wr = ring(ls, "wblk", 3, [128, 16, 512], BF16)
            stg = ring(ls, "stg", 4, [128, 512], BF16)
            stgf = ring(ls, "stgf", 2, [128, 512], F32)
            kcb = sbl(ls, "kcb", [128, 2, 2, 528], BF16); b_kcb = Buf("kcb")
            hid = sbl(ls, "hid", [128, 32], BF16); hidx = sbl(ls, "hidx", [128, 32], F32); hidt = sbl(ls, "hidt", [128, 32], F32)
            b_hid = Buf("hid"); b_hidx = Buf("hidx"); b_hidt = Buf("hidt")
            VcT = sbl(ls, "VcT", [64, 4, NCC * 128], BF16); b_VcT = Buf("VcT")
            kdec = sbl(ls, "kdec", [128, 8], F32); b_kdec = Buf("kdec")
            krd = sbl(ls, "krd", [128, 4, 1024], BF16); b_krd = Buf("krd")
            vrb = sbl(ls, "vrb", [128, 4, 1024], BF16); b_vrb = Buf("vrb")
            Rst = sbl(ls, "Rst", [128, 2, 8, 128], F32); b_Rst = Buf("Rst")
            Rown = sbl(ls, "Rown", [128, 8, 128], F32); Rownb = sbl(ls, "Rownb", [128, 8, 128], BF16); b_Rown = Buf("Rown")
            S.dma("sp", kdec[:], h_kdec[:, :], writes=[b_kdec])
            S.op("dve", lambda: nc.vector.memset(kcb[:], 0.0), writes=[b_kcb])
            S.op("dve", lambda: nc.vector.memset(Rst[:], 0.0), writes=[b_Rst])
            S.op("pool", lambda: nc.gpsimd.memset(KcT[:], 0.0), writes=[b_KcT])
            S.op("pool", lambda: nc.gpsimd.memset(VcT[:], 0.0), writes=[b_VcT])
            S.dma("sp", KcT[64:71, 0, :], h_caug[:, :], writes=[b_KcT])
            for g in range(1, 4):
                S.dma("sp", KcT[64:71, g, :], h_caug[:, :], writes=[b_KcT])

            def norm_T(xsrc, vi, hT, bh, col0):
                xt, bx = xr.next()
                S.dma("sp", xt[:], xsrc, writes=[bx])
                ss, bs = ssr.next()
                S.op("act", lambda: nc.scalar.activation(out=junk[:], in_=xt[:], func=AF.Square, accum_out=ss[:, 0:1]),
                     reads=[bx], writes=[b_junk, bs])
                rstd_from_ss(ss[:, 1:2], ss[:, 0:1], D, [bs], [bs])
                xn, bn = xnr.next()
                S.op("act", lambda: nc.scalar.activation(out=xn[:], in_=xt[:], func=AF.Copy, scale=ss[:, 1:2]), reads=[bx, bs], writes=[bn])
                for half in range(2):
                    pt, bp = psT.next()
                    for k8 in range(8):
                        kc = half * 8 + k8
                        S.op("pe", lambda: nc.tensor.transpose(out=pt[:, k8 * 128:(k8 + 1) * 128], in_=xn[:, kc * 128:(kc + 1) * 128],
                                                               identity=ident_b[:]), reads=[bn, b_ident], writes=[bp], inc=(k8 == 7))
                    dst = hT[:, half * 8:(half + 1) * 8, col0:col0 + 128]
                    S.op("dve", lambda: nc.vector.tensor_tensor(out=dst, in0=pt[:].rearrange("p (k t) -> p k t", k=8),
                                                                in1=modT[:, vi, half * 8:(half + 1) * 8].unsqueeze(2).to_broadcast([128, 8, 128]),
                                                                op=ALU.mult), reads=[bp, b_modT], writes=[bh])
                    S.op("dve", lambda: nc.vector.tensor_tensor(out=dst, in0=dst,
                                                                in1=modT[:, vi + 1, half * 8:(half + 1) * 8].unsqueeze(2).to_broadcast([128, 8, 128]),
                                                                op=ALU.add), reads=[bh, b_modT], writes=[bh])

            def load_w(c0, ncols):
                wt, bw = wr.next()
                S.dma("sp", wt[:, :, 0:ncols], w_in_v[:, :, c0:c0 + ncols], reads=[b_wscr], writes=[bw])
                return wt, bw

            def fm(wt, bw, col, M, hT, bh):
                pt, bp = psM.next()
                mmg(pt[0:M, :], [(wt[:, k, col:col + M], hT[:, k, :]) for k in range(16)], [bw, bh], bp)
                return pt, bp

            def tm(wt, bw, col, N, hT, bh, tile):
                pt, bp = psM.next()
                mmg(pt[:, 0:N], [(hT[:, k, tile * 128:(tile + 1) * 128], wt[:, k, col:col + N]) for k in range(16)], [bh, bw], bp)
                return pt, bp

            ev_rr = [0]

            def evac(out_ap, in_ap, rb, wb, scale=None):
                e = ("dve", "act")[ev_rr[0] % 2]
                ev_rr[0] += 1
                if e == "act":
                    if scale is None:
                        S.op("act", lambda: nc.scalar.copy(out=out_ap, in_=in_ap), reads=rb, writes=wb)
                    else:
                        S.op("act", lambda: nc.scalar.mul(out=out_ap, in_=in_ap, mul=scale), reads=rb, writes=wb)
                else:
                    if scale is None:
                        S.op("dve", lambda: nc.vector.tensor_copy(out=out_ap, in_=in_ap), reads=rb, writes=wb)
                    else:
                        S.op("dve", lambda: nc.vector.tensor_scalar(out=out_ap, in0=in_ap, scalar1=scale, scalar2=None, op0=ALU.mult),
                             reads=rb, writes=wb)

            for s in range(NS):
                hT, bh = hTr.next()
                for tt in range(4):
                    norm_T(x_all[(s * 4 + tt) * 128:(s * 4 + tt + 1) * 128, :], 0, hT, bh, tt * 128)
                wt, bw = load_w(C_KC, 512)
                S.op("dve", lambda: nc.vector.tensor_copy(out=kcb[:, :, :, 0:16], in_=kcb[:, :, :, 512:528]), reads=[b_kcb], writes=[b_kcb])
                for kv in range(2):
                    for gp in range(2):
                        pt, bp = fm(wt, bw, kv * 256 + gp * 128, 128, hT, bh)
                        evac(kcb[:, kv, gp, 16:528], pt[:, :], [bp], [b_kcb])
                i0 = 1 if s == 0 else 0
                ni = 32 - i0
                n0 = 32 * s - 1 + i0
                for kv in range(2):
                    for g in range(4):
                        base = (g % 2) * 64
                        pt, bp = psM.next()
                        mmg(pt[:, 0:ni], [(W1s[base:base + 64, kv, l, :],
                                            kcb[base:base + 64, kv, g // 2, 16 * i0 + l:16 * i0 + l + 16 * (ni - 1) + 1:16]) for l in range(32)],
                            [b_W1s, b_kcb], bp)
                        S.op("act", lambda: nc.scalar.activation(out=hidx[:, 0:ni], in_=pt[:, 0:ni], func=AF.Identity, bias=peb[:, kv:kv + 1]),
                             reads=[bp, b_peb], writes=[b_hidx])
                        gelu_tanh(hid[:, 0:ni], hidx[:, 0:ni], hidt[:, 0:ni], [b_hidx], [b_hid], b_hidt)
                        pt2, bp2 = psM.next()
                        mmg(pt2[0:64, 0:ni], [(W2s[:, kv, :], hid[:, 0:ni])], [b_W2, b_hid], bp2)
                        if kv == 0:
                            evac(KcT[0:64, g, n0:n0 + ni], pt2[0:64, 0:ni], [bp2], [b_KcT])
                        else:
                            evac(VcT[0:64, g, n0:n0 + ni], pt2[0:64, 0:ni], [bp2], [b_VcT])
                for (c0, kd, vd) in ((C_KS, kS_d, vS_d), (C_KW, kW_d, vW_d)):
                    wt, bw = load_w(c0, 512)
                    for g in range(4):
                        pt, bp = fm(wt, bw, g * 64, 64, hT, bh)
                        sg, bsg = stg.next()
                        evac(sg[0:64, :], pt[0:64, :], [bp], [bsg])
                        S.dma("pool", kd[g, :, s * 512:(s + 1) * 512], sg[0:64, :], reads=[bsg], writes=[b_scr], chan="scr")
                    for tt in range(4):
                        pt, bp = tm(wt, bw, 256, 256, hT, bh, tt)
                        sg, bsg = stg.next()
                        evac(sg[:, 0:256], pt[:, 0:256], [bp], [bsg])
                        S.dma("pool", vd[(s * 4 + tt) * 128:(s * 4 + tt + 1) * 128, :], sg[:, 0:256], reads=[bsg], writes=[b_scr], chan="scr")
                wk = [load_w(C_KR, 512), load_w(C_KR + 512, 512)]
                for tt in range(4):
                    for hh in range(2):
                        pt, bp = tm(wk[hh][0], wk[hh][1], 0, 512, hT, bh, tt)
                        S.op("dve", lambda: nc.vector.tensor_tensor(out=krd[:, tt, hh * 512:(hh + 1) * 512].rearrange("p (h d) -> p h d", h=4),
                                                                    in0=pt[:].rearrange("p (h d) -> p h d", h=4),
                                                                    in1=kdec[:, hh * 4:(hh + 1) * 4].unsqueeze(2).to_broadcast([128, 4, 128]),
                                                                    op=ALU.mult), reads=[bp, b_kdec], writes=[b_krd])
                wv = [load_w(C_VR, 512), load_w(C_VR + 512, 512)]
                for tt in range(4):
                    for hh in range(2):
                        pt, bp = tm(wv[hh][0], wv[hh][1], 0, 512, hT, bh, tt)
                        evac(vrb[:, tt, hh * 512:(hh + 1) * 512], pt[:, :], [bp], [b_vrb])
                for tt in range(4):
                    ci = s * 4 + tt
                    par = ci % 2
                    for hh in range(2):
                        pk, bpk = psK.next()
                        for h4 in range(4):
                            h = hh * 4 + h4
                            S.op("pe", lambda: nc.tensor.matmul(out=pk[:, h4 * 128:(h4 + 1) * 128], lhsT=krd[:, tt, h * 128:(h + 1) * 128],
                                                                rhs=vrb[:, tt, h * 128:(h + 1) * 128], start=True, stop=True),
                                 reads=[b_krd, b_vrb], writes=[bpk], inc=(h4 == 3))
                        for h4 in range(4):
                            h = hh * 4 + h4
                            S.op("dve", lambda: nc.vector.scalar_tensor_tensor(out=Rst[:, 1 - par, h, :], in0=Rst[:, par, h, :], scalar=float(cdec[h]),
                                                                               in1=pk[:, h4 * 128:(h4 + 1) * 128], op0=ALU.mult, op1=ALU.add),
                                 reads=[b_Rst, bpk], writes=[b_Rst])
                    if par == 1:
                        pass
                    if par == 0:
                        S.op("pool", lambda: nc.gpsimd.tensor_tensor(out=Rown[:], in0=Rst[:, 1, :, :], in1=Rst[:, 0, :, :], op=ALU.subtract),
                             reads=[b_Rst], writes=[b_Rown])
                        S.op("dve", lambda: nc.vector.scalar_tensor_tensor(out=Rownb[:], in0=Rown[:], scalar=jsel[:, 0:1], in1=Rst[:, 0, :, :],
                                                                            op0=ALU.mult, op1=ALU.add), reads=[b_Rown, b_Rst, b_jsel], writes=[b_Rown])
                        S.dma("pool", R_d[ci // 2], Rownb[:], reads=[b_Rown], writes=[b_scr], chan="scr")

            S.op("dve", lambda: nc.vector.memset(VcA[:, :, :, 64:65], 1.0), writes=[b_VcA])
            for g in range(4):
                S.dma("sp", VcA[:, :, g, 65:65 + NSLC], h_overlap[:, :, :], writes=[b_VcA])
            for cc in range(NCC):
                for g in range(4):
                    pt, bp = psT.next()
                    S.op("pe", lambda: nc.tensor.transpose(out=pt[:, 0:64], in_=VcT[0:64, g, cc * 128:(cc + 1) * 128], identity=ident_b[0:64, 0:64]),
                         reads=[b_VcT, b_ident], writes=[bp])
                    evac(VcA[:, cc, g, 0:64], pt[:, 0:64], [bp], [b_VcA])

            for so in range(NOS):
                hT, bh = hTr.next()
                for tt in range(4):
                    norm_T(x_own[(so * 4 + tt) * 128:(so * 4 + tt + 1) * 128, :], 0, hT, bh, tt * 128)
                t0, t1 = so * 512, (so + 1) * 512
                for blk in range(2):
                    wt, bw = load_w(C_QA + blk * 512, 512)
                    for h8 in range(8):
                        pt, bp = fm(wt, bw, h8 * 64, 64, hT, bh)
                        sg, bsg = stg.next()
                        evac(sg[0:64, :], pt[0:64, :], [bp], [bsg], scale=0.125)
                        S.dma("pool", qA_d[blk * 8 + h8, :, t0:t1], sg[0:64, :], reads=[bsg], writes=[b_scr], chan="scr")
                for (c0, dd) in ((C_QR, qR_d), (C_KR, kR_d)):
                    for blk in range(2):
                        wt, bw = load_w(c0 + blk * 512, 512)
                        for h4 in range(4):
                            pt, bp = fm(wt, bw, h4 * 128, 128, hT, bh)
                            sg, bsg = stg.next()
                            evac(sg[:, :], pt[:, :], [bp], [bsg])
                            S.dma("pool", dd[blk * 4 + h4, :, t0:t1], sg[:, :], reads=[bsg], writes=[b_scr], chan="scr")
                for blk in range(2):
                    wt, bw = load_w(C_VR + blk * 512, 512)
                    for tt in range(4):
                        pt, bp = tm(wt, bw, 0, 512, hT, bh, tt)
                        sg, bsg = stg.next()
                        evac(sg[:, :], pt[:, :], [bp], [bsg])
                        S.dma("pool", vR_d[t0 + tt * 128:t0 + (tt + 1) * 128, blk * 512:(blk + 1) * 512], sg[:, :], reads=[bsg], writes=[b_scr], chan="scr")
                for blk in range(2):
                    wt, bw = load_w(C_GR + blk * 512, 512)
                    for tt in range(4):
                        pt, bp = tm(wt, bw, 0, 512, hT, bh, tt)
                        sf, bsf = stgf.next()
                        evac(sf[:, 0:512], pt[:, :], [bp], [bsf])
                        S.dma("pool", gR_d[t0 + tt * 128:t0 + (tt + 1) * 128, blk * 512:(blk + 1) * 512], sf[:, 0:512], reads=[bsf], writes=[b_scr], chan="scr")
                wt, bw = load_w(C_GA, 48)
                for tt in range(4):
                    pt, bp = tm(wt, bw, 0, 48, hT, bh, tt)
                    sf, bsf = stgf.next()
                    evac(sf[:, 0:48], pt[:, 0:48], [bp], [bsf])
                    S.dma("pool", gA_d[t0 + tt * 128:t0 + (tt + 1) * 128, :], sf[:, 0:48], reads=[bsf], writes=[b_scr], chan="scr")
            S.barrier()


        b_omix = Buf("omix")
        NB1 = 65 + NSLC
        with scope() as ls:
            KsT = sbl(ls, "KsT", [71, 4, SEQ], BF16); b_KsT = Buf("KsT")
            VsA = sbl(ls, "VsA", [128, NT, 4, 65], BF16); b_VsA = Buf("VsA")
            Eall = sbl(ls, "Eall", [NSLC, NT, 128], BF16); b_E = Buf("Eall")
            cmab = sbl(ls, "cmab", [128, 2, 128], BF16); wmask = sbl(ls, "wmask", [128, 6, 128], BF16); b_cm = Buf("cm")
            maskrep = sbl(ls, "maskrep", [128, 8, 4, 128], BF16); b_mr = Buf("maskrep")
            g_nsaB = sbl(ls, "g_nsaB", [128, 1024], F32); b_gn = Buf("gnsa")
            QTr = ring(ls, "QT", 1, [71, 4, 512], BF16)
            KwTr = ring(ls, "KwT", 1, [71, 4, 768], BF16)
            VwAr = ring(ls, "VwA", 1, [128, 6, 4, 65], BF16)
            cm1r = ring(ls, "cm1", 2, [128, NCC, 128], BF16)
            cm1rep = sbl(ls, "cm1rep", [128, NCC, 4, 128], BF16); b_c1r = Buf("cm1rep")
            vmr = ring(ls, "vm", 1, [128, 2, NSLC], F32)
            gar = ring(ls, "ga", 2, [128, 48], F32)
            PTr = ring(ls, "PT", 3, [128, 512], BF16)
            nsT = sbl(ls, "nsT", [NSLC, 4, 128], BF16); b_nsT = Buf("nsT")
            imp = sbl(ls, "imp", [128, NSLC], F32); impr = sbl(ls, "impr", [128, NSLC], F32); b_imp = Buf("imp"); b_impr = Buf("impr")
            m8 = sbl(ls, "m8", [128, 16], F32); b_m8 = Buf("m8")
            selb = sbl(ls, "selb", [128, NSLC], F32); negs = sbl(ls, "negs", [128, NSLC], BF16); b_sel = Buf("sel")
            rden = sbl(ls, "rden", [128, 3, 4], F32); cf = sbl(ls, "cf", [128, 3, 4], F32); b_rden = Buf("rden")
            oacc = sbl(ls, "oacc", [128, 16, 64], F32); otmp = sbl(ls, "otmp", [128, 4, 64], F32); b_oacc = Buf("oacc"); b_otmp = Buf("otmp")
            osq = sbl(ls, "osq", [128, 16, 64], F32); oss = sbl(ls, "oss", [128, 16], F32); b_osq = Buf("osq")
            onb = ring(ls, "onb", 1, [128, 1024], BF16)
            psS = pring(ls, "psS", 3, [128, 512], F32)
            psO1 = psl(ls, "psO1", [128, 4, 256], F32); b_pO1 = Buf("pO1")
            psO2 = psl(ls, "psO2", [128, 512], F32); b_pO2 = Buf("pO2")
            psO3 = psl(ls, "psO3", [128, 512], F32); b_pO3 = Buf("pO3")
            psX = psl(ls, "psX", [128, 1024], BF16); b_pX = Buf("pX")
            assert NB1 <= 256
            for g in range(4):
                S.dma("sp", KsT[0:64, g, :], kS_d[g, :, :], reads=[b_scr], writes=[b_KsT])
                S.dma("sp", KsT[64:71, g, :], h_kaug[:, :], writes=[b_KsT])
            for c4 in range(0, NT, 4):
                for g in range(4):
                    S.dma("pool", VsA[:, c4:c4 + 4, g, 0:64], vS_d[c4 * 128:(c4 + 4) * 128, g * 64:(g + 1) * 64].rearrange("(c p) d -> p c d", p=128),
                          reads=[b_scr], writes=[b_VsA])
            S.op("dve", lambda: nc.vector.memset(VsA[:, :, :, 64:65], 1.0), writes=[b_VsA])
            S.dma("sp", Eall[:], h_eall[:, :, :], writes=[b_E])
            S.dma("sp", cmab[:], h_cmab[:, :, :], writes=[b_cm])
            S.dma("sp", wmask[:], h_wmask[:, :, :], writes=[b_cm])
            S.op("dve", lambda: nc.vector.tensor_copy(out=maskrep[:, 0:2, :, :], in_=cmab[:].unsqueeze(2).to_broadcast([128, 2, 4, 128])),
                 reads=[b_cm], writes=[b_mr])
            S.op("dve", lambda: nc.vector.tensor_copy(out=maskrep[:, 2:8, :, :], in_=wmask[:].unsqueeze(2).to_broadcast([128, 6, 4, 128])),
                 reads=[b_cm], writes=[b_mr])
            S.dma("sp", g_nsaB[:], g_nsa[0:1, :].partition_broadcast(128), writes=[b_gn])
            for (vt, bv) in zip(VwAr.tiles, VwAr.bufs):
                S.op("dve", lambda: nc.vector.memset(vt[:, :, :, 64:65], 1.0), writes=[bv])

            pend = []

            def flush_pv():
                while pend:
                    (PT, bP, rhsV, bV, psO, b_pO, width, first, last) = pend.pop(0)
                    for r in range(4):
                        st_ = first and (r == 0 or (width == NB1 and r == 2))
                        S.op("pe", lambda: nc.tensor.matmul(out=psO(r), lhsT=PT[:, r * 128:(r + 1) * 128], rhs=rhsV, start=st_, stop=last,
                                                            skip_group_check=True),
                             reads=[bP, bV], writes=[b_pO], inc=(r == 3))

            def attn_chunk(pairs, rb, rhsV, bV, psO, b_pO, width, first, last):
                pS, bS = psS.next()
                mmg(pS[:], pairs, rb, bS)
                PT, bP = PTr.next()
                S.op("act", lambda: nc.scalar.activation(out=PT[:], in_=pS[:], func=AF.Exp), reads=[bS], writes=[bP])
                flush_pv()
                pend.append((PT, bP, rhsV, bV, psO, b_pO, width, first, last))

            def finish_branch(bi, g, psO, b_pO, first_branch):
                flush_pv()
                for r in range(4):
                    S.op("dve", lambda: nc.vector.tensor_scalar(out=rden[:, bi, r:r + 1], in0=psO(r)[:, 64:65], scalar1=1e-30, scalar2=None, op0=ALU.max),
                         reads=[b_pO], writes=[b_rden])
                S.op("dve", lambda: nc.vector.reciprocal(out=rden[:, bi, :], in_=rden[:, bi, :]), reads=[b_rden], writes=[b_rden])
                S.op("dve", lambda: nc.vector.tensor_tensor(out=cf[:, bi, :], in0=rden[:, bi, :],
                                                            in1=ga[:, g * 12:(g + 1) * 12].rearrange("p (r b) -> p r b", b=3)[:, :, bi], op=ALU.mult),
                     reads=[b_rden, bga], writes=[b_rden])
                for r in range(4):
                    h = g * 4 + r
                    if first_branch:
                        S.op("dve", lambda: nc.vector.tensor_scalar(out=oacc[:, h, :], in0=psO(r)[:, 0:64], scalar1=cf[:, bi, r:r + 1], scalar2=None, op0=ALU.mult),
                             reads=[b_pO, b_rden], writes=[b_oacc])
                    else:
                        S.op("dve", lambda: nc.vector.scalar_tensor_tensor(out=oacc[:, h, :], in0=psO(r)[:, 0:64], scalar=cf[:, bi, r:r + 1], in1=oacc[:, h, :],
                                                                           op0=ALU.mult, op1=ALU.add), reads=[b_pO, b_rden, b_oacc], writes=[b_oacc])

            for m in range(NO):
                QT, bQ = QTr.next()
                for g in range(4):
                    S.dma("sp", QT[0:64, g, :].rearrange("d (r q) -> d r q", r=4),
                          qA_d[4 * g:4 * g + 4, :, m * 128:(m + 1) * 128].rearrange("r d q -> d r q"), reads=[b_scr], writes=[bQ])
                S.dma("sp", QT[64:71, :, :], h_qaug[m].rearrange("g a q -> a g q"), writes=[bQ])
                c0 = max(0, 2 * m - 4)
                u0 = c0 - (2 * m - 4)
                nch = 2 * m + 2 - c0
                KwT, bKw = KwTr.next()
                VwA, bVw = VwAr.next()
                for g in range(4):
                    S.dma("pool", KwT[0:64, g, u0 * 128:(u0 + nch) * 128], kW_d[g, :, c0 * 128:(c0 + nch) * 128], reads=[b_scr], writes=[bKw])
                    S.dma("pool", KwT[64:71, g, u0 * 128:(u0 + nch) * 128], h_kaug[:, c0 * 128:(c0 + nch) * 128], writes=[bKw])
                for g in range(4):
                    S.dma("pool", VwA[:, u0:u0 + nch, g, 0:64], vW_d[c0 * 128:(c0 + nch) * 128, g * 64:(g + 1) * 64].rearrange("(c p) d -> p c d", p=128),
                          reads=[b_scr], writes=[bVw])
                cm1, bc1 = cm1r.next()
                S.dma("sp", cm1[:], h_cm1[m], writes=[bc1])
                S.op("pool", lambda: nc.gpsimd.tensor_copy(out=cm1rep[:], in_=cm1[:].unsqueeze(2).to_broadcast([128, NCC, 4, 128])),
                     reads=[bc1], writes=[b_c1r])
                vm, bvm = vmr.next()
                S.dma("sp", vm[:, 0, :], h_vmask[m], writes=[bvm])
                S.dma("sp", vm[:, 1, :], h_amask[m], writes=[bvm])
                ga, bga = gar.next()
                S.dma("sp", ga[:], gA_d[m * 128:(m + 1) * 128, :], reads=[b_scr], writes=[bga])
                S.op("act", lambda: nc.scalar.activation(out=ga[:], in_=ga[:], func=AF.Sigmoid), reads=[bga], writes=[bga])
                for g in range(4):
                    for cc in range(NCC):
                        attn_chunk([(KcT[0:71, g, cc * 128:(cc + 1) * 128], QT[0:71, g, :]),
                                    (ident_b[:], cm1rep[:, cc, :, :].rearrange("p r q -> p (r q)"))],
                                   [b_KcT, bQ, b_ident, b_c1r], VcA[:, cc, g, :], b_VcA,
                                   lambda r: psO1[:, r, 0:NB1], b_pO1, NB1, cc == 0, cc == NCC - 1)
                    finish_branch(0, g, lambda r: psO1[:, r, 0:NB1], b_pO1, True)
                    S.op("dve", lambda: nc.vector.tensor_scalar(out=imp[:], in0=psO1[:, 0, 65:NB1], scalar1=rden[:, 0, 0:1], scalar2=None, op0=ALU.mult),
                         reads=[b_pO1, b_rden], writes=[b_imp])
                    for r in range(1, 4):
                        S.op("dve", lambda: nc.vector.scalar_tensor_tensor(out=imp[:], in0=psO1[:, r, 65:NB1], scalar=rden[:, 0, r:r + 1], in1=imp[:],
                                                                           op0=ALU.mult, op1=ALU.add), reads=[b_pO1, b_rden, b_imp], writes=[b_imp])
                    S.op("dve", lambda: nc.vector.tensor_tensor(out=imp[:], in0=imp[:], in1=vm[:, 0, :], op=ALU.mult), reads=[b_imp, bvm], writes=[b_imp])
                    S.op("dve", lambda: nc.vector.tensor_tensor(out=imp[:], in0=imp[:], in1=vm[:, 1, :], op=ALU.add), reads=[b_imp, bvm], writes=[b_imp])
                    S.op("dve", lambda: nc.vector.max(out=m8[:, 0:8], in_=imp[:]), reads=[b_imp], writes=[b_m8])
                    S.op("dve", lambda: nc.vector.match_replace(out=impr[:], in_to_replace=m8[:, 0:8], in_values=imp[:], imm_value=-1e30),
                         reads=[b_imp, b_m8], writes=[b_impr])
                    S.op("dve", lambda: nc.vector.max(out=m8[:, 8:16], in_=impr[:]), reads=[b_impr], writes=[b_m8])
                    S.op("dve", lambda: nc.vector.tensor_scalar(out=selb[:], in0=imp[:], scalar1=m8[:, 15:16], scalar2=1.0, op0=ALU.is_ge, op1=ALU.subtract),
                         reads=[b_imp, b_m8], writes=[b_sel])
                    S.op("dve", lambda: nc.vector.tensor_scalar(out=negs[:], in0=selb[:], scalar1=-NEG, scalar2=None, op0=ALU.mult), reads=[b_sel], writes=[b_sel])
                    S.op("pe", lambda: nc.tensor.transpose(out=psX[0:NSLC, 0:128], in_=negs[:], identity=ident_b[:]), reads=[b_sel, b_ident], writes=[b_pX])
                    S.op("dve", lambda: nc.vector.tensor_copy(out=nsT[:], in_=psX[0:NSLC, 0:128].unsqueeze(1).to_broadcast([NSLC, 4, 128])),
                         reads=[b_pX], writes=[b_nsT])
                    for u in range(u0, 6):
                        pairs = [(KwT[0:71, g, u * 128:(u + 1) * 128], QT[0:71, g, :]),
                                 (ident_b[:], maskrep[:, 2 + u, :, :].rearrange("p r q -> p (r q)"))]
                        attn_chunk(pairs, [bKw, bQ, b_ident, b_mr], VwA[:, u, g, :], bVw, lambda r: psO3[:, r * 65:(r + 1) * 65], b_pO3, 65, u == u0, u == 5)
                    finish_branch(2, g, lambda r: psO3[:, r * 65:(r + 1) * 65], b_pO3, False)
                    nc2 = 2 * m + 2
                    for c in range(nc2):
                        pairs = [(KsT[0:71, g, c * 128:(c + 1) * 128], QT[0:71, g, :]),
                                 (Eall[0:NSLC, c, :], nsT[:].rearrange("p r q -> p (r q)"))]
                        rb = [b_KsT, bQ, b_E, b_nsT]
                        if c >= 2 * m:
                            pairs.append((ident_b[:], maskrep[:, c - 2 * m, :, :].rearrange("p r q -> p (r q)")))
                            rb = rb + [b_ident, b_mr]
                        attn_chunk(pairs, rb, VsA[:, c, g, :], b_VsA, lambda r: psO2[:, r * 65:(r + 1) * 65], b_pO2, 65, c == 0, c == nc2 - 1)
                    finish_branch(1, g, lambda r: psO2[:, r * 65:(r + 1) * 65], b_pO2, False)
                S.op("dve", lambda: nc.vector.tensor_tensor(out=osq[:], in0=oacc[:], in1=oacc[:], op=ALU.mult), reads=[b_oacc], writes=[b_osq])
                S.op("dve", lambda: nc.vector.tensor_reduce(out=oss[:], in_=osq[:], axis=AX.X, op=ALU.add), reads=[b_osq], writes=[b_osq])
                rstd_from_ss(oss[:], oss[:], 64, [b_osq], [b_osq])
                S.op("dve", lambda: nc.vector.tensor_tensor(out=osq[:], in0=oacc[:], in1=oss[:].unsqueeze(2).to_broadcast([128, 16, 64]), op=ALU.mult),
                     reads=[b_oacc, b_osq], writes=[b_osq])
                on, bon = onb.next()
                S.op("dve", lambda: nc.vector.tensor_tensor(out=on[:], in0=osq[:].rearrange("p h d -> p (h d)"), in1=g_nsaB[:], op=ALU.mult),
                     reads=[b_osq, b_gn], writes=[bon])
                S.dma("sp", omix_d[m * 128:(m + 1) * 128, 0:1024], on[:], reads=[bon], writes=[b_omix], chan="omix")
            S.barrier()


        with scope() as ls:
            decT = sbl(ls, "decT", [128, 8, 128], F32); qdecB = sbl(ls, "qdecB", [128, 8, 128], F32); g_retB = sbl(ls, "g_retB", [128, 1024], F32)
            b_rc = Buf("rconst")
            S.dma("sp", decT[:], h_decT[:, :, :], writes=[b_rc]); S.dma("sp", qdecB[:], h_qdecB[:, :, :], writes=[b_rc])
            S.dma("sp", g_retB[:], g_ret[0:1, :].partition_broadcast(128), writes=[b_rc])
            qTr = ring(ls, "rqT", 2, [128, 8, 128], BF16); kTr = ring(ls, "rkT", 2, [128, 8, 128], BF16)
            vr_ = ring(ls, "rv", 2, [128, 1024], BF16); Rr = ring(ls, "rR", 2, [128, 8, 128], BF16); grr = ring(ls, "rg", 2, [128, 1024], F32)
            qd = sbl(ls, "rqd", [128, 8, 128], BF16); b_qd = Buf("qd")
            AT = sbl(ls, "rAT", [128, 8, 128], BF16); b_AT = Buf("AT")
            ro = sbl(ls, "ro", [128, 8, 128], F32); rsq = sbl(ls, "rsq", [128, 8, 128], F32); b_ro = Buf("ro"); b_rsq = Buf("rsq")
            rmu = sbl(ls, "rmu", [128, 2, 8], F32); b_rmu = Buf("rmu")
            orb = ring(ls, "orb", 2, [128, 1024], BF16)
            psA = pring(ls, "rpsA", 2, [128, 512], F32); psO = pring(ls, "rpsO", 2, [128, 512], F32)
            for m in range(NO):
                sl = slice(m * 128, (m + 1) * 128)
                qT, bq = qTr.next(); kT, bk = kTr.next(); v_, bv = vr_.next(); R_, bR = Rr.next(); gr, bg = grr.next()
                S.dma("sp", qT[:], qR_d[:, :, sl].rearrange("h d c -> d h c"), reads=[b_scr], writes=[bq])
                S.dma("sp", kT[:], kR_d[:, :, sl].rearrange("h d c -> d h c"), reads=[b_scr], writes=[bk])
                S.dma("pool", v_[:], vR_d[sl, :], reads=[b_scr], writes=[bv])
                S.dma("pool", R_[:], R_d[m], reads=[b_scr], writes=[bR])
                S.dma("pool", gr[:], gR_d[sl, :], reads=[b_scr], writes=[bg])
                S.op("pool", lambda: nc.gpsimd.tensor_tensor(out=qd[:], in0=qT[:], in1=qdecB[:], op=ALU.mult), reads=[bq, b_rc], writes=[b_qd])
                for hh in range(2):
                    pa, bpa = psA.next()
                    for h4 in range(4):
                        h = hh * 4 + h4
                        S.op("pe", lambda: nc.tensor.matmul(out=pa[:, h4 * 128:(h4 + 1) * 128], lhsT=kT[:, h, :], rhs=qT[:, h, :], start=True, stop=True),
                             reads=[bk, bq], writes=[bpa], inc=(h4 == 3))
                    S.op("dve", lambda: nc.vector.tensor_tensor(out=AT[:, hh * 4:(hh + 1) * 4, :], in0=pa[:].rearrange("p (h c) -> p h c", h=4),
                                                                in1=decT[:, hh * 4:(hh + 1) * 4, :], op=ALU.mult), reads=[bpa, b_rc], writes=[b_AT])
                for hh in range(2):
                    po, bpo = psO.next()
                    for h4 in range(4):
                        h = hh * 4 + h4
                        mmg(po[:, h4 * 128:(h4 + 1) * 128], [(AT[:, h, :], v_[:, h * 128:(h + 1) * 128]), (qd[:, h, :], R_[:, h, :])],
                            [b_AT, bv, b_qd, bR], bpo)
                    S.op("act", lambda: nc.scalar.copy(out=ro[:, hh * 4:(hh + 1) * 4, :], in_=po[:].rearrange("p (h c) -> p h c", h=4)), reads=[bpo], writes=[b_ro])
                S.op("dve", lambda: nc.vector.tensor_reduce(out=rmu[:, 0, :], in_=ro[:], axis=AX.X, op=ALU.add), reads=[b_ro], writes=[b_rmu])
                S.op("dve", lambda: nc.vector.tensor_scalar(out=rmu[:, 0, :], in0=rmu[:, 0, :], scalar1=-1.0 / 128, scalar2=None, op0=ALU.mult), reads=[b_rmu], writes=[b_rmu])
                S.op("dve", lambda: nc.vector.tensor_tensor(out=ro[:], in0=ro[:], in1=rmu[:, 0, :].unsqueeze(2).to_broadcast([128, 8, 128]), op=ALU.add),
                     reads=[b_ro, b_rmu], writes=[b_ro])
                S.op("dve", lambda: nc.vector.tensor_tensor(out=rsq[:], in0=ro[:], in1=ro[:], op=ALU.mult), reads=[b_ro], writes=[b_rsq])
                S.op("dve", lambda: nc.vector.tensor_reduce(out=rmu[:, 1, :], in_=rsq[:], axis=AX.X, op=ALU.add), reads=[b_rsq], writes=[b_rmu])
                rstd_from_ss(rmu[:, 1, :], rmu[:, 1, :], 128, [b_rmu], [b_rmu])
                S.op("dve", lambda: nc.vector.tensor_tensor(out=rsq[:], in0=ro[:], in1=rmu[:, 1, :].unsqueeze(2).to_broadcast([128, 8, 128]), op=ALU.mult),
                     reads=[b_ro, b_rmu], writes=[b_rsq])
                S.op("dve", lambda: nc.vector.tensor_tensor(out=rsq[:].rearrange("p h d -> p (h d)"), in0=rsq[:].rearrange("p h d -> p (h d)"), in1=g_retB[:], op=ALU.mult),
                     reads=[b_rsq, b_rc], writes=[b_rsq])
                S.op("act", lambda: nc.scalar.activation(out=gr[:], in_=gr[:], func=AF.Silu), reads=[bg], writes=[bg])
                ob, bob = orb.next()
                S.op("dve", lambda: nc.vector.tensor_tensor(out=ob[:], in0=rsq[:].rearrange("p h d -> p (h d)"), in1=gr[:], op=ALU.mult),
                     reads=[b_rsq, bg], writes=[bob])
                S.dma("sp", omix_d[sl, 1024:2048], ob[:], reads=[bob], writes=[b_omix], chan="omix")
            S.barrier()

        b_x1 = Buf("x1d")
        with scope() as ls:
            wo = sbl(ls, "wo", [128, 16, D], BF16); b_wo = Buf("wo")
            S.dma("sp", wo[:], w_out_bf.rearrange("(k p) n -> p k n", p=128), reads=[b_wscr], writes=[b_wo])
            omr = ring(ls, "om", 2, [128, D], BF16); oTr = ring(ls, "oT", 2, [128, 16, 128], BF16)
            xor_ = ring(ls, "xo", 2, [128, D], F32); x1r = ring(ls, "x1", 2, [128, D], F32)
            junk = sbl(ls, "junkD", [128, D], BF16); b_junk = Buf("junkD")
            ssr = ring(ls, "ssD", 2, [128, 2], F32); xnr = ring(ls, "xnD", 2, [128, D], BF16)
            h2r = ring(ls, "h2T", 2, [128, 16, 128], BF16)
            psT = pring(ls, "DpsT", 2, [128, 1024], BF16); psM = pring(ls, "DpsM", 4, [128, 512], F32)
            for m in range(NO):
                sl = slice(m * 128, (m + 1) * 128)
                om, bom = omr.next()
                S.dma("sp", om[:], omix_d[sl, :], reads=[b_omix], writes=[bom])
                oT, boT = oTr.next()
                for half in range(2):
                    pt, bp = psT.next()
                    for k8 in range(8):
                        kc = half * 8 + k8
                        S.op("pe", lambda: nc.tensor.transpose(out=pt[:, k8 * 128:(k8 + 1) * 128], in_=om[:, kc * 128:(kc + 1) * 128], identity=ident_b[:]),
                             reads=[bom, b_ident], writes=[bp], inc=(k8 == 7))
                    S.op("act", lambda: nc.scalar.copy(out=oT[:, half * 8:(half + 1) * 8, :], in_=pt[:].rearrange("p (k t) -> p k t", k=8)), reads=[bp], writes=[boT])
                xo, bxo = xor_.next()
                S.dma("pool", xo[:], x_own[sl, :], writes=[bxo])
                x1, bx1 = x1r.next()
                for n4 in range(4):
                    pt, bp = psM.next()
                    ns = slice(n4 * 512, (n4 + 1) * 512)
                    mmg(pt[:], [(oT[:, k, :], wo[:, k, ns]) for k in range(16)], [boT, b_wo], bp)
                    S.op("dve", lambda: nc.vector.tensor_tensor(out=x1[:, ns], in0=pt[:], in1=gate1B[:, ns], op=ALU.mult), reads=[bp, b_gate], writes=[bx1])
                    S.op("pool", lambda: nc.gpsimd.tensor_tensor(out=x1[:, ns], in0=x1[:, ns], in1=xo[:, ns], op=ALU.add), reads=[bx1, bxo], writes=[bx1])
                S.dma("sp", x1_d[sl, :], x1[:], reads=[bx1], writes=[b_x1], chan="x1d")
                ss, bs = ssr.next()
                S.op("act", lambda: nc.scalar.activation(out=junk[:], in_=x1[:], func=AF.Square, accum_out=ss[:, 0:1]), reads=[bx1], writes=[b_junk, bs])
                rstd_from_ss(ss[:, 1:2], ss[:, 0:1], D, [bs], [bs])
                xn, bn = xnr.next()
                S.op("act", lambda: nc.scalar.activation(out=xn[:], in_=x1[:], func=AF.Copy, scale=ss[:, 1:2]), reads=[bx1, bs], writes=[bn])
                h2T, bh2 = h2r.next()
                for half in range(2):
                    pt, bp = psT.next()
                    for k8 in range(8):
                        kc = half * 8 + k8
                        S.op("pe", lambda: nc.tensor.transpose(out=pt[:, k8 * 128:(k8 + 1) * 128], in_=xn[:, kc * 128:(kc + 1) * 128], identity=ident_b[:]),
                             reads=[bn, b_ident], writes=[bp], inc=(k8 == 7))
                    dst = h2T[:, half * 8:(half + 1) * 8, :]
                    S.op("dve", lambda: nc.vector.tensor_tensor(out=dst, in0=pt[:].rearrange("p (k t) -> p k t", k=8),
                                                                in1=modT[:, 2, half * 8:(half + 1) * 8].unsqueeze(2).to_broadcast([128, 8, 128]), op=ALU.mult),
                         reads=[bp, b_modT], writes=[bh2])
                    S.op("dve", lambda: nc.vector.tensor_tensor(out=dst, in0=dst,
                                                                in1=modT[:, 3, half * 8:(half + 1) * 8].unsqueeze(2).to_broadcast([128, 8, 128]), op=ALU.add),
                         reads=[bh2, b_modT], writes=[bh2])
                S.dma("sp", h2T_d[:, :, sl].rearrange("k p t -> p k t"), h2T[:], reads=[bh2], writes=[b_x1], chan="x1d")
            S.barrier()

        with scope() as ls:
            wq = sbl(ls, "wq", [128, 16, D], BF16); b_wo = Buf("wq")
            S.dma("pool", wq[:], w_q_bf.rearrange("(k p) n -> p k n", p=128), reads=[b_wscr], writes=[b_wo])
            h2r = ring(ls, "D2h2T", 2, [128, 16, 128], BF16)
            qTa = ring(ls, "qTa", 3, [128, 128], BF16)
            scr_ = ring(ls, "sc", 2, [128, 16, 128], F32); scx = sbl(ls, "scx", [128, 256], F32); b_scx = Buf("scx")
            v16 = sbl(ls, "v16", [128, 16, 16], F32); b_v16 = Buf("v16")
            cand = sbl(ls, "cand", [128, 8, 256], F32); b_cand = Buf("cand")
            t16 = sbl(ls, "t16", [128, 8, 16], F32); b_t16 = Buf("t16")
            tneg = sbl(ls, "tneg", [128, 8], F32); zs = sbl(ls, "zs", [128, 8], F32); ej = sbl(ls, "ej", [128, 16], F32); b_z = Buf("z")
            psQ = pring(ls, "DpsQ", 4, [128, 512], F32)
            for m in range(NO):
                sl = slice(m * 128, (m + 1) * 128)
                h2T, bh2 = h2r.next()
                S.dma("sp", h2T[:], h2T_d[:, :, sl].rearrange("k p t -> p k t"), reads=[b_x1], writes=[bh2])
                sc, bsc = scr_.next()
                for a in range(16):
                    pq, bpq = psQ.next()
                    mmg(pq[:, 0:128], [(wq[:, k, a * 128:(a + 1) * 128], h2T[:, k, :]) for k in range(16)], [b_wo, bh2], bpq)
                    qa, bqa = qTa.next()
                    S.op("act", lambda: nc.scalar.copy(out=qa[:], in_=pq[:, 0:128]), reads=[bpq], writes=[bqa])
                    pq2, bpq2 = psQ.next()
                    mmg(pq2[:, 0:128], [(qa[:], skT[:, a, :])], [bqa, b_skT], bpq2)
                    S.op("act", lambda: nc.scalar.copy(out=sc[:, a, :], in_=pq2[:, 0:128]), reads=[bpq2], writes=[bsc])
                    S.op("dve", lambda: nc.vector.max(out=v16[:, a, 0:8], in_=sc[:, a, :]), reads=[bsc], writes=[b_v16])
                    S.op("dve", lambda: nc.vector.match_replace(out=scx[:, 0:128], in_to_replace=v16[:, a, 0:8], in_values=sc[:, a, :], imm_value=-1e30),
                         reads=[bsc, b_v16], writes=[b_scx])
                    S.op("dve", lambda: nc.vector.max(out=v16[:, a, 8:16], in_=scx[:, 0:128]), reads=[b_scx], writes=[b_v16])
                S.dma("sp", sc_d[sl, :, :], sc[:], reads=[bsc], writes=[b_x1], chan="x1d")
                v16v = v16[:].rearrange("p (h t) k -> p h t k", t=2)
                S.op("dve", lambda: nc.vector.tensor_tensor(out=cand[:].rearrange("p h (i j) -> p h i j", i=16),
                                                            in0=v16v[:, :, 0, :].unsqueeze(3).to_broadcast([128, 8, 16, 16]),
                                                            in1=v16v[:, :, 1, :].unsqueeze(2).to_broadcast([128, 8, 16, 16]), op=ALU.add),
                     reads=[b_v16], writes=[b_cand])
                for h in range(8):
                    S.op("dve", lambda: nc.vector.max(out=t16[:, h, 0:8], in_=cand[:, h, :]), reads=[b_cand], writes=[b_t16])
                    S.op("dve", lambda: nc.vector.match_replace(out=scx[:], in_to_replace=t16[:, h, 0:8], in_values=cand[:, h, :], imm_value=-1e30),
                         reads=[b_cand, b_t16], writes=[b_scx])
                    S.op("dve", lambda: nc.vector.max(out=t16[:, h, 8:16], in_=scx[:]), reads=[b_scx], writes=[b_t16])
                S.op("dve", lambda: nc.vector.tensor_scalar(out=tneg[:], in0=t16[:, :, 15], scalar1=-1.0, scalar2=None, op0=ALU.mult), reads=[b_t16], writes=[b_z])
                for h in range(8):
                    S.op("act", lambda: nc.scalar.activation(out=ej[:], in_=t16[:, h, :], func=AF.Exp, bias=tneg[:, h:h + 1], accum_out=zs[:, h:h + 1]),
                         reads=[b_t16, b_z], writes=[b_z])
                S.op("act", lambda: nc.scalar.activation(out=zs[:], in_=zs[:], func=AF.Ln), reads=[b_z], writes=[b_z])
                S.op("dve", lambda: nc.vector.tensor_tensor(out=lnz_all[:, m, :], in0=zs[:], in1=t16[:, :, 15], op=ALU.add), reads=[b_z, b_t16], writes=[b_thr[m]])
                S.op("dve", lambda: nc.vector.tensor_scalar(out=thr_all[:, m, :], in0=t16[:, :, 15], scalar1=-1e-5, scalar2=None, op0=ALU.add),
                     reads=[b_t16], writes=[b_thr[m]])
            S.barrier()

        TT = 2
        with scope() as ls:
            g_finB = sbl(ls, "g_finB", [128, D], F32); b_gf = Buf("gfin")
            S.dma("sp", g_finB[:], g_fin[0:1, :].partition_broadcast(128), writes=[b_gf])
            h2T2 = sbl(ls, "Eh2T", [128, TT, 16, 128], BF16); b_h2 = [Buf("Eh2T%d" % t) for t in range(TT)]
            scr_ = ring(ls, "Esc", 1, [128, 16, 128], F32)
            e1_2 = sbl(ls, "e1", [128, TT, 8, 128], F32); e2_2 = sbl(ls, "e2", [128, TT, 8, 128], F32); tau2 = sbl(ls, "tau", [128, TT, 8], F32)
            b_e = [Buf("e%d" % t) for t in range(TT)]
            acc = sbl(ls, "acc", [128, TT, D], F32); b_acc = [Buf("acc%d" % t) for t in range(TT)]
            UTr = ring(ls, "UTb", 2, [128, 16, 512], BF16); Vr = ring(ls, "Vb", 2, [128, 4, D], BF16)
            Er = ring(ls, "E", 2, [128, 8, 512], F32)
            Gr = ring(ls, "Gb", 2, [128, 8, 512], BF16)
            agr = ring(ls, "ag", 2, [128, 512], F32)
            cTr = ring(ls, "coefT", 2, [128, 4, 128], BF16)
            junk = Gr.tiles[0][:].rearrange("p h e -> p (h e)"); b_junk = Gr.bufs[0]
            ssE = sbl(ls, "ssE", [128, 2], F32); b_ssE = Buf("ssE")
            psA = pring(ls, "EpsA", 2, [128, 512], F32); psG = pring(ls, "EpsG", 2, [128, 512], F32)
            psOut = psl(ls, "EpsO", [128, D], F32); b_pOut = Buf("EpsO")
            UT_v = UT_bf.rearrange("(k p) e -> p k e", p=128)
            for sup in range(NO // TT):
                for t in range(TT):
                    m = sup * TT + t
                    sl = slice(m * 128, (m + 1) * 128)
                    sc, bsc = scr_.next()
                    S.dma("sp", h2T2[:, t, :, :], h2T_d[:, :, sl].rearrange("k p t -> p k t"), reads=[b_x1], writes=[b_h2[t]])
                    S.dma("sp", sc[:], sc_d[sl, :, :], reads=[b_x1], writes=[bsc])
                    scv = sc[:].rearrange("p (h t) k -> p h t k", t=2)
                    S.op("dve", lambda: nc.vector.tensor_tensor(out=e1_2[:, t, :, :], in0=scv[:, :, 0, :],
                                                                in1=lnz_all[:, m, :].unsqueeze(2).to_broadcast([128, 8, 128]), op=ALU.subtract),
                         reads=[bsc, b_thr[m]], writes=[b_e[t]])
                    S.op("act", lambda: nc.scalar.activation(out=e1_2[:, t, :, :], in_=e1_2[:, t, :, :], func=AF.Exp), reads=[b_e[t]], writes=[b_e[t]])
                    S.op("act", lambda: nc.scalar.activation(out=e2_2[:, t, :, :], in_=scv[:, :, 1, :], func=AF.Exp), reads=[bsc], writes=[b_e[t]])
                    S.op("dve", lambda: nc.vector.tensor_tensor(out=tau2[:, t, :], in0=thr_all[:, m, :], in1=lnz_all[:, m, :], op=ALU.subtract),
                         reads=[b_thr[m]], writes=[b_e[t]])
                    S.op("act", lambda: nc.scalar.activation(out=tau2[:, t, :], in_=tau2[:, t, :], func=AF.Exp), reads=[b_e[t]], writes=[b_e[t]])

                wcur = [None]

                def stageA(eb, t):
                    if t == 0:
                        UTb, bU = UTr.next(); Vb, bV = Vr.next()
                        S.dma("sp", UTb[:], UT_v[:, :, eb * 512:(eb + 1) * 512], reads=[b_wscr], writes=[bU])
                        S.dma("act" if False else "sp", Vb[:], V_bf[eb * 512:(eb + 1) * 512, :].rearrange("(c p) n -> p c n", p=128), reads=[b_wscr], writes=[bV])
                        wcur[0] = (UTb, bU, Vb, bV)
                    UTb, bU, Vb, bV = wcur[0]
                    Et, bE = Er.next(); Gb, bG = Gr.next()
                    S.op("pool", lambda: nc.gpsimd.tensor_tensor(out=Et[:].rearrange("p h (a k) -> p h a k", a=4),
                                                                 in0=e1_2[:, t, :, eb * 4:(eb + 1) * 4].unsqueeze(3).to_broadcast([128, 8, 4, 128]),
                                                                 in1=e2_2[:, t, :, :].unsqueeze(2).to_broadcast([128, 8, 4, 128]), op=ALU.mult),
                         reads=[b_e[t]], writes=[bE])
                    pa, bpa = psA.next()
                    for c in range(4):
                        mmg(pa[:, c * 128:(c + 1) * 128], [(UTb[:, k, c * 128:(c + 1) * 128], h2T2[:, t, k, :]) for k in range(16)], [bU, b_h2[t]], bpa)
                    ag, bag = agr.next()
                    S.op("act", lambda: nc.scalar.activation(out=ag[:], in_=pa[:], func=AF.Gelu_apprx_tanh), reads=[bpa], writes=[bag])
                    for h in range(8):
                        S.op("dve", lambda: nc.vector.scalar_tensor_tensor(out=Gb[:, h, :], in0=Et[:, h, :], scalar=tau2[:, t, h:h + 1], in1=Et[:, h, :],
                                                                           op0=ALU.is_ge, op1=ALU.mult), reads=[bE, b_e[t]], writes=[bG])
                    return (eb, t, Vb, bV, Gb, bG, ag, bag)

                def stageB(st):
                    (eb, t, Vb, bV, Gb, bG, ag, bag) = st
                    pg, bpg = psG.next()
                    for c in range(4):
                        mmg(pg[:, c * 128:(c + 1) * 128], [(Gb[:, h, c * 128:(c + 1) * 128], ident_b[:]) for h in range(8)], [bG, b_ident], bpg)
                    cT, bcT = cTr.next()
                    S.op("dve", lambda: nc.vector.tensor_tensor(out=cT[:].rearrange("p c t -> p (c t)"), in0=pg[:], in1=ag[:], op=ALU.mult),
                         reads=[bpg, bag], writes=[bcT])
                    for n4 in range(4):
                        for c in range(4):
                            S.op("pe", lambda: nc.tensor.matmul(out=psOut[:, n4 * 512:(n4 + 1) * 512], lhsT=cT[:, c, :], rhs=Vb[:, c, n4 * 512:(n4 + 1) * 512],
                                                                start=(c == 0), stop=(c == 3)),
                                 reads=[bcT, bV], writes=[b_pOut], inc=(n4 == 3 and c == 3))
                    if eb == 0:
                        S.op("act", lambda: nc.scalar.copy(out=acc[:, t, :], in_=psOut[:]), reads=[b_pOut], writes=[b_acc[t]])
                    else:
                        S.op("dve", lambda: nc.vector.tensor_tensor(out=acc[:, t, :], in0=psOut[:], in1=acc[:, t, :], op=ALU.add),
                             reads=[b_pOut, b_acc[t]], writes=[b_acc[t]])

                prev = None
                for eb in range(32):
                    for t in range(TT):
                        cur = stageA(eb, t)
                        if prev is not None:
                            stageB(prev)
                        prev = cur
                stageB(prev)
                for t in range(TT):
                    m = sup * TT + t
                    sl = slice(m * 128, (m + 1) * 128)
                    x1, bx1 = scr_.tiles[0][:].rearrange("p a k -> p (a k)"), scr_.bufs[0]
                    S.dma("sp", x1[:, :], x1_d[sl, :], reads=[b_x1], writes=[bx1])
                    x2 = acc[:, t, :]
                    S.op("dve", lambda: nc.vector.tensor_tensor(out=x2, in0=x2, in1=gate2B[:], op=ALU.mult), reads=[b_acc[t], b_gate], writes=[b_acc[t]])
                    S.op("pool", lambda: nc.gpsimd.tensor_tensor(out=x2, in0=x2, in1=x1[:, :], op=ALU.add), reads=[b_acc[t], bx1], writes=[b_acc[t]])
                    S.op("act", lambda: nc.scalar.activation(out=junk[:, 0:D], in_=x2, func=AF.Square, accum_out=ssE[:, 0:1]), reads=[b_acc[t]], writes=[b_junk, b_ssE])
                    rstd_from_ss(ssE[:, 1:2], ssE[:, 0:1], D, [b_ssE], [b_ssE])
                    S.op("dve", lambda: nc.vector.scalar_tensor_tensor(out=x1[:, :], in0=x2, scalar=ssE[:, 1:2], in1=g_finB[:], op0=ALU.mult, op1=ALU.mult),
                         reads=[b_acc[t], b_ssE, b_gf], writes=[bx1])
                    S.dma("sp", y[sl, :], x1[:, :], reads=[bx1], writes=[b_y], chan="yout")
            S.barrier()

        if dbg:
            with scope() as ls:
                dt_ = sbl(ls, "dbgt", [128, 4096], F32); b_dt = Buf("dbgt")
                dtb = sbl(ls, "dbgtb", [128, 4096], BF16)
                if "KcT" in dbg_out:
                    S.op("dve", lambda: nc.vector.tensor_copy(out=dt_[0:71, 0:4 * NCC * 128], in_=KcT[:].rearrange("p a b -> p (a b)")), reads=[b_KcT], writes=[b_dt])
                    S.dma("sp", dbg_out["KcT"][:, :], dt_[0:71, 0:4 * NCC * 128], reads=[b_dt], writes=[Buf("o1")])
                if "qA" in dbg_out:
                    S.dma("sp", dtb[0:64, 0:SO], qA_d[5, :, :], writes=[b_dt])
                    S.op("dve", lambda: nc.vector.tensor_copy(out=dt_[0:64, 0:SO], in_=dtb[0:64, 0:SO]), reads=[b_dt], writes=[b_dt])
                    S.dma("sp", dbg_out["qA"][:, :], dt_[0:64, 0:SO], reads=[b_dt], writes=[Buf("o2")])
                if "R" in dbg_out:
                    S.dma("sp", dtb[:, 0:1024], R_d[NO - 1].rearrange("p a b -> p (a b)"), writes=[b_dt])
                    S.op("dve", lambda: nc.vector.tensor_copy(out=dt_[:, 0:1024], in_=dtb[:, 0:1024]), reads=[b_dt], writes=[b_dt])
                    S.dma("sp", dbg_out["R"][:, :], dt_[:, 0:1024], reads=[b_dt], writes=[Buf("o3")])
                S.barrier()

        if dbg and ("omix" in dbg_out or "x1" in dbg_out):
            with scope() as ls:
                tb = sbl(ls, "tapb", [128, D], BF16); tf = sbl(ls, "tapf", [128, D], F32); b_tap = Buf("tap")
                for m in range(NO):
                    sl = slice(m * 128, (m + 1) * 128)
                    if "omix" in dbg_out:
                        S.dma("sp", tb[:], omix_d[sl, :], writes=[b_tap])
                        S.op("dve", lambda: nc.vector.tensor_copy(out=tf[:], in_=tb[:]), reads=[b_tap], writes=[b_tap])
                        S.dma("sp", dbg_out["omix"][sl, :], tf[:], reads=[b_tap], writes=[b_tap])
                    if "x1" in dbg_out:
                        S.dma("sp", tf[:], x1_d[sl, :], writes=[b_tap])
                        S.dma("sp", dbg_out["x1"][sl, :], tf[:], reads=[b_tap], writes=[b_tap])
        S.barrier()
        S.finish_all()
        print("ninst", S.ninst, "nsem", S.nsem)
    return nc


def core_inputs(inp, b, j, SEQ):
    NT = SEQ // 128
    xb = np.ascontiguousarray(inp["x"][b])
    own = np.ascontiguousarray(xb.reshape(NT // 2, 2, 128, D)[:, j].reshape(SEQ // 2, D))
    m = {
        "x_all": xb, "x_own": own,
        "c_in": np.ascontiguousarray(inp["c"][b].reshape(16, 128).T),
        "w_ada": inp["w_ada"][0], "b_ada": inp["b_ada"][0].reshape(1, -1),
        "g_mix": inp["g_norm_mix"][0].reshape(1, -1), "g_ffn": inp["g_norm_ffn"][0].reshape(1, -1),
        "g_fin": inp["g_norm_final"].reshape(1, -1), "w_in": inp["w_in"][0],
        "pe_k": inp["cmp_pe_k"][0], "pe_v": inp["cmp_pe_v"][0],
        "ck_w1": inp["cmp_k_w1"][0], "ck_w2": inp["cmp_k_w2"][0], "cv_w1": inp["cmp_v_w1"][0], "cv_w2": inp["cmp_v_w2"][0],
        "g_nsa": inp["g_nsa_out"][0].reshape(1, -1), "g_ret": inp["g_ret_out"][0].reshape(1, -1),
        "w_out": inp["w_out"][0], "w_q": inp["peer_w_q"][0], "subk": inp["peer_sub_keys"][0].reshape(16, 128, 128),
        "U": inp["peer_u"][0], "V": inp["peer_v"][0],
    }
    cs = host_consts(SEQ, j)
    for k in ("ident", "kaug", "caug", "qaug", "cm1", "vmask", "amask", "overlap", "eall", "cmab", "wmask",
              "decT", "kdec", "qdecB", "jsel"):
        m[k] = np.ascontiguousarray(cs[k])
    return {k: np.ascontiguousarray(np.asarray(v)) for k, v in m.items()}


def kernel(**inputs):
    inp = {k: np.asarray(v) for k, v in inputs.items()}
    B, SEQ, _ = inp["x"].shape
    nc = build(SEQ)
    maps = [core_inputs(inp, b, j, SEQ) for b in range(B) for j in range(2)]
    res = run_bass_kernel_spmd(nc, maps, core_ids=list(range(2 * B)))
    out = np.zeros((B, SEQ, D), np.float32)
    NT = SEQ // 128
    for b in range(B):
        ov = out[b].reshape(NT // 2, 2, 128, D)
        for j in range(2):
            ov[:, j] = np.asarray(res.results[2 * b + j]["y"]).reshape(NT // 2, 128, D)
    return out
```
